# Optimizing a Trainium2 kernel written in Bass

```python
import math
import jax, jax.numpy as jnp
from jax import lax
import numpy as np

D_MODEL = 1024
BATCH = 4
SEQ = 4096
DEPTH = 2

N_MEM = 256
N_MIXERS = 2
D_MIX = D_MODEL
ML_HEADS = 4
ML_HEAD_DIM = 192
ML_WIDTH = ML_HEADS * ML_HEAD_DIM
ML_CHUNK = 64
RG_WIDTH = 768
RG_BLOCKS = 8
RG_BLOCK_DIM = RG_WIDTH // RG_BLOCKS
RG_CONV = 4
RG_C = 8.0
XA_HEADS = 4
XA_HEAD_DIM = 64
XA_WIDTH = XA_HEADS * XA_HEAD_DIM
PEER_HEADS = 8
PEER_KEYS = 128
PEER_EXPERTS = PEER_KEYS * PEER_KEYS
PEER_HALF = 128
PEER_QDIM = 2 * PEER_HALF
PEER_TOPK = 16
PEER_TOK_BLOCK = 128
DN_ALPHA = (2 * DEPTH) ** 0.25
DN_BETA = (8 * DEPTH) ** -0.25
LN_EPS = 1e-5
N_A = (DEPTH + 1) // 2
N_B = DEPTH // 2
ML_IN = 4 * ML_WIDTH + 2 * ML_HEADS + XA_WIDTH
RG_IN = 2 * RG_WIDTH + XA_WIDTH

kernel_name = "hybrid_mlstm_rglru_peer_deepnorm"


def layer_norm(x, g, b):
    xf = x.astype(jnp.float32)
    mu = jnp.mean(xf, axis=-1, keepdims=True)
    var = jnp.mean(jnp.square(xf - mu), axis=-1, keepdims=True)
    y = (xf - mu) * lax.rsqrt(var + LN_EPS) * g.astype(jnp.float32) + b.astype(jnp.float32)
    return y.astype(x.dtype)


def cross_attention(xq, mem, w_kv):
    B, S, _ = xq.shape
    M = mem.shape[1]
    q = xq.reshape(B, S, XA_HEADS, XA_HEAD_DIM).astype(jnp.float32)
    kv = (mem @ w_kv).astype(jnp.float32)
    k = kv[..., :XA_WIDTH].reshape(B, M, XA_HEADS, XA_HEAD_DIM)
    v = kv[..., XA_WIDTH:].reshape(B, M, XA_HEADS, XA_HEAD_DIM)
    s = jnp.einsum('bshd,bmhd->bhsm', q, k) * (XA_HEAD_DIM ** -0.5)
    p = jax.nn.softmax(s, axis=-1)
    o = jnp.einsum('bhsm,bmhd->bshd', p, v)
    return o.reshape(B, S, XA_WIDTH).astype(xq.dtype)


def mlstm_mixer(zm, b_gates, norm_g):
    B, S, _ = zm.shape
    H, dh, L = ML_HEADS, ML_HEAD_DIM, ML_CHUNK
    nc = S // L
    W = ML_WIDTH
    q, k, v, o, g = jnp.split(zm, [W, 2 * W, 3 * W, 4 * W], axis=-1)
    g = g.astype(jnp.float32) + b_gates.astype(jnp.float32)
    ig = g[..., :H]
    lf = jax.nn.log_sigmoid(g[..., H:])

    def to_chunks(t):
        return t.astype(jnp.float32).reshape(B, nc, L, H, dh).transpose(1, 0, 3, 2, 4)

    def gate_chunks(t):
        return t.reshape(B, nc, L, H).transpose(1, 0, 3, 2)

    qc = to_chunks(q)
    kc = to_chunks(k) * (dh ** -0.5)
    vc = to_chunks(v)
    causal = jnp.tril(jnp.ones((L, L), dtype=bool))

    def step(carry, inp):
        C, n, m = carry
        qq, kk, vv, ii, ff = inp
        b = jnp.cumsum(ff, axis=-1)
        dmat = b[..., :, None] - b[..., None, :] + ii[..., None, :]
        dmat = jnp.where(causal, dmat, -jnp.inf)
        inter = b + m[..., None]
        mj = jnp.maximum(jnp.max(dmat, axis=-1), inter)
        w = jnp.exp(dmat - mj[..., None])
        s = jnp.einsum('bhld,bhsd->bhls', qq, kk) * w
        sc = jnp.exp(inter - mj)
        num = sc[..., None] * jnp.einsum('bhld,bhde->bhle', qq, C) + jnp.einsum('bhls,bhse->bhle', s, vv)
        den = sc * jnp.einsum('bhld,bhd->bhl', qq, n) + jnp.sum(s, axis=-1)
        h = num / jnp.maximum(jnp.abs(den), jnp.exp(-mj))[..., None]
        bl = b[..., -1]
        gl = bl[..., None] - b + ii
        m_new = jnp.maximum(bl + m, jnp.max(gl, axis=-1))
        wg = jnp.exp(gl - m_new[..., None])
        decay = jnp.exp(bl + m - m_new)
        C_new = decay[..., None, None] * C + jnp.einsum('bhl,bhld,bhle->bhde', wg, kk, vv)
        n_new = decay[..., None] * n + jnp.einsum('bhl,bhld->bhd', wg, kk)
        return (C_new, n_new, m_new), h

    init = (jnp.zeros((B, H, dh, dh), jnp.float32), jnp.zeros((B, H, dh), jnp.float32),
            jnp.zeros((B, H), jnp.float32))
    _, hc = lax.scan(step, init, (qc, kc, vc, gate_chunks(ig), gate_chunks(lf)))
    h = hc.transpose(1, 0, 3, 2, 4).reshape(B, S, H, dh)
    mu = jnp.mean(h, axis=-1, keepdims=True)
    var = jnp.mean(jnp.square(h - mu), axis=-1, keepdims=True)
    hn = (h - mu) * lax.rsqrt(var + LN_EPS) * norm_g.astype(jnp.float32).reshape(H, dh)
    y = hn.reshape(B, S, W) * jax.nn.sigmoid(o.astype(jnp.float32))
    return y.astype(zm.dtype)


def rglru_mixer(zr, conv_w, conv_b, w_a, b_a, w_x, b_x, lam):
    B, S, _ = zr.shape
    gate = zr[..., :RG_WIDTH]
    xr = zr[..., RG_WIDTH:]
    xr = lax.conv_general_dilated(xr, conv_w[:, None, :], window_strides=(1,),
                                  padding=[(RG_CONV - 1, 0)],
                                  dimension_numbers=('NWC', 'WIO', 'NWC'),
                                  feature_group_count=RG_WIDTH) + conv_b
    xb = xr.reshape(B, S, RG_BLOCKS, RG_BLOCK_DIM)
    r = jax.nn.sigmoid((jnp.einsum('bsgi,gij->bsgj', xb, w_a).reshape(B, S, RG_WIDTH) + b_a).astype(jnp.float32))
    i = jax.nn.sigmoid((jnp.einsum('bsgi,gij->bsgj', xb, w_x).reshape(B, S, RG_WIDTH) + b_x).astype(jnp.float32))
    log_a = -RG_C * r * jax.nn.softplus(-lam.astype(jnp.float32))
    a = jnp.exp(log_a)
    u = jnp.sqrt(-jnp.expm1(2.0 * log_a)) * (i * xr.astype(jnp.float32))

    def combine(e1, e2):
        a1, b1 = e1
        a2, b2 = e2
        return a1 * a2, a2 * b1 + b2

    _, h = lax.associative_scan(combine, (a, u), axis=1)
    y = h * jax.nn.gelu(gate.astype(jnp.float32), approximate=False)
    return y.astype(zr.dtype)


def peer_ffn(x, w_q, subkeys, u_tab, v_tab):
    B, S, D = x.shape
    n = PEER_TOK_BLOCK
    xt = x.reshape((B * S) // n, n, D)
    sk = subkeys.astype(jnp.float32)

    def block(xb):
        q = (xb @ w_q).reshape(n, PEER_HEADS, 2, PEER_HALF).astype(jnp.float32)
        s = jnp.einsum('nhpc,pkc->nhpk', q, sk)
        sv, si = lax.top_k(s, PEER_TOPK)
        cand = sv[:, :, 0, :, None] + sv[:, :, 1, None, :]
        cand_id = si[:, :, 0, :, None] * PEER_KEYS + si[:, :, 1, None, :]
        fv, fi = lax.top_k(cand.reshape(n, PEER_HEADS, PEER_TOPK * PEER_TOPK), PEER_TOPK)
        eid = jnp.take_along_axis(cand_id.reshape(n, PEER_HEADS, PEER_TOPK * PEER_TOPK), fi, axis=-1)
        gw = jax.nn.softmax(fv, axis=-1)
        u = u_tab[eid]
        act = jax.nn.gelu(jnp.einsum('nhkd,nd->nhk', u, xb).astype(jnp.float32), approximate=False)
        coef = (gw * act).astype(x.dtype)
        return jnp.einsum('nhk,nhkd->nd', coef, v_tab[eid])

    y = lax.map(block, xt)
    return y.reshape(B, S, D)


def setup_inputs(seed: int = 0) -> dict:
    key = jax.random.key(seed)
    ks = jax.random.split(key, 24)
    D = D_MODEL
    nrm = jax.random.normal
    x = nrm(ks[0], (BATCH, SEQ, D), jnp.float32)
    mem = nrm(ks[1], (BATCH, N_MEM, D), jnp.float32)
    mlstm_w_in = nrm(ks[2], (N_A, D, ML_IN), jnp.float32) * D ** -0.5
    f_bias = jnp.linspace(3.0, 6.0, ML_HEADS, dtype=jnp.float32)
    mlstm_b_gates = jnp.concatenate([
        0.1 * nrm(ks[3], (N_A, ML_HEADS), jnp.float32),
        f_bias[None, :] + 0.1 * nrm(ks[4], (N_A, ML_HEADS), jnp.float32)], axis=-1)
    mlstm_norm_g = 1.0 + 0.02 * nrm(ks[5], (N_A, ML_WIDTH), jnp.float32)
    rglru_w_in = nrm(ks[6], (N_B, D, RG_IN), jnp.float32) * D ** -0.5
    rglru_conv_w = nrm(ks[7], (N_B, RG_CONV, RG_WIDTH), jnp.float32) * RG_CONV ** -0.5
    rglru_conv_b = 0.01 * nrm(ks[8], (N_B, RG_WIDTH), jnp.float32)
    rglru_w_a = nrm(ks[9], (N_B, RG_BLOCKS, RG_BLOCK_DIM, RG_BLOCK_DIM), jnp.float32) * RG_BLOCK_DIM ** -0.5
    rglru_b_a = 0.01 * nrm(ks[10], (N_B, RG_WIDTH), jnp.float32)
    rglru_w_x = nrm(ks[11], (N_B, RG_BLOCKS, RG_BLOCK_DIM, RG_BLOCK_DIM), jnp.float32) * RG_BLOCK_DIM ** -0.5
    rglru_b_x = 0.01 * nrm(ks[12], (N_B, RG_WIDTH), jnp.float32)
    a0 = jax.random.uniform(ks[13], (N_B, RG_WIDTH), jnp.float32, 0.9, 0.999)
    p = a0 ** (1.0 / RG_C)
    rglru_lam = jnp.log(p) - jnp.log1p(-p)
    xattn_w_kv = nrm(ks[14], (DEPTH, D, 2 * XA_WIDTH), jnp.float32) * D ** -0.5
    w_out = nrm(ks[15], (DEPTH, D_MIX, D), jnp.float32) * (D_MIX ** -0.5) * DN_BETA
    ln1_g = 1.0 + 0.02 * nrm(ks[16], (DEPTH, D), jnp.float32)
    ln1_b = 0.01 * nrm(ks[17], (DEPTH, D), jnp.float32)
    peer_w_q = nrm(ks[18], (DEPTH, D, PEER_HEADS * PEER_QDIM), jnp.float32) * D ** -0.5
    peer_subkeys = nrm(ks[19], (DEPTH, 2, PEER_KEYS, PEER_HALF), jnp.float32) * PEER_HALF ** -0.5
    peer_u = nrm(ks[20], (DEPTH, PEER_EXPERTS, D), jnp.float32) * D ** -0.5
    peer_v = nrm(ks[21], (DEPTH, PEER_EXPERTS, D), jnp.float32) * (D ** -0.5) * DN_BETA
    ln2_g = 1.0 + 0.02 * nrm(ks[22], (DEPTH, D), jnp.float32)
    ln2_b = 0.01 * nrm(ks[23], (DEPTH, D), jnp.float32)
    return {"x": x, "mem": mem,
            "mlstm_w_in": mlstm_w_in, "mlstm_b_gates": mlstm_b_gates, "mlstm_norm_g": mlstm_norm_g,
            "rglru_w_in": rglru_w_in, "rglru_conv_w": rglru_conv_w, "rglru_conv_b": rglru_conv_b,
            "rglru_w_a": rglru_w_a, "rglru_b_a": rglru_b_a, "rglru_w_x": rglru_w_x, "rglru_b_x": rglru_b_x,
            "rglru_lam": rglru_lam, "xattn_w_kv": xattn_w_kv, "w_out": w_out,
            "ln1_g": ln1_g, "ln1_b": ln1_b, "peer_w_q": peer_w_q, "peer_subkeys": peer_subkeys,
            "peer_u": peer_u, "peer_v": peer_v, "ln2_g": ln2_g, "ln2_b": ln2_b}


def reference(x, mem, mlstm_w_in, mlstm_b_gates, mlstm_norm_g, rglru_w_in, rglru_conv_w, rglru_conv_b,
              rglru_w_a, rglru_b_a, rglru_w_x, rglru_b_x, rglru_lam, xattn_w_kv, w_out,
              ln1_g, ln1_b, peer_w_q, peer_subkeys, peer_u, peer_v, ln2_g, ln2_b):
    for i in range(DEPTH):
        j = i // N_MIXERS
        if i % N_MIXERS == 0:
            z = x @ mlstm_w_in[j]
            hm = mlstm_mixer(z[..., :ML_IN - XA_WIDTH], mlstm_b_gates[j], mlstm_norm_g[j])
            xq = z[..., ML_IN - XA_WIDTH:]
        else:
            z = x @ rglru_w_in[j]
            hm = rglru_mixer(z[..., :RG_IN - XA_WIDTH], rglru_conv_w[j], rglru_conv_b[j],
                             rglru_w_a[j], rglru_b_a[j], rglru_w_x[j], rglru_b_x[j], rglru_lam[j])
            xq = z[..., RG_IN - XA_WIDTH:]
        ha = cross_attention(xq, mem, xattn_w_kv[i])
        mix = jnp.concatenate([hm, ha], axis=-1) @ w_out[i]
        x = layer_norm(DN_ALPHA * x + mix, ln1_g[i], ln1_b[i])
        y = peer_ffn(x, peer_w_q[i], peer_subkeys[i], peer_u[i], peer_v[i])
        x = layer_norm(DN_ALPHA * x + y, ln2_g[i], ln2_b[i])
    return x
```

```python
import numpy as np
from contextlib import ExitStack
import concourse.bass as bass
import concourse.mybir as mybir
from concourse.bass_utils import run_bass_kernel_spmd

F32 = mybir.dt.float32
BF16 = mybir.dt.bfloat16
U32 = mybir.dt.uint32
I32 = mybir.dt.int32
AF = mybir.ActivationFunctionType
ALU = mybir.AluOpType
AX = mybir.AxisListType


class Prog:
    NDS = 24

    def __init__(self, nc, es):
        self.nc = nc
        self.es = es
        self.eng = {"pe": nc.tensor, "dve": nc.vector, "act": nc.scalar, "pool": nc.gpsimd, "sp": nc.sync}
        self.sem = {k: es.enter_context(nc.semaphore("sem_" + k)) for k in self.eng}
        self.cnt = {k: 0 for k in self.eng}
        self.seen = {k: {} for k in self.eng}
        self.dsem = []
        self.dcnt = []
        self.dq = {}
        for q, n in (("sp", 20), ("pool", 8), ("act", 4)):
            self.dq[q] = [len(self.dsem) + i for i in range(n)]
            self.dsem += [es.enter_context(nc.semaphore(f"dsem_{q}{i}")) for i in range(n)]
            self.dcnt += [0] * n
        self.dnext = {q: 0 for q in self.dq}
        self.last_w = {}
        self.readers = {}
        self.n_ins = 0

    def _semof(self, kind, name):
        return self.sem[name] if kind == "e" else self.dsem[name]

    def _wait_ev(self, eng, kind, name, c):
        if self.seen[eng].get((kind, name), 0) >= c:
            return
        self.seen[eng][(kind, name)] = c
        self.eng[eng].wait_ge(self._semof(kind, name), c)

    def _wait(self, eng, R, W):
        deps = []
        for k in R:
            if k in self.last_w:
                deps.append(self.last_w[k])
        for k in W:
            if k in self.last_w:
                deps.append(self.last_w[k])
            deps.extend(self.readers.get(k, ()))
        best = {}
        for kind, name, c in deps:
            if kind == "e" and name == eng and eng == "pe":
                continue
            key = (kind, name)
            if c > best.get(key, 0):
                best[key] = c
        for (kind, name), c in best.items():
            self._wait_ev(eng, kind, name, c)

    def _commit(self, ev, R, W):
        for k in W:
            self.last_w[k] = ev
            self.readers[k] = []
        for k in R:
            self.readers.setdefault(k, []).append(ev)

    limit = None
    stop_at = None

    def cp(self, name):
        if self.stop_at is not None and name == self.stop_at and self.limit is None:
            self.limit = self.n_ins

    def op(self, eng, fn, R=(), W=()):
        if self.limit is not None and self.n_ins >= self.limit:
            return None
        W = list(W) + [k for k in R if k.startswith("ps") and k not in W]
        self._wait(eng, R, W)
        ins = fn(self.eng[eng])
        self.cnt[eng] += 1
        ins.then_inc(self.sem[eng], 1)
        self._commit(("e", eng, self.cnt[eng]), R, W)
        self.n_ins += 1
        return ins

    def dma(self, q, out, in_, R=(), W=(), chan=None):
        if self.limit is not None and self.n_ins >= self.limit and chan != "out":
            return None
        i = self.dq[q][self.dnext[q]]
        self.dnext[q] = (self.dnext[q] + 1) % len(self.dq[q])
        if self.dcnt[i]:
            self._wait_ev(q, "d", i, self.dcnt[i])
        self._wait(q, R, W)
        ins = self.eng[q].dma_start(out=out, in_=in_)
        self.dcnt[i] += 16
        ins.then_inc(self.dsem[i], 16)
        self._commit(("d", i, self.dcnt[i]), R, W)
        return ins

    def allgather_pairs(self, src, dst, R=(), W=()):
        self._wait("pool", R, W)
        sem = self.es.enter_context(self.nc.semaphore(f"cc_sem{len(self.dsem)}"))
        self.dsem.append(sem)
        self.dcnt.append(0)
        i = len(self.dsem) - 1
        ins = self.nc.gpsimd.collective_compute("AllGather", ALU.bypass, replica_groups=[[0, 1], [2, 3], [4, 5], [6, 7]],
                                               ins=[src.opt()], outs=[dst.opt()])
        ins.then_inc(sem)
        self.dcnt[i] = 1
        self._commit(("d", i, 1), R, W)

    def barrier(self):
        for e in self.eng:
            for k, c in self.cnt.items():
                if c:
                    self._wait_ev(e, "e", k, c)
            for i, c in enumerate(self.dcnt):
                if c:
                    self._wait_ev(e, "d", i, c)

    def wait_all(self, eng="sp"):
        for k, c in self.cnt.items():
            if c and k != eng:
                self._wait_ev(eng, "e", k, c)
        for i, c in enumerate(self.dcnt):
            if c:
                self._wait_ev(eng, "d", i, c)


D = 1024
T = 2048
NT = T // 128
G = 256
NG = T // G
H = 4
DH = 192
L = 64
ALPHA = float(4 ** 0.25)
EPS = 1e-5
NB0 = 14


def _pad_cols(a, n):
    out = np.zeros((a.shape[0], n), np.float32)
    out[:, : a.shape[1]] = a
    return out


def _blockify(cols):
    return np.ascontiguousarray(cols.reshape(8, 128, 512).transpose(1, 0, 2))


def prep_l0_weights(w_in, w_out):
    q = w_in[:, 0:768]; k = w_in[:, 768:1536]; v = w_in[:, 1536:2304]; o = w_in[:, 2304:3072]
    g = w_in[:, 3072:3080]; xq = w_in[:, 3080:3336]
    blocks = []
    for src in (q, k):
        for hp in range(2):
            cs = []
            for h in (2 * hp, 2 * hp + 1):
                cs.append(src[:, h * 192: h * 192 + 128])
                cs.append(_pad_cols(src[:, h * 192 + 128: (h + 1) * 192], 128))
            blocks.append(np.concatenate(cs, axis=1))
    blocks.append(_pad_cols(xq, 512))
    for src in (k, v, o):
        blocks.append(_pad_cols(src[:, 0:384], 512))
        blocks.append(_pad_cols(src[:, 384:768], 512))
    blocks.append(_pad_cols(g, 512))
    blocks.append(w_out[:, 0:512]); blocks.append(w_out[:, 512:1024])
    return np.stack([_blockify(np.ascontiguousarray(b, dtype=np.float32)) for b in blocks])


class Builder:
    def __init__(self, es, debug=()):
        self.es = es
        self.debug = set(debug)
        self.nc = nc = bass.Bass("TRN2", target_bir_lowering=False)
        self.P = Prog(nc, es)
        self.din = {}
        self.dout = {}
        self.wb_i = 0

    def inp(self, name, shape, dt=F32):
        ap = self.nc.dram_tensor(name, list(shape), dt, kind="ExternalInput").ap()
        self.din[name] = ap
        return ap

    def outp(self, name, shape, dt=F32):
        ap = self.nc.dram_tensor(name, list(shape), dt, kind="ExternalOutput").ap()
        self.dout[name] = ap
        return ap

    def scratch(self, name, shape, dt):
        return self.nc.dram_tensor(name, list(shape), dt, kind="Internal").ap()

    def sb(self, name, shape, dt=F32, es=None):
        return (es or self.es).enter_context(self.nc.sbuf_tensor(name + "_sb", list(shape), dt))

    def wload(self, src):
        i = self.wb_i % len(self.wblk)
        self.wb_i = (i + 1) % len(self.wblk)
        self.P.dma("sp", self.wblk[i][:].rearrange("p k c -> p (k c)"), src[0], R=[src[1]], W=[f"wblk{i}"], chan="w")
        return i

    class Sched:
        def __init__(self, b, srcs):
            self.b = b; self.srcs = srcs; self.n = 0; self.cur = None
            self.single = len(b.wblk) == 1
            self.nxt = b.wload(srcs[0]) if (srcs and not self.single) else None

        def get(self):
            if self.single:
                self.n += 1
                return self.b.wload(self.srcs[self.n - 1])
            cur = self.nxt
            self.n += 1
            self.nxt = self.b.wload(self.srcs[self.n]) if self.n < len(self.srcs) else None
            return cur

    def setup(self):
        nc, P = self.nc, self.P
        sb = self.sb
        self.ps = self.es.enter_context(nc.psum_tensor("ps", [128, 8, 512], F32))
        self.X = sb("X", [128, NT, D])
        self.ident = sb("ident", [128, 128])
        self.identb = sb("identb", [128, 128], BF16)
        self.cmask = sb("cmask", [64, 64], BF16)
        self.flag = sb("flag", [128, 1])
        d_ident = self.inp("ident", [128, 128])
        d_cmask = self.inp("cmask", [64, 64])
        d_flag = self.inp("flag", [128, 1])
        cm32 = sb("cm32", [64, 64])
        P.dma("sp", self.ident[:], d_ident, W=["ident"])
        P.dma("sp", cm32[:], d_cmask, W=["cm32"])
        P.dma("sp", self.flag[:], d_flag, W=["flag"])
        P.op("dve", lambda e: e.tensor_copy(out=self.identb[:], in_=self.ident[:]), R=["ident"], W=["identb"])
        P.op("dve", lambda e: e.tensor_copy(out=self.cmask[:], in_=cm32[:]), R=["cm32"], W=["cmask"])
        d_x = self.inp("x_own", [T, D])
        for q4 in range(4):
            P.dma("sp", self.X[:, 4 * q4:4 * q4 + 4, :],
                  d_x[512 * q4:512 * (q4 + 1), :].rearrange("(t p) d -> p t d", p=128),
                  W=[f"X{t}" for t in range(4 * q4, 4 * q4 + 4)])

    def psb(self, b0, nb=1):
        return [f"ps{b}" for b in range(b0, b0 + nb)]

    def make_xT(self, xTg, src_tiles, src_keys, kout):
        P, ps = self.P, self.ps
        for tt in range(2):
            for kc in range(8):
                P.op("pe", lambda e, tt=tt, kc=kc: e.transpose(
                    out=ps[:, kc // 4, (kc % 4) * 128:(kc % 4 + 1) * 128],
                    in_=src_tiles[tt][:, kc * 128:(kc + 1) * 128], identity=self.ident[:]),
                    R=[src_keys[tt], "ident"], W=[f"ps{kc // 4}"])
            P.op("act", lambda e, tt=tt: e.copy(
                out=xTg[:, :, tt * 128:(tt + 1) * 128],
                in_=ps[:, 0:2, :].rearrange("p b (k c) -> p (b k) c", c=128)),
                R=["ps0", "ps1"], W=[kout])

    def xattn_setup(self, layer, es):
        P, ps, sb = self.P, self.ps, self.sb
        d_memT = self.din["memT"] if "memT" in self.din else self.inp("memT", [128, 8, 256])
        d_wkv = self.inp(f"wkv{layer}", [128, 8, 512])
        kxT = sb(f"kxT_{layer}", [64, 4, 256], BF16, es=es)
        vxaug = sb(f"vxaug_{layer}", [128, 2, 4, 65], BF16, es=es)
        with ExitStack() as tes:
            memT32 = sb(f"memT32_{layer}", [128, 8, 256], es=tes)
            wkv32 = sb(f"wkv32_{layer}", [128, 8, 512], es=tes)
            memTb = sb(f"memTb_{layer}", [128, 8, 256], BF16, es=tes)
            wkvb = sb(f"wkvb_{layer}", [128, 8, 512], BF16, es=tes)
            P.dma("sp", memT32[:], d_memT, W=["memT32"])
            P.dma("sp", wkv32[:], d_wkv, W=["wkv32"])
            P.op("dve", lambda e: e.tensor_copy(out=memTb[:], in_=memT32[:]), R=["memT32"], W=["memTb"])
            P.op("pool", lambda e: e.tensor_copy(out=wkvb[:], in_=wkv32[:]), R=["wkv32"], W=["wkvb"])
            for h in range(4):
                for kc in range(8):
                    P.op("pe", lambda e, h=h, kc=kc: e.matmul(
                        ps[0:64, 2, 0:256], lhsT=wkvb[:, kc, h * 64:(h + 1) * 64], rhs=memTb[:, kc, :],
                        start=(kc == 0), stop=(kc == 7)), R=["wkvb", "memTb"], W=["ps2"])
                P.op("act", lambda e, h=h: e.copy(out=kxT[:, h, :], in_=ps[0:64, 2, 0:256]), R=["ps2"], W=["kxT"])
            P.op("pool", lambda e: e.memset(vxaug[:], 1.0), W=["vxaug"])
            for mc in range(2):
                for kc in range(8):
                    P.op("pe", lambda e, mc=mc, kc=kc: e.matmul(
                        ps[:, 3, 0:256], lhsT=memTb[:, kc, mc * 128:(mc + 1) * 128], rhs=wkvb[:, kc, 256:512],
                        start=(kc == 0), stop=(kc == 7)), R=["wkvb", "memTb"], W=["ps3"])
                P.op("act", lambda e, mc=mc: e.copy(
                    out=vxaug[:, mc, :, 0:64], in_=ps[:, 3, 0:256].rearrange("p (h d) -> p h d", h=4)),
                    R=["ps3"], W=["vxaug"])
            P.barrier()
        return kxT, vxaug

    def xattn_chunk(self, kxT, vxaug, xqT, c0, ntok, out_ap, out_key, Esb, rsb):
        P, ps = self.P, self.ps
        for h in range(4):
            for mc in range(2):
                P.op("pe", lambda e, h=h, mc=mc: e.matmul(
                    ps[:, 0, (h * 2 + mc) * 64:(h * 2 + mc) * 64 + ntok],
                    lhsT=kxT[:, h, mc * 128:(mc + 1) * 128],
                    rhs=xqT[:, h, c0:c0 + ntok], start=True, stop=True),
                    R=["kxT", "xqT"], W=["ps0"])
        P.cp("xa_st")
        P.op("act", lambda e: e.activation(
            out=Esb[:, :, 0:ntok], in_=ps[:, 0, :].rearrange("p (j t) -> p j t", t=64)[:, :, 0:ntok],
            func=AF.Exp, scale=0.125), R=["ps0"], W=["Esb"])
        P.cp("xa_exp")
        Ov = ps[0:ntok, 1, 0:260].rearrange("p (h d) -> p h d", d=65)
        for h in range(4):
            for mc in range(2):
                P.op("pe", lambda e, h=h, mc=mc: e.matmul(
                    Ov[:, h, :], lhsT=Esb[:, h * 2 + mc, 0:ntok], rhs=vxaug[:, mc, h, :],
                    start=(mc == 0), stop=(mc == 1)), R=["Esb", "vxaug"], W=["ps1"])
        P.cp("xa_ov")
        P.op("dve", lambda e: e.reciprocal(out=rsb[0:ntok, :], in_=Ov[:, :, 64]), R=["ps1"], W=["rsb"])
        P.op("dve", lambda e: e.tensor_tensor(
            out=out_ap.rearrange("p (h d) -> p h d", d=64), in0=Ov[:, :, 0:64],
            in1=rsb[0:ntok, :].unsqueeze(2).broadcast_to([ntok, 4, 64]), op=ALU.mult),
            R=["ps1", "rsb"], W=[out_key])

    def outproj_ln(self, sched, mixT, g, lng, lnb, xs, st, ag):
        P, ps = self.P, self.ps
        banks = {0: (0, 1), 1: (5, 6)}
        for half in range(2):
            wi = sched.get()
            for tt in range(2):
                bk = banks[tt][half]
                for kc in range(8):
                    P.op("pe", lambda e, tt=tt, kc=kc, bk=bk, wi=wi: e.matmul(
                        ps[:, bk, :], lhsT=mixT[:, kc, tt * 128:(tt + 1) * 128], rhs=self.wblk[wi][:, kc, :],
                        start=(kc == 0), stop=(kc == 7)), R=["mixT", f"wblk{wi}"], W=[f"ps{bk}"])
        P.cp("op_mm")
        for tt in range(2):
            t = 2 * g + tt
            b0 = banks[tt][0]
            self.resid_ln(t, self.ps[:, b0:b0 + 2, :], [f"ps{b0}", f"ps{b0 + 1}"], lng, lnb, xs, st, ag)

    def resid_ln(self, t, y_ap, y_keys, lng, lnb, xs, st, ag):
        P = self.P
        Xt = self.X[:, t, :]
        wk = Xt if xs is None else xs[:]
        kk = f"X{t}" if xs is None else "xs"
        P.op("dve", lambda e: e.scalar_tensor_tensor(
            out=wk.rearrange("p (b c) -> p b c", b=2), in0=Xt.rearrange("p (b c) -> p b c", b=2),
            scalar=ALPHA, in1=y_ap, op0=ALU.mult, op1=ALU.add), R=[f"X{t}"] + y_keys, W=[kk])
        P.cp("ln_a")
        for hf in range(2):
            P.op("dve", lambda e, hf=hf: e.bn_stats(out=st[:, hf, :], in_=wk[:, hf * 512:(hf + 1) * 512]),
                 R=[kk], W=[f"st{hf}"])
        P.op("dve", lambda e: e.bn_aggr(out=ag[:, 0:2], in_=st[:].rearrange("p a b -> p (a b)")),
             R=["st0", "st1"], W=["ag"])
        P.cp("ln_b")
        P.op("act", lambda e: e.activation(out=ag[:, 2:3], in_=ag[:, 1:2], func=AF.Sqrt, bias=EPS),
             R=["ag"], W=["ag2"])
        P.op("dve", lambda e: e.reciprocal(out=ag[:, 3:4], in_=ag[:, 2:3]), R=["ag2"], W=["ag3"])
        P.op("dve", lambda e: e.tensor_scalar(out=wk, in0=wk, scalar1=ag[:, 0:1], scalar2=ag[:, 3:4],
                                              op0=ALU.subtract, op1=ALU.mult), R=[kk, "ag", "ag3"], W=[kk])
        P.cp("ln_c")
        P.op("dve", lambda e: e.tensor_tensor(out=wk, in0=wk, in1=lng[:], op=ALU.mult), R=[kk, "lng"], W=[kk])
        P.op("dve", lambda e: e.tensor_tensor(out=Xt, in0=wk, in1=lnb[:], op=ALU.add), R=[kk, "lnb"], W=[f"X{t}"])

    def l0_mixer(self, npre=NG, nmain=NG):
        nc, P, ps, sb = self.nc, self.P, self.ps, self.sb
        ident = self.ident
        with ExitStack() as es:
            d_w0 = self.inp("w0", [NB0, 128, 4096])
            w0b = self.scratch("w0b", [NB0, 128, 4096], BF16)
            for b in range(NB0):
                P.dma("pool", w0b[b].rearrange("p (a c) -> p a c", c=2048), d_w0[b].rearrange("p (a c) -> p a c", c=2048), W=[f"w0b{b}"])
            P.barrier()
            d_xpre = self.inp("x_pre", [T, D])
            d_bgi = self.inp("bgi", [4, 1]); d_bgf = self.inp("bgf", [4, 1])
            d_ng = self.inp("norm_g", [768]); d_lng = self.inp("ln1g0", [D]); d_lnb = self.inp("ln1b0", [D])
            kxT, vxaug = self.xattn_setup(0, es)
            S = lambda n, sh, dt=F32: sb(n, sh, dt, es=es)
            self.wblk = [S("wblk0_m0", [128, 8, 512], BF16), S("wblk1_m0", [128, 8, 512], BF16)]
            xpre = S("xpre", [128, 2, D]); xTg = S("xTg", [128, 8, G], BF16)
            qT = S("qT", [128, 8, G], BF16); kT = S("kT", [128, 8, G], BF16); xqT = S("xqT", [64, 4, G], BF16)
            k_tm = S("k_tm", [64, 4, 768], BF16); vaug = S("vaug", [64, 4, 4, 193], BF16); so = S("so", [64, 4, 768], BF16)
            g_tm = S("g_tm", [64, 4, 8])
            bgi = S("bgi_s", [4, 1]); nbgf = S("nbgf", [4, 1]); zer = S("zer", [4, G])
            ig = S("ig", [4, G]); lsp = S("lsp", [4, G]); Lc = S("Lc", [4, G]); gam = S("gam", [4, G]); Mx = S("Mx", [4, G])
            t1 = S("t1", [4, G]); bet = S("bet", [4, G]); alp = S("alp", [4, G]); flo = S("flo", [4, G])
            Lcar = S("Lcar", [4, 1]); Mcar = S("Mcar", [4, 1]); Mprev = S("Mprev", [4, 4]); dec = S("dec", [4, 4])
            sel = S("sel", [4, 4, 128]); gtm = S("gtm", [64, 4, 3, 4]); decb = S("decb", [128, 4, 4])
            CA = S("CA", [128, 4, 193]); CB = S("CB", [64, 4, 193]); CAb = S("CAb", [128, 4, 193], BF16); CBb = S("CBb", [64, 4, 193], BF16)
            SmT = S("SmT", [64, 4, 64], BF16); num = S("num", [64, 4, 193]); hr = S("hr", [64, 4, 192]); sq = S("sq", [64, 4, 192])
            sm = S("sm", [64, 8, 4]); mix = [S("mix0", [64, D], BF16), S("mix1", [64, D], BF16)]
            mixT = S("mixT", [128, 8, G], BF16); Esb = S("Esb", [128, 8, 64], BF16); rsb = S("rsb", [64, 4])
            ngb = S("ngb", [64, 768]); lng = S("lng", [128, D]); lnb = S("lnb", [128, D])
            xs = S("xs", [128, D]); st = S("st", [128, 2, 6]); ag = S("ag", [128, 4])
            P.dma("sp", bgi[:], d_bgi, W=["bgi"]); P.dma("sp", nbgf[:], d_bgf, W=["nbgf"])
            P.dma("sp", ngb[:], d_ng.partition_broadcast(64), W=["ngb"])
            P.dma("sp", lng[:], d_lng.partition_broadcast(128), W=["lng"])
            P.dma("sp", lnb[:], d_lnb.partition_broadcast(128), W=["lnb"])
            P.op("dve", lambda e: e.tensor_scalar(out=nbgf[:], in0=nbgf[:], scalar1=-1.0, scalar2=None, op0=ALU.mult), R=["nbgf"], W=["nbgf"])
            for t_, k_ in ((zer, "zer"), (Lcar, "Lcar"), (Mcar, "Mcar"), (CA, "CA"), (CB, "CB")):
                P.op("pool", lambda e, t_=t_: e.memset(t_[:], 0.0), W=[k_])
            for h in range(4):
                P.op("dve", lambda e, h=h: e.tensor_copy(out=sel[:, h, :], in_=ident[0:4, h:h + 1].broadcast_to([4, 128])),
                     R=["ident"], W=["sel"])
            srcs = []
            for g in range(npre):
                srcs += [(w0b[b], f"w0b{b}") for b in (11, 5, 6, 7, 8)]
            for g in range(nmain):
                srcs += [(w0b[b], f"w0b{b}") for b in (11, 0, 1, 2, 3, 4, 5, 6, 7, 8, 9, 10, 12, 13)]
            sched = self.Sched(self, srcs)
            KS = float(DH ** -0.5)
            mixi = 0

            def tm_proj(wi, ck, ncols):
                bk = 2 + (ck % 2)
                for kc in range(8):
                    P.op("pe", lambda e, kc=kc: e.matmul(
                        ps[0:64, bk, 0:ncols], lhsT=xTg[:, kc, ck * 64:(ck + 1) * 64], rhs=self.wblk[wi][:, kc, 0:ncols],
                        start=(kc == 0), stop=(kc == 7)), R=["xTg", f"wblk{wi}"], W=[f"ps{bk}"])
                return bk

            for phase in (0, 1):
                for g in range(nmain if phase else npre):
                    main = phase == 1
                    if main:
                        srct = [self.X[:, 2 * g + tt, :] for tt in range(2)]; srck = [f"X{2 * g + tt}" for tt in range(2)]
                    else:
                        P.dma("sp", xpre[:], d_xpre[g * G:(g + 1) * G, :].rearrange("(t p) d -> p t d", p=128), W=["xpre0", "xpre1"])
                        srct = [xpre[:, tt, :] for tt in range(2)]; srck = ["xpre0", "xpre1"]
                    self.make_xT(xTg, srct, srck, "xTg")
                    wi = sched.get()
                    for ck in range(4):
                        bk = tm_proj(wi, ck, 8)
                        P.op("act", lambda e, ck=ck, bk=bk: e.copy(out=g_tm[:, ck, :], in_=ps[0:64, bk, 0:8]), R=[f"ps{bk}"], W=["g_tm"])
                    for ck in range(4):
                        for j in range(2):
                            P.op("pe", lambda e, ck=ck, j=j: e.transpose(
                                out=ps[0:4, 4, j * 256 + ck * 64: j * 256 + (ck + 1) * 64],
                                in_=g_tm[:, ck, 4 * j:4 * j + 4], identity=ident[0:64, 0:64]), R=["g_tm", "ident"], W=["ps4"])
                    if main:
                        P.op("dve", lambda e: e.tensor_scalar(out=ig[:], in0=ps[0:4, 4, 0:256], scalar1=bgi[:, 0:1], scalar2=None, op0=ALU.add),
                             R=["ps4", "bgi"], W=["ig"])
                    else:
                        P.op("dve", lambda e: e.tensor_scalar(out=ig[:], in0=ps[0:4, 4, 0:256], scalar1=bgi[:, 0:1], scalar2=self.flag[0:4, 0:1],
                                                              op0=ALU.add, op1=ALU.mult), R=["ps4", "bgi", "flag"], W=["ig"])
                    P.op("act", lambda e: e.activation(out=t1[:], in_=ps[0:4, 4, 256:512], func=AF.Exp, bias=nbgf[:, 0:1], scale=-1.0),
                         R=["ps4", "nbgf"], W=["t1"])
                    P.op("act", lambda e: e.activation(out=lsp[:], in_=t1[:], func=AF.Ln, bias=1.0), R=["t1"], W=["lsp"])
                    if not main:
                        P.op("dve", lambda e: e.tensor_scalar(out=lsp[:], in0=lsp[:], scalar1=self.flag[0:4, 0:1], scalar2=None, op0=ALU.mult),
                             R=["lsp", "flag"], W=["lsp"])
                    P.op("dve", lambda e: e.tensor_tensor_scan(out=Lc[:], data0=lsp[:], data1=zer[:], initial=Lcar[:, 0:1], op0=ALU.add, op1=ALU.add),
                         R=["lsp", "zer", "Lcar"], W=["Lc"])
                    P.op("dve", lambda e: e.tensor_tensor(out=gam[:], in0=ig[:], in1=Lc[:], op=ALU.add), R=["ig", "Lc"], W=["gam"])
                    P.op("dve", lambda e: e.tensor_tensor_scan(out=Mx[:], data0=gam[:], data1=gam[:], initial=Mcar[:, 0:1], op0=ALU.max, op1=ALU.max),
                         R=["gam", "Mcar"], W=["Mx"])
                    Mend = Mx[:].rearrange("p (c l) -> p c l", l=64)[:, :, 63]
                    P.op("dve", lambda e: e.tensor_copy(out=Mprev[:, 0:1], in_=Mcar[:, 0:1]), R=["Mcar"], W=["Mprev"])
                    P.op("dve", lambda e: e.tensor_copy(out=Mprev[:, 1:4], in_=Mend[:, 0:3]), R=["Mx"], W=["Mprev"])
                    P.op("dve", lambda e: e.tensor_copy(out=Mcar[:, 0:1], in_=Mx[:, G - 1:G]), R=["Mx", "Mprev"], W=["Mcar"])
                    P.op("dve", lambda e: e.tensor_copy(out=Lcar[:, 0:1], in_=Lc[:, G - 1:G]), R=["Lc"], W=["Lcar"])
                    P.op("dve", lambda e: e.tensor_tensor(out=dec[:], in0=Mprev[:], in1=Mend, op=ALU.subtract), R=["Mprev", "Mx"], W=["dec"])
                    P.op("act", lambda e: e.activation(out=dec[:], in_=dec[:], func=AF.Exp), R=["dec"], W=["dec"])
                    Mend_bc = Mend.unsqueeze(2).broadcast_to([4, 4, 64])
                    v3 = lambda t_: t_[:].rearrange("p (c l) -> p c l", l=64)
                    P.op("dve", lambda e: e.tensor_tensor(out=v3(bet), in0=v3(gam), in1=Mend_bc, op=ALU.subtract), R=["gam", "Mx"], W=["bet"])
                    P.op("act", lambda e: e.activation(out=bet[:], in_=bet[:], func=AF.Exp), R=["bet"], W=["bet"])
                    if main:
                        P.op("dve", lambda e: e.tensor_tensor(out=v3(alp), in0=Mend_bc, in1=v3(Mx), op=ALU.subtract), R=["Mx"], W=["alp"])
                        P.op("act", lambda e: e.activation(out=alp[:], in_=alp[:], func=AF.Exp), R=["alp"], W=["alp"])
                        P.op("dve", lambda e: e.tensor_tensor(out=flo[:], in0=Lc[:], in1=Mx[:], op=ALU.subtract), R=["Lc", "Mx"], W=["flo"])
                        P.op("act", lambda e: e.activation(out=flo[:], in_=flo[:], func=AF.Exp), R=["flo"], W=["flo"])
                    qs = (bet, alp, flo) if main else (bet,)
                    for ck in range(4):
                        for qi, qt in enumerate(qs):
                            P.op("pe", lambda e, ck=ck, qi=qi, qt=qt: e.transpose(
                                out=ps[0:64, 4, ck * 12 + qi * 4: ck * 12 + qi * 4 + 4], in_=qt[0:4, ck * 64:(ck + 1) * 64],
                                identity=ident[0:4, 0:4]), R=[("bet", "alp", "flo")[qi], "ident"], W=["ps4"])
                    if main:
                        P.op("act", lambda e: e.copy(out=gtm[:].rearrange("p c q h -> p (c q h)"), in_=ps[0:64, 4, 0:48]), R=["ps4"], W=["gtm"])
                    else:
                        P.op("act", lambda e: e.copy(out=gtm[:, :, 0, :], in_=ps[0:64, 4, 0:48].rearrange("p (c q h) -> p c q h", q=3, h=4)[:, :, 0, :]),
                             R=["ps4"], W=["gtm"])
                    for h in range(4):
                        P.op("pe", lambda e, h=h: e.matmul(
                            ps[:, 4, 64:80].rearrange("p (c h) -> p c h", h=4)[:, :, h], lhsT=sel[0:4, h, :], rhs=dec[0:4, 0:4],
                            start=True, stop=True), R=["sel", "dec", "gtm"], W=["ps4"])
                    P.op("act", lambda e: e.copy(out=decb[:].rearrange("p c h -> p (c h)"), in_=ps[:, 4, 64:80]), R=["ps4"], W=["decb"])
                    if main:
                        for blk, dst, scl in ((0, qT, 1.0), (1, qT, 1.0), (2, kT, KS), (3, kT, KS), (4, xqT, 1.0)):
                            wi = sched.get()
                            cw = 128 if blk < 4 else 64
                            for j in range(4):
                                bk = 2 + (j % 2)
                                for kc in range(8):
                                    P.op("pe", lambda e, j=j, kc=kc, bk=bk, wi=wi: e.matmul(
                                        ps[0:cw, bk, 0:G], lhsT=self.wblk[wi][:, kc, j * cw:(j + 1) * cw], rhs=xTg[:, kc, :],
                                        start=(kc == 0), stop=(kc == 7)), R=["xTg", f"wblk{wi}"], W=[f"ps{bk}"])
                                cj = (blk % 2) * 4 + j if blk < 4 else j
                                dk = {0: "qT", 1: "qT", 2: "kT", 3: "kT", 4: "xqT"}[blk]
                                P.op("act", lambda e, dst=dst, cj=cj, bk=bk, scl=scl: e.activation(
                                    out=dst[0:cw, cj, :], in_=ps[0:cw, bk, 0:G], func=AF.Copy, scale=scl), R=[f"ps{bk}"], W=[dk])
                    tms = [(0, "k"), (1, "k"), (0, "v"), (1, "v")] + ([(0, "o"), (1, "o")] if main else [])
                    for hp, kind in tms:
                        wi = sched.get()
                        for ck in range(4):
                            bk = tm_proj(wi, ck, 384)
                            src = ps[0:64, bk, 0:384]
                            if kind == "k":
                                P.op("act", lambda e, ck=ck, hp=hp, src=src: e.activation(
                                    out=k_tm[:, ck, hp * 384:(hp + 1) * 384], in_=src, func=AF.Copy, scale=KS), R=[f"ps{bk}"], W=["k_tm"])
                            elif kind == "o":
                                P.op("act", lambda e, ck=ck, hp=hp, src=src: e.activation(
                                    out=so[:, ck, hp * 384:(hp + 1) * 384], in_=src, func=AF.Sigmoid), R=[f"ps{bk}"], W=["so"])
                            else:
                                P.op("dve", lambda e, ck=ck, hp=hp, src=src: e.tensor_tensor(
                                    out=vaug[:, ck, 2 * hp:2 * hp + 2, 0:192], in0=src.rearrange("p (h d) -> p h d", h=2),
                                    in1=gtm[:, ck, 0, 2 * hp:2 * hp + 2].unsqueeze(2).broadcast_to([64, 2, 192]), op=ALU.mult),
                                    R=[f"ps{bk}", "gtm"], W=["vaug"])
                    for ck in range(4):
                        P.op("pool", lambda e, ck=ck: e.tensor_copy(out=vaug[:, ck, :, 192], in_=gtm[:, ck, 0, :]), R=["gtm"], W=["vaug"])
                    P.cp(f"proj{phase}")
                    for ck in range(4):
                        c0 = ck * 64
                        dbc = lambda n: decb[0:n, ck, :].unsqueeze(2).broadcast_to([n, 4, 193])
                        P.op("dve", lambda e: e.tensor_tensor(out=CA[:], in0=CA[:], in1=dbc(128), op=ALU.mult), R=["CA", "decb"], W=["CA"])
                        P.op("pool", lambda e: e.tensor_tensor(out=CB[:], in0=CB[:], in1=dbc(64), op=ALU.mult), R=["CB", "decb"], W=["CB"])
                        if main:
                            P.op("act", lambda e: e.copy(out=CAb[:], in_=CA[:]), R=["CA"], W=["CAb"])
                            P.op("act", lambda e: e.copy(out=CBb[:], in_=CB[:]), R=["CB"], W=["CBb"])
                            for h in range(4):
                                P.op("pe", lambda e, h=h: e.matmul(ps[0:64, 4, 256 + h * 64:256 + (h + 1) * 64], lhsT=kT[:, 2 * h, c0:c0 + 64],
                                                                   rhs=qT[:, 2 * h, c0:c0 + 64], start=True, stop=False), R=["kT", "qT"], W=["ps4"])
                                P.op("pe", lambda e, h=h: e.matmul(ps[0:64, 4, 256 + h * 64:256 + (h + 1) * 64], lhsT=kT[0:64, 2 * h + 1, c0:c0 + 64],
                                                                   rhs=qT[0:64, 2 * h + 1, c0:c0 + 64], start=False, stop=True), R=["kT", "qT"], W=["ps4"])
                            P.op("dve", lambda e: e.tensor_tensor(
                                out=SmT[:], in0=ps[0:64, 4, 256:512].rearrange("p (h l) -> p h l", h=4),
                                in1=self.cmask[:].unsqueeze(1).broadcast_to([64, 4, 64]), op=ALU.mult), R=["ps4", "cmask"], W=["SmT"])
                            Pv = ps[0:64, 5:7, :].rearrange("p b (h e) -> p (b h) e", h=2)
                            for h in range(4):
                                bk = 5 + h // 2
                                P.op("pe", lambda e, h=h: e.matmul(Pv[:, h, 0:193], lhsT=qT[:, 2 * h, c0:c0 + 64], rhs=CAb[:, h, :],
                                                                   start=True, stop=False), R=["qT", "CAb"], W=[f"ps{bk}"])
                                P.op("pe", lambda e, h=h: e.matmul(Pv[:, h, 0:193], lhsT=qT[0:64, 2 * h + 1, c0:c0 + 64], rhs=CBb[:, h, :],
                                                                   start=False, stop=False), R=["qT", "CBb"], W=[f"ps{bk}"])
                                P.op("pe", lambda e, h=h: e.matmul(Pv[:, h, 0:193], lhsT=SmT[:, h, :], rhs=vaug[:, ck, h, :],
                                                                   start=False, stop=True), R=["SmT", "vaug"], W=[f"ps{bk}"])
                        P.cp(f"P{phase}")
                        for hp in range(2):
                            dA = ps[:, 7, :].rearrange("p (h e) -> p h e", h=2)
                            dB = ps[0:64, 3, :].rearrange("p (h e) -> p h e", h=2)
                            for hh in range(2):
                                h = 2 * hp + hh
                                P.op("pe", lambda e, h=h, hh=hh: e.matmul(dA[:, hh, 0:193], lhsT=k_tm[:, ck, h * 192:h * 192 + 128], rhs=vaug[:, ck, h, :],
                                                                          start=True, stop=True), R=["k_tm", "vaug"], W=["ps7"])
                                P.op("pe", lambda e, h=h, hh=hh: e.matmul(dB[:, hh, 0:193], lhsT=k_tm[:, ck, h * 192 + 128:(h + 1) * 192], rhs=vaug[:, ck, h, :],
                                                                          start=True, stop=True), R=["k_tm", "vaug"], W=["ps3"])
                            rA = ["CA"] + (["CAb"] if main else [])
                            P.op("dve", lambda e, hp=hp, dA=dA: e.tensor_tensor(out=CA[:, 2 * hp:2 * hp + 2, :], in0=CA[:, 2 * hp:2 * hp + 2, :], in1=dA[:, :, 0:193], op=ALU.add),
                                 R=["CA", "ps7"], W=["CA"])
                            P.op("dve", lambda e, hp=hp, dB=dB: e.tensor_tensor(out=CB[:, 2 * hp:2 * hp + 2, :], in0=CB[:, 2 * hp:2 * hp + 2, :], in1=dB[:, :, 0:193], op=ALU.add),
                                 R=["CB", "ps3"], W=["CB"])
                        P.cp(f"state{phase}")
                        if not main:
                            continue
                        abc = gtm[:, ck, 1, :].unsqueeze(2).broadcast_to([64, 4, 193])
                        P.op("dve", lambda e: e.tensor_tensor(out=num[:], in0=Pv[:, :, 0:193], in1=abc, op=ALU.mult), R=["ps5", "ps6", "gtm"], W=["num"])
                        P.op("act", lambda e: e.activation(out=sm[:, 0, :], in_=num[:, :, 192], func=AF.Abs), R=["num"], W=["sm0"])
                        P.op("dve", lambda e: e.tensor_tensor(out=sm[:, 1, :], in0=sm[:, 0, :], in1=gtm[:, ck, 2, :], op=ALU.max), R=["sm0", "gtm"], W=["sm1"])
                        P.op("dve", lambda e: e.reciprocal(out=sm[:, 2, :], in_=sm[:, 1, :]), R=["sm1"], W=["sm2"])
                        P.op("dve", lambda e: e.tensor_tensor(out=hr[:], in0=num[:, :, 0:192], in1=sm[:, 2, :].unsqueeze(2).broadcast_to([64, 4, 192]), op=ALU.mult),
                             R=["num", "sm2"], W=["hr"])
                        P.cp("hr")
                        P.op("dve", lambda e: e.tensor_reduce(out=sm[:, 3, :], in_=hr[:], axis=AX.X, op=ALU.add), R=["hr"], W=["sm3"])
                        P.op("pool", lambda e: e.tensor_tensor(out=sq[:], in0=hr[:], in1=hr[:], op=ALU.mult), R=["hr"], W=["sq"])
                        P.op("dve", lambda e: e.tensor_reduce(out=sm[:, 4, :], in_=sq[:], axis=AX.X, op=ALU.add), R=["sq"], W=["sm4"])
                        P.op("dve", lambda e: e.tensor_scalar(out=sm[:, 3, :], in0=sm[:, 3, :], scalar1=1.0 / DH, scalar2=None, op0=ALU.mult), R=["sm3"], W=["sm3"])
                        P.op("dve", lambda e: e.tensor_tensor(out=sm[:, 5, :], in0=sm[:, 3, :], in1=sm[:, 3, :], op=ALU.mult), R=["sm3"], W=["sm5"])
                        P.op("dve", lambda e: e.scalar_tensor_tensor(out=sm[:, 6, :], in0=sm[:, 4, :], scalar=1.0 / DH, in1=sm[:, 5, :], op0=ALU.mult, op1=ALU.subtract),
                             R=["sm4", "sm5"], W=["sm6"])
                        P.op("act", lambda e: e.activation(out=sm[:, 6, :], in_=sm[:, 6, :], func=AF.Sqrt, bias=EPS), R=["sm6"], W=["sm6"])
                        P.op("dve", lambda e: e.reciprocal(out=sm[:, 7, :], in_=sm[:, 6, :]), R=["sm6"], W=["sm7"])
                        P.op("dve", lambda e: e.tensor_tensor(out=hr[:], in0=hr[:], in1=sm[:, 3, :].unsqueeze(2).broadcast_to([64, 4, 192]), op=ALU.subtract),
                             R=["hr", "sm3"], W=["hr"])
                        P.op("dve", lambda e: e.tensor_tensor(out=hr[:], in0=hr[:], in1=sm[:, 7, :].unsqueeze(2).broadcast_to([64, 4, 192]), op=ALU.mult),
                             R=["hr", "sm7"], W=["hr"])
                        hf = hr[:].rearrange("p h d -> p (h d)")
                        P.op("pool", lambda e: e.tensor_tensor(out=hf, in0=hf, in1=ngb[:], op=ALU.mult), R=["hr", "ngb"], W=["hr"])
                        mx_ = mix[mixi]; mk = f"mix{mixi}"; mixi ^= 1
                        P.op("pool", lambda e, mx_=mx_: e.tensor_tensor(out=mx_[:, 0:768], in0=hf, in1=so[:, ck, :], op=ALU.mult), R=["hr", "so"], W=[mk])
                        P.cp("hln")
                        self.xattn_chunk(kxT, vxaug, xqT, c0, 64, mx_[:, 768:1024], mk, Esb, rsb)
                        P.cp("xattn")
                        pb = ps[:, 2, :].bitcast(BF16)
                        for kc in range(8):
                            P.op("pe", lambda e, kc=kc, mx_=mx_: e.transpose(out=pb[:, kc * 64:(kc + 1) * 64], in_=mx_[:, kc * 128:(kc + 1) * 128],
                                                                             identity=self.identb[0:64, 0:64]), R=[mk, "identb"], W=["ps2"])
                        P.op("act", lambda e: e.copy(out=mixT[:, :, c0:c0 + 64], in_=pb[:, 0:512].rearrange("p (k t) -> p k t", t=64)), R=["ps2"], W=["mixT"])
                    P.cp(f"chunks{phase}")
                    if main:
                        self.outproj_ln(sched, mixT, g, lng, lnb, xs, st, ag)
            P.barrier()

    def finish(self, out_name="out"):
        P = self.P
        d_out = self.outp(out_name, [T, D])
        for q4 in range(4):
            P.dma("sp", d_out[512 * q4:512 * (q4 + 1), :].rearrange("(t p) d -> p t d", p=128),
                  self.X[:, 4 * q4:4 * q4 + 4, :], R=[f"X{t}" for t in range(4 * q4, 4 * q4 + 4)], chan="out")
        P.wait_all("sp")


def _consts():
    ident = np.eye(128, dtype=np.float32)
    cmask = np.triu(np.ones((64, 64), np.float32))
    return ident, cmask


def build(stages, limit=None, stop_at=None, **kw):
    es = ExitStack()
    b = Builder(es)
    b.P.limit = limit
    b.P.stop_at = stop_at
    b.setup()
    if "fused" in stages:
        P = b.P
        b.l0_mixer()
        b.peer(0)
        cc1src = b.scratch("cc1src", [4, D], F32); cc1dst = b.scratch("cc1dst", [8, D], F32)
        b.cc2src = b.scratch("cc2src", [RB, 8], F32); cc2dst = b.scratch("cc2dst", [2 * RB, 8], F32)
        P.dma("sp", cc1src[0:3], b.X[125:128, NT - 1, :], R=[f"X{NT - 1}"], W=["cc1src"])
        P.dma("sp", cc1src[3:4], b.X[127:128, NT - 1, :], R=[f"X{NT - 1}"], W=["cc1src"])
        P.barrier()
        P.allgather_pairs(cc1src, cc1dst, R=["cc1src"], W=["cc_halo"])
        P.barrier()
        b.l1(scan_only=True, halo_src=cc1dst[0:3])
        P.allgather_pairs(b.cc2src, cc2dst, R=["cc2src"], W=["cc_hend"])
        P.barrier()
        b.l1(scan_only=False, halo_src=cc1dst[0:3], hinit_src=cc2dst[0:RB])
        b.peer(1)
        b.finish()
        return b, es
    if "l0mix" in stages:
        b.l0_mixer(**{k: v for k, v in kw.items() if k in ("npre", "nmain")})
    if "peer0" in stages:
        b.peer(0, **{k: v for k, v in kw.items() if k == "ngroups"})
    if "l1scan" in stages:
        b.l1(scan_only=True)
    if "l1mix" in stages:
        b.l1(scan_only=False, **{k: v for k, v in kw.items() if k == "ngroups"})
    if "peer1" in stages:
        b.peer(1, **{k: v for k, v in kw.items() if k == "ngroups"})
    b.finish()
    return b, es


def core_inputs(inputs, stages, x_cur=None, hinit=None, halo=None):
    ident, cmask = _consts()
    x = inputs["x"] if x_cur is None else x_cur
    maps = []
    shared = {"ident": ident, "cmask": cmask}
    if "fused" in stages:
        stages = ["fused", "l0mix", "peer0", "l1scan", "l1mix", "peer1"]
    if "l0mix" in stages:
        shared["w0"] = prep_l0_weights(inputs["mlstm_w_in"][0], inputs["w_out"][0]).reshape(NB0, 128, 4096)
        shared["bgi"] = np.ascontiguousarray(inputs["mlstm_b_gates"][0, 0:4].reshape(4, 1))
        shared["bgf"] = np.ascontiguousarray(inputs["mlstm_b_gates"][0, 4:8].reshape(4, 1))
        shared["norm_g"] = np.ascontiguousarray(inputs["mlstm_norm_g"][0])
        shared["ln1g0"] = np.ascontiguousarray(inputs["ln1_g"][0]); shared["ln1b0"] = np.ascontiguousarray(inputs["ln1_b"][0])
        shared["wkv0"] = _blockify(np.ascontiguousarray(inputs["xattn_w_kv"][0]))
    for l in (0, 1):
        if f"peer{l}" in stages:
            ut, wqb, skT = prep_peer(inputs["peer_u"][l], inputs["peer_w_q"][l], inputs["peer_subkeys"][l])
            shared[f"ut{l}"] = ut; shared[f"wq{l}"] = wqb; shared[f"skT{l}"] = skT
            shared[f"pv{l}"] = np.ascontiguousarray(inputs["peer_v"][l]).reshape(128, 128, 1024)
            shared[f"ln2g{l}"] = np.ascontiguousarray(inputs["ln2_g"][l]); shared[f"ln2b{l}"] = np.ascontiguousarray(inputs["ln2_b"][l])
    if "l1scan" in stages or "l1mix" in stages:
        shared["w1"] = prep_l1_weights(inputs["rglru_w_in"][0], inputs["w_out"][1])
        shared["conv_w"] = np.ascontiguousarray(inputs["rglru_conv_w"][0].reshape(4, 8, 96).transpose(2, 1, 0))
        shared["conv_b"] = _chan(inputs["rglru_conv_b"][0]); shared["rg_ba"] = _chan(inputs["rglru_b_a"][0])
        shared["rg_bx"] = _chan(inputs["rglru_b_x"][0]); shared["rg_lam"] = _chan(inputs["rglru_lam"][0])
        shared["rg_wa"] = np.ascontiguousarray(inputs["rglru_w_a"][0].transpose(1, 0, 2))
        shared["rg_wx"] = np.ascontiguousarray(inputs["rglru_w_x"][0].transpose(1, 0, 2))
        if "l1mix" in stages:
            shared["ln1g1"] = np.ascontiguousarray(inputs["ln1_g"][1]); shared["ln1b1"] = np.ascontiguousarray(inputs["ln1_b"][1])
            shared["wkv1"] = _blockify(np.ascontiguousarray(inputs["xattn_w_kv"][1]))
    for c in range(8):
        bi, half = c // 2, c % 2
        m = dict(shared)
        if "l1scan" in stages or "l1mix" in stages:
            m["xhalo"] = np.ascontiguousarray(x[bi, T - 3:T]) if half else np.zeros((3, D), np.float32)
            m["hinit"] = np.zeros((RB, 8), np.float32) if hinit is None else np.ascontiguousarray(hinit[c])
            m["memT"] = np.ascontiguousarray(inputs["mem"][bi].T.reshape(8, 128, 256).transpose(1, 0, 2))
        m["x_own"] = np.ascontiguousarray(x[bi, half * T:(half + 1) * T])
        m["flag"] = np.full((128, 1), float(half), np.float32)
        if "l0mix" in stages:
            m["x_pre"] = np.ascontiguousarray(x[bi, 0:T]) if half else np.zeros((T, D), np.float32)
            m["memT"] = np.ascontiguousarray(inputs["mem"][bi].T.reshape(8, 128, 256).transpose(1, 0, 2))
        maps.append(m)
    return maps


DBG = {}
def prep_peer(u, wq, sk):
    ut = np.ascontiguousarray(u.reshape(128, 128, 8, 128).transpose(0, 3, 2, 1)).reshape(128, 128, 1024)
    wqb = np.stack([_blockify(np.ascontiguousarray(wq[:, b * 512:(b + 1) * 512])) for b in range(4)]).reshape(4, 128, 4096)
    skT = np.ascontiguousarray(sk.transpose(2, 0, 1))
    return ut, wqb, skT


def _peer_method(self, layer, ngroups=NG):
    nc, P, ps, sb = self.nc, self.P, self.ps, self.sb
    TN = 4
    with ExitStack() as es:
        S = lambda n, sh, dt=F32: sb(f"{n}_p{layer}", sh, dt, es=es)
        self.wblk = [S("wblk0", [128, 8, 512], BF16)]
        d_ut = self.inp(f"ut{layer}", [128, 128, 1024]); d_v = self.inp(f"pv{layer}", [128, 128, 1024])
        d_wq = self.inp(f"wq{layer}", [4, 128, 4096]); d_sk = self.inp(f"skT{layer}", [128, 2, 128])
        d_lng = self.inp(f"ln2g{layer}", [D]); d_lnb = self.inp(f"ln2b{layer}", [D])
        utb = self.scratch(f"utb{layer}", [128, 128, 1024], BF16); vb = self.scratch(f"vb{layer}", [128, 128, 1024], BF16)
        wqb = self.scratch(f"wqb{layer}", [4, 128, 4096], BF16)
        for b in range(4):
            P.dma("pool", wqb[b].rearrange("p (a c) -> p a c", c=2048), d_wq[b].rearrange("p (a c) -> p a c", c=2048), W=[f"wqb{layer}_{b}"])
        for i0 in range(0, 128, 4):
            P.dma("pool", utb[i0:i0 + 4], d_ut[i0:i0 + 4], W=[f"utb{layer}_{i0 // 4}"])
            P.dma("pool", vb[i0:i0 + 4], d_v[i0:i0 + 4], W=[f"vb{layer}_{i0 // 4}"])
        if DBG.get("cast_barrier", True):
            P.barrier()
        GT = S("GT", [128, 128, G], BF16)
        xTg = S("xTg", [128, 8, G], BF16); qT = S("qT", [128, 16, G], BF16)
        ublk = [S("ublk0", [128, 2, 1024], BF16), S("ublk1", [128, 2, 1024], BF16)]
        vblk = [S("vblk0", [128, 2, 1024], BF16), S("vblk1", [128, 2, 1024], BF16)]
        arA = S("arA", [128, 1024]); arB = S("arB", [128, 1024])
        s_sb = arA[:, 0:512].rearrange("p (j k) -> p j k", k=128); s2 = arA[:, 512:1024].rearrange("p (j k) -> p j k", k=128)
        eq = arA[:, 0:512].rearrange("p (h r a) -> p h r a", r=16, a=16); prod = arA[:, 512:1024].rearrange("p (h r a) -> p h r a", r=16, a=16)
        cand = arB[:, 0:512].rearrange("p (h a b) -> p h a b", a=16, b=16); cand2 = arB[:, 512:1024].rearrange("p (h a b) -> p h a b", a=16, b=16)
        sv = S("sv", [128, 16, 16]); si = S("si", [128, 16, 16], U32); sif = S("sif", [128, 16, 16])
        fv = S("fv", [128, 8, 16]); fp = S("fp", [128, 8, 16], U32); pf = S("pf", [128, 8, 16]); bfl = S("bfl", [128, 8, 16]); af = S("af", [128, 8, 16])
        ex = S("ex", [128, 8, 16]); zs = S("zs", [128, 8]); tri = S("tri", [128, 3, 128])
        hr3 = S("hr3", [128, 3, G])
        Bt = [S("Bt0", [128, TN, 128], BF16), S("Bt1", [128, TN, 128], BF16)]
        At = [S("At0", [128, TN, 128], BF16), S("At1", [128, TN, 128], BF16)]
        eqt = S("eqt", [128, TN, 128], BF16)
        ge = [S("ge0", [128, G], BF16), S("ge1", [128, G], BF16)]; coef = [S("coef0", [128, G], BF16), S("coef1", [128, G], BF16)]
        skT32 = S("skT32", [128, 2, 128]); skTb = S("skTb", [128, 2, 128], BF16)
        iot = S("iot", [128, 128]); iot16 = S("iot16", [128, 16])
        lng = S("lng", [128, D]); lnb = S("lnb", [128, D]); st = S("st", [128, 2, 6]); ag = S("ag", [128, 4])
        P.dma("sp", skT32[:], d_sk, W=["skT32"])
        P.op("dve", lambda e: e.tensor_copy(out=skTb[:], in_=skT32[:]), R=["skT32"], W=["skTb"])
        P.op("pool", lambda e: e.iota(iot[:], pattern=[[1, 128]], base=0, channel_multiplier=0, allow_small_or_imprecise_dtypes=True), W=["iot"])
        P.op("pool", lambda e: e.iota(iot16[:], pattern=[[1, 16]], base=0, channel_multiplier=0, allow_small_or_imprecise_dtypes=True), W=["iot16"])
        P.dma("sp", lng[:], d_lng.partition_broadcast(128), W=["lng"])
        P.dma("sp", lnb[:], d_lnb.partition_broadcast(128), W=["lnb"])
        tbi = [0]

        def load_tb(i0):
            b = tbi[0]; tbi[0] ^= 1
            P.dma("sp", ublk[b][:], utb[i0:i0 + 2].rearrange("i p c -> p i c"), R=[f"utb{layer}_{i0 // 4}"], W=[f"ublk{b}"])
            P.dma("sp", vblk[b][:], vb[i0:i0 + 2].rearrange("i p c -> p i c"), R=[f"vb{layer}_{i0 // 4}"], W=[f"vblk{b}"])
            return b

        for g in range(ngroups):
            self.make_xT(xTg, [self.X[:, 2 * g + tt, :] for tt in range(2)], [f"X{2 * g + tt}" for tt in range(2)], "xTg")
            sched = self.Sched(self, [(wqb[b], f"wqb{layer}_{b}") for b in range(4)])
            for blk in range(4):
                wi = sched.get()
                for j in range(4):
                    bk = 4 + (j % 2)
                    for kc in range(8):
                        P.op("pe", lambda e, j=j, kc=kc, bk=bk, wi=wi: e.matmul(
                            ps[:, bk, 0:G], lhsT=self.wblk[wi][:, kc, j * 128:(j + 1) * 128], rhs=xTg[:, kc, :],
                            start=(kc == 0), stop=(kc == 7)), R=["xTg", f"wblk{wi}"], W=[f"ps{bk}"])
                    P.op("act", lambda e, j=j, bk=bk, blk=blk: e.copy(out=qT[:, blk * 4 + j, :], in_=ps[:, bk, 0:G]), R=[f"ps{bk}"], W=["qT"])
            for tt in range(2):
                tok = slice(tt * 128, (tt + 1) * 128)
                for hh in range(4):
                    bk = hh
                    for j4 in range(4):
                        jj = hh * 4 + j4
                        P.op("pe", lambda e, jj=jj, j4=j4, bk=bk: e.matmul(
                            ps[:, bk, j4 * 128:(j4 + 1) * 128], lhsT=qT[:, jj, tok], rhs=skTb[:, jj % 2, :],
                            start=True, stop=True), R=["qT", "skTb"], W=[f"ps{bk}"])
                    P.op("act", lambda e, bk=bk: e.copy(out=s_sb, in_=ps[:, bk, :].rearrange("p (j k) -> p j k", k=128)),
                         R=[f"ps{bk}"], W=["arA"])
                    for j4 in range(4):
                        jj = hh * 4 + j4
                        P.op("dve", lambda e, jj=jj, j4=j4: e.max(out=sv[:, jj, 0:8], in_=s_sb[:, j4, :]), R=["arA"], W=["sv"])
                        P.op("dve", lambda e, jj=jj, j4=j4: e.max_index(out=si[:, jj, 0:8], in_max=sv[:, jj, 0:8], in_values=s_sb[:, j4, :]), R=["arA", "sv"], W=["si"])
                        P.op("dve", lambda e, jj=jj, j4=j4: e.match_replace(out=s2[:, j4, :], in_to_replace=sv[:, jj, 0:8], in_values=s_sb[:, j4, :], imm_value=-1e30),
                             R=["arA", "sv"], W=["arA"])
                        P.op("dve", lambda e, jj=jj, j4=j4: e.max(out=sv[:, jj, 8:16], in_=s2[:, j4, :]), R=["arA"], W=["sv"])
                        P.op("dve", lambda e, jj=jj, j4=j4: e.max_index(out=si[:, jj, 8:16], in_max=sv[:, jj, 8:16], in_values=s2[:, j4, :]), R=["arA", "sv"], W=["si"])
                P.op("dve", lambda e: e.tensor_copy(out=sif[:], in_=si[:]), R=["si"], W=["sif"])
                svv = sv[:].rearrange("p (h two) a -> p h two a", two=2)
                sfv = sif[:].rearrange("p (h two) a -> p h two a", two=2)
                for hq in range(4):
                    hs = slice(2 * hq, 2 * hq + 2)
                    P.op("dve", lambda e: e.tensor_tensor(out=cand, in0=svv[:, hs, 0, :].unsqueeze(3).broadcast_to([128, 2, 16, 16]),
                                                          in1=svv[:, hs, 1, :].unsqueeze(2).broadcast_to([128, 2, 16, 16]), op=ALU.add), R=["sv"], W=["arB"])
                    for h4 in range(2):
                        h = 2 * hq + h4
                        c1 = cand[:, h4].rearrange("p a b -> p (a b)"); c2 = cand2[:, h4].rearrange("p a b -> p (a b)")
                        P.op("dve", lambda e: e.max(out=fv[:, h, 0:8], in_=c1), R=["arB"], W=["fv"])
                        P.op("dve", lambda e: e.max_index(out=fp[:, h, 0:8], in_max=fv[:, h, 0:8], in_values=c1), R=["arB", "fv"], W=["fp"])
                        P.op("dve", lambda e: e.match_replace(out=c2, in_to_replace=fv[:, h, 0:8], in_values=c1, imm_value=-1e30), R=["arB", "fv"], W=["arB"])
                        P.op("dve", lambda e: e.max(out=fv[:, h, 8:16], in_=c2), R=["arB"], W=["fv"])
                        P.op("dve", lambda e: e.max_index(out=fp[:, h, 8:16], in_max=fv[:, h, 8:16], in_values=c2), R=["arB", "fv"], W=["fp"])
                gwv = tri[:, 2, :].rearrange("p (h r) -> p h r", r=16)
                P.op("dve", lambda e: e.tensor_tensor(out=ex[:], in0=fv[:], in1=fv[:, :, 0:1].broadcast_to([128, 8, 16]), op=ALU.subtract), R=["fv"], W=["ex"])
                P.op("act", lambda e: e.activation(out=ex[:], in_=ex[:], func=AF.Exp), R=["ex"], W=["ex"])
                P.op("dve", lambda e: e.tensor_reduce(out=zs[:], in_=ex[:], axis=AX.X, op=ALU.add), R=["ex"], W=["zs"])
                P.op("dve", lambda e: e.reciprocal(out=zs[:], in_=zs[:]), R=["zs"], W=["zs"])
                P.op("dve", lambda e: e.tensor_tensor(out=gwv, in0=ex[:], in1=zs[:].unsqueeze(2).broadcast_to([128, 8, 16]), op=ALU.mult), R=["ex", "zs"], W=["tri2"])
                P.op("dve", lambda e: e.tensor_single_scalar(out=pf[:].bitcast(U32), in_=fp[:], scalar=15, op=ALU.bitwise_and), R=["fp"], W=["pf"])
                P.op("dve", lambda e: e.tensor_copy(out=bfl[:], in_=pf[:].bitcast(U32)), R=["pf"], W=["bfl"])
                P.op("dve", lambda e: e.tensor_single_scalar(out=pf[:].bitcast(U32), in_=fp[:], scalar=4, op=ALU.logical_shift_right), R=["fp", "bfl"], W=["pf"])
                P.op("dve", lambda e: e.tensor_copy(out=af[:], in_=pf[:].bitcast(U32)), R=["pf"], W=["af"])
                i16 = iot16[:].unsqueeze(1).unsqueeze(1).broadcast_to([128, 2, 16, 16])
                for which, (srcidx, two) in enumerate(((af, 0), (bfl, 1))):
                    ov = tri[:, which, :].rearrange("p (h r) -> p h r", r=16)
                    for hq in range(4):
                        hs = slice(2 * hq, 2 * hq + 2)
                        P.op("dve", lambda e: e.tensor_tensor(out=eq, in0=srcidx[:, hs, :].unsqueeze(3).broadcast_to([128, 2, 16, 16]), in1=i16, op=ALU.is_equal),
                             R=["af", "bfl", "iot16"], W=["arA"])
                        P.op("dve", lambda e: e.tensor_tensor(out=prod, in0=eq, in1=sfv[:, hs, two, :].unsqueeze(2).broadcast_to([128, 2, 16, 16]), op=ALU.mult),
                             R=["arA", "sif"], W=["arA"])
                        P.op("dve", lambda e: e.tensor_reduce(out=ov[:, hs, :], in_=prod, axis=AX.X, op=ALU.add), R=["arA"], W=[f"tri{which}"])
                for q3 in range(3):
                    P.op("pe", lambda e, q3=q3: e.transpose(out=ps[:, 6, q3 * 128:(q3 + 1) * 128], in_=tri[:, q3, :], identity=self.ident[:]),
                         R=[f"tri{q3}", "ident"], W=["ps6"])
                P.op("act", lambda e: e.copy(out=hr3[:, :, tok], in_=ps[:, 6, 0:384].rearrange("p (q t) -> p q t", q=3)), R=["ps6"], W=["hr3"])
            P.cp(f"peerA{layer}")
            ib = iot[:].unsqueeze(1).broadcast_to([128, TN, 128])
            for tb in range(G // TN):
                t0 = tb * TN
                bi = tb % 2
                P.op("dve", lambda e: e.tensor_tensor(out=Bt[bi][:], in0=ib, in1=hr3[:, 1, t0:t0 + TN].unsqueeze(2).broadcast_to([128, TN, 128]), op=ALU.is_equal),
                     R=["iot", "hr3"], W=[f"Bt{bi}"])
                P.op("dve", lambda e: e.tensor_tensor(out=eqt[:], in0=ib, in1=hr3[:, 0, t0:t0 + TN].unsqueeze(2).broadcast_to([128, TN, 128]), op=ALU.is_equal),
                     R=["iot", "hr3"], W=["eqt"])
                P.op(DBG.get("at_eng", "dve"), lambda e: e.tensor_tensor(out=At[bi][:], in0=eqt[:], in1=hr3[:, 2, t0:t0 + TN].unsqueeze(2).broadcast_to([128, TN, 128]), op=ALU.mult),
                     R=["eqt", "hr3"], W=[f"At{bi}"])
                for t in range(TN):
                    tg = t0 + t
                    bk = (tg // 4) % 8
                    P.op("pe", lambda e, t=t, tg=tg, bk=bk: e.matmul(ps[:, bk, (tg % 4) * 128:(tg % 4 + 1) * 128], lhsT=Bt[bi][:, t, :], rhs=At[bi][:, t, :],
                                                                     start=True, stop=True), R=[f"Bt{bi}", f"At{bi}"], W=[f"ps{bk}"])
                if (t0 + TN) % 16 == 0 and not DBG.get("noevac"):
                    b0 = ((t0 + TN - 16) // 4) % 8
                    tq = t0 + TN - 16
                    P.op("act", lambda e: e.copy(out=GT[:, :, tq:tq + 16], in_=ps[:, b0:b0 + 4, :].rearrange("p b (t i) -> p i (b t)", t=4)),
                         R=[f"ps{b}" for b in range(b0, b0 + 4)], W=["GT"])
            P.cp(f"peerB{layer}")
            nxt = load_tb(0)
            for i in range(128):
                if i % 2 == 0:
                    cur = nxt
                    if i + 2 < 128:
                        nxt = load_tb(i + 2)
                ii = i % 2
                hb = 4 + (i % 2)
                for kc in range(8):
                    P.op("pe", lambda e, kc=kc: e.matmul(ps[:, hb, 0:G], lhsT=ublk[cur][:, ii, kc * 128:(kc + 1) * 128], rhs=xTg[:, kc, :],
                                                         start=(kc == 0), stop=(kc == 7)), R=[f"ublk{cur}", "xTg"], W=[f"ps{hb}"])
                gi = i % 2
                P.op("act", lambda e: e.activation(out=ge[gi][:], in_=ps[:, hb, 0:G], func=AF.Gelu), R=[f"ps{hb}"], W=[f"ge{gi}"])
                P.op("dve", lambda e: e.tensor_tensor(out=coef[gi][:], in0=ge[gi][:], in1=GT[:, i, :], op=ALU.mult), R=[f"ge{gi}", "GT"], W=[f"coef{gi}"])
                for tt in range(2):
                    for half in range(2):
                        yb = 2 * tt + half
                        P.op("pe", lambda e, tt=tt, half=half, yb=yb: e.matmul(
                            ps[:, yb, :], lhsT=coef[gi][:, tt * 128:(tt + 1) * 128], rhs=vblk[cur][:, ii, half * 512:(half + 1) * 512],
                            start=(i == 0), stop=(i == 127)), R=[f"coef{gi}", f"vblk{cur}"], W=[f"ps{yb}"])
            P.cp(f"peerC{layer}")
            for tt in range(2):
                self.resid_ln(2 * g + tt, ps[:, 2 * tt:2 * tt + 2, :], [f"ps{2 * tt}", f"ps{2 * tt + 1}"], lng, lnb, None, st, ag)
        P.barrier()


Builder.peer = _peer_method


NB1 = 9
RB = 96


def prep_l1_weights(w_in, w_out):
    gate = w_in[:, 0:768]; xr = w_in[:, 768:1536]; xq = w_in[:, 1536:1792]
    blocks = []
    for src in (gate, xr):
        for hb in range(2):
            blocks.append(np.concatenate([_pad_cols(src[:, g * 96:(g + 1) * 96], 128) for g in range(4 * hb, 4 * hb + 4)], axis=1))
    blocks.append(_pad_cols(xq, 512))
    out = [_blockify(np.ascontiguousarray(b, dtype=np.float32)).reshape(128, 4096) for b in blocks]
    for cb in range(4):
        blk = np.zeros((128, 10, 256), np.float32)
        for g in range(8):
            blk[0:96, g, :] = w_out[g * 96:(g + 1) * 96, cb * 256:(cb + 1) * 256]
        for c2 in range(2):
            blk[:, 8 + c2, :] = w_out[768 + c2 * 128:768 + (c2 + 1) * 128, cb * 256:(cb + 1) * 256]
        out.append(_pad_cols(blk.reshape(128, 2560), 4096))
    return np.stack(out)


def _chan(v):
    return np.ascontiguousarray(np.asarray(v, np.float32).reshape(8, 96).T)


def _l1_method(self, scan_only=False, ngroups=NG, halo_src=None, hinit_src=None):
    nc, P, ps, sb = self.nc, self.P, self.ps, self.sb
    ident = self.ident
    tag = "s" if scan_only else "m"
    with ExitStack() as es:
        S = lambda n, sh, dt=F32: sb(f"{n}_l1{tag}", sh, dt, es=es)
        if "w1" in self.din:
            d_w1 = self.din["w1"]; w1b = self.w1b
        else:
            d_w1 = self.inp("w1", [NB1, 128, 4096])
            w1b = self.w1b = self.scratch("w1b", [NB1, 128, 4096], BF16)
            for b in range(NB1):
                P.dma("pool", w1b[b].rearrange("p (a c) -> p a c", c=2048), d_w1[b].rearrange("p (a c) -> p a c", c=2048), W=[f"w1b{b}"])
            P.barrier()
        g_in = lambda n, sh: self.din[n] if n in self.din else self.inp(n, sh)
        d_halo = halo_src if halo_src is not None else g_in("xhalo", [3, D])
        d_hinit = hinit_src if hinit_src is not None else (None if (scan_only and halo_src is not None) else g_in("hinit", [RB, 8]))
        d_cw = g_in("conv_w", [RB, 8, 4]); d_cb = g_in("conv_b", [RB, 8]); d_wa = g_in("rg_wa", [RB, 8, RB]); d_wx = g_in("rg_wx", [RB, 8, RB])
        d_ba = g_in("rg_ba", [RB, 8]); d_bx = g_in("rg_bx", [RB, 8]); d_lam = g_in("rg_lam", [RB, 8])
        self.wblk = [S("wblk0", [128, 8, 512], BF16), S("wblk1", [128, 8, 512], BF16)]
        xTg = S("xTg", [128, 8, G], BF16)
        xrb = S("xrb", [RB, 8, 3 + G]); xc = S("xc", [RB, 8, G]); xcb = S("xcb", [RB, 8, G], BF16)
        cw = S("cw", [RB, 8, 4]); cb = S("cb", [RB, 8]); wa32 = S("wa32", [RB, 8, RB]); wx32 = S("wx32", [RB, 8, RB])
        wab = S("wab", [RB, 8, RB], BF16); wxb = S("wxb", [RB, 8, RB], BF16)
        ba = S("ba", [RB, 8]); bx = S("bx", [RB, 8]); lamc = S("lamc", [RB, 8]); hcar = S("hcar", [RB, 8])
        rt = S("rt", [RB, G]); it = S("it", [RB, G]); at = S("at", [RB, G]); ut = S("ut", [RB, G]); ht = S("ht", [RB, G]); tmp = S("tmp", [RB, G])
        xh = S("xh", [3, D]); xTh = S("xTh", [128, 8, 4], BF16)
        for dst, src, k in ((cw, d_cw, "cw"), (cb, d_cb, "cb"), (wa32, d_wa, "wa32"), (wx32, d_wx, "wx32"), (ba, d_ba, "ba"), (bx, d_bx, "bx"),
                            (lamc, d_lam, "lamc"), (xh, d_halo, "xh")):
            P.dma("sp", dst[:], src, R=(["cc_halo"] if (k == "xh" and halo_src is not None) else []), W=[k])
        if d_hinit is None:
            P.op("dve", lambda e: e.memset(hcar[:], 0.0), W=["hcar"])
        else:
            P.dma("sp", hcar[:], d_hinit, R=(["cc_hend"] if hinit_src is not None else []), W=["hcar"])
        if halo_src is not None:
            P.op("dve", lambda e: e.tensor_scalar(out=xh[:], in0=xh[:], scalar1=self.flag[0:3, 0:1], scalar2=None, op0=ALU.mult), R=["xh", "flag"], W=["xh"])
        if hinit_src is not None:
            P.op("dve", lambda e: e.tensor_scalar(out=hcar[:], in0=hcar[:], scalar1=self.flag[0:RB, 0:1], scalar2=None, op0=ALU.mult), R=["hcar", "flag"], W=["hcar"])
        P.op("dve", lambda e: e.tensor_copy(out=wab[:], in_=wa32[:]), R=["wa32"], W=["wab"])
        P.op("dve", lambda e: e.tensor_copy(out=wxb[:], in_=wx32[:]), R=["wx32"], W=["wxb"])
        P.op("act", lambda e: e.activation(out=lamc[:], in_=lamc[:], func=AF.Exp, scale=-1.0), R=["lamc"], W=["lamc"])
        P.op("act", lambda e: e.activation(out=lamc[:], in_=lamc[:], func=AF.Ln, bias=1.0), R=["lamc"], W=["lamc"])
        P.op("dve", lambda e: e.tensor_scalar(out=lamc[:], in0=lamc[:], scalar1=-8.0, scalar2=None, op0=ALU.mult), R=["lamc"], W=["lamc"])
        if not scan_only:
            d_lng = self.inp("ln1g1", [D]); d_lnb = self.inp("ln1b1", [D])
            kxT, vxaug = self.xattn_setup(1, es)
            gg = S("gg", [RB, 8, G], BF16); hmT = S("hmT", [RB, 8, G], BF16); xqT = S("xqT", [64, 4, G], BF16)
            hat = [S("hat0", [64, 256], BF16), S("hat1", [64, 256], BF16)]; haT = S("haT", [128, 2, G], BF16)
            Esb = S("Esb", [128, 8, 64], BF16); rsb = S("rsb", [64, 4])
            lng = S("lng", [128, D]); lnb = S("lnb", [128, D]); xs = S("xs", [128, D]); st = S("st", [128, 2, 6]); ag = S("ag", [128, 4])
            P.dma("sp", lng[:], d_lng.partition_broadcast(128), W=["lng"])
            P.dma("sp", lnb[:], d_lnb.partition_broadcast(128), W=["lnb"])
        srcs = [(w1b[b], f"w1b{b}") for b in (2, 3)]
        for g in range(ngroups):
            srcs += [(w1b[b], f"w1b{b}") for b in ((2, 3) if scan_only else (2, 3, 0, 1, 4, 5, 6, 7, 8))]
        sched = self.Sched(self, srcs)

        def xr_proj(wi, hb, rhs, n, dst_off, rkey):
            for j in range(4):
                g8 = 4 * hb + j
                bk = 2 + (j % 2)
                for kc in range(8):
                    P.op("pe", lambda e, j=j, kc=kc, bk=bk: e.matmul(
                        ps[0:RB, bk, 0:n], lhsT=self.wblk[wi][:, kc, j * 128:j * 128 + RB], rhs=rhs[:, kc, 0:n],
                        start=(kc == 0), stop=(kc == 7)), R=[rkey, f"wblk{wi}"], W=[f"ps{bk}"])
                P.op("act", lambda e, g8=g8, bk=bk: e.copy(out=xrb[:, g8, dst_off:dst_off + n], in_=ps[0:RB, bk, 0:n]), R=[f"ps{bk}"], W=["xrb"])

        for kc in range(8):
            P.op("pe", lambda e, kc=kc: e.transpose(out=ps[:, 0, kc * 4:kc * 4 + 3], in_=xh[0:3, kc * 128:(kc + 1) * 128], identity=ident[0:3, 0:3]),
                 R=["xh", "ident"], W=["ps0"])
        P.op("act", lambda e: e.copy(out=xTh[:, :, 0:3], in_=ps[:, 0, 0:32].rearrange("p (k c) -> p k c", c=4)[:, :, 0:3]), R=["ps0"], W=["xTh"])
        for hb in range(2):
            wi = sched.get()
            xr_proj(wi, hb, xTh, 3, 0, "xTh")
        mixi = 0
        for g in range(ngroups):
            self.make_xT(xTg, [self.X[:, 2 * g + tt, :] for tt in range(2)], [f"X{2 * g + tt}" for tt in range(2)], "xTg")
            for hb in range(2):
                wi = sched.get()
                xr_proj(wi, hb, xTg, G, 3, "xTg")
            if not scan_only:
                for hb in range(2):
                    wi = sched.get()
                    for j in range(4):
                        g8 = 4 * hb + j
                        bk = 2 + (j % 2)
                        for kc in range(8):
                            P.op("pe", lambda e, j=j, kc=kc, bk=bk: e.matmul(
                                ps[0:RB, bk, 0:G], lhsT=self.wblk[wi][:, kc, j * 128:j * 128 + RB], rhs=xTg[:, kc, :],
                                start=(kc == 0), stop=(kc == 7)), R=["xTg", f"wblk{wi}"], W=[f"ps{bk}"])
                        P.op("act", lambda e, g8=g8, bk=bk: e.activation(out=gg[:, g8, :], in_=ps[0:RB, bk, 0:G], func=AF.Gelu), R=[f"ps{bk}"], W=["gg"])
                wi = sched.get()
                for j in range(4):
                    bk = 2 + (j % 2)
                    for kc in range(8):
                        P.op("pe", lambda e, j=j, kc=kc, bk=bk: e.matmul(
                            ps[0:64, bk, 0:G], lhsT=self.wblk[wi][:, kc, j * 64:(j + 1) * 64], rhs=xTg[:, kc, :],
                            start=(kc == 0), stop=(kc == 7)), R=["xTg", f"wblk{wi}"], W=[f"ps{bk}"])
                    P.op("act", lambda e, j=j, bk=bk: e.copy(out=xqT[:, j, :], in_=ps[0:64, bk, 0:G]), R=[f"ps{bk}"], W=["xqT"])
            for g8 in range(8):
                for w in range(4):
                    if w == 0:
                        P.op("dve", lambda e: e.tensor_scalar(out=xc[:, g8, :], in0=xrb[:, g8, 0:G], scalar1=cw[:, g8, 0:1], scalar2=cb[:, g8:g8 + 1],
                                                              op0=ALU.mult, op1=ALU.add), R=["xrb", "cw", "cb"], W=["xc"])
                    else:
                        P.op("dve", lambda e, w=w: e.scalar_tensor_tensor(out=xc[:, g8, :], in0=xrb[:, g8, w:w + G], scalar=cw[:, g8, w:w + 1], in1=xc[:, g8, :],
                                                                          op0=ALU.mult, op1=ALU.add), R=["xrb", "cw", "xc"], W=["xc"])
                P.op("act", lambda e: e.copy(out=xcb[:, g8, :], in_=xc[:, g8, :]), R=["xc"], W=["xcb"])
            P.op("dve", lambda e: e.tensor_copy(out=xrb[:, :, 0:3], in_=xrb[:, :, G:G + 3]), R=["xrb"], W=["xrb"])
            for g8 in range(8):
                P.op("pe", lambda e: e.matmul(ps[0:RB, 4, 0:G], lhsT=wab[:, g8, :], rhs=xcb[:, g8, :], start=True, stop=True), R=["wab", "xcb"], W=["ps4"])
                P.op("pe", lambda e: e.matmul(ps[0:RB, 5, 0:G], lhsT=wxb[:, g8, :], rhs=xcb[:, g8, :], start=True, stop=True), R=["wxb", "xcb"], W=["ps5"])
                P.op("act", lambda e: e.activation(out=rt[:], in_=ps[0:RB, 4, 0:G], func=AF.Sigmoid, bias=ba[:, g8:g8 + 1]), R=["ps4", "ba"], W=["rt"])
                P.op("act", lambda e: e.activation(out=it[:], in_=ps[0:RB, 5, 0:G], func=AF.Sigmoid, bias=bx[:, g8:g8 + 1]), R=["ps5", "bx"], W=["it"])
                P.op("act", lambda e: e.activation(out=at[:], in_=rt[:], func=AF.Exp, scale=lamc[:, g8:g8 + 1]), R=["rt", "lamc"], W=["at"])
                P.op("dve", lambda e: e.tensor_tensor(out=tmp[:], in0=at[:], in1=at[:], op=ALU.mult), R=["at"], W=["tmp"])
                P.op("dve", lambda e: e.tensor_scalar(out=tmp[:], in0=tmp[:], scalar1=-1.0, scalar2=1.0, op0=ALU.mult, op1=ALU.add), R=["tmp"], W=["tmp"])
                P.op("act", lambda e: e.activation(out=tmp[:], in_=tmp[:], func=AF.Sqrt), R=["tmp"], W=["tmp"])
                P.op("dve", lambda e: e.tensor_tensor(out=ut[:], in0=it[:], in1=xc[:, g8, :], op=ALU.mult), R=["it", "xc"], W=["ut"])
                P.op("dve", lambda e: e.tensor_tensor(out=ut[:], in0=ut[:], in1=tmp[:], op=ALU.mult), R=["ut", "tmp"], W=["ut"])
                P.op("dve", lambda e: e.tensor_tensor_scan(out=ht[:], data0=at[:], data1=ut[:], initial=hcar[:, g8:g8 + 1], op0=ALU.mult, op1=ALU.add),
                     R=["at", "ut", "hcar"], W=["ht"])
                P.op("dve", lambda e: e.tensor_copy(out=hcar[:, g8:g8 + 1], in_=ht[:, G - 1:G]), R=["ht"], W=["hcar"])
                if not scan_only:
                    P.op("dve", lambda e: e.tensor_tensor(out=hmT[:, g8, :], in0=ht[:], in1=gg[:, g8, :], op=ALU.mult), R=["ht", "gg"], W=["hmT"])
            if scan_only:
                continue
            for ck in range(4):
                c0 = ck * 64
                hx = hat[mixi]; hk = f"hat{mixi}"; mixi ^= 1
                self.xattn_chunk(kxT, vxaug, xqT, c0, 64, hx[:, :], hk, Esb, rsb)
                pb = ps[:, 2, :].bitcast(BF16)
                for c2 in range(2):
                    P.op("pe", lambda e, c2=c2: e.transpose(out=pb[:, c2 * 64:(c2 + 1) * 64], in_=hx[:, c2 * 128:(c2 + 1) * 128], identity=self.identb[0:64, 0:64]),
                         R=[hk, "identb"], W=["ps2"])
                P.op("act", lambda e: e.copy(out=haT[:, :, c0:c0 + 64], in_=pb[:, 0:128].rearrange("p (k t) -> p k t", t=64)), R=["ps2"], W=["haT"])
            banks = {0: (0, 1), 1: (5, 6)}
            for cbk in range(4):
                wi = sched.get()
                wv = self.wblk[wi][:].rearrange("p k c -> p (k c)")[:, 0:2560].rearrange("p (ch c) -> p ch c", c=256)
                for tt in range(2):
                    bk = banks[tt][cbk // 2]
                    dst = ps[:, bk, (cbk % 2) * 256:(cbk % 2 + 1) * 256]
                    for ch in range(10):
                        if ch < 8:
                            lhsT = hmT[:, ch, tt * 128:(tt + 1) * 128]; rhs = wv[0:RB, ch, :]; rk = "hmT"
                        else:
                            lhsT = haT[:, ch - 8, tt * 128:(tt + 1) * 128]; rhs = wv[:, ch, :]; rk = "haT"
                        P.op("pe", lambda e, lhsT=lhsT, rhs=rhs, dst=dst, ch=ch: e.matmul(dst, lhsT=lhsT, rhs=rhs, start=(ch == 0), stop=(ch == 9)),
                             R=[rk, f"wblk{wi}"], W=[f"ps{bk}"])
            for tt in range(2):
                b0 = banks[tt][0]
                self.resid_ln(2 * g + tt, ps[:, b0:b0 + 2, :], [f"ps{b0}", f"ps{b0 + 1}"], lng, lnb, xs, st, ag)
        if scan_only:
            if halo_src is None:
                d_hend = self.outp("hend", [RB, 8])
                P.dma("sp", d_hend, hcar[:], R=["hcar"], chan="out")
            else:
                P.dma("sp", self.cc2src, hcar[:], R=["hcar"], W=["cc2src"])
        P.barrier()


Builder.l1 = _l1_method


def _run(stages, inputs, x_cur=None, hinit=None, **kw):
    b, es = build(stages, **kw)
    maps = core_inputs(inputs, stages, x_cur=x_cur, hinit=hinit)
    maps = [{k: v for k, v in m.items() if k in b.din} for m in maps]
    res = run_bass_kernel_spmd(b.nc, maps, core_ids=list(range(8)))
    es.close()
    return res.results


def _gather(results, name="out"):
    x = np.empty((4, 2 * T, D), np.float32)
    for c in range(8):
        x[c // 2, (c % 2) * T:(c % 2 + 1) * T] = results[c][name]
    return x


def kernel(**inputs):
    inputs = {k: np.asarray(v) for k, v in inputs.items()}
    return _gather(_run(["fused"], inputs))
```

```python
import numpy as np
from contextlib import ExitStack
import concourse.bass as bass
import concourse.mybir as mybir
from concourse.bass_utils import run_bass_kernel_spmd

F32 = mybir.dt.float32
BF16 = mybir.dt.bfloat16
U32 = mybir.dt.uint32
I32 = mybir.dt.int32
AF = mybir.ActivationFunctionType
ALU = mybir.AluOpType
AX = mybir.AxisListType


class Prog:
    NDS = 24

    def __init__(self, nc, es):
        self.nc = nc
        self.es = es
        self.eng = {"pe": nc.tensor, "dve": nc.vector, "act": nc.scalar, "pool": nc.gpsimd, "sp": nc.sync}
        self.sem = {k: es.enter_context(nc.semaphore("sem_" + k)) for k in self.eng}
        self.cnt = {k: 0 for k in self.eng}
        self.seen = {k: {} for k in self.eng}
        self.dsem = []
        self.dcnt = []
        self.dq = {}
        for q, n in (("sp", 20), ("pool", 8), ("act", 4)):
            self.dq[q] = [len(self.dsem) + i for i in range(n)]
            self.dsem += [es.enter_context(nc.semaphore(f"dsem_{q}{i}")) for i in range(n)]
            self.dcnt += [0] * n
        self.dnext = {q: 0 for q in self.dq}
        self.last_w = {}
        self.readers = {}
        self.n_ins = 0

    def _semof(self, kind, name):
        return self.sem[name] if kind == "e" else self.dsem[name]

    def _wait_ev(self, eng, kind, name, c):
        if self.seen[eng].get((kind, name), 0) >= c:
            return
        self.seen[eng][(kind, name)] = c
        self.eng[eng].wait_ge(self._semof(kind, name), c)

    def _wait(self, eng, R, W):
        deps = []
        for k in R:
            if k in self.last_w:
                deps.append(self.last_w[k])
        for k in W:
            if k in self.last_w:
                deps.append(self.last_w[k])
            deps.extend(self.readers.get(k, ()))
        best = {}
        for kind, name, c in deps:
            if kind == "e" and name == eng and eng == "pe":
                continue
            key = (kind, name)
            if c > best.get(key, 0):
                best[key] = c
        for (kind, name), c in best.items():
            self._wait_ev(eng, kind, name, c)

    def _commit(self, ev, R, W):
        for k in W:
            self.last_w[k] = ev
            self.readers[k] = []
        for k in R:
            self.readers.setdefault(k, []).append(ev)

    limit = None
    stop_at = None

    def cp(self, name):
        if self.stop_at is not None and name == self.stop_at and self.limit is None:
            self.limit = self.n_ins

    def op(self, eng, fn, R=(), W=()):
        if self.limit is not None and self.n_ins >= self.limit:
            return None
        W = list(W) + [k for k in R if k.startswith("ps") and k not in W]
        self._wait(eng, R, W)
        ins = fn(self.eng[eng])
        self.cnt[eng] += 1
        ins.then_inc(self.sem[eng], 1)
        self._commit(("e", eng, self.cnt[eng]), R, W)
        self.n_ins += 1
        return ins

    def dma(self, q, out, in_, R=(), W=(), chan=None):
        if self.limit is not None and self.n_ins >= self.limit and chan != "out":
            return None
        i = self.dq[q][self.dnext[q]]
        self.dnext[q] = (self.dnext[q] + 1) % len(self.dq[q])
        if self.dcnt[i]:
            self._wait_ev(q, "d", i, self.dcnt[i])
        self._wait(q, R, W)
        ins = self.eng[q].dma_start(out=out, in_=in_)
        self.dcnt[i] += 16
        ins.then_inc(self.dsem[i], 16)
        self._commit(("d", i, self.dcnt[i]), R, W)
        return ins

    def allgather_pairs(self, src, dst, R=(), W=()):
        self._wait("pool", R, W)
        sem = self.es.enter_context(self.nc.semaphore(f"cc_sem{len(self.dsem)}"))
        self.dsem.append(sem)
        self.dcnt.append(0)
        i = len(self.dsem) - 1
        ins = self.nc.gpsimd.collective_compute("AllGather", ALU.bypass, replica_groups=[[0, 1], [2, 3], [4, 5], [6, 7]],
                                               ins=[src.opt()], outs=[dst.opt()])
        ins.then_inc(sem)
        self.dcnt[i] = 1
        self._commit(("d", i, 1), R, W)

    def barrier(self):
        for e in self.eng:
            for k, c in self.cnt.items():
                if c:
                    self._wait_ev(e, "e", k, c)
            for i, c in enumerate(self.dcnt):
                if c:
                    self._wait_ev(e, "d", i, c)

    def wait_all(self, eng="sp"):
        for k, c in self.cnt.items():
            if c and k != eng:
                self._wait_ev(eng, "e", k, c)
        for i, c in enumerate(self.dcnt):
            if c:
                self._wait_ev(eng, "d", i, c)


D = 1024
T = 2048
NT = T // 128
G = 256
NG = T // G
H = 4
DH = 192
L = 64
ALPHA = float(4 ** 0.25)
EPS = 1e-5
NB0 = 14


def _pad_cols(a, n):
    out = np.zeros((a.shape[0], n), np.float32)
    out[:, : a.shape[1]] = a
    return out


def _blockify(cols):
    return np.ascontiguousarray(cols.reshape(8, 128, 512).transpose(1, 0, 2))


def prep_l0_weights(w_in, w_out):
    q = w_in[:, 0:768]; k = w_in[:, 768:1536]; v = w_in[:, 1536:2304]; o = w_in[:, 2304:3072]
    g = w_in[:, 3072:3080]; xq = w_in[:, 3080:3336]
    blocks = []
    for src in (q, k):
        for hp in range(2):
            cs = []
            for h in (2 * hp, 2 * hp + 1):
                cs.append(src[:, h * 192: h * 192 + 128])
                cs.append(_pad_cols(src[:, h * 192 + 128: (h + 1) * 192], 128))
            blocks.append(np.concatenate(cs, axis=1))
    blocks.append(_pad_cols(xq, 512))
    for src in (k, v, o):
        blocks.append(_pad_cols(src[:, 0:384], 512))
        blocks.append(_pad_cols(src[:, 384:768], 512))
    blocks.append(_pad_cols(g, 512))
    blocks.append(w_out[:, 0:512]); blocks.append(w_out[:, 512:1024])
    return np.stack([_blockify(np.ascontiguousarray(b, dtype=np.float32)) for b in blocks])


class Builder:
    def __init__(self, es, debug=()):
        self.es = es
        self.debug = set(debug)
        self.nc = nc = bass.Bass("TRN2", target_bir_lowering=False)
        self.P = Prog(nc, es)
        self.din = {}
        self.dout = {}
        self.wb_i = 0

    def inp(self, name, shape, dt=F32):
        ap = self.nc.dram_tensor(name, list(shape), dt, kind="ExternalInput").ap()
        self.din[name] = ap
        return ap

    def outp(self, name, shape, dt=F32):
        ap = self.nc.dram_tensor(name, list(shape), dt, kind="ExternalOutput").ap()
        self.dout[name] = ap
        return ap

    def scratch(self, name, shape, dt):
        return self.nc.dram_tensor(name, list(shape), dt, kind="Internal").ap()

    def sb(self, name, shape, dt=F32, es=None):
        return (es or self.es).enter_context(self.nc.sbuf_tensor(name + "_sb", list(shape), dt))

    def wload(self, src):
        i = self.wb_i % len(self.wblk)
        self.wb_i = (i + 1) % len(self.wblk)
        self.P.dma("sp", self.wblk[i][:].rearrange("p k c -> p (k c)"), src[0], R=[src[1]], W=[f"wblk{i}"], chan="w")
        return i

    class Sched:
        def __init__(self, b, srcs):
            self.b = b; self.srcs = srcs; self.n = 0; self.cur = None
            self.single = len(b.wblk) == 1
            self.nxt = b.wload(srcs[0]) if (srcs and not self.single) else None

        def get(self):
            if self.single:
                self.n += 1
                return self.b.wload(self.srcs[self.n - 1])
            cur = self.nxt
            self.n += 1
            self.nxt = self.b.wload(self.srcs[self.n]) if self.n < len(self.srcs) else None
            return cur

    def setup(self):
        nc, P = self.nc, self.P
        sb = self.sb
        self.ps = self.es.enter_context(nc.psum_tensor("ps", [128, 8, 512], F32))
        self.X = sb("X", [128, NT, D])
        self.ident = sb("ident", [128, 128])
        self.identb = sb("identb", [128, 128], BF16)
        self.cmask = sb("cmask", [64, 64], BF16)
        self.flag = sb("flag", [128, 1])
        d_ident = self.inp("ident", [128, 128])
        d_cmask = self.inp("cmask", [64, 64])
        d_flag = self.inp("flag", [128, 1])
        cm32 = sb("cm32", [64, 64])
        P.dma("sp", self.ident[:], d_ident, W=["ident"])
        P.dma("sp", cm32[:], d_cmask, W=["cm32"])
        P.dma("sp", self.flag[:], d_flag, W=["flag"])
        P.op("dve", lambda e: e.tensor_copy(out=self.identb[:], in_=self.ident[:]), R=["ident"], W=["identb"])
        P.op("dve", lambda e: e.tensor_copy(out=self.cmask[:], in_=cm32[:]), R=["cm32"], W=["cmask"])
        d_x = self.inp("x_own", [T, D])
        for q4 in range(4):
            P.dma("sp", self.X[:, 4 * q4:4 * q4 + 4, :],
                  d_x[512 * q4:512 * (q4 + 1), :].rearrange("(t p) d -> p t d", p=128),
                  W=[f"X{t}" for t in range(4 * q4, 4 * q4 + 4)])

    def psb(self, b0, nb=1):
        return [f"ps{b}" for b in range(b0, b0 + nb)]

    def make_xT(self, xTg, src_tiles, src_keys, kout):
        P, ps = self.P, self.ps
        for tt in range(2):
            for kc in range(8):
                P.op("pe", lambda e, tt=tt, kc=kc: e.transpose(
                    out=ps[:, kc // 4, (kc % 4) * 128:(kc % 4 + 1) * 128],
                    in_=src_tiles[tt][:, kc * 128:(kc + 1) * 128], identity=self.ident[:]),
                    R=[src_keys[tt], "ident"], W=[f"ps{kc // 4}"])
            P.op("act", lambda e, tt=tt: e.copy(
                out=xTg[:, :, tt * 128:(tt + 1) * 128],
                in_=ps[:, 0:2, :].rearrange("p b (k c) -> p (b k) c", c=128)),
                R=["ps0", "ps1"], W=[kout])

    def xattn_setup(self, layer, es):
        P, ps, sb = self.P, self.ps, self.sb
        d_memT = self.din["memT"] if "memT" in self.din else self.inp("memT", [128, 8, 256])
        d_wkv = self.inp(f"wkv{layer}", [128, 8, 512])
        kxT = sb(f"kxT_{layer}", [64, 4, 256], BF16, es=es)
        vxaug = sb(f"vxaug_{layer}", [128, 2, 4, 65], BF16, es=es)
        with ExitStack() as tes:
            memT32 = sb(f"memT32_{layer}", [128, 8, 256], es=tes)
            wkv32 = sb(f"wkv32_{layer}", [128, 8, 512], es=tes)
            memTb = sb(f"memTb_{layer}", [128, 8, 256], BF16, es=tes)
            wkvb = sb(f"wkvb_{layer}", [128, 8, 512], BF16, es=tes)
            P.dma("sp", memT32[:], d_memT, W=["memT32"])
            P.dma("sp", wkv32[:], d_wkv, W=["wkv32"])
            P.op("dve", lambda e: e.tensor_copy(out=memTb[:], in_=memT32[:]), R=["memT32"], W=["memTb"])
            P.op("pool", lambda e: e.tensor_copy(out=wkvb[:], in_=wkv32[:]), R=["wkv32"], W=["wkvb"])
            for h in range(4):
                for kc in range(8):
                    P.op("pe", lambda e, h=h, kc=kc: e.matmul(
                        ps[0:64, 2, 0:256], lhsT=wkvb[:, kc, h * 64:(h + 1) * 64], rhs=memTb[:, kc, :],
                        start=(kc == 0), stop=(kc == 7)), R=["wkvb", "memTb"], W=["ps2"])
                P.op("act", lambda e, h=h: e.copy(out=kxT[:, h, :], in_=ps[0:64, 2, 0:256]), R=["ps2"], W=["kxT"])
            P.op("pool", lambda e: e.memset(vxaug[:], 1.0), W=["vxaug"])
            for mc in range(2):
                for kc in range(8):
                    P.op("pe", lambda e, mc=mc, kc=kc: e.matmul(
                        ps[:, 3, 0:256], lhsT=memTb[:, kc, mc * 128:(mc + 1) * 128], rhs=wkvb[:, kc, 256:512],
                        start=(kc == 0), stop=(kc == 7)), R=["wkvb", "memTb"], W=["ps3"])
                P.op("act", lambda e, mc=mc: e.copy(
                    out=vxaug[:, mc, :, 0:64], in_=ps[:, 3, 0:256].rearrange("p (h d) -> p h d", h=4)),
                    R=["ps3"], W=["vxaug"])
            P.barrier()
        return kxT, vxaug

    def xattn_chunk(self, kxT, vxaug, xqT, c0, ntok, out_ap, out_key, Esb, rsb):
        P, ps = self.P, self.ps
        for h in range(4):
            for mc in range(2):
                P.op("pe", lambda e, h=h, mc=mc: e.matmul(
                    ps[:, 0, (h * 2 + mc) * 64:(h * 2 + mc) * 64 + ntok],
                    lhsT=kxT[:, h, mc * 128:(mc + 1) * 128],
                    rhs=xqT[:, h, c0:c0 + ntok], start=True, stop=True),
                    R=["kxT", "xqT"], W=["ps0"])
        P.cp("xa_st")
        P.op("act", lambda e: e.activation(
            out=Esb[:, :, 0:ntok], in_=ps[:, 0, :].rearrange("p (j t) -> p j t", t=64)[:, :, 0:ntok],
            func=AF.Exp, scale=0.125), R=["ps0"], W=["Esb"])
        P.cp("xa_exp")
        Ov = ps[0:ntok, 1, 0:260].rearrange("p (h d) -> p h d", d=65)
        for h in range(4):
            for mc in range(2):
                P.op("pe", lambda e, h=h, mc=mc: e.matmul(
                    Ov[:, h, :], lhsT=Esb[:, h * 2 + mc, 0:ntok], rhs=vxaug[:, mc, h, :],
                    start=(mc == 0), stop=(mc == 1)), R=["Esb", "vxaug"], W=["ps1"])
        P.cp("xa_ov")
        P.op("dve", lambda e: e.reciprocal(out=rsb[0:ntok, :], in_=Ov[:, :, 64]), R=["ps1"], W=["rsb"])
        P.op("dve", lambda e: e.tensor_tensor(
            out=out_ap.rearrange("p (h d) -> p h d", d=64), in0=Ov[:, :, 0:64],
            in1=rsb[0:ntok, :].unsqueeze(2).broadcast_to([ntok, 4, 64]), op=ALU.mult),
            R=["ps1", "rsb"], W=[out_key])

    def outproj_ln(self, sched, mixT, g, lng, lnb, xs, st, ag):
        P, ps = self.P, self.ps
        banks = {0: (0, 1), 1: (5, 6)}
        for half in range(2):
            wi = sched.get()
            for tt in range(2):
                bk = banks[tt][half]
                for kc in range(8):
                    P.op("pe", lambda e, tt=tt, kc=kc, bk=bk, wi=wi: e.matmul(
                        ps[:, bk, :], lhsT=mixT[:, kc, tt * 128:(tt + 1) * 128], rhs=self.wblk[wi][:, kc, :],
                        start=(kc == 0), stop=(kc == 7)), R=["mixT", f"wblk{wi}"], W=[f"ps{bk}"])
        P.cp("op_mm")
        for tt in range(2):
            t = 2 * g + tt
            b0 = banks[tt][0]
            self.resid_ln(t, self.ps[:, b0:b0 + 2, :], [f"ps{b0}", f"ps{b0 + 1}"], lng, lnb, xs, st, ag)

    def resid_ln(self, t, y_ap, y_keys, lng, lnb, xs, st, ag):
        P = self.P
        Xt = self.X[:, t, :]
        wk = Xt if xs is None else xs[:]
        kk = f"X{t}" if xs is None else "xs"
        P.op("dve", lambda e: e.scalar_tensor_tensor(
            out=wk.rearrange("p (b c) -> p b c", b=2), in0=Xt.rearrange("p (b c) -> p b c", b=2),
            scalar=ALPHA, in1=y_ap, op0=ALU.mult, op1=ALU.add), R=[f"X{t}"] + y_keys, W=[kk])
        P.cp("ln_a")
        for hf in range(2):
            P.op("dve", lambda e, hf=hf: e.bn_stats(out=st[:, hf, :], in_=wk[:, hf * 512:(hf + 1) * 512]),
                 R=[kk], W=[f"st{hf}"])
        P.op("dve", lambda e: e.bn_aggr(out=ag[:, 0:2], in_=st[:].rearrange("p a b -> p (a b)")),
             R=["st0", "st1"], W=["ag"])
        P.cp("ln_b")
        P.op("act", lambda e: e.activation(out=ag[:, 2:3], in_=ag[:, 1:2], func=AF.Sqrt, bias=EPS),
             R=["ag"], W=["ag2"])
        P.op("dve", lambda e: e.reciprocal(out=ag[:, 3:4], in_=ag[:, 2:3]), R=["ag2"], W=["ag3"])
        P.op("dve", lambda e: e.tensor_scalar(out=wk, in0=wk, scalar1=ag[:, 0:1], scalar2=ag[:, 3:4],
                                              op0=ALU.subtract, op1=ALU.mult), R=[kk, "ag", "ag3"], W=[kk])
        P.cp("ln_c")
        P.op("dve", lambda e: e.tensor_tensor(out=wk, in0=wk, in1=lng[:], op=ALU.mult), R=[kk, "lng"], W=[kk])
        P.op("dve", lambda e: e.tensor_tensor(out=Xt, in0=wk, in1=lnb[:], op=ALU.add), R=[kk, "lnb"], W=[f"X{t}"])

    def l0_mixer(self, npre=NG, nmain=NG):
        nc, P, ps, sb = self.nc, self.P, self.ps, self.sb
        ident = self.ident
        with ExitStack() as es:
            d_w0 = self.inp("w0", [NB0, 128, 4096])
            w0b = self.scratch("w0b", [NB0, 128, 4096], BF16)
            for b in range(NB0):
                P.dma("pool", w0b[b].rearrange("p (a c) -> p a c", c=2048), d_w0[b].rearrange("p (a c) -> p a c", c=2048), W=[f"w0b{b}"])
            P.barrier()
            d_xpre = self.inp("x_pre", [T, D])
            d_bgi = self.inp("bgi", [4, 1]); d_bgf = self.inp("bgf", [4, 1])
            d_ng = self.inp("norm_g", [768]); d_lng = self.inp("ln1g0", [D]); d_lnb = self.inp("ln1b0", [D])
            kxT, vxaug = self.xattn_setup(0, es)
            S = lambda n, sh, dt=F32: sb(n, sh, dt, es=es)
            self.wblk = [S("wblk0_m0", [128, 8, 512], BF16), S("wblk1_m0", [128, 8, 512], BF16)]
            xpre = S("xpre", [128, 2, D]); xTg = S("xTg", [128, 8, G], BF16)
            qT = S("qT", [128, 8, G], BF16); kT = S("kT", [128, 8, G], BF16); xqT = S("xqT", [64, 4, G], BF16)
            k_tm = S("k_tm", [64, 4, 768], BF16); vaug = S("vaug", [64, 4, 4, 193], BF16); so = S("so", [64, 4, 768], BF16)
            g_tm = S("g_tm", [64, 4, 8])
            bgi = S("bgi_s", [4, 1]); nbgf = S("nbgf", [4, 1]); zer = S("zer", [4, G])
            ig = S("ig", [4, G]); lsp = S("lsp", [4, G]); Lc = S("Lc", [4, G]); gam = S("gam", [4, G]); Mx = S("Mx", [4, G])
            t1 = S("t1", [4, G]); bet = S("bet", [4, G]); alp = S("alp", [4, G]); flo = S("flo", [4, G])
            Lcar = S("Lcar", [4, 1]); Mcar = S("Mcar", [4, 1]); Mprev = S("Mprev", [4, 4]); dec = S("dec", [4, 4])
            sel = S("sel", [4, 4, 128]); gtm = S("gtm", [64, 4, 3, 4]); decb = S("decb", [128, 4, 4])
            CA = S("CA", [128, 4, 193]); CB = S("CB", [64, 4, 193]); CAb = S("CAb", [128, 4, 193], BF16); CBb = S("CBb", [64, 4, 193], BF16)
            SmT = S("SmT", [64, 4, 64], BF16); num = S("num", [64, 4, 193]); hr = S("hr", [64, 4, 192]); sq = S("sq", [64, 4, 192])
            sm = S("sm", [64, 8, 4]); mix = [S("mix0", [64, D], BF16), S("mix1", [64, D], BF16)]
            mixT = S("mixT", [128, 8, G], BF16); Esb = S("Esb", [128, 8, 64], BF16); rsb = S("rsb", [64, 4])
            ngb = S("ngb", [64, 768]); lng = S("lng", [128, D]); lnb = S("lnb", [128, D])
            xs = S("xs", [128, D]); st = S("st", [128, 2, 6]); ag = S("ag", [128, 4])
            P.dma("sp", bgi[:], d_bgi, W=["bgi"]); P.dma("sp", nbgf[:], d_bgf, W=["nbgf"])
            P.dma("sp", ngb[:], d_ng.partition_broadcast(64), W=["ngb"])
            P.dma("sp", lng[:], d_lng.partition_broadcast(128), W=["lng"])
            P.dma("sp", lnb[:], d_lnb.partition_broadcast(128), W=["lnb"])
            P.op("dve", lambda e: e.tensor_scalar(out=nbgf[:], in0=nbgf[:], scalar1=-1.0, scalar2=None, op0=ALU.mult), R=["nbgf"], W=["nbgf"])
            for t_, k_ in ((zer, "zer"), (Lcar, "Lcar"), (Mcar, "Mcar"), (CA, "CA"), (CB, "CB")):
                P.op("pool", lambda e, t_=t_: e.memset(t_[:], 0.0), W=[k_])
            for h in range(4):
                P.op("dve", lambda e, h=h: e.tensor_copy(out=sel[:, h, :], in_=ident[0:4, h:h + 1].broadcast_to([4, 128])),
                     R=["ident"], W=["sel"])
            srcs = []
            for g in range(npre):
                srcs += [(w0b[b], f"w0b{b}") for b in (11, 5, 6, 7, 8)]
            for g in range(nmain):
                srcs += [(w0b[b], f"w0b{b}") for b in (11, 0, 1, 2, 3, 4, 5, 6, 7, 8, 9, 10, 12, 13)]
            sched = self.Sched(self, srcs)
            KS = float(DH ** -0.5)
            mixi = 0

            def tm_proj(wi, ck, ncols):
                bk = 2 + (ck % 2)
                for kc in range(8):
                    P.op("pe", lambda e, kc=kc: e.matmul(
                        ps[0:64, bk, 0:ncols], lhsT=xTg[:, kc, ck * 64:(ck + 1) * 64], rhs=self.wblk[wi][:, kc, 0:ncols],
                        start=(kc == 0), stop=(kc == 7)), R=["xTg", f"wblk{wi}"], W=[f"ps{bk}"])
                return bk

            for phase in (0, 1):
                for g in range(nmain if phase else npre):
                    main = phase == 1
                    if main:
                        srct = [self.X[:, 2 * g + tt, :] for tt in range(2)]; srck = [f"X{2 * g + tt}" for tt in range(2)]
                    else:
                        P.dma("sp", xpre[:], d_xpre[g * G:(g + 1) * G, :].rearrange("(t p) d -> p t d", p=128), W=["xpre0", "xpre1"])
                        srct = [xpre[:, tt, :] for tt in range(2)]; srck = ["xpre0", "xpre1"]
                    self.make_xT(xTg, srct, srck, "xTg")
                    wi = sched.get()
                    for ck in range(4):
                        bk = tm_proj(wi, ck, 8)
                        P.op("act", lambda e, ck=ck, bk=bk: e.copy(out=g_tm[:, ck, :], in_=ps[0:64, bk, 0:8]), R=[f"ps{bk}"], W=["g_tm"])
                    for ck in range(4):
                        for j in range(2):
                            P.op("pe", lambda e, ck=ck, j=j: e.transpose(
                                out=ps[0:4, 4, j * 256 + ck * 64: j * 256 + (ck + 1) * 64],
                                in_=g_tm[:, ck, 4 * j:4 * j + 4], identity=ident[0:64, 0:64]), R=["g_tm", "ident"], W=["ps4"])
                    if main:
                        P.op("dve", lambda e: e.tensor_scalar(out=ig[:], in0=ps[0:4, 4, 0:256], scalar1=bgi[:, 0:1], scalar2=None, op0=ALU.add),
                             R=["ps4", "bgi"], W=["ig"])
                    else:
                        P.op("dve", lambda e: e.tensor_scalar(out=ig[:], in0=ps[0:4, 4, 0:256], scalar1=bgi[:, 0:1], scalar2=self.flag[0:4, 0:1],
                                                              op0=ALU.add, op1=ALU.mult), R=["ps4", "bgi", "flag"], W=["ig"])
                    P.op("act", lambda e: e.activation(out=t1[:], in_=ps[0:4, 4, 256:512], func=AF.Exp, bias=nbgf[:, 0:1], scale=-1.0),
                         R=["ps4", "nbgf"], W=["t1"])
                    P.op("act", lambda e: e.activation(out=lsp[:], in_=t1[:], func=AF.Ln, bias=1.0), R=["t1"], W=["lsp"])
                    if not main:
                        P.op("dve", lambda e: e.tensor_scalar(out=lsp[:], in0=lsp[:], scalar1=self.flag[0:4, 0:1], scalar2=None, op0=ALU.mult),
                             R=["lsp", "flag"], W=["lsp"])
                    P.op("dve", lambda e: e.tensor_tensor_scan(out=Lc[:], data0=lsp[:], data1=zer[:], initial=Lcar[:, 0:1], op0=ALU.add, op1=ALU.add),
                         R=["lsp", "zer", "Lcar"], W=["Lc"])
                    P.op("dve", lambda e: e.tensor_tensor(out=gam[:], in0=ig[:], in1=Lc[:], op=ALU.add), R=["ig", "Lc"], W=["gam"])
                    P.op("dve", lambda e: e.tensor_tensor_scan(out=Mx[:], data0=gam[:], data1=gam[:], initial=Mcar[:, 0:1], op0=ALU.max, op1=ALU.max),
                         R=["gam", "Mcar"], W=["Mx"])
                    Mend = Mx[:].rearrange("p (c l) -> p c l", l=64)[:, :, 63]
                    P.op("dve", lambda e: e.tensor_copy(out=Mprev[:, 0:1], in_=Mcar[:, 0:1]), R=["Mcar"], W=["Mprev"])
                    P.op("dve", lambda e: e.tensor_copy(out=Mprev[:, 1:4], in_=Mend[:, 0:3]), R=["Mx"], W=["Mprev"])
                    P.op("dve", lambda e: e.tensor_copy(out=Mcar[:, 0:1], in_=Mx[:, G - 1:G]), R=["Mx", "Mprev"], W=["Mcar"])
                    P.op("dve", lambda e: e.tensor_copy(out=Lcar[:, 0:1], in_=Lc[:, G - 1:G]), R=["Lc"], W=["Lcar"])
                    P.op("dve", lambda e: e.tensor_tensor(out=dec[:], in0=Mprev[:], in1=Mend, op=ALU.subtract), R=["Mprev", "Mx"], W=["dec"])
                    P.op("act", lambda e: e.activation(out=dec[:], in_=dec[:], func=AF.Exp), R=["dec"], W=["dec"])
                    Mend_bc = Mend.unsqueeze(2).broadcast_to([4, 4, 64])
                    v3 = lambda t_: t_[:].rearrange("p (c l) -> p c l", l=64)
                    P.op("dve", lambda e: e.tensor_tensor(out=v3(bet), in0=v3(gam), in1=Mend_bc, op=ALU.subtract), R=["gam", "Mx"], W=["bet"])
                    P.op("act", lambda e: e.activation(out=bet[:], in_=bet[:], func=AF.Exp), R=["bet"], W=["bet"])
                    if main:
                        P.op("dve", lambda e: e.tensor_tensor(out=v3(alp), in0=Mend_bc, in1=v3(Mx), op=ALU.subtract), R=["Mx"], W=["alp"])
                        P.op("act", lambda e: e.activation(out=alp[:], in_=alp[:], func=AF.Exp), R=["alp"], W=["alp"])
                        P.op("dve", lambda e: e.tensor_tensor(out=flo[:], in0=Lc[:], in1=Mx[:], op=ALU.subtract), R=["Lc", "Mx"], W=["flo"])
                        P.op("act", lambda e: e.activation(out=flo[:], in_=flo[:], func=AF.Exp), R=["flo"], W=["flo"])
                    qs = (bet, alp, flo) if main else (bet,)
                    for ck in range(4):
                        for qi, qt in enumerate(qs):
                            P.op("pe", lambda e, ck=ck, qi=qi, qt=qt: e.transpose(
                                out=ps[0:64, 4, ck * 12 + qi * 4: ck * 12 + qi * 4 + 4], in_=qt[0:4, ck * 64:(ck + 1) * 64],
                                identity=ident[0:4, 0:4]), R=[("bet", "alp", "flo")[qi], "ident"], W=["ps4"])
                    if main:
                        P.op("act", lambda e: e.copy(out=gtm[:].rearrange("p c q h -> p (c q h)"), in_=ps[0:64, 4, 0:48]), R=["ps4"], W=["gtm"])
                    else:
                        P.op("act", lambda e: e.copy(out=gtm[:, :, 0, :], in_=ps[0:64, 4, 0:48].rearrange("p (c q h) -> p c q h", q=3, h=4)[:, :, 0, :]),
                             R=["ps4"], W=["gtm"])
                    for h in range(4):
                        P.op("pe", lambda e, h=h: e.matmul(
                            ps[:, 4, 64:80].rearrange("p (c h) -> p c h", h=4)[:, :, h], lhsT=sel[0:4, h, :], rhs=dec[0:4, 0:4],
                            start=True, stop=True), R=["sel", "dec", "gtm"], W=["ps4"])
                    P.op("act", lambda e: e.copy(out=decb[:].rearrange("p c h -> p (c h)"), in_=ps[:, 4, 64:80]), R=["ps4"], W=["decb"])
                    if main:
                        for blk, dst, scl in ((0, qT, 1.0), (1, qT, 1.0), (2, kT, KS), (3, kT, KS), (4, xqT, 1.0)):
                            wi = sched.get()
                            cw = 128 if blk < 4 else 64
                            for j in range(4):
                                bk = 2 + (j % 2)
                                for kc in range(8):
                                    P.op("pe", lambda e, j=j, kc=kc, bk=bk, wi=wi: e.matmul(
                                        ps[0:cw, bk, 0:G], lhsT=self.wblk[wi][:, kc, j * cw:(j + 1) * cw], rhs=xTg[:, kc, :],
                                        start=(kc == 0), stop=(kc == 7)), R=["xTg", f"wblk{wi}"], W=[f"ps{bk}"])
                                cj = (blk % 2) * 4 + j if blk < 4 else j
                                dk = {0: "qT", 1: "qT", 2: "kT", 3: "kT", 4: "xqT"}[blk]
                                P.op("act", lambda e, dst=dst, cj=cj, bk=bk, scl=scl: e.activation(
                                    out=dst[0:cw, cj, :], in_=ps[0:cw, bk, 0:G], func=AF.Copy, scale=scl), R=[f"ps{bk}"], W=[dk])
                    tms = [(0, "k"), (1, "k"), (0, "v"), (1, "v")] + ([(0, "o"), (1, "o")] if main else [])
                    for hp, kind in tms:
                        wi = sched.get()
                        for ck in range(4):
                            bk = tm_proj(wi, ck, 384)
                            src = ps[0:64, bk, 0:384]
                            if kind == "k":
                                P.op("act", lambda e, ck=ck, hp=hp, src=src: e.activation(
                                    out=k_tm[:, ck, hp * 384:(hp + 1) * 384], in_=src, func=AF.Copy, scale=KS), R=[f"ps{bk}"], W=["k_tm"])
                            elif kind == "o":
                                P.op("act", lambda e, ck=ck, hp=hp, src=src: e.activation(
                                    out=so[:, ck, hp * 384:(hp + 1) * 384], in_=src, func=AF.Sigmoid), R=[f"ps{bk}"], W=["so"])
                            else:
                                P.op("dve", lambda e, ck=ck, hp=hp, src=src: e.tensor_tensor(
                                    out=vaug[:, ck, 2 * hp:2 * hp + 2, 0:192], in0=src.rearrange("p (h d) -> p h d", h=2),
                                    in1=gtm[:, ck, 0, 2 * hp:2 * hp + 2].unsqueeze(2).broadcast_to([64, 2, 192]), op=ALU.mult),
                                    R=[f"ps{bk}", "gtm"], W=["vaug"])
                    for ck in range(4):
                        P.op("pool", lambda e, ck=ck: e.tensor_copy(out=vaug[:, ck, :, 192], in_=gtm[:, ck, 0, :]), R=["gtm"], W=["vaug"])
                    P.cp(f"proj{phase}")
                    for ck in range(4):
                        c0 = ck * 64
                        dbc = lambda n: decb[0:n, ck, :].unsqueeze(2).broadcast_to([n, 4, 193])
                        P.op("dve", lambda e: e.tensor_tensor(out=CA[:], in0=CA[:], in1=dbc(128), op=ALU.mult), R=["CA", "decb"], W=["CA"])
                        P.op("pool", lambda e: e.tensor_tensor(out=CB[:], in0=CB[:], in1=dbc(64), op=ALU.mult), R=["CB", "decb"], W=["CB"])
                        if main:
                            P.op("act", lambda e: e.copy(out=CAb[:], in_=CA[:]), R=["CA"], W=["CAb"])
                            P.op("act", lambda e: e.copy(out=CBb[:], in_=CB[:]), R=["CB"], W=["CBb"])
                            for h in range(4):
                                P.op("pe", lambda e, h=h: e.matmul(ps[0:64, 4, 256 + h * 64:256 + (h + 1) * 64], lhsT=kT[:, 2 * h, c0:c0 + 64],
                                                                   rhs=qT[:, 2 * h, c0:c0 + 64], start=True, stop=False), R=["kT", "qT"], W=["ps4"])
                                P.op("pe", lambda e, h=h: e.matmul(ps[0:64, 4, 256 + h * 64:256 + (h + 1) * 64], lhsT=kT[0:64, 2 * h + 1, c0:c0 + 64],
                                                                   rhs=qT[0:64, 2 * h + 1, c0:c0 + 64], start=False, stop=True), R=["kT", "qT"], W=["ps4"])
                            P.op("dve", lambda e: e.tensor_tensor(
                                out=SmT[:], in0=ps[0:64, 4, 256:512].rearrange("p (h l) -> p h l", h=4),
                                in1=self.cmask[:].unsqueeze(1).broadcast_to([64, 4, 64]), op=ALU.mult), R=["ps4", "cmask"], W=["SmT"])
                            Pv = ps[0:64, 5:7, :].rearrange("p b (h e) -> p (b h) e", h=2)
                            for h in range(4):
                                bk = 5 + h // 2
                                P.op("pe", lambda e, h=h: e.matmul(Pv[:, h, 0:193], lhsT=qT[:, 2 * h, c0:c0 + 64], rhs=CAb[:, h, :],
                                                                   start=True, stop=False), R=["qT", "CAb"], W=[f"ps{bk}"])
                                P.op("pe", lambda e, h=h: e.matmul(Pv[:, h, 0:193], lhsT=qT[0:64, 2 * h + 1, c0:c0 + 64], rhs=CBb[:, h, :],
                                                                   start=False, stop=False), R=["qT", "CBb"], W=[f"ps{bk}"])
                                P.op("pe", lambda e, h=h: e.matmul(Pv[:, h, 0:193], lhsT=SmT[:, h, :], rhs=vaug[:, ck, h, :],
                                                                   start=False, stop=True), R=["SmT", "vaug"], W=[f"ps{bk}"])
                        P.cp(f"P{phase}")
                        for hp in range(2):
                            dA = ps[:, 7, :].rearrange("p (h e) -> p h e", h=2)
                            dB = ps[0:64, 3, :].rearrange("p (h e) -> p h e", h=2)
                            for hh in range(2):
                                h = 2 * hp + hh
                                P.op("pe", lambda e, h=h, hh=hh: e.matmul(dA[:, hh, 0:193], lhsT=k_tm[:, ck, h * 192:h * 192 + 128], rhs=vaug[:, ck, h, :],
                                                                          start=True, stop=True), R=["k_tm", "vaug"], W=["ps7"])
                                P.op("pe", lambda e, h=h, hh=hh: e.matmul(dB[:, hh, 0:193], lhsT=k_tm[:, ck, h * 192 + 128:(h + 1) * 192], rhs=vaug[:, ck, h, :],
                                                                          start=True, stop=True), R=["k_tm", "vaug"], W=["ps3"])
                            rA = ["CA"] + (["CAb"] if main else [])
                            P.op("dve", lambda e, hp=hp, dA=dA: e.tensor_tensor(out=CA[:, 2 * hp:2 * hp + 2, :], in0=CA[:, 2 * hp:2 * hp + 2, :], in1=dA[:, :, 0:193], op=ALU.add),
                                 R=["CA", "ps7"], W=["CA"])
                            P.op("dve", lambda e, hp=hp, dB=dB: e.tensor_tensor(out=CB[:, 2 * hp:2 * hp + 2, :], in0=CB[:, 2 * hp:2 * hp + 2, :], in1=dB[:, :, 0:193], op=ALU.add),
                                 R=["CB", "ps3"], W=["CB"])
                        P.cp(f"state{phase}")
                        if not main:
                            continue
                        abc = gtm[:, ck, 1, :].unsqueeze(2).broadcast_to([64, 4, 193])
                        P.op("dve", lambda e: e.tensor_tensor(out=num[:], in0=Pv[:, :, 0:193], in1=abc, op=ALU.mult), R=["ps5", "ps6", "gtm"], W=["num"])
                        P.op("act", lambda e: e.activation(out=sm[:, 0, :], in_=num[:, :, 192], func=AF.Abs), R=["num"], W=["sm0"])
                        P.op("dve", lambda e: e.tensor_tensor(out=sm[:, 1, :], in0=sm[:, 0, :], in1=gtm[:, ck, 2, :], op=ALU.max), R=["sm0", "gtm"], W=["sm1"])
                        P.op("dve", lambda e: e.reciprocal(out=sm[:, 2, :], in_=sm[:, 1, :]), R=["sm1"], W=["sm2"])
                        P.op("dve", lambda e: e.tensor_tensor(out=hr[:], in0=num[:, :, 0:192], in1=sm[:, 2, :].unsqueeze(2).broadcast_to([64, 4, 192]), op=ALU.mult),
                             R=["num", "sm2"], W=["hr"])
                        P.cp("hr")
                        P.op("dve", lambda e: e.tensor_reduce(out=sm[:, 3, :], in_=hr[:], axis=AX.X, op=ALU.add), R=["hr"], W=["sm3"])
                        P.op("pool", lambda e: e.tensor_tensor(out=sq[:], in0=hr[:], in1=hr[:], op=ALU.mult), R=["hr"], W=["sq"])
                        P.op("dve", lambda e: e.tensor_reduce(out=sm[:, 4, :], in_=sq[:], axis=AX.X, op=ALU.add), R=["sq"], W=["sm4"])
                        P.op("dve", lambda e: e.tensor_scalar(out=sm[:, 3, :], in0=sm[:, 3, :], scalar1=1.0 / DH, scalar2=None, op0=ALU.mult), R=["sm3"], W=["sm3"])
                        P.op("dve", lambda e: e.tensor_tensor(out=sm[:, 5, :], in0=sm[:, 3, :], in1=sm[:, 3, :], op=ALU.mult), R=["sm3"], W=["sm5"])
                        P.op("dve", lambda e: e.scalar_tensor_tensor(out=sm[:, 6, :], in0=sm[:, 4, :], scalar=1.0 / DH, in1=sm[:, 5, :], op0=ALU.mult, op1=ALU.subtract),
                             R=["sm4", "sm5"], W=["sm6"])
                        P.op("act", lambda e: e.activation(out=sm[:, 6, :], in_=sm[:, 6, :], func=AF.Sqrt, bias=EPS), R=["sm6"], W=["sm6"])
                        P.op("dve", lambda e: e.reciprocal(out=sm[:, 7, :], in_=sm[:, 6, :]), R=["sm6"], W=["sm7"])
                        P.op("dve", lambda e: e.tensor_tensor(out=hr[:], in0=hr[:], in1=sm[:, 3, :].unsqueeze(2).broadcast_to([64, 4, 192]), op=ALU.subtract),
                             R=["hr", "sm3"], W=["hr"])
                        P.op("dve", lambda e: e.tensor_tensor(out=hr[:], in0=hr[:], in1=sm[:, 7, :].unsqueeze(2).broadcast_to([64, 4, 192]), op=ALU.mult),
                             R=["hr", "sm7"], W=["hr"])
                        hf = hr[:].rearrange("p h d -> p (h d)")
                        P.op("pool", lambda e: e.tensor_tensor(out=hf, in0=hf, in1=ngb[:], op=ALU.mult), R=["hr", "ngb"], W=["hr"])
                        mx_ = mix[mixi]; mk = f"mix{mixi}"; mixi ^= 1
                        P.op("pool", lambda e, mx_=mx_: e.tensor_tensor(out=mx_[:, 0:768], in0=hf, in1=so[:, ck, :], op=ALU.mult), R=["hr", "so"], W=[mk])
                        P.cp("hln")
                        self.xattn_chunk(kxT, vxaug, xqT, c0, 64, mx_[:, 768:1024], mk, Esb, rsb)
                        P.cp("xattn")
                        pb = ps[:, 2, :].bitcast(BF16)
                        for kc in range(8):
                            P.op("pe", lambda e, kc=kc, mx_=mx_: e.transpose(out=pb[:, kc * 64:(kc + 1) * 64], in_=mx_[:, kc * 128:(kc + 1) * 128],
                                                                             identity=self.identb[0:64, 0:64]), R=[mk, "identb"], W=["ps2"])
                        P.op("act", lambda e: e.copy(out=mixT[:, :, c0:c0 + 64], in_=pb[:, 0:512].rearrange("p (k t) -> p k t", t=64)), R=["ps2"], W=["mixT"])
                    P.cp(f"chunks{phase}")
                    if main:
                        self.outproj_ln(sched, mixT, g, lng, lnb, xs, st, ag)
            P.barrier()

    def finish(self, out_name="out"):
        P = self.P
        d_out = self.outp(out_name, [T, D])
        for q4 in range(4):
            P.dma("sp", d_out[512 * q4:512 * (q4 + 1), :].rearrange("(t p) d -> p t d", p=128),
                  self.X[:, 4 * q4:4 * q4 + 4, :], R=[f"X{t}" for t in range(4 * q4, 4 * q4 + 4)], chan="out")
        P.wait_all("sp")


def _consts():
    ident = np.eye(128, dtype=np.float32)
    cmask = np.triu(np.ones((64, 64), np.float32))
    return ident, cmask


def build(stages, limit=None, stop_at=None, **kw):
    es = ExitStack()
    b = Builder(es)
    b.P.limit = limit
    b.P.stop_at = stop_at
    b.setup()
    if "fused" in stages:
        P = b.P
        b.l0_mixer()
        b.peer(0)
        cc1src = b.scratch("cc1src", [4, D], F32); cc1dst = b.scratch("cc1dst", [8, D], F32)
        b.cc2src = b.scratch("cc2src", [RB, 8], F32); cc2dst = b.scratch("cc2dst", [2 * RB, 8], F32)
        P.dma("sp", cc1src[0:3], b.X[125:128, NT - 1, :], R=[f"X{NT - 1}"], W=["cc1src"])
        P.dma("sp", cc1src[3:4], b.X[127:128, NT - 1, :], R=[f"X{NT - 1}"], W=["cc1src"])
        P.barrier()
        P.allgather_pairs(cc1src, cc1dst, R=["cc1src"], W=["cc_halo"])
        P.barrier()
        b.l1(scan_only=True, halo_src=cc1dst[0:3])
        P.allgather_pairs(b.cc2src, cc2dst, R=["cc2src"], W=["cc_hend"])
        P.barrier()
        b.l1(scan_only=False, halo_src=cc1dst[0:3], hinit_src=cc2dst[0:RB])
        b.peer(1)
        b.finish()
        return b, es
    if "l0mix" in stages:
        b.l0_mixer(**{k: v for k, v in kw.items() if k in ("npre", "nmain")})
    if "peer0" in stages:
        b.peer(0, **{k: v for k, v in kw.items() if k == "ngroups"})
    if "l1scan" in stages:
        b.l1(scan_only=True)
    if "l1mix" in stages:
        b.l1(scan_only=False, **{k: v for k, v in kw.items() if k == "ngroups"})
    if "peer1" in stages:
        b.peer(1, **{k: v for k, v in kw.items() if k == "ngroups"})
    b.finish()
    return b, es


def core_inputs(inputs, stages, x_cur=None, hinit=None, halo=None):
    ident, cmask = _consts()
    x = inputs["x"] if x_cur is None else x_cur
    maps = []
    shared = {"ident": ident, "cmask": cmask}
    if "fused" in stages:
        stages = ["fused", "l0mix", "peer0", "l1scan", "l1mix", "peer1"]
    if "l0mix" in stages:
        shared["w0"] = prep_l0_weights(inputs["mlstm_w_in"][0], inputs["w_out"][0]).reshape(NB0, 128, 4096)
        shared["bgi"] = np.ascontiguousarray(inputs["mlstm_b_gates"][0, 0:4].reshape(4, 1))
        shared["bgf"] = np.ascontiguousarray(inputs["mlstm_b_gates"][0, 4:8].reshape(4, 1))
        shared["norm_g"] = np.ascontiguousarray(inputs["mlstm_norm_g"][0])
        shared["ln1g0"] = np.ascontiguousarray(inputs["ln1_g"][0]); shared["ln1b0"] = np.ascontiguousarray(inputs["ln1_b"][0])
        shared["wkv0"] = _blockify(np.ascontiguousarray(inputs["xattn_w_kv"][0]))
    for l in (0, 1):
        if f"peer{l}" in stages:
            ut, wqb, skT = prep_peer(inputs["peer_u"][l], inputs["peer_w_q"][l], inputs["peer_subkeys"][l])
            shared[f"ut{l}"] = ut; shared[f"wq{l}"] = wqb; shared[f"skT{l}"] = skT
            shared[f"pv{l}"] = np.ascontiguousarray(inputs["peer_v"][l]).reshape(128, 128, 1024)
            shared[f"ln2g{l}"] = np.ascontiguousarray(inputs["ln2_g"][l]); shared[f"ln2b{l}"] = np.ascontiguousarray(inputs["ln2_b"][l])
    if "l1scan" in stages or "l1mix" in stages:
        shared["w1"] = prep_l1_weights(inputs["rglru_w_in"][0], inputs["w_out"][1])
        shared["conv_w"] = np.ascontiguousarray(inputs["rglru_conv_w"][0].reshape(4, 8, 96).transpose(2, 1, 0))
        shared["conv_b"] = _chan(inputs["rglru_conv_b"][0]); shared["rg_ba"] = _chan(inputs["rglru_b_a"][0])
        shared["rg_bx"] = _chan(inputs["rglru_b_x"][0]); shared["rg_lam"] = _chan(inputs["rglru_lam"][0])
        shared["rg_wa"] = np.ascontiguousarray(inputs["rglru_w_a"][0].transpose(1, 0, 2))
        shared["rg_wx"] = np.ascontiguousarray(inputs["rglru_w_x"][0].transpose(1, 0, 2))
        if "l1mix" in stages:
            shared["ln1g1"] = np.ascontiguousarray(inputs["ln1_g"][1]); shared["ln1b1"] = np.ascontiguousarray(inputs["ln1_b"][1])
            shared["wkv1"] = _blockify(np.ascontiguousarray(inputs["xattn_w_kv"][1]))
    for c in range(8):
        bi, half = c // 2, c % 2
        m = dict(shared)
        if "l1scan" in stages or "l1mix" in stages:
            m["xhalo"] = np.ascontiguousarray(x[bi, T - 3:T]) if half else np.zeros((3, D), np.float32)
            m["hinit"] = np.zeros((RB, 8), np.float32) if hinit is None else np.ascontiguousarray(hinit[c])
            m["memT"] = np.ascontiguousarray(inputs["mem"][bi].T.reshape(8, 128, 256).transpose(1, 0, 2))
        m["x_own"] = np.ascontiguousarray(x[bi, half * T:(half + 1) * T])
        m["flag"] = np.full((128, 1), float(half), np.float32)
        if "l0mix" in stages:
            m["x_pre"] = np.ascontiguousarray(x[bi, 0:T]) if half else np.zeros((T, D), np.float32)
            m["memT"] = np.ascontiguousarray(inputs["mem"][bi].T.reshape(8, 128, 256).transpose(1, 0, 2))
        maps.append(m)
    return maps


DBG = {}
def prep_peer(u, wq, sk):
    ut = np.ascontiguousarray(u.reshape(128, 128, 8, 128).transpose(0, 3, 2, 1)).reshape(128, 128, 1024)
    wqb = np.stack([_blockify(np.ascontiguousarray(wq[:, b * 512:(b + 1) * 512])) for b in range(4)]).reshape(4, 128, 4096)
    skT = np.ascontiguousarray(sk.transpose(2, 0, 1))
    return ut, wqb, skT


def _peer_method(self, layer, ngroups=NG):
    nc, P, ps, sb = self.nc, self.P, self.ps, self.sb
    TN = 4
    with ExitStack() as es:
        S = lambda n, sh, dt=F32: sb(f"{n}_p{layer}", sh, dt, es=es)
        self.wblk = [S("wblk0", [128, 8, 512], BF16)]
        d_ut = self.inp(f"ut{layer}", [128, 128, 1024]); d_v = self.inp(f"pv{layer}", [128, 128, 1024])
        d_wq = self.inp(f"wq{layer}", [4, 128, 4096]); d_sk = self.inp(f"skT{layer}", [128, 2, 128])
        d_lng = self.inp(f"ln2g{layer}", [D]); d_lnb = self.inp(f"ln2b{layer}", [D])
        utb = self.scratch(f"utb{layer}", [128, 128, 1024], BF16); vb = self.scratch(f"vb{layer}", [128, 128, 1024], BF16)
        wqb = self.scratch(f"wqb{layer}", [4, 128, 4096], BF16)
        for b in range(4):
            P.dma("pool", wqb[b].rearrange("p (a c) -> p a c", c=2048), d_wq[b].rearrange("p (a c) -> p a c", c=2048), W=[f"wqb{layer}_{b}"])
        for i0 in range(0, 128, 4):
            P.dma("pool", utb[i0:i0 + 4], d_ut[i0:i0 + 4], W=[f"utb{layer}_{i0 // 4}"])
            P.dma("pool", vb[i0:i0 + 4], d_v[i0:i0 + 4], W=[f"vb{layer}_{i0 // 4}"])
        if DBG.get("cast_barrier", True):
            P.barrier()
        GT = S("GT", [128, 128, G], BF16)
        xTg = S("xTg", [128, 8, G], BF16); qT = S("qT", [128, 16, G], BF16)
        ublk = [S(f"ublk{k}", [128, 1024], BF16) for k in range(4)]
        vblk = [S(f"vblk{k}", [128, 1024], BF16) for k in range(4)]
        arA = S("arA", [128, 1024]); arB = S("arB", [128, 1024])
        s_sb = arA[:, 0:512].rearrange("p (j k) -> p j k", k=128); s2 = arA[:, 512:1024].rearrange("p (j k) -> p j k", k=128)
        eq = arA[:, 0:512].rearrange("p (h r a) -> p h r a", r=16, a=16); prod = arA[:, 512:1024].rearrange("p (h r a) -> p h r a", r=16, a=16)
        cand = arB[:, 0:512].rearrange("p (h a b) -> p h a b", a=16, b=16); cand2 = arB[:, 512:1024].rearrange("p (h a b) -> p h a b", a=16, b=16)
        sv = S("sv", [128, 16, 16]); si = S("si", [128, 16, 16], U32); sif = S("sif", [128, 16, 16])
        fv = S("fv", [128, 8, 16]); fp = S("fp", [128, 8, 16], U32); pf = S("pf", [128, 8, 16]); bfl = S("bfl", [128, 8, 16]); af = S("af", [128, 8, 16])
        ex = S("ex", [128, 8, 16]); zs = S("zs", [128, 8]); tri = S("tri", [128, 3, 128])
        hr3 = S("hr3", [128, 3, G], BF16)
        Bt = [S("Bt0", [128, TN, 128], BF16), S("Bt1", [128, TN, 128], BF16)]
        At = [S("At0", [128, TN, 128], BF16), S("At1", [128, TN, 128], BF16)]
        eqt = S("eqt", [128, TN, 128], BF16)
        ge = [S("ge0", [128, G], BF16), S("ge1", [128, G], BF16)]; coef = [S("coef0", [128, G], BF16), S("coef1", [128, G], BF16)]
        skT32 = S("skT32", [128, 2, 128]); skTb = S("skTb", [128, 2, 128], BF16)
        iot32 = S("iot32", [128, 128]); iot = S("iot", [128, 128], BF16); iot16 = S("iot16", [128, 16])
        lng = S("lng", [128, D]); lnb = S("lnb", [128, D]); st = S("st", [128, 2, 6]); ag = S("ag", [128, 4])
        P.dma("sp", skT32[:], d_sk, W=["skT32"])
        P.op("dve", lambda e: e.tensor_copy(out=skTb[:], in_=skT32[:]), R=["skT32"], W=["skTb"])
        P.op("pool", lambda e: e.iota(iot32[:], pattern=[[1, 128]], base=0, channel_multiplier=0, allow_small_or_imprecise_dtypes=True), W=["iot32"])
        P.op("dve", lambda e: e.tensor_copy(out=iot[:], in_=iot32[:]), R=["iot32"], W=["iot"])
        P.op("pool", lambda e: e.iota(iot16[:], pattern=[[1, 16]], base=0, channel_multiplier=0, allow_small_or_imprecise_dtypes=True), W=["iot16"])
        P.dma("sp", lng[:], d_lng.partition_broadcast(128), W=["lng"])
        P.dma("sp", lnb[:], d_lnb.partition_broadcast(128), W=["lnb"])
        def load_tb(i):
            b = i % 4
            P.dma("sp", ublk[b][:], utb[i], R=[f"utb{layer}_{i // 4}"], W=[f"ublk{b}"])
            P.dma("sp", vblk[b][:], vb[i], R=[f"vb{layer}_{i // 4}"], W=[f"vblk{b}"])

        for g in range(ngroups):
            self.make_xT(xTg, [self.X[:, 2 * g + tt, :] for tt in range(2)], [f"X{2 * g + tt}" for tt in range(2)], "xTg")
            sched = self.Sched(self, [(wqb[b], f"wqb{layer}_{b}") for b in range(4)])
            for blk in range(4):
                wi = sched.get()
                for j in range(4):
                    bk = 4 + (j % 2)
                    for kc in range(8):
                        P.op("pe", lambda e, j=j, kc=kc, bk=bk, wi=wi: e.matmul(
                            ps[:, bk, 0:G], lhsT=self.wblk[wi][:, kc, j * 128:(j + 1) * 128], rhs=xTg[:, kc, :],
                            start=(kc == 0), stop=(kc == 7)), R=["xTg", f"wblk{wi}"], W=[f"ps{bk}"])
                    P.op("act", lambda e, j=j, bk=bk, blk=blk: e.copy(out=qT[:, blk * 4 + j, :], in_=ps[:, bk, 0:G]), R=[f"ps{bk}"], W=["qT"])
            for tt in range(2):
                tok = slice(tt * 128, (tt + 1) * 128)
                for hh in range(4):
                    bk = hh
                    for j4 in range(4):
                        jj = hh * 4 + j4
                        P.op("pe", lambda e, jj=jj, j4=j4, bk=bk: e.matmul(
                            ps[:, bk, j4 * 128:(j4 + 1) * 128], lhsT=qT[:, jj, tok], rhs=skTb[:, jj % 2, :],
                            start=True, stop=True), R=["qT", "skTb"], W=[f"ps{bk}"])
                    P.op("act", lambda e, bk=bk: e.copy(out=s_sb, in_=ps[:, bk, :].rearrange("p (j k) -> p j k", k=128)),
                         R=[f"ps{bk}"], W=["arA"])
                    for j4 in range(4):
                        jj = hh * 4 + j4
                        P.op("dve", lambda e, jj=jj, j4=j4: e.max(out=sv[:, jj, 0:8], in_=s_sb[:, j4, :]), R=["arA"], W=["sv"])
                        P.op("dve", lambda e, jj=jj, j4=j4: e.max_index(out=si[:, jj, 0:8], in_max=sv[:, jj, 0:8], in_values=s_sb[:, j4, :]), R=["arA", "sv"], W=["si"])
                        P.op("dve", lambda e, jj=jj, j4=j4: e.match_replace(out=s2[:, j4, :], in_to_replace=sv[:, jj, 0:8], in_values=s_sb[:, j4, :], imm_value=-1e30),
                             R=["arA", "sv"], W=["arA"])
                        P.op("dve", lambda e, jj=jj, j4=j4: e.max(out=sv[:, jj, 8:16], in_=s2[:, j4, :]), R=["arA"], W=["sv"])
                        P.op("dve", lambda e, jj=jj, j4=j4: e.max_index(out=si[:, jj, 8:16], in_max=sv[:, jj, 8:16], in_values=s2[:, j4, :]), R=["arA", "sv"], W=["si"])
                P.op("dve", lambda e: e.tensor_copy(out=sif[:], in_=si[:]), R=["si"], W=["sif"])
                svv = sv[:].rearrange("p (h two) a -> p h two a", two=2)
                sfv = sif[:].rearrange("p (h two) a -> p h two a", two=2)
                for hq in range(4):
                    hs = slice(2 * hq, 2 * hq + 2)
                    P.op("dve", lambda e: e.tensor_tensor(out=cand, in0=svv[:, hs, 0, :].unsqueeze(3).broadcast_to([128, 2, 16, 16]),
                                                          in1=svv[:, hs, 1, :].unsqueeze(2).broadcast_to([128, 2, 16, 16]), op=ALU.add), R=["sv"], W=["arB"])
                    for h4 in range(2):
                        h = 2 * hq + h4
                        c1 = cand[:, h4].rearrange("p a b -> p (a b)"); c2 = cand2[:, h4].rearrange("p a b -> p (a b)")
                        P.op("dve", lambda e: e.max(out=fv[:, h, 0:8], in_=c1), R=["arB"], W=["fv"])
                        P.op("dve", lambda e: e.max_index(out=fp[:, h, 0:8], in_max=fv[:, h, 0:8], in_values=c1), R=["arB", "fv"], W=["fp"])
                        P.op("dve", lambda e: e.match_replace(out=c2, in_to_replace=fv[:, h, 0:8], in_values=c1, imm_value=-1e30), R=["arB", "fv"], W=["arB"])
                        P.op("dve", lambda e: e.max(out=fv[:, h, 8:16], in_=c2), R=["arB"], W=["fv"])
                        P.op("dve", lambda e: e.max_index(out=fp[:, h, 8:16], in_max=fv[:, h, 8:16], in_values=c2), R=["arB", "fv"], W=["fp"])
                gwv = tri[:, 2, :].rearrange("p (h r) -> p h r", r=16)
                P.op("dve", lambda e: e.tensor_tensor(out=ex[:], in0=fv[:], in1=fv[:, :, 0:1].broadcast_to([128, 8, 16]), op=ALU.subtract), R=["fv"], W=["ex"])
                P.op("act", lambda e: e.activation(out=ex[:], in_=ex[:], func=AF.Exp), R=["ex"], W=["ex"])
                P.op("dve", lambda e: e.tensor_reduce(out=zs[:], in_=ex[:], axis=AX.X, op=ALU.add), R=["ex"], W=["zs"])
                P.op("dve", lambda e: e.reciprocal(out=zs[:], in_=zs[:]), R=["zs"], W=["zs"])
                P.op("dve", lambda e: e.tensor_tensor(out=gwv, in0=ex[:], in1=zs[:].unsqueeze(2).broadcast_to([128, 8, 16]), op=ALU.mult), R=["ex", "zs"], W=["tri2"])
                P.op("dve", lambda e: e.tensor_single_scalar(out=pf[:].bitcast(U32), in_=fp[:], scalar=15, op=ALU.bitwise_and), R=["fp"], W=["pf"])
                P.op("dve", lambda e: e.tensor_copy(out=bfl[:], in_=pf[:].bitcast(U32)), R=["pf"], W=["bfl"])
                P.op("dve", lambda e: e.tensor_single_scalar(out=pf[:].bitcast(U32), in_=fp[:], scalar=4, op=ALU.logical_shift_right), R=["fp", "bfl"], W=["pf"])
                P.op("dve", lambda e: e.tensor_copy(out=af[:], in_=pf[:].bitcast(U32)), R=["pf"], W=["af"])
                i16 = iot16[:].unsqueeze(1).unsqueeze(1).broadcast_to([128, 2, 16, 16])
                for which, (srcidx, two) in enumerate(((af, 0), (bfl, 1))):
                    ov = tri[:, which, :].rearrange("p (h r) -> p h r", r=16)
                    for hq in range(4):
                        hs = slice(2 * hq, 2 * hq + 2)
                        P.op("dve", lambda e: e.tensor_tensor(out=eq, in0=srcidx[:, hs, :].unsqueeze(3).broadcast_to([128, 2, 16, 16]), in1=i16, op=ALU.is_equal),
                             R=["af", "bfl", "iot16"], W=["arA"])
                        P.op("dve", lambda e: e.tensor_tensor(out=prod, in0=eq, in1=sfv[:, hs, two, :].unsqueeze(2).broadcast_to([128, 2, 16, 16]), op=ALU.mult),
                             R=["arA", "sif"], W=["arA"])
                        P.op("dve", lambda e: e.tensor_reduce(out=ov[:, hs, :], in_=prod, axis=AX.X, op=ALU.add), R=["arA"], W=[f"tri{which}"])
                for q3 in range(3):
                    P.op("pe", lambda e, q3=q3: e.transpose(out=ps[:, 6, q3 * 128:(q3 + 1) * 128], in_=tri[:, q3, :], identity=self.ident[:]),
                         R=[f"tri{q3}", "ident"], W=["ps6"])
                P.op("act", lambda e: e.copy(out=hr3[:, :, tok], in_=ps[:, 6, 0:384].rearrange("p (q t) -> p q t", q=3)), R=["ps6"], W=["hr3"])
            P.cp(f"peerA{layer}")
            ib = iot[:].unsqueeze(1).broadcast_to([128, TN, 128])
            for tb in range(G // TN):
                t0 = tb * TN
                bi = tb % 2
                P.op("dve", lambda e: e.tensor_tensor(out=Bt[bi][:], in0=ib, in1=hr3[:, 1, t0:t0 + TN].unsqueeze(2).broadcast_to([128, TN, 128]), op=ALU.is_equal),
                     R=["iot", "hr3"], W=[f"Bt{bi}"])
                P.op("dve", lambda e: e.tensor_tensor(out=eqt[:], in0=ib, in1=hr3[:, 0, t0:t0 + TN].unsqueeze(2).broadcast_to([128, TN, 128]), op=ALU.is_equal),
                     R=["iot", "hr3"], W=["eqt"])
                P.op(DBG.get("at_eng", "dve"), lambda e: e.tensor_tensor(out=At[bi][:], in0=eqt[:], in1=hr3[:, 2, t0:t0 + TN].unsqueeze(2).broadcast_to([128, TN, 128]), op=ALU.mult),
                     R=["eqt", "hr3"], W=[f"At{bi}"])
                for t in range(TN):
                    tg = t0 + t
                    bk = (tg // 4) % 8
                    P.op("pe", lambda e, t=t, tg=tg, bk=bk: e.matmul(ps[:, bk, (tg % 4) * 128:(tg % 4 + 1) * 128], lhsT=Bt[bi][:, t, :], rhs=At[bi][:, t, :],
                                                                     start=True, stop=True), R=[f"Bt{bi}", f"At{bi}"], W=[f"ps{bk}"])
                if (t0 + TN) % 16 == 0 and not DBG.get("noevac"):
                    b0 = ((t0 + TN - 16) // 4) % 8
                    tq = t0 + TN - 16
                    P.op("act", lambda e: e.copy(out=GT[:, :, tq:tq + 16], in_=ps[:, b0:b0 + 4, :].rearrange("p b (t i) -> p i (b t)", t=4)),
                         R=[f"ps{b}" for b in range(b0, b0 + 4)], W=["GT"])
            P.cp(f"peerB{layer}")
            def emit_ht(i):
                hb = 4 + (i % 2)
                for kc in range(8):
                    P.op("pe", lambda e, kc=kc: e.matmul(ps[:, hb, 0:G], lhsT=ublk[i % 4][:, kc * 128:(kc + 1) * 128], rhs=xTg[:, kc, :],
                                                         start=(kc == 0), stop=(kc == 7)), R=[f"ublk{i % 4}", "xTg"], W=[f"ps{hb}"])

            def emit_rest(i):
                hb = 4 + (i % 2)
                gi = i % 2
                P.op("act", lambda e: e.activation(out=ge[gi][:], in_=ps[:, hb, 0:G], func=AF.Gelu), R=[f"ps{hb}"], W=[f"ge{gi}"])
                P.op("dve", lambda e: e.tensor_tensor(out=coef[gi][:], in0=ge[gi][:], in1=GT[:, i, :], op=ALU.mult), R=[f"ge{gi}", "GT"], W=[f"coef{gi}"])
                for tt in range(2):
                    for half in range(2):
                        yb = 2 * tt + half
                        P.op("pe", lambda e, tt=tt, half=half, yb=yb: e.matmul(
                            ps[:, yb, :], lhsT=coef[gi][:, tt * 128:(tt + 1) * 128], rhs=vblk[i % 4][:, half * 512:(half + 1) * 512],
                            start=(i == 0), stop=(i == 127)), R=[f"coef{gi}", f"vblk{i % 4}"], W=[f"ps{yb}"])

            for k in range(3):
                load_tb(k)
            emit_ht(0)
            for i in range(128):
                if i + 1 < 128:
                    emit_ht(i + 1)
                emit_rest(i)
                if i + 3 < 128:
                    load_tb(i + 3)
            P.cp(f"peerC{layer}")
            for tt in range(2):
                self.resid_ln(2 * g + tt, ps[:, 2 * tt:2 * tt + 2, :], [f"ps{2 * tt}", f"ps{2 * tt + 1}"], lng, lnb, None, st, ag)
        P.barrier()


Builder.peer = _peer_method


NB1 = 9
RB = 96


def prep_l1_weights(w_in, w_out):
    gate = w_in[:, 0:768]; xr = w_in[:, 768:1536]; xq = w_in[:, 1536:1792]
    blocks = []
    for src in (gate, xr):
        for hb in range(2):
            blocks.append(np.concatenate([_pad_cols(src[:, g * 96:(g + 1) * 96], 128) for g in range(4 * hb, 4 * hb + 4)], axis=1))
    blocks.append(_pad_cols(xq, 512))
    out = [_blockify(np.ascontiguousarray(b, dtype=np.float32)).reshape(128, 4096) for b in blocks]
    for cb in range(4):
        blk = np.zeros((128, 10, 256), np.float32)
        for g in range(8):
            blk[0:96, g, :] = w_out[g * 96:(g + 1) * 96, cb * 256:(cb + 1) * 256]
        for c2 in range(2):
            blk[:, 8 + c2, :] = w_out[768 + c2 * 128:768 + (c2 + 1) * 128, cb * 256:(cb + 1) * 256]
        out.append(_pad_cols(blk.reshape(128, 2560), 4096))
    return np.stack(out)


def _chan(v):
    return np.ascontiguousarray(np.asarray(v, np.float32).reshape(8, 96).T)


def _l1_method(self, scan_only=False, ngroups=NG, halo_src=None, hinit_src=None):
    nc, P, ps, sb = self.nc, self.P, self.ps, self.sb
    ident = self.ident
    tag = "s" if scan_only else "m"
    with ExitStack() as es:
        S = lambda n, sh, dt=F32: sb(f"{n}_l1{tag}", sh, dt, es=es)
        if "w1" in self.din:
            d_w1 = self.din["w1"]; w1b = self.w1b
        else:
            d_w1 = self.inp("w1", [NB1, 128, 4096])
            w1b = self.w1b = self.scratch("w1b", [NB1, 128, 4096], BF16)
            for b in range(NB1):
                P.dma("pool", w1b[b].rearrange("p (a c) -> p a c", c=2048), d_w1[b].rearrange("p (a c) -> p a c", c=2048), W=[f"w1b{b}"])
            P.barrier()
        g_in = lambda n, sh: self.din[n] if n in self.din else self.inp(n, sh)
        d_halo = halo_src if halo_src is not None else g_in("xhalo", [3, D])
        d_hinit = hinit_src if hinit_src is not None else (None if (scan_only and halo_src is not None) else g_in("hinit", [RB, 8]))
        d_cw = g_in("conv_w", [RB, 8, 4]); d_cb = g_in("conv_b", [RB, 8]); d_wa = g_in("rg_wa", [RB, 8, RB]); d_wx = g_in("rg_wx", [RB, 8, RB])
        d_ba = g_in("rg_ba", [RB, 8]); d_bx = g_in("rg_bx", [RB, 8]); d_lam = g_in("rg_lam", [RB, 8])
        self.wblk = [S("wblk0", [128, 8, 512], BF16), S("wblk1", [128, 8, 512], BF16)]
        xTg = S("xTg", [128, 8, G], BF16)
        xrb = S("xrb", [RB, 8, 3 + G]); xc = S("xc", [RB, 8, G]); xcb = S("xcb", [RB, 8, G], BF16)
        cw = S("cw", [RB, 8, 4]); cb = S("cb", [RB, 8]); wa32 = S("wa32", [RB, 8, RB]); wx32 = S("wx32", [RB, 8, RB])
        wab = S("wab", [RB, 8, RB], BF16); wxb = S("wxb", [RB, 8, RB], BF16)
        ba = S("ba", [RB, 8]); bx = S("bx", [RB, 8]); lamc = S("lamc", [RB, 8]); hcar = S("hcar", [RB, 8])
        rt = S("rt", [RB, G]); it = S("it", [RB, G]); at = S("at", [RB, G]); ut = S("ut", [RB, G]); ht = S("ht", [RB, G]); tmp = S("tmp", [RB, G])
        xh = S("xh", [3, D]); xTh = S("xTh", [128, 8, 4], BF16)
        for dst, src, k in ((cw, d_cw, "cw"), (cb, d_cb, "cb"), (wa32, d_wa, "wa32"), (wx32, d_wx, "wx32"), (ba, d_ba, "ba"), (bx, d_bx, "bx"),
                            (lamc, d_lam, "lamc"), (xh, d_halo, "xh")):
            P.dma("sp", dst[:], src, R=(["cc_halo"] if (k == "xh" and halo_src is not None) else []), W=[k])
        if d_hinit is None:
            P.op("dve", lambda e: e.memset(hcar[:], 0.0), W=["hcar"])
        else:
            P.dma("sp", hcar[:], d_hinit, R=(["cc_hend"] if hinit_src is not None else []), W=["hcar"])
        if halo_src is not None:
            P.op("dve", lambda e: e.tensor_scalar(out=xh[:], in0=xh[:], scalar1=self.flag[0:3, 0:1], scalar2=None, op0=ALU.mult), R=["xh", "flag"], W=["xh"])
        if hinit_src is not None:
            P.op("dve", lambda e: e.tensor_scalar(out=hcar[:], in0=hcar[:], scalar1=self.flag[0:RB, 0:1], scalar2=None, op0=ALU.mult), R=["hcar", "flag"], W=["hcar"])
        P.op("dve", lambda e: e.tensor_copy(out=wab[:], in_=wa32[:]), R=["wa32"], W=["wab"])
        P.op("dve", lambda e: e.tensor_copy(out=wxb[:], in_=wx32[:]), R=["wx32"], W=["wxb"])
        P.op("act", lambda e: e.activation(out=lamc[:], in_=lamc[:], func=AF.Exp, scale=-1.0), R=["lamc"], W=["lamc"])
        P.op("act", lambda e: e.activation(out=lamc[:], in_=lamc[:], func=AF.Ln, bias=1.0), R=["lamc"], W=["lamc"])
        P.op("dve", lambda e: e.tensor_scalar(out=lamc[:], in0=lamc[:], scalar1=-8.0, scalar2=None, op0=ALU.mult), R=["lamc"], W=["lamc"])
        if not scan_only:
            d_lng = self.inp("ln1g1", [D]); d_lnb = self.inp("ln1b1", [D])
            kxT, vxaug = self.xattn_setup(1, es)
            gg = S("gg", [RB, 8, G], BF16); hmT = S("hmT", [RB, 8, G], BF16); xqT = S("xqT", [64, 4, G], BF16)
            hat = [S("hat0", [64, 256], BF16), S("hat1", [64, 256], BF16)]; haT = S("haT", [128, 2, G], BF16)
            Esb = S("Esb", [128, 8, 64], BF16); rsb = S("rsb", [64, 4])
            lng = S("lng", [128, D]); lnb = S("lnb", [128, D]); xs = S("xs", [128, D]); st = S("st", [128, 2, 6]); ag = S("ag", [128, 4])
            P.dma("sp", lng[:], d_lng.partition_broadcast(128), W=["lng"])
            P.dma("sp", lnb[:], d_lnb.partition_broadcast(128), W=["lnb"])
        srcs = [(w1b[b], f"w1b{b}") for b in (2, 3)]
        for g in range(ngroups):
            srcs += [(w1b[b], f"w1b{b}") for b in ((2, 3) if scan_only else (2, 3, 0, 1, 4, 5, 6, 7, 8))]
        sched = self.Sched(self, srcs)

        def xr_proj(wi, hb, rhs, n, dst_off, rkey):
            for j in range(4):
                g8 = 4 * hb + j
                bk = 2 + (j % 2)
                for kc in range(8):
                    P.op("pe", lambda e, j=j, kc=kc, bk=bk: e.matmul(
                        ps[0:RB, bk, 0:n], lhsT=self.wblk[wi][:, kc, j * 128:j * 128 + RB], rhs=rhs[:, kc, 0:n],
                        start=(kc == 0), stop=(kc == 7)), R=[rkey, f"wblk{wi}"], W=[f"ps{bk}"])
                P.op("act", lambda e, g8=g8, bk=bk: e.copy(out=xrb[:, g8, dst_off:dst_off + n], in_=ps[0:RB, bk, 0:n]), R=[f"ps{bk}"], W=["xrb"])

        for kc in range(8):
            P.op("pe", lambda e, kc=kc: e.transpose(out=ps[:, 0, kc * 4:kc * 4 + 3], in_=xh[0:3, kc * 128:(kc + 1) * 128], identity=ident[0:3, 0:3]),
                 R=["xh", "ident"], W=["ps0"])
        P.op("act", lambda e: e.copy(out=xTh[:, :, 0:3], in_=ps[:, 0, 0:32].rearrange("p (k c) -> p k c", c=4)[:, :, 0:3]), R=["ps0"], W=["xTh"])
        for hb in range(2):
            wi = sched.get()
            xr_proj(wi, hb, xTh, 3, 0, "xTh")
        mixi = 0
        for g in range(ngroups):
            self.make_xT(xTg, [self.X[:, 2 * g + tt, :] for tt in range(2)], [f"X{2 * g + tt}" for tt in range(2)], "xTg")
            for hb in range(2):
                wi = sched.get()
                xr_proj(wi, hb, xTg, G, 3, "xTg")
            if not scan_only:
                for hb in range(2):
                    wi = sched.get()
                    for j in range(4):
                        g8 = 4 * hb + j
                        bk = 2 + (j % 2)
                        for kc in range(8):
                            P.op("pe", lambda e, j=j, kc=kc, bk=bk: e.matmul(
                                ps[0:RB, bk, 0:G], lhsT=self.wblk[wi][:, kc, j * 128:j * 128 + RB], rhs=xTg[:, kc, :],
                                start=(kc == 0), stop=(kc == 7)), R=["xTg", f"wblk{wi}"], W=[f"ps{bk}"])
                        P.op("act", lambda e, g8=g8, bk=bk: e.activation(out=gg[:, g8, :], in_=ps[0:RB, bk, 0:G], func=AF.Gelu), R=[f"ps{bk}"], W=["gg"])
                wi = sched.get()
                for j in range(4):
                    bk = 2 + (j % 2)
                    for kc in range(8):
                        P.op("pe", lambda e, j=j, kc=kc, bk=bk: e.matmul(
                            ps[0:64, bk, 0:G], lhsT=self.wblk[wi][:, kc, j * 64:(j + 1) * 64], rhs=xTg[:, kc, :],
                            start=(kc == 0), stop=(kc == 7)), R=["xTg", f"wblk{wi}"], W=[f"ps{bk}"])
                    P.op("act", lambda e, j=j, bk=bk: e.copy(out=xqT[:, j, :], in_=ps[0:64, bk, 0:G]), R=[f"ps{bk}"], W=["xqT"])
            for g8 in range(8):
                for w in range(4):
                    if w == 0:
                        P.op("dve", lambda e: e.tensor_scalar(out=xc[:, g8, :], in0=xrb[:, g8, 0:G], scalar1=cw[:, g8, 0:1], scalar2=cb[:, g8:g8 + 1],
                                                              op0=ALU.mult, op1=ALU.add), R=["xrb", "cw", "cb"], W=["xc"])
                    else:
                        P.op("dve", lambda e, w=w: e.scalar_tensor_tensor(out=xc[:, g8, :], in0=xrb[:, g8, w:w + G], scalar=cw[:, g8, w:w + 1], in1=xc[:, g8, :],
                                                                          op0=ALU.mult, op1=ALU.add), R=["xrb", "cw", "xc"], W=["xc"])
                P.op("act", lambda e: e.copy(out=xcb[:, g8, :], in_=xc[:, g8, :]), R=["xc"], W=["xcb"])
            P.op("dve", lambda e: e.tensor_copy(out=xrb[:, :, 0:3], in_=xrb[:, :, G:G + 3]), R=["xrb"], W=["xrb"])
            for g8 in range(8):
                P.op("pe", lambda e: e.matmul(ps[0:RB, 4, 0:G], lhsT=wab[:, g8, :], rhs=xcb[:, g8, :], start=True, stop=True), R=["wab", "xcb"], W=["ps4"])
                P.op("pe", lambda e: e.matmul(ps[0:RB, 5, 0:G], lhsT=wxb[:, g8, :], rhs=xcb[:, g8, :], start=True, stop=True), R=["wxb", "xcb"], W=["ps5"])
                P.op("act", lambda e: e.activation(out=rt[:], in_=ps[0:RB, 4, 0:G], func=AF.Sigmoid, bias=ba[:, g8:g8 + 1]), R=["ps4", "ba"], W=["rt"])
                P.op("act", lambda e: e.activation(out=it[:], in_=ps[0:RB, 5, 0:G], func=AF.Sigmoid, bias=bx[:, g8:g8 + 1]), R=["ps5", "bx"], W=["it"])
                P.op("act", lambda e: e.activation(out=at[:], in_=rt[:], func=AF.Exp, scale=lamc[:, g8:g8 + 1]), R=["rt", "lamc"], W=["at"])
                P.op("dve", lambda e: e.tensor_tensor(out=tmp[:], in0=at[:], in1=at[:], op=ALU.mult), R=["at"], W=["tmp"])
                P.op("dve", lambda e: e.tensor_scalar(out=tmp[:], in0=tmp[:], scalar1=-1.0, scalar2=1.0, op0=ALU.mult, op1=ALU.add), R=["tmp"], W=["tmp"])
                P.op("act", lambda e: e.activation(out=tmp[:], in_=tmp[:], func=AF.Sqrt), R=["tmp"], W=["tmp"])
                P.op("dve", lambda e: e.tensor_tensor(out=ut[:], in0=it[:], in1=xc[:, g8, :], op=ALU.mult), R=["it", "xc"], W=["ut"])
                P.op("dve", lambda e: e.tensor_tensor(out=ut[:], in0=ut[:], in1=tmp[:], op=ALU.mult), R=["ut", "tmp"], W=["ut"])
                P.op("dve", lambda e: e.tensor_tensor_scan(out=ht[:], data0=at[:], data1=ut[:], initial=hcar[:, g8:g8 + 1], op0=ALU.mult, op1=ALU.add),
                     R=["at", "ut", "hcar"], W=["ht"])
                P.op("dve", lambda e: e.tensor_copy(out=hcar[:, g8:g8 + 1], in_=ht[:, G - 1:G]), R=["ht"], W=["hcar"])
                if not scan_only:
                    P.op("dve", lambda e: e.tensor_tensor(out=hmT[:, g8, :], in0=ht[:], in1=gg[:, g8, :], op=ALU.mult), R=["ht", "gg"], W=["hmT"])
            if scan_only:
                continue
            for ck in range(4):
                c0 = ck * 64
                hx = hat[mixi]; hk = f"hat{mixi}"; mixi ^= 1
                self.xattn_chunk(kxT, vxaug, xqT, c0, 64, hx[:, :], hk, Esb, rsb)
                pb = ps[:, 2, :].bitcast(BF16)
                for c2 in range(2):
                    P.op("pe", lambda e, c2=c2: e.transpose(out=pb[:, c2 * 64:(c2 + 1) * 64], in_=hx[:, c2 * 128:(c2 + 1) * 128], identity=self.identb[0:64, 0:64]),
                         R=[hk, "identb"], W=["ps2"])
                P.op("act", lambda e: e.copy(out=haT[:, :, c0:c0 + 64], in_=pb[:, 0:128].rearrange("p (k t) -> p k t", t=64)), R=["ps2"], W=["haT"])
            banks = {0: (0, 1), 1: (5, 6)}
            for cbk in range(4):
                wi = sched.get()
                wv = self.wblk[wi][:].rearrange("p k c -> p (k c)")[:, 0:2560].rearrange("p (ch c) -> p ch c", c=256)
                for tt in range(2):
                    bk = banks[tt][cbk // 2]
                    dst = ps[:, bk, (cbk % 2) * 256:(cbk % 2 + 1) * 256]
                    for ch in range(10):
                        if ch < 8:
                            lhsT = hmT[:, ch, tt * 128:(tt + 1) * 128]; rhs = wv[0:RB, ch, :]; rk = "hmT"
                        else:
                            lhsT = haT[:, ch - 8, tt * 128:(tt + 1) * 128]; rhs = wv[:, ch, :]; rk = "haT"
                        P.op("pe", lambda e, lhsT=lhsT, rhs=rhs, dst=dst, ch=ch: e.matmul(dst, lhsT=lhsT, rhs=rhs, start=(ch == 0), stop=(ch == 9)),
                             R=[rk, f"wblk{wi}"], W=[f"ps{bk}"])
            for tt in range(2):
                b0 = banks[tt][0]
                self.resid_ln(2 * g + tt, ps[:, b0:b0 + 2, :], [f"ps{b0}", f"ps{b0 + 1}"], lng, lnb, xs, st, ag)
        if scan_only:
            if halo_src is None:
                d_hend = self.outp("hend", [RB, 8])
                P.dma("sp", d_hend, hcar[:], R=["hcar"], chan="out")
            else:
                P.dma("sp", self.cc2src, hcar[:], R=["hcar"], W=["cc2src"])
        P.barrier()


Builder.l1 = _l1_method


def _run(stages, inputs, x_cur=None, hinit=None, **kw):
    b, es = build(stages, **kw)
    maps = core_inputs(inputs, stages, x_cur=x_cur, hinit=hinit)
    maps = [{k: v for k, v in m.items() if k in b.din} for m in maps]
    res = run_bass_kernel_spmd(b.nc, maps, core_ids=list(range(8)))
    es.close()
    return res.results


def _gather(results, name="out"):
    x = np.empty((4, 2 * T, D), np.float32)
    for c in range(8):
        x[c // 2, (c % 2) * T:(c % 2 + 1) * T] = results[c][name]
    return x


def kernel(**inputs):
    inputs = {k: np.asarray(v) for k, v in inputs.items()}
    return _gather(_run(["fused"], inputs))
```

```python
import numpy as np
from contextlib import ExitStack
import concourse.bass as bass
import concourse.mybir as mybir
from concourse.bass_utils import run_bass_kernel_spmd

F32 = mybir.dt.float32
BF16 = mybir.dt.bfloat16
U32 = mybir.dt.uint32
I32 = mybir.dt.int32
AF = mybir.ActivationFunctionType
ALU = mybir.AluOpType
AX = mybir.AxisListType


class Prog:
    NDS = 24

    def __init__(self, nc, es):
        self.nc = nc
        self.es = es
        self.eng = {"pe": nc.tensor, "dve": nc.vector, "act": nc.scalar, "pool": nc.gpsimd, "sp": nc.sync}
        self.sem = {k: es.enter_context(nc.semaphore("sem_" + k)) for k in self.eng}
        self.cnt = {k: 0 for k in self.eng}
        self.seen = {k: {} for k in self.eng}
        self.dsem = []
        self.dcnt = []
        self.dq = {}
        for q, n in (("sp", 20), ("pool", 8), ("act", 4)):
            self.dq[q] = [len(self.dsem) + i for i in range(n)]
            self.dsem += [es.enter_context(nc.semaphore(f"dsem_{q}{i}")) for i in range(n)]
            self.dcnt += [0] * n
        self.dnext = {q: 0 for q in self.dq}
        self.last_w = {}
        self.readers = {}
        self.n_ins = 0

    def _semof(self, kind, name):
        return self.sem[name] if kind == "e" else self.dsem[name]

    def _wait_ev(self, eng, kind, name, c):
        if self.seen[eng].get((kind, name), 0) >= c:
            return
        self.seen[eng][(kind, name)] = c
        self.eng[eng].wait_ge(self._semof(kind, name), c)

    def _wait(self, eng, R, W):
        deps = []
        for k in R:
            if k in self.last_w:
                deps.append(self.last_w[k])
        for k in W:
            if k in self.last_w:
                deps.append(self.last_w[k])
            deps.extend(self.readers.get(k, ()))
        best = {}
        for kind, name, c in deps:
            if kind == "e" and name == eng and eng == "pe":
                continue
            key = (kind, name)
            if c > best.get(key, 0):
                best[key] = c
        for (kind, name), c in best.items():
            self._wait_ev(eng, kind, name, c)

    def _commit(self, ev, R, W):
        for k in W:
            self.last_w[k] = ev
            self.readers[k] = []
        for k in R:
            self.readers.setdefault(k, []).append(ev)

    limit = None
    stop_at = None

    def cp(self, name):
        if self.stop_at is not None and name == self.stop_at and self.limit is None:
            self.limit = self.n_ins

    def op(self, eng, fn, R=(), W=()):
        if self.limit is not None and self.n_ins >= self.limit:
            return None
        W = list(W) + [k for k in R if k.startswith("ps") and k not in W]
        self._wait(eng, R, W)
        ins = fn(self.eng[eng])
        self.cnt[eng] += 1
        ins.then_inc(self.sem[eng], 1)
        self._commit(("e", eng, self.cnt[eng]), R, W)
        self.n_ins += 1
        return ins

    def dma(self, q, out, in_, R=(), W=(), chan=None):
        if self.limit is not None and self.n_ins >= self.limit and chan != "out":
            return None
        i = self.dq[q][self.dnext[q]]
        self.dnext[q] = (self.dnext[q] + 1) % len(self.dq[q])
        if self.dcnt[i]:
            self._wait_ev(q, "d", i, self.dcnt[i])
        self._wait(q, R, W)
        ins = self.eng[q].dma_start(out=out, in_=in_)
        self.dcnt[i] += 16
        ins.then_inc(self.dsem[i], 16)
        self._commit(("d", i, self.dcnt[i]), R, W)
        return ins

    def allgather_pairs(self, src, dst, R=(), W=()):
        self._wait("pool", R, W)
        sem = self.es.enter_context(self.nc.semaphore(f"cc_sem{len(self.dsem)}"))
        self.dsem.append(sem)
        self.dcnt.append(0)
        i = len(self.dsem) - 1
        ins = self.nc.gpsimd.collective_compute("AllGather", ALU.bypass, replica_groups=[[0, 1], [2, 3], [4, 5], [6, 7]],
                                               ins=[src.opt()], outs=[dst.opt()])
        ins.then_inc(sem)
        self.dcnt[i] = 1
        self._commit(("d", i, 1), R, W)

    def barrier(self):
        for e in self.eng:
            for k, c in self.cnt.items():
                if c:
                    self._wait_ev(e, "e", k, c)
            for i, c in enumerate(self.dcnt):
                if c:
                    self._wait_ev(e, "d", i, c)

    def wait_all(self, eng="sp"):
        for k, c in self.cnt.items():
            if c and k != eng:
                self._wait_ev(eng, "e", k, c)
        for i, c in enumerate(self.dcnt):
            if c:
                self._wait_ev(eng, "d", i, c)


D = 1024
T = 2048
NT = T // 128
G = 256
NG = T // G
H = 4
DH = 192
L = 64
ALPHA = float(4 ** 0.25)
EPS = 1e-5
NB0 = 14


def _pad_cols(a, n):
    out = np.zeros((a.shape[0], n), np.float32)
    out[:, : a.shape[1]] = a
    return out


def _blockify(cols):
    return np.ascontiguousarray(cols.reshape(8, 128, 512).transpose(1, 0, 2))


def prep_l0_weights(w_in, w_out):
    q = w_in[:, 0:768]; k = w_in[:, 768:1536]; v = w_in[:, 1536:2304]; o = w_in[:, 2304:3072]
    g = w_in[:, 3072:3080]; xq = w_in[:, 3080:3336]
    blocks = []
    for src in (q, k):
        for hp in range(2):
            cs = []
            for h in (2 * hp, 2 * hp + 1):
                cs.append(src[:, h * 192: h * 192 + 128])
                cs.append(_pad_cols(src[:, h * 192 + 128: (h + 1) * 192], 128))
            blocks.append(np.concatenate(cs, axis=1))
    blocks.append(_pad_cols(xq, 512))
    for src in (k, v, o):
        blocks.append(_pad_cols(src[:, 0:384], 512))
        blocks.append(_pad_cols(src[:, 384:768], 512))
    blocks.append(_pad_cols(g, 512))
    blocks.append(w_out[:, 0:512]); blocks.append(w_out[:, 512:1024])
    return np.stack([_blockify(np.ascontiguousarray(b, dtype=np.float32)) for b in blocks])


class Builder:
    def __init__(self, es, debug=()):
        self.es = es
        self.debug = set(debug)
        self.nc = nc = bass.Bass("TRN2", target_bir_lowering=False)
        self.P = Prog(nc, es)
        self.din = {}
        self.dout = {}
        self.wb_i = 0

    def inp(self, name, shape, dt=F32):
        ap = self.nc.dram_tensor(name, list(shape), dt, kind="ExternalInput").ap()
        self.din[name] = ap
        return ap

    def outp(self, name, shape, dt=F32):
        ap = self.nc.dram_tensor(name, list(shape), dt, kind="ExternalOutput").ap()
        self.dout[name] = ap
        return ap

    def scratch(self, name, shape, dt):
        return self.nc.dram_tensor(name, list(shape), dt, kind="Internal").ap()

    def sb(self, name, shape, dt=F32, es=None):
        return (es or self.es).enter_context(self.nc.sbuf_tensor(name + "_sb", list(shape), dt))

    def wload(self, src):
        i = self.wb_i % len(self.wblk)
        self.wb_i = (i + 1) % len(self.wblk)
        self.P.dma("sp", self.wblk[i][:].rearrange("p k c -> p (k c)"), src[0], R=[src[1]], W=[f"wblk{i}"], chan="w")
        return i

    class Sched:
        def __init__(self, b, srcs):
            self.b = b; self.srcs = srcs; self.n = 0; self.cur = None
            self.single = len(b.wblk) == 1
            self.nxt = b.wload(srcs[0]) if (srcs and not self.single) else None

        def get(self):
            if self.single:
                self.n += 1
                return self.b.wload(self.srcs[self.n - 1])
            cur = self.nxt
            self.n += 1
            self.nxt = self.b.wload(self.srcs[self.n]) if self.n < len(self.srcs) else None
            return cur

    def setup(self):
        nc, P = self.nc, self.P
        sb = self.sb
        self.ps = self.es.enter_context(nc.psum_tensor("ps", [128, 8, 512], F32))
        self.X = sb("X", [128, NT, D])
        self.ident = sb("ident", [128, 128])
        self.identb = sb("identb", [128, 128], BF16)
        self.cmask = sb("cmask", [64, 64], BF16)
        self.flag = sb("flag", [128, 1])
        d_ident = self.inp("ident", [128, 128])
        d_cmask = self.inp("cmask", [64, 64])
        d_flag = self.inp("flag", [128, 1])
        cm32 = sb("cm32", [64, 64])
        P.dma("sp", self.ident[:], d_ident, W=["ident"])
        P.dma("sp", cm32[:], d_cmask, W=["cm32"])
        P.dma("sp", self.flag[:], d_flag, W=["flag"])
        P.op("dve", lambda e: e.tensor_copy(out=self.identb[:], in_=self.ident[:]), R=["ident"], W=["identb"])
        P.op("dve", lambda e: e.tensor_copy(out=self.cmask[:], in_=cm32[:]), R=["cm32"], W=["cmask"])
        d_x = self.inp("x_own", [T, D])
        for q4 in range(4):
            P.dma("sp", self.X[:, 4 * q4:4 * q4 + 4, :],
                  d_x[512 * q4:512 * (q4 + 1), :].rearrange("(t p) d -> p t d", p=128),
                  W=[f"X{t}" for t in range(4 * q4, 4 * q4 + 4)])

    def psb(self, b0, nb=1):
        return [f"ps{b}" for b in range(b0, b0 + nb)]

    def make_xT(self, xTg, src_tiles, src_keys, kout, b0=0):
        P, ps = self.P, self.ps
        for tt in range(2):
            for kc in range(8):
                P.op("pe", lambda e, tt=tt, kc=kc: e.transpose(
                    out=ps[:, b0 + kc // 4, (kc % 4) * 128:(kc % 4 + 1) * 128],
                    in_=src_tiles[tt][:, kc * 128:(kc + 1) * 128], identity=self.ident[:]),
                    R=[src_keys[tt], "ident"], W=[f"ps{b0 + kc // 4}"])
            P.op("act", lambda e, tt=tt: e.copy(
                out=xTg[:, :, tt * 128:(tt + 1) * 128],
                in_=ps[:, b0:b0 + 2, :].rearrange("p b (k c) -> p (b k) c", c=128)),
                R=[f"ps{b0}", f"ps{b0 + 1}"], W=[kout])

    def xattn_setup(self, layer, es):
        P, ps, sb = self.P, self.ps, self.sb
        d_memT = self.din["memT"] if "memT" in self.din else self.inp("memT", [128, 8, 256])
        d_wkv = self.inp(f"wkv{layer}", [128, 8, 512])
        kxT = sb(f"kxT_{layer}", [64, 4, 256], BF16, es=es)
        vxaug = sb(f"vxaug_{layer}", [128, 2, 4, 65], BF16, es=es)
        with ExitStack() as tes:
            memT32 = sb(f"memT32_{layer}", [128, 8, 256], es=tes)
            wkv32 = sb(f"wkv32_{layer}", [128, 8, 512], es=tes)
            memTb = sb(f"memTb_{layer}", [128, 8, 256], BF16, es=tes)
            wkvb = sb(f"wkvb_{layer}", [128, 8, 512], BF16, es=tes)
            P.dma("sp", memT32[:], d_memT, W=["memT32"])
            P.dma("sp", wkv32[:], d_wkv, W=["wkv32"])
            P.op("dve", lambda e: e.tensor_copy(out=memTb[:], in_=memT32[:]), R=["memT32"], W=["memTb"])
            P.op("pool", lambda e: e.tensor_copy(out=wkvb[:], in_=wkv32[:]), R=["wkv32"], W=["wkvb"])
            for h in range(4):
                for kc in range(8):
                    P.op("pe", lambda e, h=h, kc=kc: e.matmul(
                        ps[0:64, 2, 0:256], lhsT=wkvb[:, kc, h * 64:(h + 1) * 64], rhs=memTb[:, kc, :],
                        start=(kc == 0), stop=(kc == 7)), R=["wkvb", "memTb"], W=["ps2"])
                P.op("act", lambda e, h=h: e.copy(out=kxT[:, h, :], in_=ps[0:64, 2, 0:256]), R=["ps2"], W=["kxT"])
            P.op("pool", lambda e: e.memset(vxaug[:], 1.0), W=["vxaug"])
            for mc in range(2):
                for kc in range(8):
                    P.op("pe", lambda e, mc=mc, kc=kc: e.matmul(
                        ps[:, 3, 0:256], lhsT=memTb[:, kc, mc * 128:(mc + 1) * 128], rhs=wkvb[:, kc, 256:512],
                        start=(kc == 0), stop=(kc == 7)), R=["wkvb", "memTb"], W=["ps3"])
                P.op("act", lambda e, mc=mc: e.copy(
                    out=vxaug[:, mc, :, 0:64], in_=ps[:, 3, 0:256].rearrange("p (h d) -> p h d", h=4)),
                    R=["ps3"], W=["vxaug"])
            P.barrier()
        return kxT, vxaug

    def xattn_chunk(self, kxT, vxaug, xqT, c0, ntok, out_ap, out_key, Esb, rsb):
        P, ps = self.P, self.ps
        for h in range(4):
            for mc in range(2):
                P.op("pe", lambda e, h=h, mc=mc: e.matmul(
                    ps[:, 0, (h * 2 + mc) * 64:(h * 2 + mc) * 64 + ntok],
                    lhsT=kxT[:, h, mc * 128:(mc + 1) * 128],
                    rhs=xqT[:, h, c0:c0 + ntok], start=True, stop=True),
                    R=["kxT", "xqT"], W=["ps0"])
        P.cp("xa_st")
        P.op("act", lambda e: e.activation(
            out=Esb[:, :, 0:ntok], in_=ps[:, 0, :].rearrange("p (j t) -> p j t", t=64)[:, :, 0:ntok],
            func=AF.Exp, scale=0.125), R=["ps0"], W=["Esb"])
        P.cp("xa_exp")
        Ov = ps[0:ntok, 1, 0:260].rearrange("p (h d) -> p h d", d=65)
        for h in range(4):
            for mc in range(2):
                P.op("pe", lambda e, h=h, mc=mc: e.matmul(
                    Ov[:, h, :], lhsT=Esb[:, h * 2 + mc, 0:ntok], rhs=vxaug[:, mc, h, :],
                    start=(mc == 0), stop=(mc == 1)), R=["Esb", "vxaug"], W=["ps1"])
        P.cp("xa_ov")
        P.op("dve", lambda e: e.reciprocal(out=rsb[0:ntok, :], in_=Ov[:, :, 64]), R=["ps1"], W=["rsb"])
        P.op("dve", lambda e: e.tensor_tensor(
            out=out_ap.rearrange("p (h d) -> p h d", d=64), in0=Ov[:, :, 0:64],
            in1=rsb[0:ntok, :].unsqueeze(2).broadcast_to([ntok, 4, 64]), op=ALU.mult),
            R=["ps1", "rsb"], W=[out_key])

    def outproj_ln(self, sched, mixT, g, lng, lnb, xs, st, ag):
        P, ps = self.P, self.ps
        banks = {0: (0, 1), 1: (5, 6)}
        for half in range(2):
            wi = sched.get()
            for tt in range(2):
                bk = banks[tt][half]
                for kc in range(8):
                    P.op("pe", lambda e, tt=tt, kc=kc, bk=bk, wi=wi: e.matmul(
                        ps[:, bk, :], lhsT=mixT[:, kc, tt * 128:(tt + 1) * 128], rhs=self.wblk[wi][:, kc, :],
                        start=(kc == 0), stop=(kc == 7)), R=["mixT", f"wblk{wi}"], W=[f"ps{bk}"])
        P.cp("op_mm")
        for tt in range(2):
            t = 2 * g + tt
            b0 = banks[tt][0]
            self.resid_ln(t, self.ps[:, b0:b0 + 2, :], [f"ps{b0}", f"ps{b0 + 1}"], lng, lnb, xs, st, ag)

    def resid_ln(self, t, y_ap, y_keys, lng, lnb, xs, st, ag):
        P = self.P
        Xt = self.X[:, t, :]
        wk = Xt if xs is None else xs[:]
        kk = f"X{t}" if xs is None else "xs"
        P.op("dve", lambda e: e.scalar_tensor_tensor(
            out=wk.rearrange("p (b c) -> p b c", b=2), in0=Xt.rearrange("p (b c) -> p b c", b=2),
            scalar=ALPHA, in1=y_ap, op0=ALU.mult, op1=ALU.add), R=[f"X{t}"] + y_keys, W=[kk])
        P.cp("ln_a")
        for hf in range(2):
            P.op("dve", lambda e, hf=hf: e.bn_stats(out=st[:, hf, :], in_=wk[:, hf * 512:(hf + 1) * 512]),
                 R=[kk], W=[f"st{hf}"])
        P.op("dve", lambda e: e.bn_aggr(out=ag[:, 0:2], in_=st[:].rearrange("p a b -> p (a b)")),
             R=["st0", "st1"], W=["ag"])
        P.cp("ln_b")
        P.op("act", lambda e: e.activation(out=ag[:, 2:3], in_=ag[:, 1:2], func=AF.Sqrt, bias=EPS),
             R=["ag"], W=["ag2"])
        P.op("dve", lambda e: e.reciprocal(out=ag[:, 3:4], in_=ag[:, 2:3]), R=["ag2"], W=["ag3"])
        P.op("dve", lambda e: e.tensor_scalar(out=wk, in0=wk, scalar1=ag[:, 0:1], scalar2=ag[:, 3:4],
                                              op0=ALU.subtract, op1=ALU.mult), R=[kk, "ag", "ag3"], W=[kk])
        P.cp("ln_c")
        P.op("dve", lambda e: e.tensor_tensor(out=wk, in0=wk, in1=lng[:], op=ALU.mult), R=[kk, "lng"], W=[kk])
        P.op("dve", lambda e: e.tensor_tensor(out=Xt, in0=wk, in1=lnb[:], op=ALU.add), R=[kk, "lnb"], W=[f"X{t}"])

    def l0_mixer(self, npre=NG, nmain=NG):
        nc, P, ps, sb = self.nc, self.P, self.ps, self.sb
        ident = self.ident
        with ExitStack() as es:
            d_w0 = self.inp("w0", [NB0, 128, 4096])
            w0b = self.scratch("w0b", [NB0, 128, 4096], BF16)
            for b in range(NB0):
                P.dma("pool", w0b[b].rearrange("p (a c) -> p a c", c=2048), d_w0[b].rearrange("p (a c) -> p a c", c=2048), W=[f"w0b{b}"])
            P.barrier()
            d_xpre = self.inp("x_pre", [T, D])
            d_bgi = self.inp("bgi", [4, 1]); d_bgf = self.inp("bgf", [4, 1])
            d_ng = self.inp("norm_g", [768]); d_lng = self.inp("ln1g0", [D]); d_lnb = self.inp("ln1b0", [D])
            kxT, vxaug = self.xattn_setup(0, es)
            S = lambda n, sh, dt=F32: sb(n, sh, dt, es=es)
            self.wblk = [S("wblk0_m0", [128, 8, 512], BF16), S("wblk1_m0", [128, 8, 512], BF16)]
            xpre = S("xpre", [128, 2, D]); xTg = S("xTg", [128, 8, G], BF16)
            qT = S("qT", [128, 8, G], BF16); kT = S("kT", [128, 8, G], BF16); xqT = S("xqT", [64, 4, G], BF16)
            k_tm = S("k_tm", [64, 4, 768], BF16); vaug = S("vaug", [64, 4, 4, 193], BF16); so = S("so", [64, 4, 768], BF16)
            g_tm = S("g_tm", [64, 4, 8])
            bgi = S("bgi_s", [4, 1]); nbgf = S("nbgf", [4, 1]); zer = S("zer", [4, G])
            ig = S("ig", [4, G]); lsp = S("lsp", [4, G]); Lc = S("Lc", [4, G]); gam = S("gam", [4, G]); Mx = S("Mx", [4, G])
            t1 = S("t1", [4, G]); bet = S("bet", [4, G]); alp = S("alp", [4, G]); flo = S("flo", [4, G])
            Lcar = S("Lcar", [4, 1]); Mcar = S("Mcar", [4, 1]); Mprev = S("Mprev", [4, 4]); dec = S("dec", [4, 4])
            sel = S("sel", [4, 4, 128]); gtm = S("gtm", [64, 4, 3, 4]); decb = S("decb", [128, 4, 4])
            CA = S("CA", [128, 4, 193]); CB = S("CB", [64, 4, 193]); CAb = S("CAb", [128, 4, 193], BF16); CBb = S("CBb", [64, 4, 193], BF16)
            SmT = S("SmT", [64, 4, 64], BF16); num = S("num", [64, 4, 193]); hr = S("hr", [64, 4, 192]); sq = S("sq", [64, 4, 192])
            sm = S("sm", [64, 8, 4]); mix = [S("mix0", [64, D], BF16), S("mix1", [64, D], BF16)]
            mixT = S("mixT", [128, 8, G], BF16); Esb = S("Esb", [128, 8, 64], BF16); rsb = S("rsb", [64, 4])
            ngb = S("ngb", [64, 768]); lng = S("lng", [128, D]); lnb = S("lnb", [128, D])
            xs = S("xs", [128, D]); st = S("st", [128, 2, 6]); ag = S("ag", [128, 4])
            P.dma("sp", bgi[:], d_bgi, W=["bgi"]); P.dma("sp", nbgf[:], d_bgf, W=["nbgf"])
            P.dma("sp", ngb[:], d_ng.partition_broadcast(64), W=["ngb"])
            P.dma("sp", lng[:], d_lng.partition_broadcast(128), W=["lng"])
            P.dma("sp", lnb[:], d_lnb.partition_broadcast(128), W=["lnb"])
            P.op("dve", lambda e: e.tensor_scalar(out=nbgf[:], in0=nbgf[:], scalar1=-1.0, scalar2=None, op0=ALU.mult), R=["nbgf"], W=["nbgf"])
            for t_, k_ in ((zer, "zer"), (Lcar, "Lcar"), (Mcar, "Mcar"), (CA, "CA"), (CB, "CB")):
                P.op("pool", lambda e, t_=t_: e.memset(t_[:], 0.0), W=[k_])
            for h in range(4):
                P.op("dve", lambda e, h=h: e.tensor_copy(out=sel[:, h, :], in_=ident[0:4, h:h + 1].broadcast_to([4, 128])),
                     R=["ident"], W=["sel"])
            srcs = []
            for g in range(npre):
                srcs += [(w0b[b], f"w0b{b}") for b in (11, 5, 6, 7, 8)]
            for g in range(nmain):
                srcs += [(w0b[b], f"w0b{b}") for b in (11, 0, 1, 2, 3, 4, 5, 6, 7, 8, 9, 10, 12, 13)]
            sched = self.Sched(self, srcs)
            KS = float(DH ** -0.5)
            mixi = 0

            def tm_proj(wi, ck, ncols):
                bk = 2 + (ck % 2)
                for kc in range(8):
                    P.op("pe", lambda e, kc=kc: e.matmul(
                        ps[0:64, bk, 0:ncols], lhsT=xTg[:, kc, ck * 64:(ck + 1) * 64], rhs=self.wblk[wi][:, kc, 0:ncols],
                        start=(kc == 0), stop=(kc == 7)), R=["xTg", f"wblk{wi}"], W=[f"ps{bk}"])
                return bk

            for phase in (0, 1):
                for g in range(nmain if phase else npre):
                    main = phase == 1
                    if main:
                        srct = [self.X[:, 2 * g + tt, :] for tt in range(2)]; srck = [f"X{2 * g + tt}" for tt in range(2)]
                    else:
                        P.dma("sp", xpre[:], d_xpre[g * G:(g + 1) * G, :].rearrange("(t p) d -> p t d", p=128), W=["xpre0", "xpre1"])
                        srct = [xpre[:, tt, :] for tt in range(2)]; srck = ["xpre0", "xpre1"]
                    self.make_xT(xTg, srct, srck, "xTg")
                    wi = sched.get()
                    for ck in range(4):
                        bk = tm_proj(wi, ck, 8)
                        P.op("act", lambda e, ck=ck, bk=bk: e.copy(out=g_tm[:, ck, :], in_=ps[0:64, bk, 0:8]), R=[f"ps{bk}"], W=["g_tm"])
                    for ck in range(4):
                        for j in range(2):
                            P.op("pe", lambda e, ck=ck, j=j: e.transpose(
                                out=ps[0:4, 4, j * 256 + ck * 64: j * 256 + (ck + 1) * 64],
                                in_=g_tm[:, ck, 4 * j:4 * j + 4], identity=ident[0:64, 0:64]), R=["g_tm", "ident"], W=["ps4"])
                    if main:
                        P.op("dve", lambda e: e.tensor_scalar(out=ig[:], in0=ps[0:4, 4, 0:256], scalar1=bgi[:, 0:1], scalar2=None, op0=ALU.add),
                             R=["ps4", "bgi"], W=["ig"])
                    else:
                        P.op("dve", lambda e: e.tensor_scalar(out=ig[:], in0=ps[0:4, 4, 0:256], scalar1=bgi[:, 0:1], scalar2=self.flag[0:4, 0:1],
                                                              op0=ALU.add, op1=ALU.mult), R=["ps4", "bgi", "flag"], W=["ig"])
                    P.op("act", lambda e: e.activation(out=t1[:], in_=ps[0:4, 4, 256:512], func=AF.Exp, bias=nbgf[:, 0:1], scale=-1.0),
                         R=["ps4", "nbgf"], W=["t1"])
                    P.op("act", lambda e: e.activation(out=lsp[:], in_=t1[:], func=AF.Ln, bias=1.0), R=["t1"], W=["lsp"])
                    if not main:
                        P.op("dve", lambda e: e.tensor_scalar(out=lsp[:], in0=lsp[:], scalar1=self.flag[0:4, 0:1], scalar2=None, op0=ALU.mult),
                             R=["lsp", "flag"], W=["lsp"])
                    P.op("dve", lambda e: e.tensor_tensor_scan(out=Lc[:], data0=lsp[:], data1=zer[:], initial=Lcar[:, 0:1], op0=ALU.add, op1=ALU.add),
                         R=["lsp", "zer", "Lcar"], W=["Lc"])
                    P.op("dve", lambda e: e.tensor_tensor(out=gam[:], in0=ig[:], in1=Lc[:], op=ALU.add), R=["ig", "Lc"], W=["gam"])
                    P.op("dve", lambda e: e.tensor_tensor_scan(out=Mx[:], data0=gam[:], data1=gam[:], initial=Mcar[:, 0:1], op0=ALU.max, op1=ALU.max),
                         R=["gam", "Mcar"], W=["Mx"])
                    Mend = Mx[:].rearrange("p (c l) -> p c l", l=64)[:, :, 63]
                    P.op("dve", lambda e: e.tensor_copy(out=Mprev[:, 0:1], in_=Mcar[:, 0:1]), R=["Mcar"], W=["Mprev"])
                    P.op("dve", lambda e: e.tensor_copy(out=Mprev[:, 1:4], in_=Mend[:, 0:3]), R=["Mx"], W=["Mprev"])
                    P.op("dve", lambda e: e.tensor_copy(out=Mcar[:, 0:1], in_=Mx[:, G - 1:G]), R=["Mx", "Mprev"], W=["Mcar"])
                    P.op("dve", lambda e: e.tensor_copy(out=Lcar[:, 0:1], in_=Lc[:, G - 1:G]), R=["Lc"], W=["Lcar"])
                    P.op("dve", lambda e: e.tensor_tensor(out=dec[:], in0=Mprev[:], in1=Mend, op=ALU.subtract), R=["Mprev", "Mx"], W=["dec"])
                    P.op("act", lambda e: e.activation(out=dec[:], in_=dec[:], func=AF.Exp), R=["dec"], W=["dec"])
                    Mend_bc = Mend.unsqueeze(2).broadcast_to([4, 4, 64])
                    v3 = lambda t_: t_[:].rearrange("p (c l) -> p c l", l=64)
                    P.op("dve", lambda e: e.tensor_tensor(out=v3(bet), in0=v3(gam), in1=Mend_bc, op=ALU.subtract), R=["gam", "Mx"], W=["bet"])
                    P.op("act", lambda e: e.activation(out=bet[:], in_=bet[:], func=AF.Exp), R=["bet"], W=["bet"])
                    if main:
                        P.op("dve", lambda e: e.tensor_tensor(out=v3(alp), in0=Mend_bc, in1=v3(Mx), op=ALU.subtract), R=["Mx"], W=["alp"])
                        P.op("act", lambda e: e.activation(out=alp[:], in_=alp[:], func=AF.Exp), R=["alp"], W=["alp"])
                        P.op("dve", lambda e: e.tensor_tensor(out=flo[:], in0=Lc[:], in1=Mx[:], op=ALU.subtract), R=["Lc", "Mx"], W=["flo"])
                        P.op("act", lambda e: e.activation(out=flo[:], in_=flo[:], func=AF.Exp), R=["flo"], W=["flo"])
                    qs = (bet, alp, flo) if main else (bet,)
                    for ck in range(4):
                        for qi, qt in enumerate(qs):
                            P.op("pe", lambda e, ck=ck, qi=qi, qt=qt: e.transpose(
                                out=ps[0:64, 4, ck * 12 + qi * 4: ck * 12 + qi * 4 + 4], in_=qt[0:4, ck * 64:(ck + 1) * 64],
                                identity=ident[0:4, 0:4]), R=[("bet", "alp", "flo")[qi], "ident"], W=["ps4"])
                    if main:
                        P.op("act", lambda e: e.copy(out=gtm[:].rearrange("p c q h -> p (c q h)"), in_=ps[0:64, 4, 0:48]), R=["ps4"], W=["gtm"])
                    else:
                        P.op("act", lambda e: e.copy(out=gtm[:, :, 0, :], in_=ps[0:64, 4, 0:48].rearrange("p (c q h) -> p c q h", q=3, h=4)[:, :, 0, :]),
                             R=["ps4"], W=["gtm"])
                    for h in range(4):
                        P.op("pe", lambda e, h=h: e.matmul(
                            ps[:, 4, 64:80].rearrange("p (c h) -> p c h", h=4)[:, :, h], lhsT=sel[0:4, h, :], rhs=dec[0:4, 0:4],
                            start=True, stop=True), R=["sel", "dec", "gtm"], W=["ps4"])
                    P.op("act", lambda e: e.copy(out=decb[:].rearrange("p c h -> p (c h)"), in_=ps[:, 4, 64:80]), R=["ps4"], W=["decb"])
                    if main:
                        for blk, dst, scl in ((0, qT, 1.0), (1, qT, 1.0), (2, kT, KS), (3, kT, KS), (4, xqT, 1.0)):
                            wi = sched.get()
                            cw = 128 if blk < 4 else 64
                            for j in range(4):
                                bk = 2 + (j % 2)
                                for kc in range(8):
                                    P.op("pe", lambda e, j=j, kc=kc, bk=bk, wi=wi: e.matmul(
                                        ps[0:cw, bk, 0:G], lhsT=self.wblk[wi][:, kc, j * cw:(j + 1) * cw], rhs=xTg[:, kc, :],
                                        start=(kc == 0), stop=(kc == 7)), R=["xTg", f"wblk{wi}"], W=[f"ps{bk}"])
                                cj = (blk % 2) * 4 + j if blk < 4 else j
                                dk = {0: "qT", 1: "qT", 2: "kT", 3: "kT", 4: "xqT"}[blk]
                                P.op("act", lambda e, dst=dst, cj=cj, bk=bk, scl=scl: e.activation(
                                    out=dst[0:cw, cj, :], in_=ps[0:cw, bk, 0:G], func=AF.Copy, scale=scl), R=[f"ps{bk}"], W=[dk])
                    tms = [(0, "k"), (1, "k"), (0, "v"), (1, "v")] + ([(0, "o"), (1, "o")] if main else [])
                    for hp, kind in tms:
                        wi = sched.get()
                        for ck in range(4):
                            bk = tm_proj(wi, ck, 384)
                            src = ps[0:64, bk, 0:384]
                            if kind == "k":
                                P.op("act", lambda e, ck=ck, hp=hp, src=src: e.activation(
                                    out=k_tm[:, ck, hp * 384:(hp + 1) * 384], in_=src, func=AF.Copy, scale=KS), R=[f"ps{bk}"], W=["k_tm"])
                            elif kind == "o":
                                P.op("act", lambda e, ck=ck, hp=hp, src=src: e.activation(
                                    out=so[:, ck, hp * 384:(hp + 1) * 384], in_=src, func=AF.Sigmoid), R=[f"ps{bk}"], W=["so"])
                            else:
                                P.op("dve", lambda e, ck=ck, hp=hp, src=src: e.tensor_tensor(
                                    out=vaug[:, ck, 2 * hp:2 * hp + 2, 0:192], in0=src.rearrange("p (h d) -> p h d", h=2),
                                    in1=gtm[:, ck, 0, 2 * hp:2 * hp + 2].unsqueeze(2).broadcast_to([64, 2, 192]), op=ALU.mult),
                                    R=[f"ps{bk}", "gtm"], W=["vaug"])
                    for ck in range(4):
                        P.op("pool", lambda e, ck=ck: e.tensor_copy(out=vaug[:, ck, :, 192], in_=gtm[:, ck, 0, :]), R=["gtm"], W=["vaug"])
                    P.cp(f"proj{phase}")
                    for ck in range(4):
                        c0 = ck * 64
                        dbc = lambda n: decb[0:n, ck, :].unsqueeze(2).broadcast_to([n, 4, 193])
                        P.op("dve", lambda e: e.tensor_tensor(out=CA[:], in0=CA[:], in1=dbc(128), op=ALU.mult), R=["CA", "decb"], W=["CA"])
                        P.op("pool", lambda e: e.tensor_tensor(out=CB[:], in0=CB[:], in1=dbc(64), op=ALU.mult), R=["CB", "decb"], W=["CB"])
                        if main:
                            P.op("act", lambda e: e.copy(out=CAb[:], in_=CA[:]), R=["CA"], W=["CAb"])
                            P.op("act", lambda e: e.copy(out=CBb[:], in_=CB[:]), R=["CB"], W=["CBb"])
                            for h in range(4):
                                P.op("pe", lambda e, h=h: e.matmul(ps[0:64, 4, 256 + h * 64:256 + (h + 1) * 64], lhsT=kT[:, 2 * h, c0:c0 + 64],
                                                                   rhs=qT[:, 2 * h, c0:c0 + 64], start=True, stop=False), R=["kT", "qT"], W=["ps4"])
                                P.op("pe", lambda e, h=h: e.matmul(ps[0:64, 4, 256 + h * 64:256 + (h + 1) * 64], lhsT=kT[0:64, 2 * h + 1, c0:c0 + 64],
                                                                   rhs=qT[0:64, 2 * h + 1, c0:c0 + 64], start=False, stop=True), R=["kT", "qT"], W=["ps4"])
                            P.op("dve", lambda e: e.tensor_tensor(
                                out=SmT[:], in0=ps[0:64, 4, 256:512].rearrange("p (h l) -> p h l", h=4),
                                in1=self.cmask[:].unsqueeze(1).broadcast_to([64, 4, 64]), op=ALU.mult), R=["ps4", "cmask"], W=["SmT"])
                            Pv = ps[0:64, 5:7, :].rearrange("p b (h e) -> p (b h) e", h=2)
                            for h in range(4):
                                bk = 5 + h // 2
                                P.op("pe", lambda e, h=h: e.matmul(Pv[:, h, 0:193], lhsT=qT[:, 2 * h, c0:c0 + 64], rhs=CAb[:, h, :],
                                                                   start=True, stop=False), R=["qT", "CAb"], W=[f"ps{bk}"])
                                P.op("pe", lambda e, h=h: e.matmul(Pv[:, h, 0:193], lhsT=qT[0:64, 2 * h + 1, c0:c0 + 64], rhs=CBb[:, h, :],
                                                                   start=False, stop=False), R=["qT", "CBb"], W=[f"ps{bk}"])
                                P.op("pe", lambda e, h=h: e.matmul(Pv[:, h, 0:193], lhsT=SmT[:, h, :], rhs=vaug[:, ck, h, :],
                                                                   start=False, stop=True), R=["SmT", "vaug"], W=[f"ps{bk}"])
                        P.cp(f"P{phase}")
                        for hp in range(2):
                            dA = ps[:, 7, :].rearrange("p (h e) -> p h e", h=2)
                            dB = ps[0:64, 3, :].rearrange("p (h e) -> p h e", h=2)
                            for hh in range(2):
                                h = 2 * hp + hh
                                P.op("pe", lambda e, h=h, hh=hh: e.matmul(dA[:, hh, 0:193], lhsT=k_tm[:, ck, h * 192:h * 192 + 128], rhs=vaug[:, ck, h, :],
                                                                          start=True, stop=True), R=["k_tm", "vaug"], W=["ps7"])
                                P.op("pe", lambda e, h=h, hh=hh: e.matmul(dB[:, hh, 0:193], lhsT=k_tm[:, ck, h * 192 + 128:(h + 1) * 192], rhs=vaug[:, ck, h, :],
                                                                          start=True, stop=True), R=["k_tm", "vaug"], W=["ps3"])
                            rA = ["CA"] + (["CAb"] if main else [])
                            P.op("dve", lambda e, hp=hp, dA=dA: e.tensor_tensor(out=CA[:, 2 * hp:2 * hp + 2, :], in0=CA[:, 2 * hp:2 * hp + 2, :], in1=dA[:, :, 0:193], op=ALU.add),
                                 R=["CA", "ps7"], W=["CA"])
                            P.op("dve", lambda e, hp=hp, dB=dB: e.tensor_tensor(out=CB[:, 2 * hp:2 * hp + 2, :], in0=CB[:, 2 * hp:2 * hp + 2, :], in1=dB[:, :, 0:193], op=ALU.add),
                                 R=["CB", "ps3"], W=["CB"])
                        P.cp(f"state{phase}")
                        if not main:
                            continue
                        abc = gtm[:, ck, 1, :].unsqueeze(2).broadcast_to([64, 4, 193])
                        P.op("dve", lambda e: e.tensor_tensor(out=num[:], in0=Pv[:, :, 0:193], in1=abc, op=ALU.mult), R=["ps5", "ps6", "gtm"], W=["num"])
                        P.op("act", lambda e: e.activation(out=sm[:, 0, :], in_=num[:, :, 192], func=AF.Abs), R=["num"], W=["sm0"])
                        P.op("dve", lambda e: e.tensor_tensor(out=sm[:, 1, :], in0=sm[:, 0, :], in1=gtm[:, ck, 2, :], op=ALU.max), R=["sm0", "gtm"], W=["sm1"])
                        P.op("dve", lambda e: e.reciprocal(out=sm[:, 2, :], in_=sm[:, 1, :]), R=["sm1"], W=["sm2"])
                        P.op("dve", lambda e: e.tensor_tensor(out=hr[:], in0=num[:, :, 0:192], in1=sm[:, 2, :].unsqueeze(2).broadcast_to([64, 4, 192]), op=ALU.mult),
                             R=["num", "sm2"], W=["hr"])
                        P.cp("hr")
                        P.op("dve", lambda e: e.tensor_reduce(out=sm[:, 3, :], in_=hr[:], axis=AX.X, op=ALU.add), R=["hr"], W=["sm3"])
                        P.op("pool", lambda e: e.tensor_tensor(out=sq[:], in0=hr[:], in1=hr[:], op=ALU.mult), R=["hr"], W=["sq"])
                        P.op("dve", lambda e: e.tensor_reduce(out=sm[:, 4, :], in_=sq[:], axis=AX.X, op=ALU.add), R=["sq"], W=["sm4"])
                        P.op("dve", lambda e: e.tensor_scalar(out=sm[:, 3, :], in0=sm[:, 3, :], scalar1=1.0 / DH, scalar2=None, op0=ALU.mult), R=["sm3"], W=["sm3"])
                        P.op("dve", lambda e: e.tensor_tensor(out=sm[:, 5, :], in0=sm[:, 3, :], in1=sm[:, 3, :], op=ALU.mult), R=["sm3"], W=["sm5"])
                        P.op("dve", lambda e: e.scalar_tensor_tensor(out=sm[:, 6, :], in0=sm[:, 4, :], scalar=1.0 / DH, in1=sm[:, 5, :], op0=ALU.mult, op1=ALU.subtract),
                             R=["sm4", "sm5"], W=["sm6"])
                        P.op("act", lambda e: e.activation(out=sm[:, 6, :], in_=sm[:, 6, :], func=AF.Sqrt, bias=EPS), R=["sm6"], W=["sm6"])
                        P.op("dve", lambda e: e.reciprocal(out=sm[:, 7, :], in_=sm[:, 6, :]), R=["sm6"], W=["sm7"])
                        P.op("dve", lambda e: e.tensor_tensor(out=hr[:], in0=hr[:], in1=sm[:, 3, :].unsqueeze(2).broadcast_to([64, 4, 192]), op=ALU.subtract),
                             R=["hr", "sm3"], W=["hr"])
                        P.op("dve", lambda e: e.tensor_tensor(out=hr[:], in0=hr[:], in1=sm[:, 7, :].unsqueeze(2).broadcast_to([64, 4, 192]), op=ALU.mult),
                             R=["hr", "sm7"], W=["hr"])
                        hf = hr[:].rearrange("p h d -> p (h d)")
                        P.op("pool", lambda e: e.tensor_tensor(out=hf, in0=hf, in1=ngb[:], op=ALU.mult), R=["hr", "ngb"], W=["hr"])
                        mx_ = mix[mixi]; mk = f"mix{mixi}"; mixi ^= 1
                        P.op("pool", lambda e, mx_=mx_: e.tensor_tensor(out=mx_[:, 0:768], in0=hf, in1=so[:, ck, :], op=ALU.mult), R=["hr", "so"], W=[mk])
                        P.cp("hln")
                        self.xattn_chunk(kxT, vxaug, xqT, c0, 64, mx_[:, 768:1024], mk, Esb, rsb)
                        P.cp("xattn")
                        pb = ps[:, 2, :].bitcast(BF16)
                        for kc in range(8):
                            P.op("pe", lambda e, kc=kc, mx_=mx_: e.transpose(out=pb[:, kc * 64:(kc + 1) * 64], in_=mx_[:, kc * 128:(kc + 1) * 128],
                                                                             identity=self.identb[0:64, 0:64]), R=[mk, "identb"], W=["ps2"])
                        P.op("act", lambda e: e.copy(out=mixT[:, :, c0:c0 + 64], in_=pb[:, 0:512].rearrange("p (k t) -> p k t", t=64)), R=["ps2"], W=["mixT"])
                    P.cp(f"chunks{phase}")
                    if main:
                        self.outproj_ln(sched, mixT, g, lng, lnb, xs, st, ag)
            P.barrier()

    def finish(self, out_name="out"):
        P = self.P
        d_out = self.outp(out_name, [T, D])
        for q4 in range(4):
            P.dma("sp", d_out[512 * q4:512 * (q4 + 1), :].rearrange("(t p) d -> p t d", p=128),
                  self.X[:, 4 * q4:4 * q4 + 4, :], R=[f"X{t}" for t in range(4 * q4, 4 * q4 + 4)], chan="out")
        P.wait_all("sp")


def _consts():
    ident = np.eye(128, dtype=np.float32)
    cmask = np.triu(np.ones((64, 64), np.float32))
    return ident, cmask


def build(stages, limit=None, stop_at=None, **kw):
    es = ExitStack()
    b = Builder(es)
    b.P.limit = limit
    b.P.stop_at = stop_at
    b.setup()
    if "fused" in stages:
        P = b.P
        b.l0_mixer()
        b.peer(0)
        cc1src = b.scratch("cc1src", [4, D], F32); cc1dst = b.scratch("cc1dst", [8, D], F32)
        b.cc2src = b.scratch("cc2src", [RB, 8], F32); cc2dst = b.scratch("cc2dst", [2 * RB, 8], F32)
        P.dma("sp", cc1src[0:3], b.X[125:128, NT - 1, :], R=[f"X{NT - 1}"], W=["cc1src"])
        P.dma("sp", cc1src[3:4], b.X[127:128, NT - 1, :], R=[f"X{NT - 1}"], W=["cc1src"])
        P.barrier()
        P.allgather_pairs(cc1src, cc1dst, R=["cc1src"], W=["cc_halo"])
        P.barrier()
        b.l1(scan_only=True, halo_src=cc1dst[0:3])
        P.allgather_pairs(b.cc2src, cc2dst, R=["cc2src"], W=["cc_hend"])
        P.barrier()
        b.l1(scan_only=False, halo_src=cc1dst[0:3], hinit_src=cc2dst[0:RB])
        b.peer(1)
        b.finish()
        return b, es
    if "l0mix" in stages:
        b.l0_mixer(**{k: v for k, v in kw.items() if k in ("npre", "nmain")})
    if "peer0" in stages:
        b.peer(0, **{k: v for k, v in kw.items() if k == "ngroups"})
    if "l1scan" in stages:
        b.l1(scan_only=True)
    if "l1mix" in stages:
        b.l1(scan_only=False, **{k: v for k, v in kw.items() if k == "ngroups"})
    if "peer1" in stages:
        b.peer(1, **{k: v for k, v in kw.items() if k == "ngroups"})
    b.finish()
    return b, es


def core_inputs(inputs, stages, x_cur=None, hinit=None, halo=None):
    ident, cmask = _consts()
    x = inputs["x"] if x_cur is None else x_cur
    maps = []
    shared = {"ident": ident, "cmask": cmask}
    if "fused" in stages:
        stages = ["fused", "l0mix", "peer0", "l1scan", "l1mix", "peer1"]
    if "l0mix" in stages:
        shared["w0"] = prep_l0_weights(inputs["mlstm_w_in"][0], inputs["w_out"][0]).reshape(NB0, 128, 4096)
        shared["bgi"] = np.ascontiguousarray(inputs["mlstm_b_gates"][0, 0:4].reshape(4, 1))
        shared["bgf"] = np.ascontiguousarray(inputs["mlstm_b_gates"][0, 4:8].reshape(4, 1))
        shared["norm_g"] = np.ascontiguousarray(inputs["mlstm_norm_g"][0])
        shared["ln1g0"] = np.ascontiguousarray(inputs["ln1_g"][0]); shared["ln1b0"] = np.ascontiguousarray(inputs["ln1_b"][0])
        shared["wkv0"] = _blockify(np.ascontiguousarray(inputs["xattn_w_kv"][0]))
    for l in (0, 1):
        if f"peer{l}" in stages:
            ut, wqb, skT = prep_peer(inputs["peer_u"][l], inputs["peer_w_q"][l], inputs["peer_subkeys"][l])
            shared[f"ut{l}"] = ut; shared[f"wq{l}"] = wqb; shared[f"skT{l}"] = skT
            shared[f"pv{l}"] = np.ascontiguousarray(inputs["peer_v"][l]).reshape(128, 128, 1024)
            shared[f"ln2g{l}"] = np.ascontiguousarray(inputs["ln2_g"][l]); shared[f"ln2b{l}"] = np.ascontiguousarray(inputs["ln2_b"][l])
    if "l1scan" in stages or "l1mix" in stages:
        shared["w1"] = prep_l1_weights(inputs["rglru_w_in"][0], inputs["w_out"][1])
        shared["conv_w"] = np.ascontiguousarray(inputs["rglru_conv_w"][0].reshape(4, 8, 96).transpose(2, 1, 0))
        shared["conv_b"] = _chan(inputs["rglru_conv_b"][0]); shared["rg_ba"] = _chan(inputs["rglru_b_a"][0])
        shared["rg_bx"] = _chan(inputs["rglru_b_x"][0]); shared["rg_lam"] = _chan(inputs["rglru_lam"][0])
        shared["rg_wa"] = np.ascontiguousarray(inputs["rglru_w_a"][0].transpose(1, 0, 2))
        shared["rg_wx"] = np.ascontiguousarray(inputs["rglru_w_x"][0].transpose(1, 0, 2))
        if "l1mix" in stages:
            shared["ln1g1"] = np.ascontiguousarray(inputs["ln1_g"][1]); shared["ln1b1"] = np.ascontiguousarray(inputs["ln1_b"][1])
            shared["wkv1"] = _blockify(np.ascontiguousarray(inputs["xattn_w_kv"][1]))
    for c in range(8):
        bi, half = c // 2, c % 2
        m = dict(shared)
        if "l1scan" in stages or "l1mix" in stages:
            m["xhalo"] = np.ascontiguousarray(x[bi, T - 3:T]) if half else np.zeros((3, D), np.float32)
            m["hinit"] = np.zeros((RB, 8), np.float32) if hinit is None else np.ascontiguousarray(hinit[c])
            m["memT"] = np.ascontiguousarray(inputs["mem"][bi].T.reshape(8, 128, 256).transpose(1, 0, 2))
        m["x_own"] = np.ascontiguousarray(x[bi, half * T:(half + 1) * T])
        m["flag"] = np.full((128, 1), float(half), np.float32)
        if "l0mix" in stages:
            m["x_pre"] = np.ascontiguousarray(x[bi, 0:T]) if half else np.zeros((T, D), np.float32)
            m["memT"] = np.ascontiguousarray(inputs["mem"][bi].T.reshape(8, 128, 256).transpose(1, 0, 2))
        maps.append(m)
    return maps


DBG = {}
def prep_peer(u, wq, sk):
    ut = np.ascontiguousarray(u.reshape(128, 128, 8, 128).transpose(0, 3, 2, 1)).reshape(128, 128, 1024)
    wqb = np.stack([_blockify(np.ascontiguousarray(wq[:, b * 512:(b + 1) * 512])) for b in range(4)]).reshape(4, 128, 4096)
    skT = np.ascontiguousarray(sk.transpose(2, 0, 1))
    return ut, wqb, skT


def _peer_method(self, layer, ngroups=NG):
    nc, P, ps, sb = self.nc, self.P, self.ps, self.sb
    TN = 4
    with ExitStack() as es:
        S = lambda n, sh, dt=F32: sb(f"{n}_p{layer}", sh, dt, es=es)
        self.wblk = [S("wblk0", [128, 8, 512], BF16)]
        d_ut = self.inp(f"ut{layer}", [128, 128, 1024]); d_v = self.inp(f"pv{layer}", [128, 128, 1024])
        d_wq = self.inp(f"wq{layer}", [4, 128, 4096]); d_sk = self.inp(f"skT{layer}", [128, 2, 128])
        d_lng = self.inp(f"ln2g{layer}", [D]); d_lnb = self.inp(f"ln2b{layer}", [D])
        utb = self.scratch(f"utb{layer}", [128, 128, 1024], BF16); vb = self.scratch(f"vb{layer}", [128, 128, 1024], BF16)
        wqb = self.scratch(f"wqb{layer}", [4, 128, 4096], BF16)
        for b in range(4):
            P.dma("pool", wqb[b].rearrange("p (a c) -> p a c", c=2048), d_wq[b].rearrange("p (a c) -> p a c", c=2048), W=[f"wqb{layer}_{b}"])
        for i0 in range(0, 128, 4):
            P.dma("pool", utb[i0:i0 + 4], d_ut[i0:i0 + 4], W=[f"utb{layer}_{i0 // 4}"])
            P.dma("pool", vb[i0:i0 + 4], d_v[i0:i0 + 4], W=[f"vb{layer}_{i0 // 4}"])
        if DBG.get("cast_barrier", True):
            P.barrier()
        GT = S("GT", [128, 128, G], BF16)
        xTg = S("xTg", [128, 8, G], BF16); qT = S("qT", [128, 16, G], BF16)
        ublk = [S(f"ublk{k}", [128, 1024], BF16) for k in range(4)]
        vblk = [S(f"vblk{k}", [128, 1024], BF16) for k in range(4)]
        arA = S("arA", [128, 1024]); arB = S("arB", [128, 1024])
        s_sb = arA[:, 0:512].rearrange("p (j k) -> p j k", k=128); s2 = arA[:, 512:1024].rearrange("p (j k) -> p j k", k=128)
        eq = arA[:, 0:512].rearrange("p (h r a) -> p h r a", r=16, a=16); prod = arA[:, 512:1024].rearrange("p (h r a) -> p h r a", r=16, a=16)
        cand = arB[:, 0:512].rearrange("p (h a b) -> p h a b", a=16, b=16); cand2 = arB[:, 512:1024].rearrange("p (h a b) -> p h a b", a=16, b=16)
        sv = S("sv", [128, 16, 16]); si = S("si", [128, 16, 16], U32); sif = S("sif", [128, 16, 16])
        fv = S("fv", [128, 8, 16]); fp = S("fp", [128, 8, 16], U32); pf = S("pf", [128, 8, 16]); bfl = S("bfl", [128, 8, 16]); af = S("af", [128, 8, 16])
        ex = S("ex", [128, 8, 16]); zs = S("zs", [128, 8]); tri = S("tri", [128, 3, 128])
        hr3 = S("hr3", [128, 3, G], BF16)
        Bt = [S("Bt0", [128, TN, 128], BF16), S("Bt1", [128, TN, 128], BF16)]
        At = [S("At0", [128, TN, 128], BF16), S("At1", [128, TN, 128], BF16)]
        eqt = S("eqt", [128, TN, 128], BF16)
        ge = [S("ge0", [128, G], BF16), S("ge1", [128, G], BF16)]; coef = [S("coef0", [128, G], BF16), S("coef1", [128, G], BF16)]
        skT32 = S("skT32", [128, 2, 128]); skTb = S("skTb", [128, 2, 128], BF16)
        iot32 = S("iot32", [128, 128]); iot = S("iot", [128, 128], BF16); iot16 = S("iot16", [128, 16])
        lng = S("lng", [128, D]); lnb = S("lnb", [128, D]); st = S("st", [128, 2, 6]); ag = S("ag", [128, 4])
        P.dma("sp", skT32[:], d_sk, W=["skT32"])
        P.op("dve", lambda e: e.tensor_copy(out=skTb[:], in_=skT32[:]), R=["skT32"], W=["skTb"])
        P.op("pool", lambda e: e.iota(iot32[:], pattern=[[1, 128]], base=0, channel_multiplier=0, allow_small_or_imprecise_dtypes=True), W=["iot32"])
        P.op("dve", lambda e: e.tensor_copy(out=iot[:], in_=iot32[:]), R=["iot32"], W=["iot"])
        P.op("pool", lambda e: e.iota(iot16[:], pattern=[[1, 16]], base=0, channel_multiplier=0, allow_small_or_imprecise_dtypes=True), W=["iot16"])
        P.dma("sp", lng[:], d_lng.partition_broadcast(128), W=["lng"])
        P.dma("sp", lnb[:], d_lnb.partition_broadcast(128), W=["lnb"])
        def load_tb(i):
            b = i % 4
            P.dma("sp", ublk[b][:], utb[i], R=[f"utb{layer}_{i // 4}"], W=[f"ublk{b}"])
            P.dma("sp", vblk[b][:], vb[i], R=[f"vb{layer}_{i // 4}"], W=[f"vblk{b}"])

        xTgs = [xTg, S("xTgB", [128, 8, G], BF16)]

        def phaseA1(g):
                xTg = xTgs[g % 2]; xk = f"xTg{g % 2}"
                self.make_xT(xTg, [self.X[:, 2 * g + tt, :] for tt in range(2)], [f"X{2 * g + tt}" for tt in range(2)], xk)
                sched = self.Sched(self, [(wqb[b], f"wqb{layer}_{b}") for b in range(4)])
                for blk in range(4):
                    wi = sched.get()
                    for j in range(4):
                        bk = 4 + (j % 2)
                        for kc in range(8):
                            P.op("pe", lambda e, j=j, kc=kc, bk=bk, wi=wi: e.matmul(
                                ps[:, bk, 0:G], lhsT=self.wblk[wi][:, kc, j * 128:(j + 1) * 128], rhs=xTg[:, kc, :],
                                start=(kc == 0), stop=(kc == 7)), R=[xk, f"wblk{wi}"], W=[f"ps{bk}"])
                        P.op("act", lambda e, j=j, bk=bk, blk=blk: e.copy(out=qT[:, blk * 4 + j, :], in_=ps[:, bk, 0:G]), R=[f"ps{bk}"], W=["qT"])

        def phaseA2(g):
            xTg = xTgs[g % 2]
            for tt in range(2):
                tok = slice(tt * 128, (tt + 1) * 128)
                for hh in range(4):
                    bk = 6 + (hh % 2)
                    for j4 in range(4):
                        jj = hh * 4 + j4
                        P.op("pe", lambda e, jj=jj, j4=j4, bk=bk: e.matmul(
                            ps[:, bk, j4 * 128:(j4 + 1) * 128], lhsT=qT[:, jj, tok], rhs=skTb[:, jj % 2, :],
                            start=True, stop=True), R=["qT", "skTb"], W=[f"ps{bk}"])
                    P.op("act", lambda e, bk=bk: e.copy(out=s_sb, in_=ps[:, bk, :].rearrange("p (j k) -> p j k", k=128)),
                         R=[f"ps{bk}"], W=["arA"])
                    yield
                    for j4 in range(4):
                        jj = hh * 4 + j4
                        P.op("dve", lambda e, jj=jj, j4=j4: e.max(out=sv[:, jj, 0:8], in_=s_sb[:, j4, :]), R=["arA"], W=["sv"])
                        P.op("dve", lambda e, jj=jj, j4=j4: e.max_index(out=si[:, jj, 0:8], in_max=sv[:, jj, 0:8], in_values=s_sb[:, j4, :]), R=["arA", "sv"], W=["si"])
                        P.op("dve", lambda e, jj=jj, j4=j4: e.match_replace(out=s2[:, j4, :], in_to_replace=sv[:, jj, 0:8], in_values=s_sb[:, j4, :], imm_value=-1e30),
                             R=["arA", "sv"], W=["arA"])
                        P.op("dve", lambda e, jj=jj, j4=j4: e.max(out=sv[:, jj, 8:16], in_=s2[:, j4, :]), R=["arA"], W=["sv"])
                        P.op("dve", lambda e, jj=jj, j4=j4: e.max_index(out=si[:, jj, 8:16], in_max=sv[:, jj, 8:16], in_values=s2[:, j4, :]), R=["arA", "sv"], W=["si"])
                        yield
                P.op("dve", lambda e: e.tensor_copy(out=sif[:], in_=si[:]), R=["si"], W=["sif"])
                svv = sv[:].rearrange("p (h two) a -> p h two a", two=2)
                sfv = sif[:].rearrange("p (h two) a -> p h two a", two=2)
                for hq in range(4):
                    hs = slice(2 * hq, 2 * hq + 2)
                    P.op("dve", lambda e: e.tensor_tensor(out=cand, in0=svv[:, hs, 0, :].unsqueeze(3).broadcast_to([128, 2, 16, 16]),
                                                          in1=svv[:, hs, 1, :].unsqueeze(2).broadcast_to([128, 2, 16, 16]), op=ALU.add), R=["sv"], W=["arB"])
                    for h4 in range(2):
                        h = 2 * hq + h4
                        c1 = cand[:, h4].rearrange("p a b -> p (a b)"); c2 = cand2[:, h4].rearrange("p a b -> p (a b)")
                        P.op("dve", lambda e: e.max(out=fv[:, h, 0:8], in_=c1), R=["arB"], W=["fv"])
                        P.op("dve", lambda e: e.max_index(out=fp[:, h, 0:8], in_max=fv[:, h, 0:8], in_values=c1), R=["arB", "fv"], W=["fp"])
                        P.op("dve", lambda e: e.match_replace(out=c2, in_to_replace=fv[:, h, 0:8], in_values=c1, imm_value=-1e30), R=["arB", "fv"], W=["arB"])
                        P.op("dve", lambda e: e.max(out=fv[:, h, 8:16], in_=c2), R=["arB"], W=["fv"])
                        P.op("dve", lambda e: e.max_index(out=fp[:, h, 8:16], in_max=fv[:, h, 8:16], in_values=c2), R=["arB", "fv"], W=["fp"])
                        yield
                gwv = tri[:, 2, :].rearrange("p (h r) -> p h r", r=16)
                P.op("dve", lambda e: e.tensor_tensor(out=ex[:], in0=fv[:], in1=fv[:, :, 0:1].broadcast_to([128, 8, 16]), op=ALU.subtract), R=["fv"], W=["ex"])
                P.op("act", lambda e: e.activation(out=ex[:], in_=ex[:], func=AF.Exp), R=["ex"], W=["ex"])
                P.op("dve", lambda e: e.tensor_reduce(out=zs[:], in_=ex[:], axis=AX.X, op=ALU.add), R=["ex"], W=["zs"])
                P.op("dve", lambda e: e.reciprocal(out=zs[:], in_=zs[:]), R=["zs"], W=["zs"])
                P.op("dve", lambda e: e.tensor_tensor(out=gwv, in0=ex[:], in1=zs[:].unsqueeze(2).broadcast_to([128, 8, 16]), op=ALU.mult), R=["ex", "zs"], W=["tri2"])
                yield
                P.op("dve", lambda e: e.tensor_single_scalar(out=pf[:].bitcast(U32), in_=fp[:], scalar=15, op=ALU.bitwise_and), R=["fp"], W=["pf"])
                P.op("dve", lambda e: e.tensor_copy(out=bfl[:], in_=pf[:].bitcast(U32)), R=["pf"], W=["bfl"])
                P.op("dve", lambda e: e.tensor_single_scalar(out=pf[:].bitcast(U32), in_=fp[:], scalar=4, op=ALU.logical_shift_right), R=["fp", "bfl"], W=["pf"])
                P.op("dve", lambda e: e.tensor_copy(out=af[:], in_=pf[:].bitcast(U32)), R=["pf"], W=["af"])
                yield
                i16 = iot16[:].unsqueeze(1).unsqueeze(1).broadcast_to([128, 2, 16, 16])
                for which, (srcidx, two) in enumerate(((af, 0), (bfl, 1))):
                    ov = tri[:, which, :].rearrange("p (h r) -> p h r", r=16)
                    for hq in range(4):
                        hs = slice(2 * hq, 2 * hq + 2)
                        P.op("dve", lambda e: e.tensor_tensor(out=eq, in0=srcidx[:, hs, :].unsqueeze(3).broadcast_to([128, 2, 16, 16]), in1=i16, op=ALU.is_equal),
                             R=["af", "bfl", "iot16"], W=["arA"])
                        P.op("dve", lambda e: e.tensor_tensor(out=prod, in0=eq, in1=sfv[:, hs, two, :].unsqueeze(2).broadcast_to([128, 2, 16, 16]), op=ALU.mult),
                             R=["arA", "sif"], W=["arA"])
                        P.op("dve", lambda e: e.tensor_reduce(out=ov[:, hs, :], in_=prod, axis=AX.X, op=ALU.add), R=["arA"], W=[f"tri{which}"])
                        yield
                for q3 in range(3):
                    P.op("pe", lambda e, q3=q3: e.transpose(out=ps[:, 7, q3 * 128:(q3 + 1) * 128], in_=tri[:, q3, :], identity=self.ident[:]),
                         R=[f"tri{q3}", "ident"], W=["ps7"])
                P.op("act", lambda e: e.copy(out=hr3[:, :, tok], in_=ps[:, 7, 0:384].rearrange("p (q t) -> p q t", q=3)), R=["ps7"], W=["hr3"])

            yield

        phaseA1(0)
        for _ in phaseA2(0):
            pass
        for g in range(ngroups):
            xTg = xTgs[g % 2]; xk = f"xTg{g % 2}"
            P.cp(f"peerA{layer}")
            ib = iot[:].unsqueeze(1).broadcast_to([128, TN, 128])
            for tb in range(G // TN):
                t0 = tb * TN
                bi = tb % 2
                P.op("dve", lambda e: e.tensor_tensor(out=Bt[bi][:], in0=ib, in1=hr3[:, 1, t0:t0 + TN].unsqueeze(2).broadcast_to([128, TN, 128]), op=ALU.is_equal),
                     R=["iot", "hr3"], W=[f"Bt{bi}"])
                P.op("dve", lambda e: e.tensor_tensor(out=eqt[:], in0=ib, in1=hr3[:, 0, t0:t0 + TN].unsqueeze(2).broadcast_to([128, TN, 128]), op=ALU.is_equal),
                     R=["iot", "hr3"], W=["eqt"])
                P.op(DBG.get("at_eng", "dve"), lambda e: e.tensor_tensor(out=At[bi][:], in0=eqt[:], in1=hr3[:, 2, t0:t0 + TN].unsqueeze(2).broadcast_to([128, TN, 128]), op=ALU.mult),
                     R=["eqt", "hr3"], W=[f"At{bi}"])
                for t in range(TN):
                    tg = t0 + t
                    bk = (tg // 4) % 8
                    P.op("pe", lambda e, t=t, tg=tg, bk=bk: e.matmul(ps[:, bk, (tg % 4) * 128:(tg % 4 + 1) * 128], lhsT=Bt[bi][:, t, :], rhs=At[bi][:, t, :],
                                                                     start=True, stop=True), R=[f"Bt{bi}", f"At{bi}"], W=[f"ps{bk}"])
                if (t0 + TN) % 16 == 0 and not DBG.get("noevac"):
                    b0 = ((t0 + TN - 16) // 4) % 8
                    tq = t0 + TN - 16
                    P.op("act", lambda e: e.copy(out=GT[:, :, tq:tq + 16], in_=ps[:, b0:b0 + 4, :].rearrange("p b (t i) -> p i (b t)", t=4)),
                         R=[f"ps{b}" for b in range(b0, b0 + 4)], W=["GT"])
            P.cp(f"peerB{layer}")
            def emit_ht(i):
                hb = 4 + (i % 2)
                for kc in range(8):
                    P.op("pe", lambda e, kc=kc: e.matmul(ps[:, hb, 0:G], lhsT=ublk[i % 4][:, kc * 128:(kc + 1) * 128], rhs=xTg[:, kc, :],
                                                         start=(kc == 0), stop=(kc == 7)), R=[f"ublk{i % 4}", xk], W=[f"ps{hb}"])

            def emit_rest(i):
                hb = 4 + (i % 2)
                gi = i % 2
                P.op("act", lambda e: e.activation(out=ge[gi][:], in_=ps[:, hb, 0:G], func=AF.Gelu), R=[f"ps{hb}"], W=[f"ge{gi}"])
                P.op("dve", lambda e: e.tensor_tensor(out=coef[gi][:], in0=ge[gi][:], in1=GT[:, i, :], op=ALU.mult), R=[f"ge{gi}", "GT"], W=[f"coef{gi}"])
                for tt in range(2):
                    for half in range(2):
                        yb = 2 * tt + half
                        P.op("pe", lambda e, tt=tt, half=half, yb=yb: e.matmul(
                            ps[:, yb, :], lhsT=coef[gi][:, tt * 128:(tt + 1) * 128], rhs=vblk[i % 4][:, half * 512:(half + 1) * 512],
                            start=(i == 0), stop=(i == 127)), R=[f"coef{gi}", f"vblk{i % 4}"], W=[f"ps{yb}"])

            genA = None
            if g + 1 < ngroups:
                phaseA1(g + 1)
                genA = phaseA2(g + 1)
            for k in range(3):
                load_tb(k)
            emit_ht(0)
            for i in range(128):
                if i + 1 < 128:
                    emit_ht(i + 1)
                emit_rest(i)
                if i + 3 < 128:
                    load_tb(i + 3)
                if genA is not None and i >= 4:
                    next(genA, None)
            if genA is not None:
                for _ in genA:
                    pass
            P.cp(f"peerC{layer}")
            for tt in range(2):
                self.resid_ln(2 * g + tt, ps[:, 2 * tt:2 * tt + 2, :], [f"ps{2 * tt}", f"ps{2 * tt + 1}"], lng, lnb, None, st, ag)
        P.barrier()


Builder.peer = _peer_method


NB1 = 9
RB = 96


def prep_l1_weights(w_in, w_out):
    gate = w_in[:, 0:768]; xr = w_in[:, 768:1536]; xq = w_in[:, 1536:1792]
    blocks = []
    for src in (gate, xr):
        for hb in range(2):
            blocks.append(np.concatenate([_pad_cols(src[:, g * 96:(g + 1) * 96], 128) for g in range(4 * hb, 4 * hb + 4)], axis=1))
    blocks.append(_pad_cols(xq, 512))
    out = [_blockify(np.ascontiguousarray(b, dtype=np.float32)).reshape(128, 4096) for b in blocks]
    for cb in range(4):
        blk = np.zeros((128, 10, 256), np.float32)
        for g in range(8):
            blk[0:96, g, :] = w_out[g * 96:(g + 1) * 96, cb * 256:(cb + 1) * 256]
        for c2 in range(2):
            blk[:, 8 + c2, :] = w_out[768 + c2 * 128:768 + (c2 + 1) * 128, cb * 256:(cb + 1) * 256]
        out.append(_pad_cols(blk.reshape(128, 2560), 4096))
    return np.stack(out)


def _chan(v):
    return np.ascontiguousarray(np.asarray(v, np.float32).reshape(8, 96).T)


def _l1_method(self, scan_only=False, ngroups=NG, halo_src=None, hinit_src=None):
    nc, P, ps, sb = self.nc, self.P, self.ps, self.sb
    ident = self.ident
    tag = "s" if scan_only else "m"
    with ExitStack() as es:
        S = lambda n, sh, dt=F32: sb(f"{n}_l1{tag}", sh, dt, es=es)
        if "w1" in self.din:
            d_w1 = self.din["w1"]; w1b = self.w1b
        else:
            d_w1 = self.inp("w1", [NB1, 128, 4096])
            w1b = self.w1b = self.scratch("w1b", [NB1, 128, 4096], BF16)
            for b in range(NB1):
                P.dma("pool", w1b[b].rearrange("p (a c) -> p a c", c=2048), d_w1[b].rearrange("p (a c) -> p a c", c=2048), W=[f"w1b{b}"])
            P.barrier()
        g_in = lambda n, sh: self.din[n] if n in self.din else self.inp(n, sh)
        d_halo = halo_src if halo_src is not None else g_in("xhalo", [3, D])
        d_hinit = hinit_src if hinit_src is not None else (None if (scan_only and halo_src is not None) else g_in("hinit", [RB, 8]))
        d_cw = g_in("conv_w", [RB, 8, 4]); d_cb = g_in("conv_b", [RB, 8]); d_wa = g_in("rg_wa", [RB, 8, RB]); d_wx = g_in("rg_wx", [RB, 8, RB])
        d_ba = g_in("rg_ba", [RB, 8]); d_bx = g_in("rg_bx", [RB, 8]); d_lam = g_in("rg_lam", [RB, 8])
        self.wblk = [S("wblk0", [128, 8, 512], BF16), S("wblk1", [128, 8, 512], BF16)]
        xTg = S("xTg", [128, 8, G], BF16)
        xrb = S("xrb", [RB, 8, 3 + G]); xc = S("xc", [RB, 8, G]); xcb = S("xcb", [RB, 8, G], BF16)
        cw = S("cw", [RB, 8, 4]); cb = S("cb", [RB, 8]); wa32 = S("wa32", [RB, 8, RB]); wx32 = S("wx32", [RB, 8, RB])
        wab = S("wab", [RB, 8, RB], BF16); wxb = S("wxb", [RB, 8, RB], BF16)
        ba = S("ba", [RB, 8]); bx = S("bx", [RB, 8]); lamc = S("lamc", [RB, 8]); hcar = S("hcar", [RB, 8])
        rt = S("rt", [RB, G]); it = S("it", [RB, G]); at = S("at", [RB, G]); ut = S("ut", [RB, G]); ht = S("ht", [RB, G]); tmp = S("tmp", [RB, G])
        xh = S("xh", [3, D]); xTh = S("xTh", [128, 8, 4], BF16)
        for dst, src, k in ((cw, d_cw, "cw"), (cb, d_cb, "cb"), (wa32, d_wa, "wa32"), (wx32, d_wx, "wx32"), (ba, d_ba, "ba"), (bx, d_bx, "bx"),
                            (lamc, d_lam, "lamc"), (xh, d_halo, "xh")):
            P.dma("sp", dst[:], src, R=(["cc_halo"] if (k == "xh" and halo_src is not None) else []), W=[k])
        if d_hinit is None:
            P.op("dve", lambda e: e.memset(hcar[:], 0.0), W=["hcar"])
        else:
            P.dma("sp", hcar[:], d_hinit, R=(["cc_hend"] if hinit_src is not None else []), W=["hcar"])
        if halo_src is not None:
            P.op("dve", lambda e: e.tensor_scalar(out=xh[:], in0=xh[:], scalar1=self.flag[0:3, 0:1], scalar2=None, op0=ALU.mult), R=["xh", "flag"], W=["xh"])
        if hinit_src is not None:
            P.op("dve", lambda e: e.tensor_scalar(out=hcar[:], in0=hcar[:], scalar1=self.flag[0:RB, 0:1], scalar2=None, op0=ALU.mult), R=["hcar", "flag"], W=["hcar"])
        P.op("dve", lambda e: e.tensor_copy(out=wab[:], in_=wa32[:]), R=["wa32"], W=["wab"])
        P.op("dve", lambda e: e.tensor_copy(out=wxb[:], in_=wx32[:]), R=["wx32"], W=["wxb"])
        P.op("act", lambda e: e.activation(out=lamc[:], in_=lamc[:], func=AF.Exp, scale=-1.0), R=["lamc"], W=["lamc"])
        P.op("act", lambda e: e.activation(out=lamc[:], in_=lamc[:], func=AF.Ln, bias=1.0), R=["lamc"], W=["lamc"])
        P.op("dve", lambda e: e.tensor_scalar(out=lamc[:], in0=lamc[:], scalar1=-8.0, scalar2=None, op0=ALU.mult), R=["lamc"], W=["lamc"])
        if not scan_only:
            d_lng = self.inp("ln1g1", [D]); d_lnb = self.inp("ln1b1", [D])
            kxT, vxaug = self.xattn_setup(1, es)
            gg = S("gg", [RB, 8, G], BF16); hmT = S("hmT", [RB, 8, G], BF16); xqT = S("xqT", [64, 4, G], BF16)
            hat = [S("hat0", [64, 256], BF16), S("hat1", [64, 256], BF16)]; haT = S("haT", [128, 2, G], BF16)
            Esb = S("Esb", [128, 8, 64], BF16); rsb = S("rsb", [64, 4])
            lng = S("lng", [128, D]); lnb = S("lnb", [128, D]); xs = S("xs", [128, D]); st = S("st", [128, 2, 6]); ag = S("ag", [128, 4])
            P.dma("sp", lng[:], d_lng.partition_broadcast(128), W=["lng"])
            P.dma("sp", lnb[:], d_lnb.partition_broadcast(128), W=["lnb"])
        srcs = [(w1b[b], f"w1b{b}") for b in (2, 3)]
        for g in range(ngroups):
            srcs += [(w1b[b], f"w1b{b}") for b in ((2, 3) if scan_only else (2, 3, 0, 1, 4, 5, 6, 7, 8))]
        sched = self.Sched(self, srcs)

        def xr_proj(wi, hb, rhs, n, dst_off, rkey):
            for j in range(4):
                g8 = 4 * hb + j
                bk = 2 + (j % 2)
                for kc in range(8):
                    P.op("pe", lambda e, j=j, kc=kc, bk=bk: e.matmul(
                        ps[0:RB, bk, 0:n], lhsT=self.wblk[wi][:, kc, j * 128:j * 128 + RB], rhs=rhs[:, kc, 0:n],
                        start=(kc == 0), stop=(kc == 7)), R=[rkey, f"wblk{wi}"], W=[f"ps{bk}"])
                P.op("act", lambda e, g8=g8, bk=bk: e.copy(out=xrb[:, g8, dst_off:dst_off + n], in_=ps[0:RB, bk, 0:n]), R=[f"ps{bk}"], W=["xrb"])

        for kc in range(8):
            P.op("pe", lambda e, kc=kc: e.transpose(out=ps[:, 0, kc * 4:kc * 4 + 3], in_=xh[0:3, kc * 128:(kc + 1) * 128], identity=ident[0:3, 0:3]),
                 R=["xh", "ident"], W=["ps0"])
        P.op("act", lambda e: e.copy(out=xTh[:, :, 0:3], in_=ps[:, 0, 0:32].rearrange("p (k c) -> p k c", c=4)[:, :, 0:3]), R=["ps0"], W=["xTh"])
        for hb in range(2):
            wi = sched.get()
            xr_proj(wi, hb, xTh, 3, 0, "xTh")
        mixi = 0
        for g in range(ngroups):
            self.make_xT(xTg, [self.X[:, 2 * g + tt, :] for tt in range(2)], [f"X{2 * g + tt}" for tt in range(2)], "xTg")
            for hb in range(2):
                wi = sched.get()
                xr_proj(wi, hb, xTg, G, 3, "xTg")
            if not scan_only:
                for hb in range(2):
                    wi = sched.get()
                    for j in range(4):
                        g8 = 4 * hb + j
                        bk = 2 + (j % 2)
                        for kc in range(8):
                            P.op("pe", lambda e, j=j, kc=kc, bk=bk: e.matmul(
                                ps[0:RB, bk, 0:G], lhsT=self.wblk[wi][:, kc, j * 128:j * 128 + RB], rhs=xTg[:, kc, :],
                                start=(kc == 0), stop=(kc == 7)), R=["xTg", f"wblk{wi}"], W=[f"ps{bk}"])
                        P.op("act", lambda e, g8=g8, bk=bk: e.activation(out=gg[:, g8, :], in_=ps[0:RB, bk, 0:G], func=AF.Gelu), R=[f"ps{bk}"], W=["gg"])
                wi = sched.get()
                for j in range(4):
                    bk = 2 + (j % 2)
                    for kc in range(8):
                        P.op("pe", lambda e, j=j, kc=kc, bk=bk: e.matmul(
                            ps[0:64, bk, 0:G], lhsT=self.wblk[wi][:, kc, j * 64:(j + 1) * 64], rhs=xTg[:, kc, :],
                            start=(kc == 0), stop=(kc == 7)), R=["xTg", f"wblk{wi}"], W=[f"ps{bk}"])
                    P.op("act", lambda e, j=j, bk=bk: e.copy(out=xqT[:, j, :], in_=ps[0:64, bk, 0:G]), R=[f"ps{bk}"], W=["xqT"])
            for g8 in range(8):
                for w in range(4):
                    if w == 0:
                        P.op("dve", lambda e: e.tensor_scalar(out=xc[:, g8, :], in0=xrb[:, g8, 0:G], scalar1=cw[:, g8, 0:1], scalar2=cb[:, g8:g8 + 1],
                                                              op0=ALU.mult, op1=ALU.add), R=["xrb", "cw", "cb"], W=["xc"])
                    else:
                        P.op("dve", lambda e, w=w: e.scalar_tensor_tensor(out=xc[:, g8, :], in0=xrb[:, g8, w:w + G], scalar=cw[:, g8, w:w + 1], in1=xc[:, g8, :],
                                                                          op0=ALU.mult, op1=ALU.add), R=["xrb", "cw", "xc"], W=["xc"])
                P.op("act", lambda e: e.copy(out=xcb[:, g8, :], in_=xc[:, g8, :]), R=["xc"], W=["xcb"])
            P.op("dve", lambda e: e.tensor_copy(out=xrb[:, :, 0:3], in_=xrb[:, :, G:G + 3]), R=["xrb"], W=["xrb"])
            for g8 in range(8):
                P.op("pe", lambda e: e.matmul(ps[0:RB, 4, 0:G], lhsT=wab[:, g8, :], rhs=xcb[:, g8, :], start=True, stop=True), R=["wab", "xcb"], W=["ps4"])
                P.op("pe", lambda e: e.matmul(ps[0:RB, 5, 0:G], lhsT=wxb[:, g8, :], rhs=xcb[:, g8, :], start=True, stop=True), R=["wxb", "xcb"], W=["ps5"])
                P.op("act", lambda e: e.activation(out=rt[:], in_=ps[0:RB, 4, 0:G], func=AF.Sigmoid, bias=ba[:, g8:g8 + 1]), R=["ps4", "ba"], W=["rt"])
                P.op("act", lambda e: e.activation(out=it[:], in_=ps[0:RB, 5, 0:G], func=AF.Sigmoid, bias=bx[:, g8:g8 + 1]), R=["ps5", "bx"], W=["it"])
                P.op("act", lambda e: e.activation(out=at[:], in_=rt[:], func=AF.Exp, scale=lamc[:, g8:g8 + 1]), R=["rt", "lamc"], W=["at"])
                P.op("dve", lambda e: e.tensor_tensor(out=tmp[:], in0=at[:], in1=at[:], op=ALU.mult), R=["at"], W=["tmp"])
                P.op("dve", lambda e: e.tensor_scalar(out=tmp[:], in0=tmp[:], scalar1=-1.0, scalar2=1.0, op0=ALU.mult, op1=ALU.add), R=["tmp"], W=["tmp"])
                P.op("act", lambda e: e.activation(out=tmp[:], in_=tmp[:], func=AF.Sqrt), R=["tmp"], W=["tmp"])
                P.op("dve", lambda e: e.tensor_tensor(out=ut[:], in0=it[:], in1=xc[:, g8, :], op=ALU.mult), R=["it", "xc"], W=["ut"])
                P.op("dve", lambda e: e.tensor_tensor(out=ut[:], in0=ut[:], in1=tmp[:], op=ALU.mult), R=["ut", "tmp"], W=["ut"])
                P.op("dve", lambda e: e.tensor_tensor_scan(out=ht[:], data0=at[:], data1=ut[:], initial=hcar[:, g8:g8 + 1], op0=ALU.mult, op1=ALU.add),
                     R=["at", "ut", "hcar"], W=["ht"])
                P.op("dve", lambda e: e.tensor_copy(out=hcar[:, g8:g8 + 1], in_=ht[:, G - 1:G]), R=["ht"], W=["hcar"])
                if not scan_only:
                    P.op("dve", lambda e: e.tensor_tensor(out=hmT[:, g8, :], in0=ht[:], in1=gg[:, g8, :], op=ALU.mult), R=["ht", "gg"], W=["hmT"])
            if scan_only:
                continue
            for ck in range(4):
                c0 = ck * 64
                hx = hat[mixi]; hk = f"hat{mixi}"; mixi ^= 1
                self.xattn_chunk(kxT, vxaug, xqT, c0, 64, hx[:, :], hk, Esb, rsb)
                pb = ps[:, 2, :].bitcast(BF16)
                for c2 in range(2):
                    P.op("pe", lambda e, c2=c2: e.transpose(out=pb[:, c2 * 64:(c2 + 1) * 64], in_=hx[:, c2 * 128:(c2 + 1) * 128], identity=self.identb[0:64, 0:64]),
                         R=[hk, "identb"], W=["ps2"])
                P.op("act", lambda e: e.copy(out=haT[:, :, c0:c0 + 64], in_=pb[:, 0:128].rearrange("p (k t) -> p k t", t=64)), R=["ps2"], W=["haT"])
            banks = {0: (0, 1), 1: (5, 6)}
            for cbk in range(4):
                wi = sched.get()
                wv = self.wblk[wi][:].rearrange("p k c -> p (k c)")[:, 0:2560].rearrange("p (ch c) -> p ch c", c=256)
                for tt in range(2):
                    bk = banks[tt][cbk // 2]
                    dst = ps[:, bk, (cbk % 2) * 256:(cbk % 2 + 1) * 256]
                    for ch in range(10):
                        if ch < 8:
                            lhsT = hmT[:, ch, tt * 128:(tt + 1) * 128]; rhs = wv[0:RB, ch, :]; rk = "hmT"
                        else:
                            lhsT = haT[:, ch - 8, tt * 128:(tt + 1) * 128]; rhs = wv[:, ch, :]; rk = "haT"
                        P.op("pe", lambda e, lhsT=lhsT, rhs=rhs, dst=dst, ch=ch: e.matmul(dst, lhsT=lhsT, rhs=rhs, start=(ch == 0), stop=(ch == 9)),
                             R=[rk, f"wblk{wi}"], W=[f"ps{bk}"])
            for tt in range(2):
                b0 = banks[tt][0]
                self.resid_ln(2 * g + tt, ps[:, b0:b0 + 2, :], [f"ps{b0}", f"ps{b0 + 1}"], lng, lnb, xs, st, ag)
        if scan_only:
            if halo_src is None:
                d_hend = self.outp("hend", [RB, 8])
                P.dma("sp", d_hend, hcar[:], R=["hcar"], chan="out")
            else:
                P.dma("sp", self.cc2src, hcar[:], R=["hcar"], W=["cc2src"])
        P.barrier()


Builder.l1 = _l1_method


def _run(stages, inputs, x_cur=None, hinit=None, **kw):
    b, es = build(stages, **kw)
    maps = core_inputs(inputs, stages, x_cur=x_cur, hinit=hinit)
    maps = [{k: v for k, v in m.items() if k in b.din} for m in maps]
    res = run_bass_kernel_spmd(b.nc, maps, core_ids=list(range(8)))
    es.close()
    return res.results


def _gather(results, name="out"):
    x = np.empty((4, 2 * T, D), np.float32)
    for c in range(8):
        x[c // 2, (c % 2) * T:(c % 2 + 1) * T] = results[c][name]
    return x


def kernel(**inputs):
    inputs = {k: np.asarray(v) for k, v in inputs.items()}
    return _gather(_run(["fused"], inputs))
```

```python
import numpy as np
from contextlib import ExitStack
import concourse.bass as bass
import concourse.mybir as mybir
from concourse.bass_utils import run_bass_kernel_spmd

F32 = mybir.dt.float32
BF16 = mybir.dt.bfloat16
U32 = mybir.dt.uint32
I32 = mybir.dt.int32
AF = mybir.ActivationFunctionType
ALU = mybir.AluOpType
AX = mybir.AxisListType


class Prog:
    NDS = 24

    def __init__(self, nc, es):
        self.nc = nc
        self.es = es
        self.eng = {"pe": nc.tensor, "dve": nc.vector, "act": nc.scalar, "pool": nc.gpsimd, "sp": nc.sync}
        self.sem = {k: es.enter_context(nc.semaphore("sem_" + k)) for k in self.eng}
        self.cnt = {k: 0 for k in self.eng}
        self.seen = {k: {} for k in self.eng}
        self.dsem = []
        self.dcnt = []
        self.dq = {}
        for q, n in (("sp", 20), ("pool", 8), ("act", 4)):
            self.dq[q] = [len(self.dsem) + i for i in range(n)]
            self.dsem += [es.enter_context(nc.semaphore(f"dsem_{q}{i}")) for i in range(n)]
            self.dcnt += [0] * n
        self.dnext = {q: 0 for q in self.dq}
        self.last_w = {}
        self.readers = {}
        self.n_ins = 0

    def _semof(self, kind, name):
        return self.sem[name] if kind == "e" else self.dsem[name]

    def _wait_ev(self, eng, kind, name, c):
        if self.seen[eng].get((kind, name), 0) >= c:
            return
        self.seen[eng][(kind, name)] = c
        self.eng[eng].wait_ge(self._semof(kind, name), c)

    def _wait(self, eng, R, W):
        deps = []
        for k in R:
            if k in self.last_w:
                deps.append(self.last_w[k])
        for k in W:
            if k in self.last_w:
                deps.append(self.last_w[k])
            deps.extend(self.readers.get(k, ()))
        best = {}
        for kind, name, c in deps:
            if kind == "e" and name == eng and eng == "pe":
                continue
            key = (kind, name)
            if c > best.get(key, 0):
                best[key] = c
        for (kind, name), c in best.items():
            self._wait_ev(eng, kind, name, c)

    def _commit(self, ev, R, W):
        for k in W:
            self.last_w[k] = ev
            self.readers[k] = []
        for k in R:
            self.readers.setdefault(k, []).append(ev)

    limit = None
    stop_at = None

    def cp(self, name):
        if self.stop_at is not None and name == self.stop_at and self.limit is None:
            self.limit = self.n_ins

    def op(self, eng, fn, R=(), W=()):
        if self.limit is not None and self.n_ins >= self.limit:
            return None
        W = list(W) + [k for k in R if k.startswith("ps") and k not in W]
        self._wait(eng, R, W)
        ins = fn(self.eng[eng])
        self.cnt[eng] += 1
        ins.then_inc(self.sem[eng], 1)
        self._commit(("e", eng, self.cnt[eng]), R, W)
        self.n_ins += 1
        return ins

    def dma(self, q, out, in_, R=(), W=(), chan=None):
        if self.limit is not None and self.n_ins >= self.limit and chan != "out":
            return None
        i = self.dq[q][self.dnext[q]]
        self.dnext[q] = (self.dnext[q] + 1) % len(self.dq[q])
        if self.dcnt[i]:
            self._wait_ev(q, "d", i, self.dcnt[i])
        self._wait(q, R, W)
        ins = self.eng[q].dma_start(out=out, in_=in_)
        self.dcnt[i] += 16
        ins.then_inc(self.dsem[i], 16)
        self._commit(("d", i, self.dcnt[i]), R, W)
        return ins

    def allgather_pairs(self, src, dst, R=(), W=()):
        self._wait("pool", R, W)
        sem = self.es.enter_context(self.nc.semaphore(f"cc_sem{len(self.dsem)}"))
        self.dsem.append(sem)
        self.dcnt.append(0)
        i = len(self.dsem) - 1
        ins = self.nc.gpsimd.collective_compute("AllGather", ALU.bypass, replica_groups=[[0, 1], [2, 3], [4, 5], [6, 7]],
                                               ins=[src.opt()], outs=[dst.opt()])
        ins.then_inc(sem)
        self.dcnt[i] = 1
        self._commit(("d", i, 1), R, W)

    def barrier(self):
        for e in self.eng:
            for k, c in self.cnt.items():
                if c:
                    self._wait_ev(e, "e", k, c)
            for i, c in enumerate(self.dcnt):
                if c:
                    self._wait_ev(e, "d", i, c)

    def wait_all(self, eng="sp"):
        for k, c in self.cnt.items():
            if c and k != eng:
                self._wait_ev(eng, "e", k, c)
        for i, c in enumerate(self.dcnt):
            if c:
                self._wait_ev(eng, "d", i, c)


D = 1024
T = 2048
NT = T // 128
G = 256
NG = T // G
H = 4
DH = 192
L = 64
ALPHA = float(4 ** 0.25)
EPS = 1e-5
NB0 = 14


def _pad_cols(a, n):
    out = np.zeros((a.shape[0], n), np.float32)
    out[:, : a.shape[1]] = a
    return out


def _blockify(cols):
    return np.ascontiguousarray(cols.reshape(8, 128, 512).transpose(1, 0, 2))


def prep_l0_weights(w_in, w_out):
    q = w_in[:, 0:768]; k = w_in[:, 768:1536]; v = w_in[:, 1536:2304]; o = w_in[:, 2304:3072]
    g = w_in[:, 3072:3080]; xq = w_in[:, 3080:3336]
    blocks = []
    for src in (q, k):
        for hp in range(2):
            cs = []
            for h in (2 * hp, 2 * hp + 1):
                cs.append(src[:, h * 192: h * 192 + 128])
                cs.append(_pad_cols(src[:, h * 192 + 128: (h + 1) * 192], 128))
            blocks.append(np.concatenate(cs, axis=1))
    blocks.append(_pad_cols(xq, 512))
    for src in (k, v, o):
        blocks.append(_pad_cols(src[:, 0:384], 512))
        blocks.append(_pad_cols(src[:, 384:768], 512))
    blocks.append(_pad_cols(g, 512))
    blocks.append(w_out[:, 0:512]); blocks.append(w_out[:, 512:1024])
    return np.stack([_blockify(np.ascontiguousarray(b, dtype=np.float32)) for b in blocks])


class Builder:
    def __init__(self, es, debug=()):
        self.es = es
        self.debug = set(debug)
        self.nc = nc = bass.Bass("TRN2", target_bir_lowering=False)
        self.P = Prog(nc, es)
        self.din = {}
        self.dout = {}
        self.wb_i = 0

    def inp(self, name, shape, dt=F32):
        ap = self.nc.dram_tensor(name, list(shape), dt, kind="ExternalInput").ap()
        self.din[name] = ap
        return ap

    def outp(self, name, shape, dt=F32):
        ap = self.nc.dram_tensor(name, list(shape), dt, kind="ExternalOutput").ap()
        self.dout[name] = ap
        return ap

    def scratch(self, name, shape, dt):
        return self.nc.dram_tensor(name, list(shape), dt, kind="Internal").ap()

    def sb(self, name, shape, dt=F32, es=None):
        return (es or self.es).enter_context(self.nc.sbuf_tensor(name + "_sb", list(shape), dt))

    def wload(self, src):
        i = self.wb_i % len(self.wblk)
        self.wb_i = (i + 1) % len(self.wblk)
        self.P.dma("sp", self.wblk[i][:].rearrange("p k c -> p (k c)"), src[0], R=[src[1]], W=[f"wblk{i}"], chan="w")
        return i

    class Sched:
        def __init__(self, b, srcs):
            self.b = b; self.srcs = srcs; self.n = 0; self.cur = None
            self.single = len(b.wblk) == 1
            self.nxt = b.wload(srcs[0]) if (srcs and not self.single) else None

        def get(self):
            if self.single:
                self.n += 1
                return self.b.wload(self.srcs[self.n - 1])
            cur = self.nxt
            self.n += 1
            self.nxt = self.b.wload(self.srcs[self.n]) if self.n < len(self.srcs) else None
            return cur

    def setup(self):
        nc, P = self.nc, self.P
        sb = self.sb
        self.ps = self.es.enter_context(nc.psum_tensor("ps", [128, 8, 512], F32))
        self.X = sb("X", [128, NT, D])
        self.ident = sb("ident", [128, 128])
        self.identb = sb("identb", [128, 128], BF16)
        self.cmask = sb("cmask", [64, 64], BF16)
        self.flag = sb("flag", [128, 1])
        d_ident = self.inp("ident", [128, 128])
        d_cmask = self.inp("cmask", [64, 64])
        d_flag = self.inp("flag", [128, 1])
        cm32 = sb("cm32", [64, 64])
        P.dma("sp", self.ident[:], d_ident, W=["ident"])
        P.dma("sp", cm32[:], d_cmask, W=["cm32"])
        P.dma("sp", self.flag[:], d_flag, W=["flag"])
        P.op("dve", lambda e: e.tensor_copy(out=self.identb[:], in_=self.ident[:]), R=["ident"], W=["identb"])
        P.op("dve", lambda e: e.tensor_copy(out=self.cmask[:], in_=cm32[:]), R=["cm32"], W=["cmask"])
        d_x = self.inp("x_own", [T, D])
        for q4 in range(4):
            P.dma("sp", self.X[:, 4 * q4:4 * q4 + 4, :],
                  d_x[512 * q4:512 * (q4 + 1), :].rearrange("(t p) d -> p t d", p=128),
                  W=[f"X{t}" for t in range(4 * q4, 4 * q4 + 4)])

    def psb(self, b0, nb=1):
        return [f"ps{b}" for b in range(b0, b0 + nb)]

    def make_xT(self, xTg, src_tiles, src_keys, kout, b0=0):
        P, ps = self.P, self.ps
        for tt in range(2):
            for kc in range(8):
                P.op("pe", lambda e, tt=tt, kc=kc: e.transpose(
                    out=ps[:, b0 + kc // 4, (kc % 4) * 128:(kc % 4 + 1) * 128],
                    in_=src_tiles[tt][:, kc * 128:(kc + 1) * 128], identity=self.ident[:]),
                    R=[src_keys[tt], "ident"], W=[f"ps{b0 + kc // 4}"])
            P.op("act", lambda e, tt=tt: e.copy(
                out=xTg[:, :, tt * 128:(tt + 1) * 128],
                in_=ps[:, b0:b0 + 2, :].rearrange("p b (k c) -> p (b k) c", c=128)),
                R=[f"ps{b0}", f"ps{b0 + 1}"], W=[kout])

    def xattn_setup(self, layer, es):
        P, ps, sb = self.P, self.ps, self.sb
        d_memT = self.din["memT"] if "memT" in self.din else self.inp("memT", [128, 8, 256])
        d_wkv = self.inp(f"wkv{layer}", [128, 8, 512])
        kxT = sb(f"kxT_{layer}", [64, 4, 256], BF16, es=es)
        vxaug = sb(f"vxaug_{layer}", [128, 2, 4, 65], BF16, es=es)
        with ExitStack() as tes:
            memT32 = sb(f"memT32_{layer}", [128, 8, 256], es=tes)
            wkv32 = sb(f"wkv32_{layer}", [128, 8, 512], es=tes)
            memTb = sb(f"memTb_{layer}", [128, 8, 256], BF16, es=tes)
            wkvb = sb(f"wkvb_{layer}", [128, 8, 512], BF16, es=tes)
            P.dma("sp", memT32[:], d_memT, W=["memT32"])
            P.dma("sp", wkv32[:], d_wkv, W=["wkv32"])
            P.op("dve", lambda e: e.tensor_copy(out=memTb[:], in_=memT32[:]), R=["memT32"], W=["memTb"])
            P.op("pool", lambda e: e.tensor_copy(out=wkvb[:], in_=wkv32[:]), R=["wkv32"], W=["wkvb"])
            for h in range(4):
                for kc in range(8):
                    P.op("pe", lambda e, h=h, kc=kc: e.matmul(
                        ps[0:64, 2, 0:256], lhsT=wkvb[:, kc, h * 64:(h + 1) * 64], rhs=memTb[:, kc, :],
                        start=(kc == 0), stop=(kc == 7)), R=["wkvb", "memTb"], W=["ps2"])
                P.op("act", lambda e, h=h: e.copy(out=kxT[:, h, :], in_=ps[0:64, 2, 0:256]), R=["ps2"], W=["kxT"])
            P.op("pool", lambda e: e.memset(vxaug[:], 1.0), W=["vxaug"])
            for mc in range(2):
                for kc in range(8):
                    P.op("pe", lambda e, mc=mc, kc=kc: e.matmul(
                        ps[:, 3, 0:256], lhsT=memTb[:, kc, mc * 128:(mc + 1) * 128], rhs=wkvb[:, kc, 256:512],
                        start=(kc == 0), stop=(kc == 7)), R=["wkvb", "memTb"], W=["ps3"])
                P.op("act", lambda e, mc=mc: e.copy(
                    out=vxaug[:, mc, :, 0:64], in_=ps[:, 3, 0:256].rearrange("p (h d) -> p h d", h=4)),
                    R=["ps3"], W=["vxaug"])
            P.barrier()
        return kxT, vxaug

    def xattn_chunk(self, kxT, vxaug, xqT, c0, ntok, out_ap, out_key, Esb, rsb):
        P, ps = self.P, self.ps
        for h in range(4):
            for mc in range(2):
                P.op("pe", lambda e, h=h, mc=mc: e.matmul(
                    ps[:, 0, (h * 2 + mc) * 64:(h * 2 + mc) * 64 + ntok],
                    lhsT=kxT[:, h, mc * 128:(mc + 1) * 128],
                    rhs=xqT[:, h, c0:c0 + ntok], start=True, stop=True),
                    R=["kxT", "xqT"], W=["ps0"])
        P.cp("xa_st")
        P.op("act", lambda e: e.activation(
            out=Esb[:, :, 0:ntok], in_=ps[:, 0, :].rearrange("p (j t) -> p j t", t=64)[:, :, 0:ntok],
            func=AF.Exp, scale=0.125), R=["ps0"], W=["Esb"])
        P.cp("xa_exp")
        Ov = ps[0:ntok, 1, 0:260].rearrange("p (h d) -> p h d", d=65)
        for h in range(4):
            for mc in range(2):
                P.op("pe", lambda e, h=h, mc=mc: e.matmul(
                    Ov[:, h, :], lhsT=Esb[:, h * 2 + mc, 0:ntok], rhs=vxaug[:, mc, h, :],
                    start=(mc == 0), stop=(mc == 1)), R=["Esb", "vxaug"], W=["ps1"])
        P.cp("xa_ov")
        P.op("dve", lambda e: e.reciprocal(out=rsb[0:ntok, :], in_=Ov[:, :, 64]), R=["ps1"], W=["rsb"])
        P.op("dve", lambda e: e.tensor_tensor(
            out=out_ap.rearrange("p (h d) -> p h d", d=64), in0=Ov[:, :, 0:64],
            in1=rsb[0:ntok, :].unsqueeze(2).broadcast_to([ntok, 4, 64]), op=ALU.mult),
            R=["ps1", "rsb"], W=[out_key])

    def outproj_ln(self, sched, mixT, g, lng, lnb, xs, st, ag):
        P, ps = self.P, self.ps
        banks = {0: (0, 1), 1: (5, 6)}
        for half in range(2):
            wi = sched.get()
            for tt in range(2):
                bk = banks[tt][half]
                for kc in range(8):
                    P.op("pe", lambda e, tt=tt, kc=kc, bk=bk, wi=wi: e.matmul(
                        ps[:, bk, :], lhsT=mixT[:, kc, tt * 128:(tt + 1) * 128], rhs=self.wblk[wi][:, kc, :],
                        start=(kc == 0), stop=(kc == 7)), R=["mixT", f"wblk{wi}"], W=[f"ps{bk}"])
        P.cp("op_mm")
        for tt in range(2):
            t = 2 * g + tt
            b0 = banks[tt][0]
            self.resid_ln(t, self.ps[:, b0:b0 + 2, :], [f"ps{b0}", f"ps{b0 + 1}"], lng, lnb, xs, st, ag)

    def resid_ln(self, t, y_ap, y_keys, lng, lnb, xs, st, ag):
        P = self.P
        Xt = self.X[:, t, :]
        wk = Xt if xs is None else xs[:]
        kk = f"X{t}" if xs is None else "xs"
        P.op("dve", lambda e: e.scalar_tensor_tensor(
            out=wk.rearrange("p (b c) -> p b c", b=2), in0=Xt.rearrange("p (b c) -> p b c", b=2),
            scalar=ALPHA, in1=y_ap, op0=ALU.mult, op1=ALU.add), R=[f"X{t}"] + y_keys, W=[kk])
        P.cp("ln_a")
        for hf in range(2):
            P.op("dve", lambda e, hf=hf: e.bn_stats(out=st[:, hf, :], in_=wk[:, hf * 512:(hf + 1) * 512]),
                 R=[kk], W=[f"st{hf}"])
        P.op("dve", lambda e: e.bn_aggr(out=ag[:, 0:2], in_=st[:].rearrange("p a b -> p (a b)")),
             R=["st0", "st1"], W=["ag"])
        P.cp("ln_b")
        P.op("act", lambda e: e.activation(out=ag[:, 2:3], in_=ag[:, 1:2], func=AF.Sqrt, bias=EPS),
             R=["ag"], W=["ag2"])
        P.op("dve", lambda e: e.reciprocal(out=ag[:, 3:4], in_=ag[:, 2:3]), R=["ag2"], W=["ag3"])
        P.op("dve", lambda e: e.tensor_scalar(out=wk, in0=wk, scalar1=ag[:, 0:1], scalar2=ag[:, 3:4],
                                              op0=ALU.subtract, op1=ALU.mult), R=[kk, "ag", "ag3"], W=[kk])
        P.cp("ln_c")
        P.op("dve", lambda e: e.tensor_tensor(out=wk, in0=wk, in1=lng[:], op=ALU.mult), R=[kk, "lng"], W=[kk])
        P.op("dve", lambda e: e.tensor_tensor(out=Xt, in0=wk, in1=lnb[:], op=ALU.add), R=[kk, "lnb"], W=[f"X{t}"])

    def l0_mixer(self, npre=NG, nmain=NG):
        nc, P, ps, sb = self.nc, self.P, self.ps, self.sb
        ident = self.ident
        with ExitStack() as es:
            d_w0 = self.inp("w0", [NB0, 128, 4096])
            w0b = self.scratch("w0b", [NB0, 128, 4096], BF16)
            for b in range(NB0):
                P.dma("pool", w0b[b].rearrange("p (a c) -> p a c", c=2048), d_w0[b].rearrange("p (a c) -> p a c", c=2048), W=[f"w0b{b}"])
            P.barrier()
            d_xpre = self.inp("x_pre", [T, D])
            d_bgi = self.inp("bgi", [4, 1]); d_bgf = self.inp("bgf", [4, 1])
            d_ng = self.inp("norm_g", [768]); d_lng = self.inp("ln1g0", [D]); d_lnb = self.inp("ln1b0", [D])
            kxT, vxaug = self.xattn_setup(0, es)
            S = lambda n, sh, dt=F32: sb(n, sh, dt, es=es)
            self.wblk = [S("wblk0_m0", [128, 8, 512], BF16), S("wblk1_m0", [128, 8, 512], BF16)]
            xpre = S("xpre", [128, 2, D]); xTg = S("xTg", [128, 8, G], BF16)
            qT = S("qT", [128, 8, G], BF16); kT = S("kT", [128, 8, G], BF16); xqT = S("xqT", [64, 4, G], BF16)
            k_tm = S("k_tm", [64, 4, 768], BF16); vaug = S("vaug", [64, 4, 4, 193], BF16); so = S("so", [64, 4, 768], BF16)
            g_tm = S("g_tm", [64, 4, 8])
            bgi = S("bgi_s", [4, 1]); nbgf = S("nbgf", [4, 1]); zer = S("zer", [4, G])
            ig = S("ig", [4, G]); lsp = S("lsp", [4, G]); Lc = S("Lc", [4, G]); gam = S("gam", [4, G]); Mx = S("Mx", [4, G])
            t1 = S("t1", [4, G]); bet = S("bet", [4, G]); alp = S("alp", [4, G]); flo = S("flo", [4, G])
            Lcar = S("Lcar", [4, 1]); Mcar = S("Mcar", [4, 1]); Mprev = S("Mprev", [4, 4]); dec = S("dec", [4, 4])
            sel = S("sel", [4, 4, 128]); gtm = S("gtm", [64, 4, 3, 4]); decb = S("decb", [128, 4, 4])
            CA = S("CA", [128, 4, 193]); CB = S("CB", [64, 4, 193]); CAb = S("CAb", [128, 4, 193], BF16); CBb = S("CBb", [64, 4, 193], BF16)
            SmT = S("SmT", [64, 4, 64], BF16); num = S("num", [64, 4, 193]); hr = S("hr", [64, 4, 192]); sq = S("sq", [64, 4, 192])
            sm = S("sm", [64, 8, 4]); mix = [S("mix0", [64, D], BF16), S("mix1", [64, D], BF16)]
            mixT = S("mixT", [128, 8, G], BF16); Esb = S("Esb", [128, 8, 64], BF16); rsb = S("rsb", [64, 4])
            ngb = S("ngb", [64, 768]); lng = S("lng", [128, D]); lnb = S("lnb", [128, D])
            xs = S("xs", [128, D]); st = S("st", [128, 2, 6]); ag = S("ag", [128, 4])
            P.dma("sp", bgi[:], d_bgi, W=["bgi"]); P.dma("sp", nbgf[:], d_bgf, W=["nbgf"])
            P.dma("sp", ngb[:], d_ng.partition_broadcast(64), W=["ngb"])
            P.dma("sp", lng[:], d_lng.partition_broadcast(128), W=["lng"])
            P.dma("sp", lnb[:], d_lnb.partition_broadcast(128), W=["lnb"])
            P.op("dve", lambda e: e.tensor_scalar(out=nbgf[:], in0=nbgf[:], scalar1=-1.0, scalar2=None, op0=ALU.mult), R=["nbgf"], W=["nbgf"])
            for t_, k_ in ((zer, "zer"), (Lcar, "Lcar"), (Mcar, "Mcar"), (CA, "CA"), (CB, "CB")):
                P.op("pool", lambda e, t_=t_: e.memset(t_[:], 0.0), W=[k_])
            for h in range(4):
                P.op("dve", lambda e, h=h: e.tensor_copy(out=sel[:, h, :], in_=ident[0:4, h:h + 1].broadcast_to([4, 128])),
                     R=["ident"], W=["sel"])
            srcs = []
            for g in range(npre):
                srcs += [(w0b[b], f"w0b{b}") for b in (11, 5, 6, 7, 8)]
            for g in range(nmain):
                srcs += [(w0b[b], f"w0b{b}") for b in (11, 0, 1, 2, 3, 4, 5, 6, 7, 8, 9, 10, 12, 13)]
            sched = self.Sched(self, srcs)
            KS = float(DH ** -0.5)
            mixi = 0

            def tm_proj(wi, ck, ncols):
                bk = 2 + (ck % 2)
                for kc in range(8):
                    P.op("pe", lambda e, kc=kc: e.matmul(
                        ps[0:64, bk, 0:ncols], lhsT=xTg[:, kc, ck * 64:(ck + 1) * 64], rhs=self.wblk[wi][:, kc, 0:ncols],
                        start=(kc == 0), stop=(kc == 7)), R=["xTg", f"wblk{wi}"], W=[f"ps{bk}"])
                return bk

            for phase in (0, 1):
                for g in range(nmain if phase else npre):
                    main = phase == 1
                    if main:
                        srct = [self.X[:, 2 * g + tt, :] for tt in range(2)]; srck = [f"X{2 * g + tt}" for tt in range(2)]
                    else:
                        P.dma("sp", xpre[:], d_xpre[g * G:(g + 1) * G, :].rearrange("(t p) d -> p t d", p=128), W=["xpre0", "xpre1"])
                        srct = [xpre[:, tt, :] for tt in range(2)]; srck = ["xpre0", "xpre1"]
                    self.make_xT(xTg, srct, srck, "xTg")
                    wi = sched.get()
                    for ck in range(4):
                        bk = tm_proj(wi, ck, 8)
                        P.op("act", lambda e, ck=ck, bk=bk: e.copy(out=g_tm[:, ck, :], in_=ps[0:64, bk, 0:8]), R=[f"ps{bk}"], W=["g_tm"])
                    for ck in range(4):
                        for j in range(2):
                            P.op("pe", lambda e, ck=ck, j=j: e.transpose(
                                out=ps[0:4, 4, j * 256 + ck * 64: j * 256 + (ck + 1) * 64],
                                in_=g_tm[:, ck, 4 * j:4 * j + 4], identity=ident[0:64, 0:64]), R=["g_tm", "ident"], W=["ps4"])
                    if main:
                        P.op("dve", lambda e: e.tensor_scalar(out=ig[:], in0=ps[0:4, 4, 0:256], scalar1=bgi[:, 0:1], scalar2=None, op0=ALU.add),
                             R=["ps4", "bgi"], W=["ig"])
                    else:
                        P.op("dve", lambda e: e.tensor_scalar(out=ig[:], in0=ps[0:4, 4, 0:256], scalar1=bgi[:, 0:1], scalar2=self.flag[0:4, 0:1],
                                                              op0=ALU.add, op1=ALU.mult), R=["ps4", "bgi", "flag"], W=["ig"])
                    P.op("act", lambda e: e.activation(out=t1[:], in_=ps[0:4, 4, 256:512], func=AF.Exp, bias=nbgf[:, 0:1], scale=-1.0),
                         R=["ps4", "nbgf"], W=["t1"])
                    P.op("act", lambda e: e.activation(out=lsp[:], in_=t1[:], func=AF.Ln, bias=1.0), R=["t1"], W=["lsp"])
                    if not main:
                        P.op("dve", lambda e: e.tensor_scalar(out=lsp[:], in0=lsp[:], scalar1=self.flag[0:4, 0:1], scalar2=None, op0=ALU.mult),
                             R=["lsp", "flag"], W=["lsp"])
                    P.op("dve", lambda e: e.tensor_tensor_scan(out=Lc[:], data0=lsp[:], data1=zer[:], initial=Lcar[:, 0:1], op0=ALU.add, op1=ALU.add),
                         R=["lsp", "zer", "Lcar"], W=["Lc"])
                    P.op("dve", lambda e: e.tensor_tensor(out=gam[:], in0=ig[:], in1=Lc[:], op=ALU.add), R=["ig", "Lc"], W=["gam"])
                    P.op("dve", lambda e: e.tensor_tensor_scan(out=Mx[:], data0=gam[:], data1=gam[:], initial=Mcar[:, 0:1], op0=ALU.max, op1=ALU.max),
                         R=["gam", "Mcar"], W=["Mx"])
                    Mend = Mx[:].rearrange("p (c l) -> p c l", l=64)[:, :, 63]
                    P.op("dve", lambda e: e.tensor_copy(out=Mprev[:, 0:1], in_=Mcar[:, 0:1]), R=["Mcar"], W=["Mprev"])
                    P.op("dve", lambda e: e.tensor_copy(out=Mprev[:, 1:4], in_=Mend[:, 0:3]), R=["Mx"], W=["Mprev"])
                    P.op("dve", lambda e: e.tensor_copy(out=Mcar[:, 0:1], in_=Mx[:, G - 1:G]), R=["Mx", "Mprev"], W=["Mcar"])
                    P.op("dve", lambda e: e.tensor_copy(out=Lcar[:, 0:1], in_=Lc[:, G - 1:G]), R=["Lc"], W=["Lcar"])
                    P.op("dve", lambda e: e.tensor_tensor(out=dec[:], in0=Mprev[:], in1=Mend, op=ALU.subtract), R=["Mprev", "Mx"], W=["dec"])
                    P.op("act", lambda e: e.activation(out=dec[:], in_=dec[:], func=AF.Exp), R=["dec"], W=["dec"])
                    Mend_bc = Mend.unsqueeze(2).broadcast_to([4, 4, 64])
                    v3 = lambda t_: t_[:].rearrange("p (c l) -> p c l", l=64)
                    P.op("dve", lambda e: e.tensor_tensor(out=v3(bet), in0=v3(gam), in1=Mend_bc, op=ALU.subtract), R=["gam", "Mx"], W=["bet"])
                    P.op("act", lambda e: e.activation(out=bet[:], in_=bet[:], func=AF.Exp), R=["bet"], W=["bet"])
                    if main:
                        P.op("dve", lambda e: e.tensor_tensor(out=v3(alp), in0=Mend_bc, in1=v3(Mx), op=ALU.subtract), R=["Mx"], W=["alp"])
                        P.op("act", lambda e: e.activation(out=alp[:], in_=alp[:], func=AF.Exp), R=["alp"], W=["alp"])
                        P.op("dve", lambda e: e.tensor_tensor(out=flo[:], in0=Lc[:], in1=Mx[:], op=ALU.subtract), R=["Lc", "Mx"], W=["flo"])
                        P.op("act", lambda e: e.activation(out=flo[:], in_=flo[:], func=AF.Exp), R=["flo"], W=["flo"])
                    qs = (bet, alp, flo) if main else (bet,)
                    for ck in range(4):
                        for qi, qt in enumerate(qs):
                            P.op("pe", lambda e, ck=ck, qi=qi, qt=qt: e.transpose(
                                out=ps[0:64, 4, ck * 12 + qi * 4: ck * 12 + qi * 4 + 4], in_=qt[0:4, ck * 64:(ck + 1) * 64],
                                identity=ident[0:4, 0:4]), R=[("bet", "alp", "flo")[qi], "ident"], W=["ps4"])
                    if main:
                        P.op("act", lambda e: e.copy(out=gtm[:].rearrange("p c q h -> p (c q h)"), in_=ps[0:64, 4, 0:48]), R=["ps4"], W=["gtm"])
                    else:
                        P.op("act", lambda e: e.copy(out=gtm[:, :, 0, :], in_=ps[0:64, 4, 0:48].rearrange("p (c q h) -> p c q h", q=3, h=4)[:, :, 0, :]),
                             R=["ps4"], W=["gtm"])
                    for h in range(4):
                        P.op("pe", lambda e, h=h: e.matmul(
                            ps[:, 4, 64:80].rearrange("p (c h) -> p c h", h=4)[:, :, h], lhsT=sel[0:4, h, :], rhs=dec[0:4, 0:4],
                            start=True, stop=True), R=["sel", "dec", "gtm"], W=["ps4"])
                    P.op("act", lambda e: e.copy(out=decb[:].rearrange("p c h -> p (c h)"), in_=ps[:, 4, 64:80]), R=["ps4"], W=["decb"])
                    if main:
                        for blk, dst, scl in ((0, qT, 1.0), (1, qT, 1.0), (2, kT, KS), (3, kT, KS), (4, xqT, 1.0)):
                            wi = sched.get()
                            cw = 128 if blk < 4 else 64
                            for j in range(4):
                                bk = 2 + (j % 2)
                                for kc in range(8):
                                    P.op("pe", lambda e, j=j, kc=kc, bk=bk, wi=wi: e.matmul(
                                        ps[0:cw, bk, 0:G], lhsT=self.wblk[wi][:, kc, j * cw:(j + 1) * cw], rhs=xTg[:, kc, :],
                                        start=(kc == 0), stop=(kc == 7)), R=["xTg", f"wblk{wi}"], W=[f"ps{bk}"])
                                cj = (blk % 2) * 4 + j if blk < 4 else j
                                dk = {0: "qT", 1: "qT", 2: "kT", 3: "kT", 4: "xqT"}[blk]
                                P.op("act", lambda e, dst=dst, cj=cj, bk=bk, scl=scl: e.activation(
                                    out=dst[0:cw, cj, :], in_=ps[0:cw, bk, 0:G], func=AF.Copy, scale=scl), R=[f"ps{bk}"], W=[dk])
                    tms = [(0, "k"), (1, "k"), (0, "v"), (1, "v")] + ([(0, "o"), (1, "o")] if main else [])
                    for hp, kind in tms:
                        wi = sched.get()
                        for ck in range(4):
                            bk = tm_proj(wi, ck, 384)
                            src = ps[0:64, bk, 0:384]
                            if kind == "k":
                                P.op("act", lambda e, ck=ck, hp=hp, src=src: e.activation(
                                    out=k_tm[:, ck, hp * 384:(hp + 1) * 384], in_=src, func=AF.Copy, scale=KS), R=[f"ps{bk}"], W=["k_tm"])
                            elif kind == "o":
                                P.op("act", lambda e, ck=ck, hp=hp, src=src: e.activation(
                                    out=so[:, ck, hp * 384:(hp + 1) * 384], in_=src, func=AF.Sigmoid), R=[f"ps{bk}"], W=["so"])
                            else:
                                P.op("dve", lambda e, ck=ck, hp=hp, src=src: e.tensor_tensor(
                                    out=vaug[:, ck, 2 * hp:2 * hp + 2, 0:192], in0=src.rearrange("p (h d) -> p h d", h=2),
                                    in1=gtm[:, ck, 0, 2 * hp:2 * hp + 2].unsqueeze(2).broadcast_to([64, 2, 192]), op=ALU.mult),
                                    R=[f"ps{bk}", "gtm"], W=["vaug"])
                    for ck in range(4):
                        P.op("pool", lambda e, ck=ck: e.tensor_copy(out=vaug[:, ck, :, 192], in_=gtm[:, ck, 0, :]), R=["gtm"], W=["vaug"])
                    P.cp(f"proj{phase}")
                    Pv = ps[0:64, 5:7, :].rearrange("p b (h e) -> p (b h) e", h=2)

                    def front(ck):
                        c0 = ck * 64
                        dbc = lambda n: decb[0:n, ck, :].unsqueeze(2).broadcast_to([n, 4, 193])
                        P.op("dve", lambda e: e.tensor_tensor(out=CA[:], in0=CA[:], in1=dbc(128), op=ALU.mult), R=["CA", "decb"], W=["CA"])
                        P.op("pool", lambda e: e.tensor_tensor(out=CB[:], in0=CB[:], in1=dbc(64), op=ALU.mult), R=["CB", "decb"], W=["CB"])
                        if main:
                            P.op("act", lambda e: e.copy(out=CAb[:], in_=CA[:]), R=["CA"], W=["CAb"])
                            P.op("act", lambda e: e.copy(out=CBb[:], in_=CB[:]), R=["CB"], W=["CBb"])
                            for h in range(4):
                                P.op("pe", lambda e, h=h: e.matmul(ps[0:64, 4, 256 + h * 64:256 + (h + 1) * 64], lhsT=kT[:, 2 * h, c0:c0 + 64],
                                                                   rhs=qT[:, 2 * h, c0:c0 + 64], start=True, stop=False), R=["kT", "qT"], W=["ps4"])
                                P.op("pe", lambda e, h=h: e.matmul(ps[0:64, 4, 256 + h * 64:256 + (h + 1) * 64], lhsT=kT[0:64, 2 * h + 1, c0:c0 + 64],
                                                                   rhs=qT[0:64, 2 * h + 1, c0:c0 + 64], start=False, stop=True), R=["kT", "qT"], W=["ps4"])
                            P.op("dve", lambda e: e.tensor_tensor(
                                out=SmT[:], in0=ps[0:64, 4, 256:512].rearrange("p (h l) -> p h l", h=4),
                                in1=self.cmask[:].unsqueeze(1).broadcast_to([64, 4, 64]), op=ALU.mult), R=["ps4", "cmask"], W=["SmT"])
                            for h in range(4):
                                bk = 5 + h // 2
                                P.op("pe", lambda e, h=h: e.matmul(Pv[:, h, 0:193], lhsT=qT[:, 2 * h, c0:c0 + 64], rhs=CAb[:, h, :],
                                                                   start=True, stop=False), R=["qT", "CAb"], W=[f"ps{bk}"])
                                P.op("pe", lambda e, h=h: e.matmul(Pv[:, h, 0:193], lhsT=qT[0:64, 2 * h + 1, c0:c0 + 64], rhs=CBb[:, h, :],
                                                                   start=False, stop=False), R=["qT", "CBb"], W=[f"ps{bk}"])
                                P.op("pe", lambda e, h=h: e.matmul(Pv[:, h, 0:193], lhsT=SmT[:, h, :], rhs=vaug[:, ck, h, :],
                                                                   start=False, stop=True), R=["SmT", "vaug"], W=[f"ps{bk}"])
                        P.cp(f"P{phase}")
                        for hp in range(2):
                            dA = ps[:, 7, :].rearrange("p (h e) -> p h e", h=2)
                            dB = ps[0:64, 3, :].rearrange("p (h e) -> p h e", h=2)
                            for hh in range(2):
                                h = 2 * hp + hh
                                P.op("pe", lambda e, h=h, hh=hh: e.matmul(dA[:, hh, 0:193], lhsT=k_tm[:, ck, h * 192:h * 192 + 128], rhs=vaug[:, ck, h, :],
                                                                          start=True, stop=True), R=["k_tm", "vaug"], W=["ps7"])
                                P.op("pe", lambda e, h=h, hh=hh: e.matmul(dB[:, hh, 0:193], lhsT=k_tm[:, ck, h * 192 + 128:(h + 1) * 192], rhs=vaug[:, ck, h, :],
                                                                          start=True, stop=True), R=["k_tm", "vaug"], W=["ps3"])
                            rA = ["CA"] + (["CAb"] if main else [])
                            P.op("dve", lambda e, hp=hp, dA=dA: e.tensor_tensor(out=CA[:, 2 * hp:2 * hp + 2, :], in0=CA[:, 2 * hp:2 * hp + 2, :], in1=dA[:, :, 0:193], op=ALU.add),
                                 R=["CA", "ps7"], W=["CA"])
                            P.op("dve", lambda e, hp=hp, dB=dB: e.tensor_tensor(out=CB[:, 2 * hp:2 * hp + 2, :], in0=CB[:, 2 * hp:2 * hp + 2, :], in1=dB[:, :, 0:193], op=ALU.add),
                                 R=["CB", "ps3"], W=["CB"])

                    def back_a(ck):
                        abc = gtm[:, ck, 1, :].unsqueeze(2).broadcast_to([64, 4, 193])
                        P.op("dve", lambda e: e.tensor_tensor(out=num[:], in0=Pv[:, :, 0:193], in1=abc, op=ALU.mult), R=["ps5", "ps6", "gtm"], W=["num"])

                    def back_b(ck):
                        nonlocal mixi
                        c0 = ck * 64
                        P.op("act", lambda e: e.activation(out=sm[:, 0, :], in_=num[:, :, 192], func=AF.Abs), R=["num"], W=["sm0"])
                        P.op("dve", lambda e: e.tensor_tensor(out=sm[:, 1, :], in0=sm[:, 0, :], in1=gtm[:, ck, 2, :], op=ALU.max), R=["sm0", "gtm"], W=["sm1"])
                        P.op("dve", lambda e: e.reciprocal(out=sm[:, 2, :], in_=sm[:, 1, :]), R=["sm1"], W=["sm2"])
                        P.op("dve", lambda e: e.tensor_tensor(out=hr[:], in0=num[:, :, 0:192], in1=sm[:, 2, :].unsqueeze(2).broadcast_to([64, 4, 192]), op=ALU.mult),
                             R=["num", "sm2"], W=["hr"])
                        P.cp("hr")
                        P.op("dve", lambda e: e.tensor_reduce(out=sm[:, 3, :], in_=hr[:], axis=AX.X, op=ALU.add), R=["hr"], W=["sm3"])
                        P.op("pool", lambda e: e.tensor_tensor(out=sq[:], in0=hr[:], in1=hr[:], op=ALU.mult), R=["hr"], W=["sq"])
                        P.op("dve", lambda e: e.tensor_reduce(out=sm[:, 4, :], in_=sq[:], axis=AX.X, op=ALU.add), R=["sq"], W=["sm4"])
                        P.op("dve", lambda e: e.tensor_scalar(out=sm[:, 3, :], in0=sm[:, 3, :], scalar1=1.0 / DH, scalar2=None, op0=ALU.mult), R=["sm3"], W=["sm3"])
                        P.op("dve", lambda e: e.tensor_tensor(out=sm[:, 5, :], in0=sm[:, 3, :], in1=sm[:, 3, :], op=ALU.mult), R=["sm3"], W=["sm5"])
                        P.op("dve", lambda e: e.scalar_tensor_tensor(out=sm[:, 6, :], in0=sm[:, 4, :], scalar=1.0 / DH, in1=sm[:, 5, :], op0=ALU.mult, op1=ALU.subtract),
                             R=["sm4", "sm5"], W=["sm6"])
                        P.op("act", lambda e: e.activation(out=sm[:, 6, :], in_=sm[:, 6, :], func=AF.Sqrt, bias=EPS), R=["sm6"], W=["sm6"])
                        P.op("dve", lambda e: e.reciprocal(out=sm[:, 7, :], in_=sm[:, 6, :]), R=["sm6"], W=["sm7"])
                        P.op("dve", lambda e: e.tensor_tensor(out=hr[:], in0=hr[:], in1=sm[:, 3, :].unsqueeze(2).broadcast_to([64, 4, 192]), op=ALU.subtract),
                             R=["hr", "sm3"], W=["hr"])
                        P.op("dve", lambda e: e.tensor_tensor(out=hr[:], in0=hr[:], in1=sm[:, 7, :].unsqueeze(2).broadcast_to([64, 4, 192]), op=ALU.mult),
                             R=["hr", "sm7"], W=["hr"])
                        hf = hr[:].rearrange("p h d -> p (h d)")
                        P.op("pool", lambda e: e.tensor_tensor(out=hf, in0=hf, in1=ngb[:], op=ALU.mult), R=["hr", "ngb"], W=["hr"])
                        mx_ = mix[mixi]; mk = f"mix{mixi}"; mixi ^= 1
                        P.op("pool", lambda e, mx_=mx_: e.tensor_tensor(out=mx_[:, 0:768], in0=hf, in1=so[:, ck, :], op=ALU.mult), R=["hr", "so"], W=[mk])
                        P.cp("hln")
                        self.xattn_chunk(kxT, vxaug, xqT, c0, 64, mx_[:, 768:1024], mk, Esb, rsb)
                        P.cp("xattn")
                        pb = ps[:, 2, :].bitcast(BF16)
                        for kc in range(8):
                            P.op("pe", lambda e, kc=kc, mx_=mx_: e.transpose(out=pb[:, kc * 64:(kc + 1) * 64], in_=mx_[:, kc * 128:(kc + 1) * 128],
                                                                             identity=self.identb[0:64, 0:64]), R=[mk, "identb"], W=["ps2"])
                        P.op("act", lambda e: e.copy(out=mixT[:, :, c0:c0 + 64], in_=pb[:, 0:512].rearrange("p (k t) -> p k t", t=64)), R=["ps2"], W=["mixT"])


                    if not main:
                        for ck in range(4):
                            front(ck)
                    else:
                        front(0)
                        for ck in range(4):
                            back_a(ck)
                            if ck + 1 < 4:
                                front(ck + 1)
                            back_b(ck)
                    P.cp(f"chunks{phase}")
                    if main:
                        self.outproj_ln(sched, mixT, g, lng, lnb, xs, st, ag)
            P.barrier()

    def finish(self, out_name="out"):
        P = self.P
        d_out = self.outp(out_name, [T, D])
        for q4 in range(4):
            P.dma("sp", d_out[512 * q4:512 * (q4 + 1), :].rearrange("(t p) d -> p t d", p=128),
                  self.X[:, 4 * q4:4 * q4 + 4, :], R=[f"X{t}" for t in range(4 * q4, 4 * q4 + 4)], chan="out")
        P.wait_all("sp")


def _consts():
    ident = np.eye(128, dtype=np.float32)
    cmask = np.triu(np.ones((64, 64), np.float32))
    return ident, cmask


def build(stages, limit=None, stop_at=None, **kw):
    es = ExitStack()
    b = Builder(es)
    b.P.limit = limit
    b.P.stop_at = stop_at
    b.setup()
    if "fused" in stages:
        P = b.P
        b.l0_mixer()
        b.peer(0)
        cc1src = b.scratch("cc1src", [4, D], F32); cc1dst = b.scratch("cc1dst", [8, D], F32)
        b.cc2src = b.scratch("cc2src", [RB, 8], F32); cc2dst = b.scratch("cc2dst", [2 * RB, 8], F32)
        P.dma("sp", cc1src[0:3], b.X[125:128, NT - 1, :], R=[f"X{NT - 1}"], W=["cc1src"])
        P.dma("sp", cc1src[3:4], b.X[127:128, NT - 1, :], R=[f"X{NT - 1}"], W=["cc1src"])
        P.barrier()
        P.allgather_pairs(cc1src, cc1dst, R=["cc1src"], W=["cc_halo"])
        P.barrier()
        b.l1(scan_only=True, halo_src=cc1dst[0:3])
        P.allgather_pairs(b.cc2src, cc2dst, R=["cc2src"], W=["cc_hend"])
        P.barrier()
        b.l1(scan_only=False, halo_src=cc1dst[0:3], hinit_src=cc2dst[0:RB])
        b.peer(1)
        b.finish()
        return b, es
    if "l0mix" in stages:
        b.l0_mixer(**{k: v for k, v in kw.items() if k in ("npre", "nmain")})
    if "peer0" in stages:
        b.peer(0, **{k: v for k, v in kw.items() if k == "ngroups"})
    if "l1scan" in stages:
        b.l1(scan_only=True)
    if "l1mix" in stages:
        b.l1(scan_only=False, **{k: v for k, v in kw.items() if k == "ngroups"})
    if "peer1" in stages:
        b.peer(1, **{k: v for k, v in kw.items() if k == "ngroups"})
    b.finish()
    return b, es


def core_inputs(inputs, stages, x_cur=None, hinit=None, halo=None):
    ident, cmask = _consts()
    x = inputs["x"] if x_cur is None else x_cur
    maps = []
    shared = {"ident": ident, "cmask": cmask}
    if "fused" in stages:
        stages = ["fused", "l0mix", "peer0", "l1scan", "l1mix", "peer1"]
    if "l0mix" in stages:
        shared["w0"] = prep_l0_weights(inputs["mlstm_w_in"][0], inputs["w_out"][0]).reshape(NB0, 128, 4096)
        shared["bgi"] = np.ascontiguousarray(inputs["mlstm_b_gates"][0, 0:4].reshape(4, 1))
        shared["bgf"] = np.ascontiguousarray(inputs["mlstm_b_gates"][0, 4:8].reshape(4, 1))
        shared["norm_g"] = np.ascontiguousarray(inputs["mlstm_norm_g"][0])
        shared["ln1g0"] = np.ascontiguousarray(inputs["ln1_g"][0]); shared["ln1b0"] = np.ascontiguousarray(inputs["ln1_b"][0])
        shared["wkv0"] = _blockify(np.ascontiguousarray(inputs["xattn_w_kv"][0]))
    for l in (0, 1):
        if f"peer{l}" in stages:
            ut, wqb, skT = prep_peer(inputs["peer_u"][l], inputs["peer_w_q"][l], inputs["peer_subkeys"][l])
            shared[f"ut{l}"] = ut; shared[f"wq{l}"] = wqb; shared[f"skT{l}"] = skT
            shared[f"pv{l}"] = np.ascontiguousarray(inputs["peer_v"][l]).reshape(128, 128, 1024)
            shared[f"ln2g{l}"] = np.ascontiguousarray(inputs["ln2_g"][l]); shared[f"ln2b{l}"] = np.ascontiguousarray(inputs["ln2_b"][l])
    if "l1scan" in stages or "l1mix" in stages:
        shared["w1"] = prep_l1_weights(inputs["rglru_w_in"][0], inputs["w_out"][1])
        shared["conv_w"] = np.ascontiguousarray(inputs["rglru_conv_w"][0].reshape(4, 8, 96).transpose(2, 1, 0))
        shared["conv_b"] = _chan(inputs["rglru_conv_b"][0]); shared["rg_ba"] = _chan(inputs["rglru_b_a"][0])
        shared["rg_bx"] = _chan(inputs["rglru_b_x"][0]); shared["rg_lam"] = _chan(inputs["rglru_lam"][0])
        shared["rg_wa"] = np.ascontiguousarray(inputs["rglru_w_a"][0].transpose(1, 0, 2))
        shared["rg_wx"] = np.ascontiguousarray(inputs["rglru_w_x"][0].transpose(1, 0, 2))
        if "l1mix" in stages:
            shared["ln1g1"] = np.ascontiguousarray(inputs["ln1_g"][1]); shared["ln1b1"] = np.ascontiguousarray(inputs["ln1_b"][1])
            shared["wkv1"] = _blockify(np.ascontiguousarray(inputs["xattn_w_kv"][1]))
    for c in range(8):
        bi, half = c // 2, c % 2
        m = dict(shared)
        if "l1scan" in stages or "l1mix" in stages:
            m["xhalo"] = np.ascontiguousarray(x[bi, T - 3:T]) if half else np.zeros((3, D), np.float32)
            m["hinit"] = np.zeros((RB, 8), np.float32) if hinit is None else np.ascontiguousarray(hinit[c])
            m["memT"] = np.ascontiguousarray(inputs["mem"][bi].T.reshape(8, 128, 256).transpose(1, 0, 2))
        m["x_own"] = np.ascontiguousarray(x[bi, half * T:(half + 1) * T])
        m["flag"] = np.full((128, 1), float(half), np.float32)
        if "l0mix" in stages:
            m["x_pre"] = np.ascontiguousarray(x[bi, 0:T]) if half else np.zeros((T, D), np.float32)
            m["memT"] = np.ascontiguousarray(inputs["mem"][bi].T.reshape(8, 128, 256).transpose(1, 0, 2))
        maps.append(m)
    return maps


DBG = {}
def prep_peer(u, wq, sk):
    ut = np.ascontiguousarray(u.reshape(128, 128, 8, 128).transpose(0, 3, 2, 1)).reshape(128, 128, 1024)
    wqb = np.stack([_blockify(np.ascontiguousarray(wq[:, b * 512:(b + 1) * 512])) for b in range(4)]).reshape(4, 128, 4096)
    skT = np.ascontiguousarray(sk.transpose(2, 0, 1))
    return ut, wqb, skT


def _peer_method(self, layer, ngroups=NG):
    nc, P, ps, sb = self.nc, self.P, self.ps, self.sb
    TN = 4
    with ExitStack() as es:
        S = lambda n, sh, dt=F32: sb(f"{n}_p{layer}", sh, dt, es=es)
        self.wblk = [S("wblk0", [128, 8, 512], BF16)]
        d_ut = self.inp(f"ut{layer}", [128, 128, 1024]); d_v = self.inp(f"pv{layer}", [128, 128, 1024])
        d_wq = self.inp(f"wq{layer}", [4, 128, 4096]); d_sk = self.inp(f"skT{layer}", [128, 2, 128])
        d_lng = self.inp(f"ln2g{layer}", [D]); d_lnb = self.inp(f"ln2b{layer}", [D])
        utb = self.scratch(f"utb{layer}", [128, 128, 1024], BF16); vb = self.scratch(f"vb{layer}", [128, 128, 1024], BF16)
        wqb = self.scratch(f"wqb{layer}", [4, 128, 4096], BF16)
        for b in range(4):
            P.dma("pool", wqb[b].rearrange("p (a c) -> p a c", c=2048), d_wq[b].rearrange("p (a c) -> p a c", c=2048), W=[f"wqb{layer}_{b}"])
        for i0 in range(0, 128, 4):
            P.dma("pool", utb[i0:i0 + 4], d_ut[i0:i0 + 4], W=[f"utb{layer}_{i0 // 4}"])
            P.dma("pool", vb[i0:i0 + 4], d_v[i0:i0 + 4], W=[f"vb{layer}_{i0 // 4}"])
        if DBG.get("cast_barrier", True):
            P.barrier()
        GT = S("GT", [128, 128, G], BF16)
        xTg = S("xTg", [128, 8, G], BF16); qT = S("qT", [128, 16, G], BF16)
        ublk = [S(f"ublk{k}", [128, 1024], BF16) for k in range(4)]
        vblk = [S(f"vblk{k}", [128, 1024], BF16) for k in range(4)]
        arA = S("arA", [128, 1024]); arB = S("arB", [128, 1024])
        s_sb = arA[:, 0:512].rearrange("p (j k) -> p j k", k=128); s2 = arA[:, 512:1024].rearrange("p (j k) -> p j k", k=128)
        eq = arA[:, 0:512].rearrange("p (h r a) -> p h r a", r=16, a=16); prod = arA[:, 512:1024].rearrange("p (h r a) -> p h r a", r=16, a=16)
        cand = arB[:, 0:512].rearrange("p (h a b) -> p h a b", a=16, b=16); cand2 = arB[:, 512:1024].rearrange("p (h a b) -> p h a b", a=16, b=16)
        sv = S("sv", [128, 16, 16]); si = S("si", [128, 16, 16], U32); sif = S("sif", [128, 16, 16])
        fv = S("fv", [128, 8, 16]); fp = S("fp", [128, 8, 16], U32); pf = S("pf", [128, 8, 16]); bfl = S("bfl", [128, 8, 16]); af = S("af", [128, 8, 16])
        ex = S("ex", [128, 8, 16]); zs = S("zs", [128, 8]); tri = S("tri", [128, 3, 128])
        hr3 = S("hr3", [128, 3, G], BF16)
        Bt = [S("Bt0", [128, TN, 128], BF16), S("Bt1", [128, TN, 128], BF16)]
        At = [S("At0", [128, TN, 128], BF16), S("At1", [128, TN, 128], BF16)]
        eqt = S("eqt", [128, TN, 128], BF16)
        ge = [S("ge0", [128, G], BF16), S("ge1", [128, G], BF16)]; coef = [S("coef0", [128, G], BF16), S("coef1", [128, G], BF16)]
        skT32 = S("skT32", [128, 2, 128]); skTb = S("skTb", [128, 2, 128], BF16)
        iot32 = S("iot32", [128, 128]); iot = S("iot", [128, 128], BF16); iot16 = S("iot16", [128, 16])
        lng = S("lng", [128, D]); lnb = S("lnb", [128, D]); st = S("st", [128, 2, 6]); ag = S("ag", [128, 4])
        P.dma("sp", skT32[:], d_sk, W=["skT32"])
        P.op("dve", lambda e: e.tensor_copy(out=skTb[:], in_=skT32[:]), R=["skT32"], W=["skTb"])
        P.op("pool", lambda e: e.iota(iot32[:], pattern=[[1, 128]], base=0, channel_multiplier=0, allow_small_or_imprecise_dtypes=True), W=["iot32"])
        P.op("dve", lambda e: e.tensor_copy(out=iot[:], in_=iot32[:]), R=["iot32"], W=["iot"])
        P.op("pool", lambda e: e.iota(iot16[:], pattern=[[1, 16]], base=0, channel_multiplier=0, allow_small_or_imprecise_dtypes=True), W=["iot16"])
        P.dma("sp", lng[:], d_lng.partition_broadcast(128), W=["lng"])
        P.dma("sp", lnb[:], d_lnb.partition_broadcast(128), W=["lnb"])
        def load_tb(i):
            b = i % 4
            P.dma("sp", ublk[b][:], utb[i], R=[f"utb{layer}_{i // 4}"], W=[f"ublk{b}"])
            P.dma("sp", vblk[b][:], vb[i], R=[f"vb{layer}_{i // 4}"], W=[f"vblk{b}"])

        xTgs = [xTg, S("xTgB", [128, 8, G], BF16)]

        def phaseA1(g):
                xTg = xTgs[g % 2]; xk = f"xTg{g % 2}"
                self.make_xT(xTg, [self.X[:, 2 * g + tt, :] for tt in range(2)], [f"X{2 * g + tt}" for tt in range(2)], xk)
                sched = self.Sched(self, [(wqb[b], f"wqb{layer}_{b}") for b in range(4)])
                for blk in range(4):
                    wi = sched.get()
                    for j in range(4):
                        bk = 4 + (j % 2)
                        for kc in range(8):
                            P.op("pe", lambda e, j=j, kc=kc, bk=bk, wi=wi: e.matmul(
                                ps[:, bk, 0:G], lhsT=self.wblk[wi][:, kc, j * 128:(j + 1) * 128], rhs=xTg[:, kc, :],
                                start=(kc == 0), stop=(kc == 7)), R=[xk, f"wblk{wi}"], W=[f"ps{bk}"])
                        P.op("act", lambda e, j=j, bk=bk, blk=blk: e.copy(out=qT[:, blk * 4 + j, :], in_=ps[:, bk, 0:G]), R=[f"ps{bk}"], W=["qT"])

        def phaseA2(g):
            xTg = xTgs[g % 2]
            for tt in range(2):
                tok = slice(tt * 128, (tt + 1) * 128)
                for hh in range(4):
                    bk = 6 + (hh % 2)
                    for j4 in range(4):
                        jj = hh * 4 + j4
                        P.op("pe", lambda e, jj=jj, j4=j4, bk=bk: e.matmul(
                            ps[:, bk, j4 * 128:(j4 + 1) * 128], lhsT=qT[:, jj, tok], rhs=skTb[:, jj % 2, :],
                            start=True, stop=True), R=["qT", "skTb"], W=[f"ps{bk}"])
                    P.op("act", lambda e, bk=bk: e.copy(out=s_sb, in_=ps[:, bk, :].rearrange("p (j k) -> p j k", k=128)),
                         R=[f"ps{bk}"], W=["arA"])
                    yield
                    for j4 in range(4):
                        jj = hh * 4 + j4
                        P.op("dve", lambda e, jj=jj, j4=j4: e.max(out=sv[:, jj, 0:8], in_=s_sb[:, j4, :]), R=["arA"], W=["sv"])
                        P.op("dve", lambda e, jj=jj, j4=j4: e.max_index(out=si[:, jj, 0:8], in_max=sv[:, jj, 0:8], in_values=s_sb[:, j4, :]), R=["arA", "sv"], W=["si"])
                        P.op("dve", lambda e, jj=jj, j4=j4: e.match_replace(out=s2[:, j4, :], in_to_replace=sv[:, jj, 0:8], in_values=s_sb[:, j4, :], imm_value=-1e30),
                             R=["arA", "sv"], W=["arA"])
                        P.op("dve", lambda e, jj=jj, j4=j4: e.max(out=sv[:, jj, 8:16], in_=s2[:, j4, :]), R=["arA"], W=["sv"])
                        P.op("dve", lambda e, jj=jj, j4=j4: e.max_index(out=si[:, jj, 8:16], in_max=sv[:, jj, 8:16], in_values=s2[:, j4, :]), R=["arA", "sv"], W=["si"])
                        yield
                P.op("dve", lambda e: e.tensor_copy(out=sif[:], in_=si[:]), R=["si"], W=["sif"])
                svv = sv[:].rearrange("p (h two) a -> p h two a", two=2)
                sfv = sif[:].rearrange("p (h two) a -> p h two a", two=2)
                for hq in range(4):
                    hs = slice(2 * hq, 2 * hq + 2)
                    P.op("dve", lambda e: e.tensor_tensor(out=cand, in0=svv[:, hs, 0, :].unsqueeze(3).broadcast_to([128, 2, 16, 16]),
                                                          in1=svv[:, hs, 1, :].unsqueeze(2).broadcast_to([128, 2, 16, 16]), op=ALU.add), R=["sv"], W=["arB"])
                    for h4 in range(2):
                        h = 2 * hq + h4
                        c1 = cand[:, h4].rearrange("p a b -> p (a b)"); c2 = cand2[:, h4].rearrange("p a b -> p (a b)")
                        P.op("dve", lambda e: e.max(out=fv[:, h, 0:8], in_=c1), R=["arB"], W=["fv"])
                        P.op("dve", lambda e: e.max_index(out=fp[:, h, 0:8], in_max=fv[:, h, 0:8], in_values=c1), R=["arB", "fv"], W=["fp"])
                        P.op("dve", lambda e: e.match_replace(out=c2, in_to_replace=fv[:, h, 0:8], in_values=c1, imm_value=-1e30), R=["arB", "fv"], W=["arB"])
                        P.op("dve", lambda e: e.max(out=fv[:, h, 8:16], in_=c2), R=["arB"], W=["fv"])
                        P.op("dve", lambda e: e.max_index(out=fp[:, h, 8:16], in_max=fv[:, h, 8:16], in_values=c2), R=["arB", "fv"], W=["fp"])
                        yield
                gwv = tri[:, 2, :].rearrange("p (h r) -> p h r", r=16)
                P.op("dve", lambda e: e.tensor_tensor(out=ex[:], in0=fv[:], in1=fv[:, :, 0:1].broadcast_to([128, 8, 16]), op=ALU.subtract), R=["fv"], W=["ex"])
                P.op("act", lambda e: e.activation(out=ex[:], in_=ex[:], func=AF.Exp), R=["ex"], W=["ex"])
                P.op("dve", lambda e: e.tensor_reduce(out=zs[:], in_=ex[:], axis=AX.X, op=ALU.add), R=["ex"], W=["zs"])
                P.op("dve", lambda e: e.reciprocal(out=zs[:], in_=zs[:]), R=["zs"], W=["zs"])
                P.op("dve", lambda e: e.tensor_tensor(out=gwv, in0=ex[:], in1=zs[:].unsqueeze(2).broadcast_to([128, 8, 16]), op=ALU.mult), R=["ex", "zs"], W=["tri2"])
                yield
                P.op("dve", lambda e: e.tensor_single_scalar(out=pf[:].bitcast(U32), in_=fp[:], scalar=15, op=ALU.bitwise_and), R=["fp"], W=["pf"])
                P.op("dve", lambda e: e.tensor_copy(out=bfl[:], in_=pf[:].bitcast(U32)), R=["pf"], W=["bfl"])
                P.op("dve", lambda e: e.tensor_single_scalar(out=pf[:].bitcast(U32), in_=fp[:], scalar=4, op=ALU.logical_shift_right), R=["fp", "bfl"], W=["pf"])
                P.op("dve", lambda e: e.tensor_copy(out=af[:], in_=pf[:].bitcast(U32)), R=["pf"], W=["af"])
                yield
                i16 = iot16[:].unsqueeze(1).unsqueeze(1).broadcast_to([128, 2, 16, 16])
                for which, (srcidx, two) in enumerate(((af, 0), (bfl, 1))):
                    ov = tri[:, which, :].rearrange("p (h r) -> p h r", r=16)
                    for hq in range(4):
                        hs = slice(2 * hq, 2 * hq + 2)
                        P.op("dve", lambda e: e.tensor_tensor(out=eq, in0=srcidx[:, hs, :].unsqueeze(3).broadcast_to([128, 2, 16, 16]), in1=i16, op=ALU.is_equal),
                             R=["af", "bfl", "iot16"], W=["arA"])
                        P.op("dve", lambda e: e.tensor_tensor(out=prod, in0=eq, in1=sfv[:, hs, two, :].unsqueeze(2).broadcast_to([128, 2, 16, 16]), op=ALU.mult),
                             R=["arA", "sif"], W=["arA"])
                        P.op("dve", lambda e: e.tensor_reduce(out=ov[:, hs, :], in_=prod, axis=AX.X, op=ALU.add), R=["arA"], W=[f"tri{which}"])
                        yield
                for q3 in range(3):
                    P.op("pe", lambda e, q3=q3: e.transpose(out=ps[:, 7, q3 * 128:(q3 + 1) * 128], in_=tri[:, q3, :], identity=self.ident[:]),
                         R=[f"tri{q3}", "ident"], W=["ps7"])
                P.op("act", lambda e: e.copy(out=hr3[:, :, tok], in_=ps[:, 7, 0:384].rearrange("p (q t) -> p q t", q=3)), R=["ps7"], W=["hr3"])

            yield

        phaseA1(0)
        for _ in phaseA2(0):
            pass
        for g in range(ngroups):
            xTg = xTgs[g % 2]; xk = f"xTg{g % 2}"
            P.cp(f"peerA{layer}")
            ib = iot[:].unsqueeze(1).broadcast_to([128, TN, 128])
            for tb in range(G // TN):
                t0 = tb * TN
                bi = tb % 2
                P.op("dve", lambda e: e.tensor_tensor(out=Bt[bi][:], in0=ib, in1=hr3[:, 1, t0:t0 + TN].unsqueeze(2).broadcast_to([128, TN, 128]), op=ALU.is_equal),
                     R=["iot", "hr3"], W=[f"Bt{bi}"])
                P.op("dve", lambda e: e.tensor_tensor(out=eqt[:], in0=ib, in1=hr3[:, 0, t0:t0 + TN].unsqueeze(2).broadcast_to([128, TN, 128]), op=ALU.is_equal),
                     R=["iot", "hr3"], W=["eqt"])
                P.op(DBG.get("at_eng", "dve"), lambda e: e.tensor_tensor(out=At[bi][:], in0=eqt[:], in1=hr3[:, 2, t0:t0 + TN].unsqueeze(2).broadcast_to([128, TN, 128]), op=ALU.mult),
                     R=["eqt", "hr3"], W=[f"At{bi}"])
                for t in range(TN):
                    tg = t0 + t
                    bk = (tg // 4) % 8
                    P.op("pe", lambda e, t=t, tg=tg, bk=bk: e.matmul(ps[:, bk, (tg % 4) * 128:(tg % 4 + 1) * 128], lhsT=Bt[bi][:, t, :], rhs=At[bi][:, t, :],
                                                                     start=True, stop=True), R=[f"Bt{bi}", f"At{bi}"], W=[f"ps{bk}"])
                if (t0 + TN) % 16 == 0 and not DBG.get("noevac"):
                    b0 = ((t0 + TN - 16) // 4) % 8
                    tq = t0 + TN - 16
                    P.op("act", lambda e: e.copy(out=GT[:, :, tq:tq + 16], in_=ps[:, b0:b0 + 4, :].rearrange("p b (t i) -> p i (b t)", t=4)),
                         R=[f"ps{b}" for b in range(b0, b0 + 4)], W=["GT"])
            P.cp(f"peerB{layer}")
            def emit_ht(i):
                hb = 4 + (i % 2)
                for kc in range(8):
                    P.op("pe", lambda e, kc=kc: e.matmul(ps[:, hb, 0:G], lhsT=ublk[i % 4][:, kc * 128:(kc + 1) * 128], rhs=xTg[:, kc, :],
                                                         start=(kc == 0), stop=(kc == 7)), R=[f"ublk{i % 4}", xk], W=[f"ps{hb}"])

            def emit_rest(i):
                hb = 4 + (i % 2)
                gi = i % 2
                P.op("act", lambda e: e.activation(out=ge[gi][:], in_=ps[:, hb, 0:G], func=AF.Gelu), R=[f"ps{hb}"], W=[f"ge{gi}"])
                P.op("dve", lambda e: e.tensor_tensor(out=coef[gi][:], in0=ge[gi][:], in1=GT[:, i, :], op=ALU.mult), R=[f"ge{gi}", "GT"], W=[f"coef{gi}"])
                for tt in range(2):
                    for half in range(2):
                        yb = 2 * tt + half
                        P.op("pe", lambda e, tt=tt, half=half, yb=yb: e.matmul(
                            ps[:, yb, :], lhsT=coef[gi][:, tt * 128:(tt + 1) * 128], rhs=vblk[i % 4][:, half * 512:(half + 1) * 512],
                            start=(i == 0), stop=(i == 127)), R=[f"coef{gi}", f"vblk{i % 4}"], W=[f"ps{yb}"])

            genA = None
            if g + 1 < ngroups:
                phaseA1(g + 1)
                genA = phaseA2(g + 1)
            for k in range(3):
                load_tb(k)
            emit_ht(0)
            for i in range(128):
                if i + 1 < 128:
                    emit_ht(i + 1)
                emit_rest(i)
                if i + 3 < 128:
                    load_tb(i + 3)
                if genA is not None and i >= 4:
                    next(genA, None)
            if genA is not None:
                for _ in genA:
                    pass
            P.cp(f"peerC{layer}")
            for tt in range(2):
                self.resid_ln(2 * g + tt, ps[:, 2 * tt:2 * tt + 2, :], [f"ps{2 * tt}", f"ps{2 * tt + 1}"], lng, lnb, None, st, ag)
        P.barrier()


Builder.peer = _peer_method


NB1 = 9
RB = 96


def prep_l1_weights(w_in, w_out):
    gate = w_in[:, 0:768]; xr = w_in[:, 768:1536]; xq = w_in[:, 1536:1792]
    blocks = []
    for src in (gate, xr):
        for hb in range(2):
            blocks.append(np.concatenate([_pad_cols(src[:, g * 96:(g + 1) * 96], 128) for g in range(4 * hb, 4 * hb + 4)], axis=1))
    blocks.append(_pad_cols(xq, 512))
    out = [_blockify(np.ascontiguousarray(b, dtype=np.float32)).reshape(128, 4096) for b in blocks]
    for cb in range(4):
        blk = np.zeros((128, 10, 256), np.float32)
        for g in range(8):
            blk[0:96, g, :] = w_out[g * 96:(g + 1) * 96, cb * 256:(cb + 1) * 256]
        for c2 in range(2):
            blk[:, 8 + c2, :] = w_out[768 + c2 * 128:768 + (c2 + 1) * 128, cb * 256:(cb + 1) * 256]
        out.append(_pad_cols(blk.reshape(128, 2560), 4096))
    return np.stack(out)


def _chan(v):
    return np.ascontiguousarray(np.asarray(v, np.float32).reshape(8, 96).T)


def _l1_method(self, scan_only=False, ngroups=NG, halo_src=None, hinit_src=None):
    nc, P, ps, sb = self.nc, self.P, self.ps, self.sb
    ident = self.ident
    tag = "s" if scan_only else "m"
    with ExitStack() as es:
        S = lambda n, sh, dt=F32: sb(f"{n}_l1{tag}", sh, dt, es=es)
        if "w1" in self.din:
            d_w1 = self.din["w1"]; w1b = self.w1b
        else:
            d_w1 = self.inp("w1", [NB1, 128, 4096])
            w1b = self.w1b = self.scratch("w1b", [NB1, 128, 4096], BF16)
            for b in range(NB1):
                P.dma("pool", w1b[b].rearrange("p (a c) -> p a c", c=2048), d_w1[b].rearrange("p (a c) -> p a c", c=2048), W=[f"w1b{b}"])
            P.barrier()
        g_in = lambda n, sh: self.din[n] if n in self.din else self.inp(n, sh)
        d_halo = halo_src if halo_src is not None else g_in("xhalo", [3, D])
        d_hinit = hinit_src if hinit_src is not None else (None if (scan_only and halo_src is not None) else g_in("hinit", [RB, 8]))
        d_cw = g_in("conv_w", [RB, 8, 4]); d_cb = g_in("conv_b", [RB, 8]); d_wa = g_in("rg_wa", [RB, 8, RB]); d_wx = g_in("rg_wx", [RB, 8, RB])
        d_ba = g_in("rg_ba", [RB, 8]); d_bx = g_in("rg_bx", [RB, 8]); d_lam = g_in("rg_lam", [RB, 8])
        self.wblk = [S("wblk0", [128, 8, 512], BF16), S("wblk1", [128, 8, 512], BF16)]
        xTg = S("xTg", [128, 8, G], BF16)
        xrb = S("xrb", [RB, 8, 3 + G]); xc = S("xc", [RB, 8, G]); xcb = S("xcb", [RB, 8, G], BF16)
        cw = S("cw", [RB, 8, 4]); cb = S("cb", [RB, 8]); wa32 = S("wa32", [RB, 8, RB]); wx32 = S("wx32", [RB, 8, RB])
        wab = S("wab", [RB, 8, RB], BF16); wxb = S("wxb", [RB, 8, RB], BF16)
        ba = S("ba", [RB, 8]); bx = S("bx", [RB, 8]); lamc = S("lamc", [RB, 8]); hcar = S("hcar", [RB, 8])
        rt = S("rt", [RB, 8, G]); it = S("it", [RB, 8, G]); at = S("at", [RB, 8, G]); ut = S("ut", [RB, 8, G]); ht = S("ht", [RB, 8, G]); tmp = S("tmp", [RB, 8, G])
        xh = S("xh", [3, D]); xTh = S("xTh", [128, 8, 4], BF16)
        for dst, src, k in ((cw, d_cw, "cw"), (cb, d_cb, "cb"), (wa32, d_wa, "wa32"), (wx32, d_wx, "wx32"), (ba, d_ba, "ba"), (bx, d_bx, "bx"),
                            (lamc, d_lam, "lamc"), (xh, d_halo, "xh")):
            P.dma("sp", dst[:], src, R=(["cc_halo"] if (k == "xh" and halo_src is not None) else []), W=[k])
        if d_hinit is None:
            P.op("dve", lambda e: e.memset(hcar[:], 0.0), W=["hcar"] + [f"hcar{k}" for k in range(8)])
        else:
            P.dma("sp", hcar[:], d_hinit, R=(["cc_hend"] if hinit_src is not None else []), W=["hcar"] + [f"hcar{k}" for k in range(8)])
        if halo_src is not None:
            P.op("dve", lambda e: e.tensor_scalar(out=xh[:], in0=xh[:], scalar1=self.flag[0:3, 0:1], scalar2=None, op0=ALU.mult), R=["xh", "flag"], W=["xh"])
        if hinit_src is not None:
            P.op("dve", lambda e: e.tensor_scalar(out=hcar[:], in0=hcar[:], scalar1=self.flag[0:RB, 0:1], scalar2=None, op0=ALU.mult), R=["hcar", "flag"], W=["hcar"] + [f"hcar{k}" for k in range(8)])
        P.op("dve", lambda e: e.tensor_copy(out=wab[:], in_=wa32[:]), R=["wa32"], W=["wab"])
        P.op("dve", lambda e: e.tensor_copy(out=wxb[:], in_=wx32[:]), R=["wx32"], W=["wxb"])
        P.op("act", lambda e: e.activation(out=lamc[:], in_=lamc[:], func=AF.Exp, scale=-1.0), R=["lamc"], W=["lamc"])
        P.op("act", lambda e: e.activation(out=lamc[:], in_=lamc[:], func=AF.Ln, bias=1.0), R=["lamc"], W=["lamc"])
        P.op("dve", lambda e: e.tensor_scalar(out=lamc[:], in0=lamc[:], scalar1=-8.0, scalar2=None, op0=ALU.mult), R=["lamc"], W=["lamc"])
        if not scan_only:
            d_lng = self.inp("ln1g1", [D]); d_lnb = self.inp("ln1b1", [D])
            kxT, vxaug = self.xattn_setup(1, es)
            gg = S("gg", [RB, 8, G], BF16); hmT = S("hmT", [RB, 8, G], BF16); xqT = S("xqT", [64, 4, G], BF16)
            hat = [S("hat0", [64, 256], BF16), S("hat1", [64, 256], BF16)]; haT = S("haT", [128, 2, G], BF16)
            Esb = S("Esb", [128, 8, 64], BF16); rsb = S("rsb", [64, 4])
            lng = S("lng", [128, D]); lnb = S("lnb", [128, D]); xs = S("xs", [128, D]); st = S("st", [128, 2, 6]); ag = S("ag", [128, 4])
            P.dma("sp", lng[:], d_lng.partition_broadcast(128), W=["lng"])
            P.dma("sp", lnb[:], d_lnb.partition_broadcast(128), W=["lnb"])
        srcs = [(w1b[b], f"w1b{b}") for b in (2, 3)]
        for g in range(ngroups):
            srcs += [(w1b[b], f"w1b{b}") for b in ((2, 3) if scan_only else (2, 3, 0, 1, 4, 5, 6, 7, 8))]
        sched = self.Sched(self, srcs)

        def xr_proj(wi, hb, rhs, n, dst_off, rkey):
            for j in range(4):
                g8 = 4 * hb + j
                bk = 2 + (j % 2)
                for kc in range(8):
                    P.op("pe", lambda e, j=j, kc=kc, bk=bk: e.matmul(
                        ps[0:RB, bk, 0:n], lhsT=self.wblk[wi][:, kc, j * 128:j * 128 + RB], rhs=rhs[:, kc, 0:n],
                        start=(kc == 0), stop=(kc == 7)), R=[rkey, f"wblk{wi}"], W=[f"ps{bk}"])
                P.op("act", lambda e, g8=g8, bk=bk: e.copy(out=xrb[:, g8, dst_off:dst_off + n], in_=ps[0:RB, bk, 0:n]), R=[f"ps{bk}"], W=["xrb"])

        for kc in range(8):
            P.op("pe", lambda e, kc=kc: e.transpose(out=ps[:, 0, kc * 4:kc * 4 + 3], in_=xh[0:3, kc * 128:(kc + 1) * 128], identity=ident[0:3, 0:3]),
                 R=["xh", "ident"], W=["ps0"])
        P.op("act", lambda e: e.copy(out=xTh[:, :, 0:3], in_=ps[:, 0, 0:32].rearrange("p (k c) -> p k c", c=4)[:, :, 0:3]), R=["ps0"], W=["xTh"])
        for hb in range(2):
            wi = sched.get()
            xr_proj(wi, hb, xTh, 3, 0, "xTh")
        mixi = 0
        for g in range(ngroups):
            self.make_xT(xTg, [self.X[:, 2 * g + tt, :] for tt in range(2)], [f"X{2 * g + tt}" for tt in range(2)], "xTg")
            for hb in range(2):
                wi = sched.get()
                xr_proj(wi, hb, xTg, G, 3, "xTg")
            if not scan_only:
                for hb in range(2):
                    wi = sched.get()
                    for j in range(4):
                        g8 = 4 * hb + j
                        bk = 2 + (j % 2)
                        for kc in range(8):
                            P.op("pe", lambda e, j=j, kc=kc, bk=bk: e.matmul(
                                ps[0:RB, bk, 0:G], lhsT=self.wblk[wi][:, kc, j * 128:j * 128 + RB], rhs=xTg[:, kc, :],
                                start=(kc == 0), stop=(kc == 7)), R=["xTg", f"wblk{wi}"], W=[f"ps{bk}"])
                        P.op("act", lambda e, g8=g8, bk=bk: e.activation(out=gg[:, g8, :], in_=ps[0:RB, bk, 0:G], func=AF.Gelu), R=[f"ps{bk}"], W=["gg"])
                wi = sched.get()
                for j in range(4):
                    bk = 2 + (j % 2)
                    for kc in range(8):
                        P.op("pe", lambda e, j=j, kc=kc, bk=bk: e.matmul(
                            ps[0:64, bk, 0:G], lhsT=self.wblk[wi][:, kc, j * 64:(j + 1) * 64], rhs=xTg[:, kc, :],
                            start=(kc == 0), stop=(kc == 7)), R=["xTg", f"wblk{wi}"], W=[f"ps{bk}"])
                    P.op("act", lambda e, j=j, bk=bk: e.copy(out=xqT[:, j, :], in_=ps[0:64, bk, 0:G]), R=[f"ps{bk}"], W=["xqT"])
            B8 = range(8)
            for w in range(4):
                for g8 in B8:
                    if w == 0:
                        P.op("dve", lambda e, g8=g8: e.tensor_scalar(out=xc[:, g8, :], in0=xrb[:, g8, 0:G], scalar1=cw[:, g8, 0:1], scalar2=cb[:, g8:g8 + 1],
                                                                     op0=ALU.mult, op1=ALU.add), R=["xrb", "cw", "cb"], W=[f"xc{g8}"])
                    else:
                        P.op("dve", lambda e, g8=g8, w=w: e.scalar_tensor_tensor(out=xc[:, g8, :], in0=xrb[:, g8, w:w + G], scalar=cw[:, g8, w:w + 1], in1=xc[:, g8, :],
                                                                                 op0=ALU.mult, op1=ALU.add), R=["xrb", "cw", f"xc{g8}"], W=[f"xc{g8}"])
            for g8 in B8:
                P.op("act", lambda e, g8=g8: e.copy(out=xcb[:, g8, :], in_=xc[:, g8, :]), R=[f"xc{g8}"], W=[f"xcb{g8}"])
            P.op("dve", lambda e: e.tensor_copy(out=xrb[:, :, 0:3], in_=xrb[:, :, G:G + 3]), R=["xrb"], W=["xrb"])
            for g8 in B8:
                pa, pb_ = 4 + 2 * (g8 % 2), 5 + 2 * (g8 % 2)
                P.op("pe", lambda e, g8=g8, pa=pa: e.matmul(ps[0:RB, pa, 0:G], lhsT=wab[:, g8, :], rhs=xcb[:, g8, :], start=True, stop=True), R=["wab", f"xcb{g8}"], W=[f"ps{pa}"])
                P.op("pe", lambda e, g8=g8, pb_=pb_: e.matmul(ps[0:RB, pb_, 0:G], lhsT=wxb[:, g8, :], rhs=xcb[:, g8, :], start=True, stop=True), R=["wxb", f"xcb{g8}"], W=[f"ps{pb_}"])
                P.op("act", lambda e, g8=g8, pa=pa: e.activation(out=rt[:, g8, :], in_=ps[0:RB, pa, 0:G], func=AF.Sigmoid, bias=ba[:, g8:g8 + 1]), R=[f"ps{pa}", "ba"], W=[f"rt{g8}"])
                P.op("act", lambda e, g8=g8, pb_=pb_: e.activation(out=it[:, g8, :], in_=ps[0:RB, pb_, 0:G], func=AF.Sigmoid, bias=bx[:, g8:g8 + 1]), R=[f"ps{pb_}", "bx"], W=[f"it{g8}"])
            for g8 in B8:
                P.op("act", lambda e, g8=g8: e.activation(out=at[:, g8, :], in_=rt[:, g8, :], func=AF.Exp, scale=lamc[:, g8:g8 + 1]), R=[f"rt{g8}", "lamc"], W=[f"at{g8}"])
            for g8 in B8:
                P.op("dve", lambda e, g8=g8: e.tensor_tensor(out=tmp[:, g8, :], in0=at[:, g8, :], in1=at[:, g8, :], op=ALU.mult), R=[f"at{g8}"], W=[f"tmp{g8}"])
            for g8 in B8:
                P.op("dve", lambda e, g8=g8: e.tensor_scalar(out=tmp[:, g8, :], in0=tmp[:, g8, :], scalar1=-1.0, scalar2=1.0, op0=ALU.mult, op1=ALU.add), R=[f"tmp{g8}"], W=[f"tmp{g8}"])
            for g8 in B8:
                P.op("dve", lambda e, g8=g8: e.tensor_tensor(out=ut[:, g8, :], in0=it[:, g8, :], in1=xc[:, g8, :], op=ALU.mult), R=[f"it{g8}", f"xc{g8}"], W=[f"ut{g8}"])
            for g8 in B8:
                P.op("act", lambda e, g8=g8: e.activation(out=tmp[:, g8, :], in_=tmp[:, g8, :], func=AF.Sqrt), R=[f"tmp{g8}"], W=[f"tmp{g8}"])
            for g8 in B8:
                P.op("dve", lambda e, g8=g8: e.tensor_tensor(out=ut[:, g8, :], in0=ut[:, g8, :], in1=tmp[:, g8, :], op=ALU.mult), R=[f"ut{g8}", f"tmp{g8}"], W=[f"ut{g8}"])
            for g8 in B8:
                P.op("dve", lambda e, g8=g8: e.tensor_tensor_scan(out=ht[:, g8, :], data0=at[:, g8, :], data1=ut[:, g8, :], initial=hcar[:, g8:g8 + 1], op0=ALU.mult, op1=ALU.add),
                     R=[f"at{g8}", f"ut{g8}", f"hcar{g8}"], W=[f"ht{g8}"])
            for g8 in B8:
                P.op("dve", lambda e, g8=g8: e.tensor_copy(out=hcar[:, g8:g8 + 1], in_=ht[:, g8, G - 1:G]), R=[f"ht{g8}"], W=[f"hcar{g8}"])
            if not scan_only:
                for g8 in B8:
                    P.op("dve", lambda e, g8=g8: e.tensor_tensor(out=hmT[:, g8, :], in0=ht[:, g8, :], in1=gg[:, g8, :], op=ALU.mult), R=[f"ht{g8}", "gg"], W=["hmT"])
            if scan_only:
                continue
            for ck in range(4):
                c0 = ck * 64
                hx = hat[mixi]; hk = f"hat{mixi}"; mixi ^= 1
                self.xattn_chunk(kxT, vxaug, xqT, c0, 64, hx[:, :], hk, Esb, rsb)
                pb = ps[:, 2, :].bitcast(BF16)
                for c2 in range(2):
                    P.op("pe", lambda e, c2=c2: e.transpose(out=pb[:, c2 * 64:(c2 + 1) * 64], in_=hx[:, c2 * 128:(c2 + 1) * 128], identity=self.identb[0:64, 0:64]),
                         R=[hk, "identb"], W=["ps2"])
                P.op("act", lambda e: e.copy(out=haT[:, :, c0:c0 + 64], in_=pb[:, 0:128].rearrange("p (k t) -> p k t", t=64)), R=["ps2"], W=["haT"])
            banks = {0: (0, 1), 1: (5, 6)}
            for cbk in range(4):
                wi = sched.get()
                wv = self.wblk[wi][:].rearrange("p k c -> p (k c)")[:, 0:2560].rearrange("p (ch c) -> p ch c", c=256)
                for tt in range(2):
                    bk = banks[tt][cbk // 2]
                    dst = ps[:, bk, (cbk % 2) * 256:(cbk % 2 + 1) * 256]
                    for ch in range(10):
                        if ch < 8:
                            lhsT = hmT[:, ch, tt * 128:(tt + 1) * 128]; rhs = wv[0:RB, ch, :]; rk = "hmT"
                        else:
                            lhsT = haT[:, ch - 8, tt * 128:(tt + 1) * 128]; rhs = wv[:, ch, :]; rk = "haT"
                        P.op("pe", lambda e, lhsT=lhsT, rhs=rhs, dst=dst, ch=ch: e.matmul(dst, lhsT=lhsT, rhs=rhs, start=(ch == 0), stop=(ch == 9)),
                             R=[rk, f"wblk{wi}"], W=[f"ps{bk}"])
            for tt in range(2):
                b0 = banks[tt][0]
                self.resid_ln(2 * g + tt, ps[:, b0:b0 + 2, :], [f"ps{b0}", f"ps{b0 + 1}"], lng, lnb, xs, st, ag)
        if scan_only:
            if halo_src is None:
                d_hend = self.outp("hend", [RB, 8])
                P.dma("sp", d_hend, hcar[:], R=[f"hcar{k}" for k in range(8)], chan="out")
            else:
                P.dma("sp", self.cc2src, hcar[:], R=[f"hcar{k}" for k in range(8)], W=["cc2src"])
        P.barrier()


Builder.l1 = _l1_method


def _run(stages, inputs, x_cur=None, hinit=None, **kw):
    b, es = build(stages, **kw)
    maps = core_inputs(inputs, stages, x_cur=x_cur, hinit=hinit)
    maps = [{k: v for k, v in m.items() if k in b.din} for m in maps]
    res = run_bass_kernel_spmd(b.nc, maps, core_ids=list(range(8)))
    es.close()
    return res.results


def _gather(results, name="out"):
    x = np.empty((4, 2 * T, D), np.float32)
    for c in range(8):
        x[c // 2, (c % 2) * T:(c % 2 + 1) * T] = results[c][name]
    return x


def kernel(**inputs):
    inputs = {k: np.asarray(v) for k, v in inputs.items()}
    return _gather(_run(["fused"], inputs))
```

```python
import numpy as np
from contextlib import ExitStack
import concourse.bass as bass
import concourse.mybir as mybir
from concourse.bass_utils import run_bass_kernel_spmd

F32 = mybir.dt.float32
BF16 = mybir.dt.bfloat16
U32 = mybir.dt.uint32
I32 = mybir.dt.int32
AF = mybir.ActivationFunctionType
ALU = mybir.AluOpType
AX = mybir.AxisListType


class Prog:
    NDS = 24

    def __init__(self, nc, es):
        self.nc = nc
        self.es = es
        self.eng = {"pe": nc.tensor, "dve": nc.vector, "act": nc.scalar, "pool": nc.gpsimd, "sp": nc.sync}
        self.sem = {k: es.enter_context(nc.semaphore("sem_" + k)) for k in self.eng}
        self.cnt = {k: 0 for k in self.eng}
        self.seen = {k: {} for k in self.eng}
        self.dsem = []
        self.dcnt = []
        self.dq = {}
        for q, n in (("sp", 20), ("pool", 8), ("act", 4)):
            self.dq[q] = [len(self.dsem) + i for i in range(n)]
            self.dsem += [es.enter_context(nc.semaphore(f"dsem_{q}{i}")) for i in range(n)]
            self.dcnt += [0] * n
        self.dnext = {q: 0 for q in self.dq}
        self.last_w = {}
        self.readers = {}
        self.n_ins = 0

    def _semof(self, kind, name):
        return self.sem[name] if kind == "e" else self.dsem[name]

    def _wait_ev(self, eng, kind, name, c):
        if self.seen[eng].get((kind, name), 0) >= c:
            return
        self.seen[eng][(kind, name)] = c
        self.eng[eng].wait_ge(self._semof(kind, name), c)

    def _wait(self, eng, R, W):
        deps = []
        for k in R:
            if k in self.last_w:
                deps.append(self.last_w[k])
        for k in W:
            if k in self.last_w:
                deps.append(self.last_w[k])
            deps.extend(self.readers.get(k, ()))
        best = {}
        for kind, name, c in deps:
            if kind == "e" and name == eng and eng == "pe":
                continue
            key = (kind, name)
            if c > best.get(key, 0):
                best[key] = c
        for (kind, name), c in best.items():
            self._wait_ev(eng, kind, name, c)

    def _commit(self, ev, R, W):
        for k in W:
            self.last_w[k] = ev
            self.readers[k] = []
        for k in R:
            self.readers.setdefault(k, []).append(ev)

    limit = None
    stop_at = None

    def cp(self, name):
        if self.stop_at is not None and name == self.stop_at and self.limit is None:
            self.limit = self.n_ins

    def op(self, eng, fn, R=(), W=()):
        if self.limit is not None and self.n_ins >= self.limit:
            return None
        W = list(W) + [k for k in R if k.startswith("ps") and k not in W]
        self._wait(eng, R, W)
        ins = fn(self.eng[eng])
        self.cnt[eng] += 1
        ins.then_inc(self.sem[eng], 1)
        self._commit(("e", eng, self.cnt[eng]), R, W)
        self.n_ins += 1
        return ins

    def dma(self, q, out, in_, R=(), W=(), chan=None):
        if self.limit is not None and self.n_ins >= self.limit and chan != "out":
            return None
        i = self.dq[q][self.dnext[q]]
        self.dnext[q] = (self.dnext[q] + 1) % len(self.dq[q])
        if self.dcnt[i]:
            self._wait_ev(q, "d", i, self.dcnt[i])
        self._wait(q, R, W)
        ins = self.eng[q].dma_start(out=out, in_=in_)
        self.dcnt[i] += 16
        ins.then_inc(self.dsem[i], 16)
        self._commit(("d", i, self.dcnt[i]), R, W)
        return ins

    def allgather_pairs(self, src, dst, R=(), W=()):
        self._wait("pool", R, W)
        sem = self.es.enter_context(self.nc.semaphore(f"cc_sem{len(self.dsem)}"))
        self.dsem.append(sem)
        self.dcnt.append(0)
        i = len(self.dsem) - 1
        ins = self.nc.gpsimd.collective_compute("AllGather", ALU.bypass, replica_groups=[[0, 1], [2, 3], [4, 5], [6, 7]],
                                               ins=[src.opt()], outs=[dst.opt()])
        ins.then_inc(sem)
        self.dcnt[i] = 1
        self._commit(("d", i, 1), R, W)

    def barrier(self, pool_dma=True):
        for e in self.eng:
            for k, c in self.cnt.items():
                if c:
                    self._wait_ev(e, "e", k, c)
            for i, c in enumerate(self.dcnt):
                if c and (pool_dma or i not in self.dq["pool"]):
                    self._wait_ev(e, "d", i, c)

    def wait_all(self, eng="sp"):
        for k, c in self.cnt.items():
            if c and k != eng:
                self._wait_ev(eng, "e", k, c)
        for i, c in enumerate(self.dcnt):
            if c:
                self._wait_ev(eng, "d", i, c)


D = 1024
T = 2048
NT = T // 128
G = 256
NG = T // G
H = 4
DH = 192
L = 64
ALPHA = float(4 ** 0.25)
EPS = 1e-5
NB0 = 14


def _pad_cols(a, n):
    out = np.zeros((a.shape[0], n), np.float32)
    out[:, : a.shape[1]] = a
    return out


def _blockify(cols):
    return np.ascontiguousarray(cols.reshape(8, 128, 512).transpose(1, 0, 2))


def prep_l0_weights(w_in, w_out):
    q = w_in[:, 0:768]; k = w_in[:, 768:1536]; v = w_in[:, 1536:2304]; o = w_in[:, 2304:3072]
    g = w_in[:, 3072:3080]; xq = w_in[:, 3080:3336]
    blocks = []
    for src in (q, k):
        for hp in range(2):
            cs = []
            for h in (2 * hp, 2 * hp + 1):
                cs.append(src[:, h * 192: h * 192 + 128])
                cs.append(_pad_cols(src[:, h * 192 + 128: (h + 1) * 192], 128))
            blocks.append(np.concatenate(cs, axis=1))
    blocks.append(_pad_cols(xq, 512))
    for src in (k, v, o):
        blocks.append(_pad_cols(src[:, 0:384], 512))
        blocks.append(_pad_cols(src[:, 384:768], 512))
    blocks.append(_pad_cols(g, 512))
    blocks.append(w_out[:, 0:512]); blocks.append(w_out[:, 512:1024])
    return np.stack([_blockify(np.ascontiguousarray(b, dtype=np.float32)) for b in blocks])


class Builder:
    def __init__(self, es, debug=()):
        self.es = es
        self.debug = set(debug)
        self.nc = nc = bass.Bass("TRN2", target_bir_lowering=False)
        self.P = Prog(nc, es)
        self.din = {}
        self.dout = {}
        self.wb_i = 0
        self.peer_casts = {}

    def inp(self, name, shape, dt=F32):
        ap = self.nc.dram_tensor(name, list(shape), dt, kind="ExternalInput").ap()
        self.din[name] = ap
        return ap

    def outp(self, name, shape, dt=F32):
        ap = self.nc.dram_tensor(name, list(shape), dt, kind="ExternalOutput").ap()
        self.dout[name] = ap
        return ap

    def scratch(self, name, shape, dt):
        return self.nc.dram_tensor(name, list(shape), dt, kind="Internal").ap()

    def sb(self, name, shape, dt=F32, es=None):
        return (es or self.es).enter_context(self.nc.sbuf_tensor(name + "_sb", list(shape), dt))

    def wload(self, src):
        i = self.wb_i % len(self.wblk)
        self.wb_i = (i + 1) % len(self.wblk)
        self.P.dma("sp", self.wblk[i][:].rearrange("p k c -> p (k c)"), src[0], R=[src[1]], W=[f"wblk{i}"], chan="w")
        return i

    class Sched:
        def __init__(self, b, srcs):
            self.b = b; self.srcs = srcs; self.n = 0; self.cur = None
            self.single = len(b.wblk) == 1
            self.nxt = b.wload(srcs[0]) if (srcs and not self.single) else None

        def get(self):
            if self.single:
                self.n += 1
                return self.b.wload(self.srcs[self.n - 1])
            cur = self.nxt
            self.n += 1
            self.nxt = self.b.wload(self.srcs[self.n]) if self.n < len(self.srcs) else None
            return cur

    def setup(self):
        nc, P = self.nc, self.P
        sb = self.sb
        self.ps = self.es.enter_context(nc.psum_tensor("ps", [128, 8, 512], F32))
        self.X = sb("X", [128, NT, D])
        self.ident = sb("ident", [128, 128])
        self.identb = sb("identb", [128, 128], BF16)
        self.cmask = sb("cmask", [64, 64], BF16)
        self.flag = sb("flag", [128, 1])
        d_ident = self.inp("ident", [128, 128])
        d_cmask = self.inp("cmask", [64, 64])
        d_flag = self.inp("flag", [128, 1])
        cm32 = sb("cm32", [64, 64])
        P.dma("sp", self.ident[:], d_ident, W=["ident"])
        P.dma("sp", cm32[:], d_cmask, W=["cm32"])
        P.dma("sp", self.flag[:], d_flag, W=["flag"])
        P.op("dve", lambda e: e.tensor_copy(out=self.identb[:], in_=self.ident[:]), R=["ident"], W=["identb"])
        P.op("dve", lambda e: e.tensor_copy(out=self.cmask[:], in_=cm32[:]), R=["cm32"], W=["cmask"])
        d_x = self.inp("x_own", [T, D])
        for q4 in range(4):
            P.dma("sp", self.X[:, 4 * q4:4 * q4 + 4, :],
                  d_x[512 * q4:512 * (q4 + 1), :].rearrange("(t p) d -> p t d", p=128),
                  W=[f"X{t}" for t in range(4 * q4, 4 * q4 + 4)])

    def psb(self, b0, nb=1):
        return [f"ps{b}" for b in range(b0, b0 + nb)]

    def make_xT(self, xTg, src_tiles, src_keys, kout, b0=0):
        P, ps = self.P, self.ps
        for tt in range(2):
            for kc in range(8):
                P.op("pe", lambda e, tt=tt, kc=kc: e.transpose(
                    out=ps[:, b0 + kc // 4, (kc % 4) * 128:(kc % 4 + 1) * 128],
                    in_=src_tiles[tt][:, kc * 128:(kc + 1) * 128], identity=self.ident[:]),
                    R=[src_keys[tt], "ident"], W=[f"ps{b0 + kc // 4}"])
            P.op("act", lambda e, tt=tt: e.copy(
                out=xTg[:, :, tt * 128:(tt + 1) * 128],
                in_=ps[:, b0:b0 + 2, :].rearrange("p b (k c) -> p (b k) c", c=128)),
                R=[f"ps{b0}", f"ps{b0 + 1}"], W=[kout])

    def xattn_setup(self, layer, es):
        P, ps, sb = self.P, self.ps, self.sb
        d_memT = self.din["memT"] if "memT" in self.din else self.inp("memT", [128, 8, 256])
        d_wkv = self.inp(f"wkv{layer}", [128, 8, 512])
        kxT = sb(f"kxT_{layer}", [64, 4, 256], BF16, es=es)
        vxaug = sb(f"vxaug_{layer}", [128, 2, 4, 65], BF16, es=es)
        with ExitStack() as tes:
            memT32 = sb(f"memT32_{layer}", [128, 8, 256], es=tes)
            wkv32 = sb(f"wkv32_{layer}", [128, 8, 512], es=tes)
            memTb = sb(f"memTb_{layer}", [128, 8, 256], BF16, es=tes)
            wkvb = sb(f"wkvb_{layer}", [128, 8, 512], BF16, es=tes)
            P.dma("sp", memT32[:], d_memT, W=["memT32"])
            P.dma("sp", wkv32[:], d_wkv, W=["wkv32"])
            P.op("dve", lambda e: e.tensor_copy(out=memTb[:], in_=memT32[:]), R=["memT32"], W=["memTb"])
            P.op("pool", lambda e: e.tensor_copy(out=wkvb[:], in_=wkv32[:]), R=["wkv32"], W=["wkvb"])
            for h in range(4):
                for kc in range(8):
                    P.op("pe", lambda e, h=h, kc=kc: e.matmul(
                        ps[0:64, 2, 0:256], lhsT=wkvb[:, kc, h * 64:(h + 1) * 64], rhs=memTb[:, kc, :],
                        start=(kc == 0), stop=(kc == 7)), R=["wkvb", "memTb"], W=["ps2"])
                P.op("act", lambda e, h=h: e.copy(out=kxT[:, h, :], in_=ps[0:64, 2, 0:256]), R=["ps2"], W=["kxT"])
            P.op("pool", lambda e: e.memset(vxaug[:], 1.0), W=["vxaug"])
            for mc in range(2):
                for kc in range(8):
                    P.op("pe", lambda e, mc=mc, kc=kc: e.matmul(
                        ps[:, 3, 0:256], lhsT=memTb[:, kc, mc * 128:(mc + 1) * 128], rhs=wkvb[:, kc, 256:512],
                        start=(kc == 0), stop=(kc == 7)), R=["wkvb", "memTb"], W=["ps3"])
                P.op("act", lambda e, mc=mc: e.copy(
                    out=vxaug[:, mc, :, 0:64], in_=ps[:, 3, 0:256].rearrange("p (h d) -> p h d", h=4)),
                    R=["ps3"], W=["vxaug"])
            P.barrier(pool_dma=False)
        return kxT, vxaug

    def xattn_chunk(self, kxT, vxaug, xqT, c0, ntok, out_ap, out_key, Esb, rsb):
        P, ps = self.P, self.ps
        for h in range(4):
            for mc in range(2):
                P.op("pe", lambda e, h=h, mc=mc: e.matmul(
                    ps[:, 0, (h * 2 + mc) * 64:(h * 2 + mc) * 64 + ntok],
                    lhsT=kxT[:, h, mc * 128:(mc + 1) * 128],
                    rhs=xqT[:, h, c0:c0 + ntok], start=True, stop=True),
                    R=["kxT", "xqT"], W=["ps0"])
        P.cp("xa_st")
        P.op("act", lambda e: e.activation(
            out=Esb[:, :, 0:ntok], in_=ps[:, 0, :].rearrange("p (j t) -> p j t", t=64)[:, :, 0:ntok],
            func=AF.Exp, scale=0.125), R=["ps0"], W=["Esb"])
        P.cp("xa_exp")
        Ov = ps[0:ntok, 1, 0:260].rearrange("p (h d) -> p h d", d=65)
        for h in range(4):
            for mc in range(2):
                P.op("pe", lambda e, h=h, mc=mc: e.matmul(
                    Ov[:, h, :], lhsT=Esb[:, h * 2 + mc, 0:ntok], rhs=vxaug[:, mc, h, :],
                    start=(mc == 0), stop=(mc == 1)), R=["Esb", "vxaug"], W=["ps1"])
        P.cp("xa_ov")
        P.op("dve", lambda e: e.reciprocal(out=rsb[0:ntok, :], in_=Ov[:, :, 64]), R=["ps1"], W=["rsb"])
        P.op("dve", lambda e: e.tensor_tensor(
            out=out_ap.rearrange("p (h d) -> p h d", d=64), in0=Ov[:, :, 0:64],
            in1=rsb[0:ntok, :].unsqueeze(2).broadcast_to([ntok, 4, 64]), op=ALU.mult),
            R=["ps1", "rsb"], W=[out_key])

    def outproj_ln(self, sched, mixT, g, lng, lnb, xs, st, ag):
        P, ps = self.P, self.ps
        banks = {0: (0, 1), 1: (5, 6)}
        for half in range(2):
            wi = sched.get()
            for tt in range(2):
                bk = banks[tt][half]
                for kc in range(8):
                    P.op("pe", lambda e, tt=tt, kc=kc, bk=bk, wi=wi: e.matmul(
                        ps[:, bk, :], lhsT=mixT[:, kc, tt * 128:(tt + 1) * 128], rhs=self.wblk[wi][:, kc, :],
                        start=(kc == 0), stop=(kc == 7)), R=["mixT", f"wblk{wi}"], W=[f"ps{bk}"])
        P.cp("op_mm")
        for tt in range(2):
            t = 2 * g + tt
            b0 = banks[tt][0]
            self.resid_ln(t, self.ps[:, b0:b0 + 2, :], [f"ps{b0}", f"ps{b0 + 1}"], lng, lnb, xs, st, ag)

    def resid_ln(self, t, y_ap, y_keys, lng, lnb, xs, st, ag):
        P = self.P
        Xt = self.X[:, t, :]
        wk = Xt if xs is None else xs[:]
        kk = f"X{t}" if xs is None else "xs"
        P.op("dve", lambda e: e.scalar_tensor_tensor(
            out=wk.rearrange("p (b c) -> p b c", b=2), in0=Xt.rearrange("p (b c) -> p b c", b=2),
            scalar=ALPHA, in1=y_ap, op0=ALU.mult, op1=ALU.add), R=[f"X{t}"] + y_keys, W=[kk])
        P.cp("ln_a")
        for hf in range(2):
            P.op("dve", lambda e, hf=hf: e.bn_stats(out=st[:, hf, :], in_=wk[:, hf * 512:(hf + 1) * 512]),
                 R=[kk], W=[f"st{hf}"])
        P.op("dve", lambda e: e.bn_aggr(out=ag[:, 0:2], in_=st[:].rearrange("p a b -> p (a b)")),
             R=["st0", "st1"], W=["ag"])
        P.cp("ln_b")
        P.op("act", lambda e: e.activation(out=ag[:, 2:3], in_=ag[:, 1:2], func=AF.Sqrt, bias=EPS),
             R=["ag"], W=["ag2"])
        P.op("dve", lambda e: e.reciprocal(out=ag[:, 3:4], in_=ag[:, 2:3]), R=["ag2"], W=["ag3"])
        P.op("dve", lambda e: e.tensor_scalar(out=wk, in0=wk, scalar1=ag[:, 0:1], scalar2=ag[:, 3:4],
                                              op0=ALU.subtract, op1=ALU.mult), R=[kk, "ag", "ag3"], W=[kk])
        P.cp("ln_c")
        P.op("dve", lambda e: e.tensor_tensor(out=wk, in0=wk, in1=lng[:], op=ALU.mult), R=[kk, "lng"], W=[kk])
        P.op("dve", lambda e: e.tensor_tensor(out=Xt, in0=wk, in1=lnb[:], op=ALU.add), R=[kk, "lnb"], W=[f"X{t}"])

    def l0_mixer(self, npre=NG, nmain=NG):
        nc, P, ps, sb = self.nc, self.P, self.ps, self.sb
        ident = self.ident
        with ExitStack() as es:
            d_w0 = self.inp("w0", [NB0, 128, 4096])
            w0b = self.scratch("w0b", [NB0, 128, 4096], BF16)
            for b in range(NB0):
                P.dma("pool", w0b[b].rearrange("p (a c) -> p a c", c=2048), d_w0[b].rearrange("p (a c) -> p a c", c=2048), W=[f"w0b{b}"])
            P.barrier()
            d_xpre = self.inp("x_pre", [T, D])
            d_bgi = self.inp("bgi", [4, 1]); d_bgf = self.inp("bgf", [4, 1])
            d_ng = self.inp("norm_g", [768]); d_lng = self.inp("ln1g0", [D]); d_lnb = self.inp("ln1b0", [D])
            kxT, vxaug = self.xattn_setup(0, es)
            S = lambda n, sh, dt=F32: sb(n, sh, dt, es=es)
            self.wblk = [S("wblk0_m0", [128, 8, 512], BF16), S("wblk1_m0", [128, 8, 512], BF16)]
            xpre = S("xpre", [128, 2, D]); xTg = S("xTg", [128, 8, G], BF16)
            qT = S("qT", [128, 8, G], BF16); kT = S("kT", [128, 8, G], BF16); xqT = S("xqT", [64, 4, G], BF16)
            k_tm = S("k_tm", [64, 4, 768], BF16); vaug = S("vaug", [64, 4, 4, 193], BF16); so = S("so", [64, 4, 768], BF16)
            g_tm = S("g_tm", [64, 4, 8])
            bgi = S("bgi_s", [4, 1]); nbgf = S("nbgf", [4, 1]); zer = S("zer", [4, G])
            ig = S("ig", [4, G]); lsp = S("lsp", [4, G]); Lc = S("Lc", [4, G]); gam = S("gam", [4, G]); Mx = S("Mx", [4, G])
            t1 = S("t1", [4, G]); bet = S("bet", [4, G]); alp = S("alp", [4, G]); flo = S("flo", [4, G])
            Lcar = S("Lcar", [4, 1]); Mcar = S("Mcar", [4, 1]); Mprev = S("Mprev", [4, 4]); dec = S("dec", [4, 4])
            sel = S("sel", [4, 4, 128]); gtm = S("gtm", [64, 4, 3, 4]); decb = S("decb", [128, 4, 4])
            CA = S("CA", [128, 4, 193]); CB = S("CB", [64, 4, 193]); CAb = S("CAb", [128, 4, 193], BF16); CBb = S("CBb", [64, 4, 193], BF16)
            SmT = S("SmT", [64, 4, 64], BF16); num = S("num", [64, 4, 193]); hr = S("hr", [64, 4, 192]); sq = S("sq", [64, 4, 192])
            sm = S("sm", [64, 8, 4]); mix = [S("mix0", [64, D], BF16), S("mix1", [64, D], BF16)]
            mixT = S("mixT", [128, 8, G], BF16); Esb = S("Esb", [128, 8, 64], BF16); rsb = S("rsb", [64, 4])
            ngb = S("ngb", [64, 768]); lng = S("lng", [128, D]); lnb = S("lnb", [128, D])
            xs = S("xs", [128, D]); st = S("st", [128, 2, 6]); ag = S("ag", [128, 4])
            P.dma("sp", bgi[:], d_bgi, W=["bgi"]); P.dma("sp", nbgf[:], d_bgf, W=["nbgf"])
            P.dma("sp", ngb[:], d_ng.partition_broadcast(64), W=["ngb"])
            P.dma("sp", lng[:], d_lng.partition_broadcast(128), W=["lng"])
            P.dma("sp", lnb[:], d_lnb.partition_broadcast(128), W=["lnb"])
            P.op("dve", lambda e: e.tensor_scalar(out=nbgf[:], in0=nbgf[:], scalar1=-1.0, scalar2=None, op0=ALU.mult), R=["nbgf"], W=["nbgf"])
            for t_, k_ in ((zer, "zer"), (Lcar, "Lcar"), (Mcar, "Mcar"), (CA, "CA"), (CB, "CB")):
                P.op("pool", lambda e, t_=t_: e.memset(t_[:], 0.0), W=[k_])
            for h in range(4):
                P.op("dve", lambda e, h=h: e.tensor_copy(out=sel[:, h, :], in_=ident[0:4, h:h + 1].broadcast_to([4, 128])),
                     R=["ident"], W=["sel"])
            srcs = []
            for g in range(npre):
                srcs += [(w0b[b], f"w0b{b}") for b in (11, 5, 6, 7, 8)]
            for g in range(nmain):
                srcs += [(w0b[b], f"w0b{b}") for b in (11, 0, 1, 2, 3, 4, 5, 6, 7, 8, 9, 10, 12, 13)]
            sched = self.Sched(self, srcs)
            KS = float(DH ** -0.5)
            mixi = 0

            def tm_proj(wi, ck, ncols):
                bk = 2 + (ck % 2)
                for kc in range(8):
                    P.op("pe", lambda e, kc=kc: e.matmul(
                        ps[0:64, bk, 0:ncols], lhsT=xTg[:, kc, ck * 64:(ck + 1) * 64], rhs=self.wblk[wi][:, kc, 0:ncols],
                        start=(kc == 0), stop=(kc == 7)), R=["xTg", f"wblk{wi}"], W=[f"ps{bk}"])
                return bk

            for phase in (0, 1):
                for g in range(nmain if phase else npre):
                    main = phase == 1
                    if main:
                        srct = [self.X[:, 2 * g + tt, :] for tt in range(2)]; srck = [f"X{2 * g + tt}" for tt in range(2)]
                    else:
                        P.dma("sp", xpre[:], d_xpre[g * G:(g + 1) * G, :].rearrange("(t p) d -> p t d", p=128), W=["xpre0", "xpre1"])
                        srct = [xpre[:, tt, :] for tt in range(2)]; srck = ["xpre0", "xpre1"]
                    self.make_xT(xTg, srct, srck, "xTg")
                    wi = sched.get()
                    for ck in range(4):
                        bk = tm_proj(wi, ck, 8)
                        P.op("act", lambda e, ck=ck, bk=bk: e.copy(out=g_tm[:, ck, :], in_=ps[0:64, bk, 0:8]), R=[f"ps{bk}"], W=["g_tm"])
                    for ck in range(4):
                        for j in range(2):
                            P.op("pe", lambda e, ck=ck, j=j: e.transpose(
                                out=ps[0:4, 4, j * 256 + ck * 64: j * 256 + (ck + 1) * 64],
                                in_=g_tm[:, ck, 4 * j:4 * j + 4], identity=ident[0:64, 0:64]), R=["g_tm", "ident"], W=["ps4"])
                    if main:
                        P.op("dve", lambda e: e.tensor_scalar(out=ig[:], in0=ps[0:4, 4, 0:256], scalar1=bgi[:, 0:1], scalar2=None, op0=ALU.add),
                             R=["ps4", "bgi"], W=["ig"])
                    else:
                        P.op("dve", lambda e: e.tensor_scalar(out=ig[:], in0=ps[0:4, 4, 0:256], scalar1=bgi[:, 0:1], scalar2=self.flag[0:4, 0:1],
                                                              op0=ALU.add, op1=ALU.mult), R=["ps4", "bgi", "flag"], W=["ig"])
                    P.op("act", lambda e: e.activation(out=t1[:], in_=ps[0:4, 4, 256:512], func=AF.Exp, bias=nbgf[:, 0:1], scale=-1.0),
                         R=["ps4", "nbgf"], W=["t1"])
                    P.op("act", lambda e: e.activation(out=lsp[:], in_=t1[:], func=AF.Ln, bias=1.0), R=["t1"], W=["lsp"])
                    if not main:
                        P.op("dve", lambda e: e.tensor_scalar(out=lsp[:], in0=lsp[:], scalar1=self.flag[0:4, 0:1], scalar2=None, op0=ALU.mult),
                             R=["lsp", "flag"], W=["lsp"])
                    P.op("dve", lambda e: e.tensor_tensor_scan(out=Lc[:], data0=lsp[:], data1=zer[:], initial=Lcar[:, 0:1], op0=ALU.add, op1=ALU.add),
                         R=["lsp", "zer", "Lcar"], W=["Lc"])
                    P.op("dve", lambda e: e.tensor_tensor(out=gam[:], in0=ig[:], in1=Lc[:], op=ALU.add), R=["ig", "Lc"], W=["gam"])
                    P.op("dve", lambda e: e.tensor_tensor_scan(out=Mx[:], data0=gam[:], data1=gam[:], initial=Mcar[:, 0:1], op0=ALU.max, op1=ALU.max),
                         R=["gam", "Mcar"], W=["Mx"])
                    Mend = Mx[:].rearrange("p (c l) -> p c l", l=64)[:, :, 63]
                    P.op("dve", lambda e: e.tensor_copy(out=Mprev[:, 0:1], in_=Mcar[:, 0:1]), R=["Mcar"], W=["Mprev"])
                    P.op("dve", lambda e: e.tensor_copy(out=Mprev[:, 1:4], in_=Mend[:, 0:3]), R=["Mx"], W=["Mprev"])
                    P.op("dve", lambda e: e.tensor_copy(out=Mcar[:, 0:1], in_=Mx[:, G - 1:G]), R=["Mx", "Mprev"], W=["Mcar"])
                    P.op("dve", lambda e: e.tensor_copy(out=Lcar[:, 0:1], in_=Lc[:, G - 1:G]), R=["Lc"], W=["Lcar"])
                    P.op("dve", lambda e: e.tensor_tensor(out=dec[:], in0=Mprev[:], in1=Mend, op=ALU.subtract), R=["Mprev", "Mx"], W=["dec"])
                    P.op("act", lambda e: e.activation(out=dec[:], in_=dec[:], func=AF.Exp), R=["dec"], W=["dec"])
                    Mend_bc = Mend.unsqueeze(2).broadcast_to([4, 4, 64])
                    v3 = lambda t_: t_[:].rearrange("p (c l) -> p c l", l=64)
                    P.op("dve", lambda e: e.tensor_tensor(out=v3(bet), in0=v3(gam), in1=Mend_bc, op=ALU.subtract), R=["gam", "Mx"], W=["bet"])
                    P.op("act", lambda e: e.activation(out=bet[:], in_=bet[:], func=AF.Exp), R=["bet"], W=["bet"])
                    if main:
                        P.op("dve", lambda e: e.tensor_tensor(out=v3(alp), in0=Mend_bc, in1=v3(Mx), op=ALU.subtract), R=["Mx"], W=["alp"])
                        P.op("act", lambda e: e.activation(out=alp[:], in_=alp[:], func=AF.Exp), R=["alp"], W=["alp"])
                        P.op("dve", lambda e: e.tensor_tensor(out=flo[:], in0=Lc[:], in1=Mx[:], op=ALU.subtract), R=["Lc", "Mx"], W=["flo"])
                        P.op("act", lambda e: e.activation(out=flo[:], in_=flo[:], func=AF.Exp), R=["flo"], W=["flo"])
                    qs = (bet, alp, flo) if main else (bet,)
                    for ck in range(4):
                        for qi, qt in enumerate(qs):
                            P.op("pe", lambda e, ck=ck, qi=qi, qt=qt: e.transpose(
                                out=ps[0:64, 4, ck * 12 + qi * 4: ck * 12 + qi * 4 + 4], in_=qt[0:4, ck * 64:(ck + 1) * 64],
                                identity=ident[0:4, 0:4]), R=[("bet", "alp", "flo")[qi], "ident"], W=["ps4"])
                    if main:
                        P.op("act", lambda e: e.copy(out=gtm[:].rearrange("p c q h -> p (c q h)"), in_=ps[0:64, 4, 0:48]), R=["ps4"], W=["gtm"])
                    else:
                        P.op("act", lambda e: e.copy(out=gtm[:, :, 0, :], in_=ps[0:64, 4, 0:48].rearrange("p (c q h) -> p c q h", q=3, h=4)[:, :, 0, :]),
                             R=["ps4"], W=["gtm"])
                    for h in range(4):
                        P.op("pe", lambda e, h=h: e.matmul(
                            ps[:, 4, 64:80].rearrange("p (c h) -> p c h", h=4)[:, :, h], lhsT=sel[0:4, h, :], rhs=dec[0:4, 0:4],
                            start=True, stop=True), R=["sel", "dec", "gtm"], W=["ps4"])
                    P.op("act", lambda e: e.copy(out=decb[:].rearrange("p c h -> p (c h)"), in_=ps[:, 4, 64:80]), R=["ps4"], W=["decb"])
                    if main:
                        for blk, dst, scl in ((0, qT, 1.0), (1, qT, 1.0), (2, kT, KS), (3, kT, KS), (4, xqT, 1.0)):
                            wi = sched.get()
                            cw = 128 if blk < 4 else 64
                            for j in range(4):
                                bk = 2 + (j % 2)
                                for kc in range(8):
                                    P.op("pe", lambda e, j=j, kc=kc, bk=bk, wi=wi: e.matmul(
                                        ps[0:cw, bk, 0:G], lhsT=self.wblk[wi][:, kc, j * cw:(j + 1) * cw], rhs=xTg[:, kc, :],
                                        start=(kc == 0), stop=(kc == 7)), R=["xTg", f"wblk{wi}"], W=[f"ps{bk}"])
                                cj = (blk % 2) * 4 + j if blk < 4 else j
                                dk = {0: "qT", 1: "qT", 2: "kT", 3: "kT", 4: "xqT"}[blk]
                                P.op("act", lambda e, dst=dst, cj=cj, bk=bk, scl=scl: e.activation(
                                    out=dst[0:cw, cj, :], in_=ps[0:cw, bk, 0:G], func=AF.Copy, scale=scl), R=[f"ps{bk}"], W=[dk])
                    tms = [(0, "k"), (1, "k"), (0, "v"), (1, "v")] + ([(0, "o"), (1, "o")] if main else [])
                    for hp, kind in tms:
                        wi = sched.get()
                        for ck in range(4):
                            bk = tm_proj(wi, ck, 384)
                            src = ps[0:64, bk, 0:384]
                            if kind == "k":
                                P.op("act", lambda e, ck=ck, hp=hp, src=src: e.activation(
                                    out=k_tm[:, ck, hp * 384:(hp + 1) * 384], in_=src, func=AF.Copy, scale=KS), R=[f"ps{bk}"], W=["k_tm"])
                            elif kind == "o":
                                P.op("act", lambda e, ck=ck, hp=hp, src=src: e.activation(
                                    out=so[:, ck, hp * 384:(hp + 1) * 384], in_=src, func=AF.Sigmoid), R=[f"ps{bk}"], W=["so"])
                            else:
                                P.op("dve", lambda e, ck=ck, hp=hp, src=src: e.tensor_tensor(
                                    out=vaug[:, ck, 2 * hp:2 * hp + 2, 0:192], in0=src.rearrange("p (h d) -> p h d", h=2),
                                    in1=gtm[:, ck, 0, 2 * hp:2 * hp + 2].unsqueeze(2).broadcast_to([64, 2, 192]), op=ALU.mult),
                                    R=[f"ps{bk}", "gtm"], W=["vaug"])
                    for ck in range(4):
                        P.op("pool", lambda e, ck=ck: e.tensor_copy(out=vaug[:, ck, :, 192], in_=gtm[:, ck, 0, :]), R=["gtm"], W=["vaug"])
                    P.cp(f"proj{phase}")
                    Pv = ps[0:64, 5:7, :].rearrange("p b (h e) -> p (b h) e", h=2)

                    def front(ck):
                        c0 = ck * 64
                        dbc = lambda n: decb[0:n, ck, :].unsqueeze(2).broadcast_to([n, 4, 193])
                        P.op("dve", lambda e: e.tensor_tensor(out=CA[:], in0=CA[:], in1=dbc(128), op=ALU.mult), R=["CA", "decb"], W=["CA"])
                        P.op("pool", lambda e: e.tensor_tensor(out=CB[:], in0=CB[:], in1=dbc(64), op=ALU.mult), R=["CB", "decb"], W=["CB"])
                        if main:
                            P.op("act", lambda e: e.copy(out=CAb[:], in_=CA[:]), R=["CA"], W=["CAb"])
                            P.op("act", lambda e: e.copy(out=CBb[:], in_=CB[:]), R=["CB"], W=["CBb"])
                            for h in range(4):
                                P.op("pe", lambda e, h=h: e.matmul(ps[0:64, 4, 256 + h * 64:256 + (h + 1) * 64], lhsT=kT[:, 2 * h, c0:c0 + 64],
                                                                   rhs=qT[:, 2 * h, c0:c0 + 64], start=True, stop=False), R=["kT", "qT"], W=["ps4"])
                                P.op("pe", lambda e, h=h: e.matmul(ps[0:64, 4, 256 + h * 64:256 + (h + 1) * 64], lhsT=kT[0:64, 2 * h + 1, c0:c0 + 64],
                                                                   rhs=qT[0:64, 2 * h + 1, c0:c0 + 64], start=False, stop=True), R=["kT", "qT"], W=["ps4"])
                            P.op("dve", lambda e: e.tensor_tensor(
                                out=SmT[:], in0=ps[0:64, 4, 256:512].rearrange("p (h l) -> p h l", h=4),
                                in1=self.cmask[:].unsqueeze(1).broadcast_to([64, 4, 64]), op=ALU.mult), R=["ps4", "cmask"], W=["SmT"])
                            for h in range(4):
                                bk = 5 + h // 2
                                P.op("pe", lambda e, h=h: e.matmul(Pv[:, h, 0:193], lhsT=qT[:, 2 * h, c0:c0 + 64], rhs=CAb[:, h, :],
                                                                   start=True, stop=False), R=["qT", "CAb"], W=[f"ps{bk}"])
                                P.op("pe", lambda e, h=h: e.matmul(Pv[:, h, 0:193], lhsT=qT[0:64, 2 * h + 1, c0:c0 + 64], rhs=CBb[:, h, :],
                                                                   start=False, stop=False), R=["qT", "CBb"], W=[f"ps{bk}"])
                                P.op("pe", lambda e, h=h: e.matmul(Pv[:, h, 0:193], lhsT=SmT[:, h, :], rhs=vaug[:, ck, h, :],
                                                                   start=False, stop=True), R=["SmT", "vaug"], W=[f"ps{bk}"])
                        P.cp(f"P{phase}")
                        for hp in range(2):
                            dA = ps[:, 7, :].rearrange("p (h e) -> p h e", h=2)
                            dB = ps[0:64, 3, :].rearrange("p (h e) -> p h e", h=2)
                            for hh in range(2):
                                h = 2 * hp + hh
                                P.op("pe", lambda e, h=h, hh=hh: e.matmul(dA[:, hh, 0:193], lhsT=k_tm[:, ck, h * 192:h * 192 + 128], rhs=vaug[:, ck, h, :],
                                                                          start=True, stop=True), R=["k_tm", "vaug"], W=["ps7"])
                                P.op("pe", lambda e, h=h, hh=hh: e.matmul(dB[:, hh, 0:193], lhsT=k_tm[:, ck, h * 192 + 128:(h + 1) * 192], rhs=vaug[:, ck, h, :],
                                                                          start=True, stop=True), R=["k_tm", "vaug"], W=["ps3"])
                            rA = ["CA"] + (["CAb"] if main else [])
                            P.op("dve", lambda e, hp=hp, dA=dA: e.tensor_tensor(out=CA[:, 2 * hp:2 * hp + 2, :], in0=CA[:, 2 * hp:2 * hp + 2, :], in1=dA[:, :, 0:193], op=ALU.add),
                                 R=["CA", "ps7"], W=["CA"])
                            P.op("dve", lambda e, hp=hp, dB=dB: e.tensor_tensor(out=CB[:, 2 * hp:2 * hp + 2, :], in0=CB[:, 2 * hp:2 * hp + 2, :], in1=dB[:, :, 0:193], op=ALU.add),
                                 R=["CB", "ps3"], W=["CB"])

                    def back_a(ck):
                        abc = gtm[:, ck, 1, :].unsqueeze(2).broadcast_to([64, 4, 193])
                        P.op("dve", lambda e: e.tensor_tensor(out=num[:], in0=Pv[:, :, 0:193], in1=abc, op=ALU.mult), R=["ps5", "ps6", "gtm"], W=["num"])

                    def back_b(ck):
                        nonlocal mixi
                        c0 = ck * 64
                        P.op("act", lambda e: e.activation(out=sm[:, 0, :], in_=num[:, :, 192], func=AF.Abs), R=["num"], W=["sm0"])
                        P.op("dve", lambda e: e.tensor_tensor(out=sm[:, 1, :], in0=sm[:, 0, :], in1=gtm[:, ck, 2, :], op=ALU.max), R=["sm0", "gtm"], W=["sm1"])
                        P.op("dve", lambda e: e.reciprocal(out=sm[:, 2, :], in_=sm[:, 1, :]), R=["sm1"], W=["sm2"])
                        P.op("dve", lambda e: e.tensor_tensor(out=hr[:], in0=num[:, :, 0:192], in1=sm[:, 2, :].unsqueeze(2).broadcast_to([64, 4, 192]), op=ALU.mult),
                             R=["num", "sm2"], W=["hr"])
                        P.cp("hr")
                        P.op("dve", lambda e: e.tensor_reduce(out=sm[:, 3, :], in_=hr[:], axis=AX.X, op=ALU.add), R=["hr"], W=["sm3"])
                        P.op("pool", lambda e: e.tensor_tensor(out=sq[:], in0=hr[:], in1=hr[:], op=ALU.mult), R=["hr"], W=["sq"])
                        P.op("dve", lambda e: e.tensor_reduce(out=sm[:, 4, :], in_=sq[:], axis=AX.X, op=ALU.add), R=["sq"], W=["sm4"])
                        P.op("dve", lambda e: e.tensor_scalar(out=sm[:, 3, :], in0=sm[:, 3, :], scalar1=1.0 / DH, scalar2=None, op0=ALU.mult), R=["sm3"], W=["sm3"])
                        P.op("dve", lambda e: e.tensor_tensor(out=sm[:, 5, :], in0=sm[:, 3, :], in1=sm[:, 3, :], op=ALU.mult), R=["sm3"], W=["sm5"])
                        P.op("dve", lambda e: e.scalar_tensor_tensor(out=sm[:, 6, :], in0=sm[:, 4, :], scalar=1.0 / DH, in1=sm[:, 5, :], op0=ALU.mult, op1=ALU.subtract),
                             R=["sm4", "sm5"], W=["sm6"])
                        P.op("act", lambda e: e.activation(out=sm[:, 6, :], in_=sm[:, 6, :], func=AF.Sqrt, bias=EPS), R=["sm6"], W=["sm6"])
                        P.op("dve", lambda e: e.reciprocal(out=sm[:, 7, :], in_=sm[:, 6, :]), R=["sm6"], W=["sm7"])
                        P.op("dve", lambda e: e.tensor_tensor(out=hr[:], in0=hr[:], in1=sm[:, 3, :].unsqueeze(2).broadcast_to([64, 4, 192]), op=ALU.subtract),
                             R=["hr", "sm3"], W=["hr"])
                        P.op("dve", lambda e: e.tensor_tensor(out=hr[:], in0=hr[:], in1=sm[:, 7, :].unsqueeze(2).broadcast_to([64, 4, 192]), op=ALU.mult),
                             R=["hr", "sm7"], W=["hr"])
                        hf = hr[:].rearrange("p h d -> p (h d)")
                        P.op("pool", lambda e: e.tensor_tensor(out=hf, in0=hf, in1=ngb[:], op=ALU.mult), R=["hr", "ngb"], W=["hr"])
                        mx_ = mix[mixi]; mk = f"mix{mixi}"; mixi ^= 1
                        P.op("pool", lambda e, mx_=mx_: e.tensor_tensor(out=mx_[:, 0:768], in0=hf, in1=so[:, ck, :], op=ALU.mult), R=["hr", "so"], W=[mk])
                        P.cp("hln")
                        self.xattn_chunk(kxT, vxaug, xqT, c0, 64, mx_[:, 768:1024], mk, Esb, rsb)
                        P.cp("xattn")
                        pb = ps[:, 2, :].bitcast(BF16)
                        for kc in range(8):
                            P.op("pe", lambda e, kc=kc, mx_=mx_: e.transpose(out=pb[:, kc * 64:(kc + 1) * 64], in_=mx_[:, kc * 128:(kc + 1) * 128],
                                                                             identity=self.identb[0:64, 0:64]), R=[mk, "identb"], W=["ps2"])
                        P.op("act", lambda e: e.copy(out=mixT[:, :, c0:c0 + 64], in_=pb[:, 0:512].rearrange("p (k t) -> p k t", t=64)), R=["ps2"], W=["mixT"])


                    if not main:
                        for ck in range(4):
                            front(ck)
                    else:
                        front(0)
                        for ck in range(4):
                            back_a(ck)
                            if ck + 1 < 4:
                                front(ck + 1)
                            back_b(ck)
                    P.cp(f"chunks{phase}")
                    if main:
                        self.outproj_ln(sched, mixT, g, lng, lnb, xs, st, ag)
            P.barrier()

    def finish(self, out_name="out"):
        P = self.P
        d_out = self.outp(out_name, [T, D])
        for q4 in range(4):
            P.dma("sp", d_out[512 * q4:512 * (q4 + 1), :].rearrange("(t p) d -> p t d", p=128),
                  self.X[:, 4 * q4:4 * q4 + 4, :], R=[f"X{t}" for t in range(4 * q4, 4 * q4 + 4)], chan="out")
        P.wait_all("sp")


def _consts():
    ident = np.eye(128, dtype=np.float32)
    cmask = np.triu(np.ones((64, 64), np.float32))
    return ident, cmask


def build(stages, limit=None, stop_at=None, **kw):
    es = ExitStack()
    b = Builder(es)
    b.P.limit = limit
    b.P.stop_at = stop_at
    b.setup()
    if "fused" in stages:
        P = b.P
        b.l0_mixer()
        b.peer(0)
        cc1src = b.scratch("cc1src", [4, D], F32); cc1dst = b.scratch("cc1dst", [8, D], F32)
        b.cc2src = b.scratch("cc2src", [RB, 8], F32); cc2dst = b.scratch("cc2dst", [2 * RB, 8], F32)
        P.dma("sp", cc1src[0:3], b.X[125:128, NT - 1, :], R=[f"X{NT - 1}"], W=["cc1src"])
        P.dma("sp", cc1src[3:4], b.X[127:128, NT - 1, :], R=[f"X{NT - 1}"], W=["cc1src"])
        P.barrier()
        P.allgather_pairs(cc1src, cc1dst, R=["cc1src"], W=["cc_halo"])
        P.barrier()
        b.l1_cast()
        if DBG.get("early_cast1", True):
            b.peer_cast(1)
        b.l1(scan_only=True, halo_src=cc1dst[0:3])
        P.allgather_pairs(b.cc2src, cc2dst, R=["cc2src"], W=["cc_hend"])
        P.barrier(pool_dma=False)
        b.l1(scan_only=False, halo_src=cc1dst[0:3], hinit_src=cc2dst[0:RB])
        b.peer(1)
        b.finish()
        return b, es
    if "l0mix" in stages:
        b.l0_mixer(**{k: v for k, v in kw.items() if k in ("npre", "nmain")})
    if "peer0" in stages:
        b.peer(0, **{k: v for k, v in kw.items() if k == "ngroups"})
    if "l1scan" in stages:
        b.l1(scan_only=True)
    if "l1mix" in stages:
        b.l1(scan_only=False, **{k: v for k, v in kw.items() if k == "ngroups"})
    if "peer1" in stages:
        b.peer(1, **{k: v for k, v in kw.items() if k == "ngroups"})
    b.finish()
    return b, es


def core_inputs(inputs, stages, x_cur=None, hinit=None, halo=None):
    ident, cmask = _consts()
    x = inputs["x"] if x_cur is None else x_cur
    maps = []
    shared = {"ident": ident, "cmask": cmask}
    if "fused" in stages:
        stages = ["fused", "l0mix", "peer0", "l1scan", "l1mix", "peer1"]
    if "l0mix" in stages:
        shared["w0"] = prep_l0_weights(inputs["mlstm_w_in"][0], inputs["w_out"][0]).reshape(NB0, 128, 4096)
        shared["bgi"] = np.ascontiguousarray(inputs["mlstm_b_gates"][0, 0:4].reshape(4, 1))
        shared["bgf"] = np.ascontiguousarray(inputs["mlstm_b_gates"][0, 4:8].reshape(4, 1))
        shared["norm_g"] = np.ascontiguousarray(inputs["mlstm_norm_g"][0])
        shared["ln1g0"] = np.ascontiguousarray(inputs["ln1_g"][0]); shared["ln1b0"] = np.ascontiguousarray(inputs["ln1_b"][0])
        shared["wkv0"] = _blockify(np.ascontiguousarray(inputs["xattn_w_kv"][0]))
    for l in (0, 1):
        if f"peer{l}" in stages:
            ut, wqb, skT = prep_peer(inputs["peer_u"][l], inputs["peer_w_q"][l], inputs["peer_subkeys"][l])
            shared[f"ut{l}"] = ut; shared[f"wq{l}"] = wqb; shared[f"skT{l}"] = skT
            shared[f"pv{l}"] = np.ascontiguousarray(inputs["peer_v"][l]).reshape(128, 128, 1024)
            shared[f"ln2g{l}"] = np.ascontiguousarray(inputs["ln2_g"][l]); shared[f"ln2b{l}"] = np.ascontiguousarray(inputs["ln2_b"][l])
    if "l1scan" in stages or "l1mix" in stages:
        shared["w1"] = prep_l1_weights(inputs["rglru_w_in"][0], inputs["w_out"][1])
        shared["conv_w"] = np.ascontiguousarray(inputs["rglru_conv_w"][0].reshape(4, 8, 96).transpose(2, 1, 0))
        shared["conv_b"] = _chan(inputs["rglru_conv_b"][0]); shared["rg_ba"] = _chan(inputs["rglru_b_a"][0])
        shared["rg_bx"] = _chan(inputs["rglru_b_x"][0]); shared["rg_lam"] = _chan(inputs["rglru_lam"][0])
        shared["rg_wa"] = np.ascontiguousarray(inputs["rglru_w_a"][0].transpose(1, 0, 2))
        shared["rg_wx"] = np.ascontiguousarray(inputs["rglru_w_x"][0].transpose(1, 0, 2))
        if "l1mix" in stages:
            shared["ln1g1"] = np.ascontiguousarray(inputs["ln1_g"][1]); shared["ln1b1"] = np.ascontiguousarray(inputs["ln1_b"][1])
            shared["wkv1"] = _blockify(np.ascontiguousarray(inputs["xattn_w_kv"][1]))
    for c in range(8):
        bi, half = c // 2, c % 2
        m = dict(shared)
        if "l1scan" in stages or "l1mix" in stages:
            m["xhalo"] = np.ascontiguousarray(x[bi, T - 3:T]) if half else np.zeros((3, D), np.float32)
            m["hinit"] = np.zeros((RB, 8), np.float32) if hinit is None else np.ascontiguousarray(hinit[c])
            m["memT"] = np.ascontiguousarray(inputs["mem"][bi].T.reshape(8, 128, 256).transpose(1, 0, 2))
        m["x_own"] = np.ascontiguousarray(x[bi, half * T:(half + 1) * T])
        m["flag"] = np.full((128, 1), float(half), np.float32)
        if "l0mix" in stages:
            m["x_pre"] = np.ascontiguousarray(x[bi, 0:T]) if half else np.zeros((T, D), np.float32)
            m["memT"] = np.ascontiguousarray(inputs["mem"][bi].T.reshape(8, 128, 256).transpose(1, 0, 2))
        maps.append(m)
    return maps


DBG = {}
def prep_peer(u, wq, sk):
    ut = np.ascontiguousarray(u.reshape(128, 128, 8, 128).transpose(0, 3, 2, 1)).reshape(128, 128, 1024)
    wqb = np.stack([_blockify(np.ascontiguousarray(wq[:, b * 512:(b + 1) * 512])) for b in range(4)]).reshape(4, 128, 4096)
    skT = np.ascontiguousarray(sk.transpose(2, 0, 1))
    return ut, wqb, skT


def _peer_method(self, layer, ngroups=NG):
    nc, P, ps, sb = self.nc, self.P, self.ps, self.sb
    TN = 4
    with ExitStack() as es:
        S = lambda n, sh, dt=F32: sb(f"{n}_p{layer}", sh, dt, es=es)
        self.wblk = [S("wblk0", [128, 8, 512], BF16)]
        d_sk = self.inp(f"skT{layer}", [128, 2, 128])
        d_lng = self.inp(f"ln2g{layer}", [D]); d_lnb = self.inp(f"ln2b{layer}", [D])
        if layer not in self.peer_casts:
            self.peer_cast(layer)
        utb, vb, wqb = self.peer_casts[layer]
        if DBG.get("cast_barrier", True):
            P.barrier()
        GT = S("GT", [128, 128, G], BF16)
        xTg = S("xTg", [128, 8, G], BF16); qT = S("qT", [128, 16, G], BF16)
        ublk = [S(f"ublk{k}", [128, 1024], BF16) for k in range(4)]
        vblk = [S(f"vblk{k}", [128, 1024], BF16) for k in range(4)]
        arA = S("arA", [128, 1024]); arB = S("arB", [128, 1024])
        s_sb = arA[:, 0:512].rearrange("p (j k) -> p j k", k=128); s2 = arA[:, 512:1024].rearrange("p (j k) -> p j k", k=128)
        eq = arA[:, 0:512].rearrange("p (h r a) -> p h r a", r=16, a=16); prod = arA[:, 512:1024].rearrange("p (h r a) -> p h r a", r=16, a=16)
        cand = arB[:, 0:512].rearrange("p (h a b) -> p h a b", a=16, b=16); cand2 = arB[:, 512:1024].rearrange("p (h a b) -> p h a b", a=16, b=16)
        sv = S("sv", [128, 16, 16]); si = S("si", [128, 16, 16], U32); sif = S("sif", [128, 16, 16])
        fv = S("fv", [128, 8, 16]); fp = S("fp", [128, 8, 16], U32); pf = S("pf", [128, 8, 16]); bfl = S("bfl", [128, 8, 16]); af = S("af", [128, 8, 16])
        ex = S("ex", [128, 8, 16]); zs = S("zs", [128, 8]); tri = S("tri", [128, 3, 128])
        hr3 = S("hr3", [128, 3, G], BF16)
        Bt = [S("Bt0", [128, TN, 128], BF16), S("Bt1", [128, TN, 128], BF16)]
        At = [S("At0", [128, TN, 128], BF16), S("At1", [128, TN, 128], BF16)]
        eqt = S("eqt", [128, TN, 128], BF16)
        ge = [S("ge0", [128, G], BF16), S("ge1", [128, G], BF16)]; coef = [S("coef0", [128, G], BF16), S("coef1", [128, G], BF16)]
        skT32 = S("skT32", [128, 2, 128]); skTb = S("skTb", [128, 2, 128], BF16)
        iot32 = S("iot32", [128, 128]); iot = S("iot", [128, 128], BF16); iot16 = S("iot16", [128, 16])
        lng = S("lng", [128, D]); lnb = S("lnb", [128, D]); st = S("st", [128, 2, 6]); ag = S("ag", [128, 4])
        P.dma("sp", skT32[:], d_sk, W=["skT32"])
        P.op("dve", lambda e: e.tensor_copy(out=skTb[:], in_=skT32[:]), R=["skT32"], W=["skTb"])
        P.op("pool", lambda e: e.iota(iot32[:], pattern=[[1, 128]], base=0, channel_multiplier=0, allow_small_or_imprecise_dtypes=True), W=["iot32"])
        P.op("dve", lambda e: e.tensor_copy(out=iot[:], in_=iot32[:]), R=["iot32"], W=["iot"])
        P.op("pool", lambda e: e.iota(iot16[:], pattern=[[1, 16]], base=0, channel_multiplier=0, allow_small_or_imprecise_dtypes=True), W=["iot16"])
        P.dma("sp", lng[:], d_lng.partition_broadcast(128), W=["lng"])
        P.dma("sp", lnb[:], d_lnb.partition_broadcast(128), W=["lnb"])
        def load_tb(i):
            b = i % 4
            P.dma("sp", ublk[b][:], utb[i], R=[f"utb{layer}_{i // 4}"], W=[f"ublk{b}"])
            P.dma("sp", vblk[b][:], vb[i], R=[f"vb{layer}_{i // 4}"], W=[f"vblk{b}"])

        xTgs = [xTg, S("xTgB", [128, 8, G], BF16)]

        def phaseA1(g):
                xTg = xTgs[g % 2]; xk = f"xTg{g % 2}"
                self.make_xT(xTg, [self.X[:, 2 * g + tt, :] for tt in range(2)], [f"X{2 * g + tt}" for tt in range(2)], xk)
                sched = self.Sched(self, [(wqb[b], f"wqb{layer}_{b}") for b in range(4)])
                for blk in range(4):
                    wi = sched.get()
                    for j in range(4):
                        bk = 4 + (j % 2)
                        for kc in range(8):
                            P.op("pe", lambda e, j=j, kc=kc, bk=bk, wi=wi: e.matmul(
                                ps[:, bk, 0:G], lhsT=self.wblk[wi][:, kc, j * 128:(j + 1) * 128], rhs=xTg[:, kc, :],
                                start=(kc == 0), stop=(kc == 7)), R=[xk, f"wblk{wi}"], W=[f"ps{bk}"])
                        P.op("act", lambda e, j=j, bk=bk, blk=blk: e.copy(out=qT[:, blk * 4 + j, :], in_=ps[:, bk, 0:G]), R=[f"ps{bk}"], W=["qT"])

        def phaseA2(g):
            xTg = xTgs[g % 2]
            for tt in range(2):
                tok = slice(tt * 128, (tt + 1) * 128)
                for hh in range(4):
                    bk = 6 + (hh % 2)
                    for j4 in range(4):
                        jj = hh * 4 + j4
                        P.op("pe", lambda e, jj=jj, j4=j4, bk=bk: e.matmul(
                            ps[:, bk, j4 * 128:(j4 + 1) * 128], lhsT=qT[:, jj, tok], rhs=skTb[:, jj % 2, :],
                            start=True, stop=True), R=["qT", "skTb"], W=[f"ps{bk}"])
                    P.op("act", lambda e, bk=bk: e.copy(out=s_sb, in_=ps[:, bk, :].rearrange("p (j k) -> p j k", k=128)),
                         R=[f"ps{bk}"], W=["arA"])
                    yield
                    for j4 in range(4):
                        jj = hh * 4 + j4
                        P.op("dve", lambda e, jj=jj, j4=j4: e.max(out=sv[:, jj, 0:8], in_=s_sb[:, j4, :]), R=["arA"], W=["sv"])
                        P.op("dve", lambda e, jj=jj, j4=j4: e.max_index(out=si[:, jj, 0:8], in_max=sv[:, jj, 0:8], in_values=s_sb[:, j4, :]), R=["arA", "sv"], W=["si"])
                        P.op("dve", lambda e, jj=jj, j4=j4: e.match_replace(out=s2[:, j4, :], in_to_replace=sv[:, jj, 0:8], in_values=s_sb[:, j4, :], imm_value=-1e30),
                             R=["arA", "sv"], W=["arA"])
                        P.op("dve", lambda e, jj=jj, j4=j4: e.max(out=sv[:, jj, 8:16], in_=s2[:, j4, :]), R=["arA"], W=["sv"])
                        P.op("dve", lambda e, jj=jj, j4=j4: e.max_index(out=si[:, jj, 8:16], in_max=sv[:, jj, 8:16], in_values=s2[:, j4, :]), R=["arA", "sv"], W=["si"])
                        yield
                P.op("dve", lambda e: e.tensor_copy(out=sif[:], in_=si[:]), R=["si"], W=["sif"])
                svv = sv[:].rearrange("p (h two) a -> p h two a", two=2)
                sfv = sif[:].rearrange("p (h two) a -> p h two a", two=2)
                for hq in range(4):
                    hs = slice(2 * hq, 2 * hq + 2)
                    P.op("dve", lambda e: e.tensor_tensor(out=cand, in0=svv[:, hs, 0, :].unsqueeze(3).broadcast_to([128, 2, 16, 16]),
                                                          in1=svv[:, hs, 1, :].unsqueeze(2).broadcast_to([128, 2, 16, 16]), op=ALU.add), R=["sv"], W=["arB"])
                    for h4 in range(2):
                        h = 2 * hq + h4
                        c1 = cand[:, h4].rearrange("p a b -> p (a b)"); c2 = cand2[:, h4].rearrange("p a b -> p (a b)")
                        P.op("dve", lambda e: e.max(out=fv[:, h, 0:8], in_=c1), R=["arB"], W=["fv"])
                        P.op("dve", lambda e: e.max_index(out=fp[:, h, 0:8], in_max=fv[:, h, 0:8], in_values=c1), R=["arB", "fv"], W=["fp"])
                        P.op("dve", lambda e: e.match_replace(out=c2, in_to_replace=fv[:, h, 0:8], in_values=c1, imm_value=-1e30), R=["arB", "fv"], W=["arB"])
                        P.op("dve", lambda e: e.max(out=fv[:, h, 8:16], in_=c2), R=["arB"], W=["fv"])
                        P.op("dve", lambda e: e.max_index(out=fp[:, h, 8:16], in_max=fv[:, h, 8:16], in_values=c2), R=["arB", "fv"], W=["fp"])
                        yield
                gwv = tri[:, 2, :].rearrange("p (h r) -> p h r", r=16)
                P.op("dve", lambda e: e.tensor_tensor(out=ex[:], in0=fv[:], in1=fv[:, :, 0:1].broadcast_to([128, 8, 16]), op=ALU.subtract), R=["fv"], W=["ex"])
                P.op("act", lambda e: e.activation(out=ex[:], in_=ex[:], func=AF.Exp), R=["ex"], W=["ex"])
                P.op("dve", lambda e: e.tensor_reduce(out=zs[:], in_=ex[:], axis=AX.X, op=ALU.add), R=["ex"], W=["zs"])
                P.op("dve", lambda e: e.reciprocal(out=zs[:], in_=zs[:]), R=["zs"], W=["zs"])
                P.op("dve", lambda e: e.tensor_tensor(out=gwv, in0=ex[:], in1=zs[:].unsqueeze(2).broadcast_to([128, 8, 16]), op=ALU.mult), R=["ex", "zs"], W=["tri2"])
                yield
                P.op("dve", lambda e: e.tensor_single_scalar(out=pf[:].bitcast(U32), in_=fp[:], scalar=15, op=ALU.bitwise_and), R=["fp"], W=["pf"])
                P.op("dve", lambda e: e.tensor_copy(out=bfl[:], in_=pf[:].bitcast(U32)), R=["pf"], W=["bfl"])
                P.op("dve", lambda e: e.tensor_single_scalar(out=pf[:].bitcast(U32), in_=fp[:], scalar=4, op=ALU.logical_shift_right), R=["fp", "bfl"], W=["pf"])
                P.op("dve", lambda e: e.tensor_copy(out=af[:], in_=pf[:].bitcast(U32)), R=["pf"], W=["af"])
                yield
                i16 = iot16[:].unsqueeze(1).unsqueeze(1).broadcast_to([128, 2, 16, 16])
                for which, (srcidx, two) in enumerate(((af, 0), (bfl, 1))):
                    ov = tri[:, which, :].rearrange("p (h r) -> p h r", r=16)
                    for hq in range(4):
                        hs = slice(2 * hq, 2 * hq + 2)
                        P.op("dve", lambda e: e.tensor_tensor(out=eq, in0=srcidx[:, hs, :].unsqueeze(3).broadcast_to([128, 2, 16, 16]), in1=i16, op=ALU.is_equal),
                             R=["af", "bfl", "iot16"], W=["arA"])
                        P.op("dve", lambda e: e.tensor_tensor(out=prod, in0=eq, in1=sfv[:, hs, two, :].unsqueeze(2).broadcast_to([128, 2, 16, 16]), op=ALU.mult),
                             R=["arA", "sif"], W=["arA"])
                        P.op("dve", lambda e: e.tensor_reduce(out=ov[:, hs, :], in_=prod, axis=AX.X, op=ALU.add), R=["arA"], W=[f"tri{which}"])
                        yield
                for q3 in range(3):
                    P.op("pe", lambda e, q3=q3: e.transpose(out=ps[:, 7, q3 * 128:(q3 + 1) * 128], in_=tri[:, q3, :], identity=self.ident[:]),
                         R=[f"tri{q3}", "ident"], W=["ps7"])
                P.op("act", lambda e: e.copy(out=hr3[:, :, tok], in_=ps[:, 7, 0:384].rearrange("p (q t) -> p q t", q=3)), R=["ps7"], W=["hr3"])

            yield

        phaseA1(0)
        for _ in phaseA2(0):
            pass
        for g in range(ngroups):
            xTg = xTgs[g % 2]; xk = f"xTg{g % 2}"
            P.cp(f"peerA{layer}")
            ib = iot[:].unsqueeze(1).broadcast_to([128, TN, 128])
            for tb in range(G // TN):
                t0 = tb * TN
                bi = tb % 2
                P.op("dve", lambda e: e.tensor_tensor(out=Bt[bi][:], in0=ib, in1=hr3[:, 1, t0:t0 + TN].unsqueeze(2).broadcast_to([128, TN, 128]), op=ALU.is_equal),
                     R=["iot", "hr3"], W=[f"Bt{bi}"])
                P.op("dve", lambda e: e.tensor_tensor(out=eqt[:], in0=ib, in1=hr3[:, 0, t0:t0 + TN].unsqueeze(2).broadcast_to([128, TN, 128]), op=ALU.is_equal),
                     R=["iot", "hr3"], W=["eqt"])
                P.op(DBG.get("at_eng", "dve"), lambda e: e.tensor_tensor(out=At[bi][:], in0=eqt[:], in1=hr3[:, 2, t0:t0 + TN].unsqueeze(2).broadcast_to([128, TN, 128]), op=ALU.mult),
                     R=["eqt", "hr3"], W=[f"At{bi}"])
                for t in range(TN):
                    tg = t0 + t
                    bk = (tg // 4) % 8
                    P.op("pe", lambda e, t=t, tg=tg, bk=bk: e.matmul(ps[:, bk, (tg % 4) * 128:(tg % 4 + 1) * 128], lhsT=Bt[bi][:, t, :], rhs=At[bi][:, t, :],
                                                                     start=True, stop=True), R=[f"Bt{bi}", f"At{bi}"], W=[f"ps{bk}"])
                if (t0 + TN) % 16 == 0 and not DBG.get("noevac"):
                    b0 = ((t0 + TN - 16) // 4) % 8
                    tq = t0 + TN - 16
                    P.op("act", lambda e: e.copy(out=GT[:, :, tq:tq + 16], in_=ps[:, b0:b0 + 4, :].rearrange("p b (t i) -> p i (b t)", t=4)),
                         R=[f"ps{b}" for b in range(b0, b0 + 4)], W=["GT"])
            P.cp(f"peerB{layer}")
            def emit_ht(i):
                hb = 4 + (i % 2)
                for kc in range(8):
                    P.op("pe", lambda e, kc=kc: e.matmul(ps[:, hb, 0:G], lhsT=ublk[i % 4][:, kc * 128:(kc + 1) * 128], rhs=xTg[:, kc, :],
                                                         start=(kc == 0), stop=(kc == 7)), R=[f"ublk{i % 4}", xk], W=[f"ps{hb}"])

            def emit_rest(i):
                hb = 4 + (i % 2)
                gi = i % 2
                P.op("act", lambda e: e.activation(out=ge[gi][:], in_=ps[:, hb, 0:G], func=AF.Gelu), R=[f"ps{hb}"], W=[f"ge{gi}"])
                P.op("dve", lambda e: e.tensor_tensor(out=coef[gi][:], in0=ge[gi][:], in1=GT[:, i, :], op=ALU.mult), R=[f"ge{gi}", "GT"], W=[f"coef{gi}"])
                for tt in range(2):
                    for half in range(2):
                        yb = 2 * tt + half
                        P.op("pe", lambda e, tt=tt, half=half, yb=yb: e.matmul(
                            ps[:, yb, :], lhsT=coef[gi][:, tt * 128:(tt + 1) * 128], rhs=vblk[i % 4][:, half * 512:(half + 1) * 512],
                            start=(i == 0), stop=(i == 127)), R=[f"coef{gi}", f"vblk{i % 4}"], W=[f"ps{yb}"])

            genA = None
            if g + 1 < ngroups:
                phaseA1(g + 1)
                genA = phaseA2(g + 1)
            for k in range(3):
                load_tb(k)
            emit_ht(0)
            for i in range(128):
                if i + 1 < 128:
                    emit_ht(i + 1)
                emit_rest(i)
                if i + 3 < 128:
                    load_tb(i + 3)
                if genA is not None and i >= 4:
                    next(genA, None)
            if genA is not None:
                for _ in genA:
                    pass
            P.cp(f"peerC{layer}")
            for tt in range(2):
                self.resid_ln(2 * g + tt, ps[:, 2 * tt:2 * tt + 2, :], [f"ps{2 * tt}", f"ps{2 * tt + 1}"], lng, lnb, None, st, ag)
        P.barrier()


def _peer_cast(self, layer):
    P = self.P
    d_ut = self.inp(f"ut{layer}", [128, 128, 1024]); d_v = self.inp(f"pv{layer}", [128, 128, 1024])
    d_wq = self.inp(f"wq{layer}", [4, 128, 4096])
    utb = self.scratch(f"utb{layer}", [128, 128, 1024], BF16); vb = self.scratch(f"vb{layer}", [128, 128, 1024], BF16)
    wqb = self.scratch(f"wqb{layer}", [4, 128, 4096], BF16)
    for b in range(4):
        P.dma("pool", wqb[b].rearrange("p (a c) -> p a c", c=2048), d_wq[b].rearrange("p (a c) -> p a c", c=2048), W=[f"wqb{layer}_{b}"])
    for i0 in range(0, 128, 4):
        P.dma("pool", utb[i0:i0 + 4], d_ut[i0:i0 + 4], W=[f"utb{layer}_{i0 // 4}"])
        P.dma("pool", vb[i0:i0 + 4], d_v[i0:i0 + 4], W=[f"vb{layer}_{i0 // 4}"])
    self.peer_casts[layer] = (utb, vb, wqb)


Builder.peer = _peer_method
Builder.peer_cast = _peer_cast


NB1 = 9
RB = 96


def prep_l1_weights(w_in, w_out):
    gate = w_in[:, 0:768]; xr = w_in[:, 768:1536]; xq = w_in[:, 1536:1792]
    blocks = []
    for src in (gate, xr):
        for hb in range(2):
            blocks.append(np.concatenate([_pad_cols(src[:, g * 96:(g + 1) * 96], 128) for g in range(4 * hb, 4 * hb + 4)], axis=1))
    blocks.append(_pad_cols(xq, 512))
    out = [_blockify(np.ascontiguousarray(b, dtype=np.float32)).reshape(128, 4096) for b in blocks]
    for cb in range(4):
        blk = np.zeros((128, 10, 256), np.float32)
        for g in range(8):
            blk[0:96, g, :] = w_out[g * 96:(g + 1) * 96, cb * 256:(cb + 1) * 256]
        for c2 in range(2):
            blk[:, 8 + c2, :] = w_out[768 + c2 * 128:768 + (c2 + 1) * 128, cb * 256:(cb + 1) * 256]
        out.append(_pad_cols(blk.reshape(128, 2560), 4096))
    return np.stack(out)


def _chan(v):
    return np.ascontiguousarray(np.asarray(v, np.float32).reshape(8, 96).T)


def _l1_method(self, scan_only=False, ngroups=NG, halo_src=None, hinit_src=None):
    nc, P, ps, sb = self.nc, self.P, self.ps, self.sb
    ident = self.ident
    tag = "s" if scan_only else "m"
    with ExitStack() as es:
        S = lambda n, sh, dt=F32: sb(f"{n}_l1{tag}", sh, dt, es=es)
        if "w1" not in self.din:
            self.l1_cast()
        w1b = self.w1b
        g_in = lambda n, sh: self.din[n] if n in self.din else self.inp(n, sh)
        d_halo = halo_src if halo_src is not None else g_in("xhalo", [3, D])
        d_hinit = hinit_src if hinit_src is not None else (None if (scan_only and halo_src is not None) else g_in("hinit", [RB, 8]))
        d_cw = g_in("conv_w", [RB, 8, 4]); d_cb = g_in("conv_b", [RB, 8]); d_wa = g_in("rg_wa", [RB, 8, RB]); d_wx = g_in("rg_wx", [RB, 8, RB])
        d_ba = g_in("rg_ba", [RB, 8]); d_bx = g_in("rg_bx", [RB, 8]); d_lam = g_in("rg_lam", [RB, 8])
        self.wblk = [S("wblk0", [128, 8, 512], BF16), S("wblk1", [128, 8, 512], BF16)]
        xTg = S("xTg", [128, 8, G], BF16)
        xrb = S("xrb", [RB, 8, 3 + G]); xc = S("xc", [RB, 8, G]); xcb = S("xcb", [RB, 8, G], BF16)
        cw = S("cw", [RB, 8, 4]); cb = S("cb", [RB, 8]); wa32 = S("wa32", [RB, 8, RB]); wx32 = S("wx32", [RB, 8, RB])
        wab = S("wab", [RB, 8, RB], BF16); wxb = S("wxb", [RB, 8, RB], BF16)
        ba = S("ba", [RB, 8]); bx = S("bx", [RB, 8]); lamc = S("lamc", [RB, 8]); hcar = S("hcar", [RB, 8])
        rt = S("rt", [RB, 8, G]); it = S("it", [RB, 8, G]); at = S("at", [RB, 8, G]); ut = S("ut", [RB, 8, G]); ht = S("ht", [RB, 8, G]); tmp = S("tmp", [RB, 8, G])
        xh = S("xh", [3, D]); xTh = S("xTh", [128, 8, 4], BF16)
        for dst, src, k in ((cw, d_cw, "cw"), (cb, d_cb, "cb"), (wa32, d_wa, "wa32"), (wx32, d_wx, "wx32"), (ba, d_ba, "ba"), (bx, d_bx, "bx"),
                            (lamc, d_lam, "lamc"), (xh, d_halo, "xh")):
            P.dma("sp", dst[:], src, R=(["cc_halo"] if (k == "xh" and halo_src is not None) else []), W=[k])
        if d_hinit is None:
            P.op("dve", lambda e: e.memset(hcar[:], 0.0), W=["hcar"] + [f"hcar{k}" for k in range(8)])
        else:
            P.dma("sp", hcar[:], d_hinit, R=(["cc_hend"] if hinit_src is not None else []), W=["hcar"] + [f"hcar{k}" for k in range(8)])
        if halo_src is not None:
            P.op("dve", lambda e: e.tensor_scalar(out=xh[:], in0=xh[:], scalar1=self.flag[0:3, 0:1], scalar2=None, op0=ALU.mult), R=["xh", "flag"], W=["xh"])
        if hinit_src is not None:
            P.op("dve", lambda e: e.tensor_scalar(out=hcar[:], in0=hcar[:], scalar1=self.flag[0:RB, 0:1], scalar2=None, op0=ALU.mult), R=["hcar", "flag"], W=["hcar"] + [f"hcar{k}" for k in range(8)])
        P.op("dve", lambda e: e.tensor_copy(out=wab[:], in_=wa32[:]), R=["wa32"], W=["wab"])
        P.op("dve", lambda e: e.tensor_copy(out=wxb[:], in_=wx32[:]), R=["wx32"], W=["wxb"])
        P.op("act", lambda e: e.activation(out=lamc[:], in_=lamc[:], func=AF.Exp, scale=-1.0), R=["lamc"], W=["lamc"])
        P.op("act", lambda e: e.activation(out=lamc[:], in_=lamc[:], func=AF.Ln, bias=1.0), R=["lamc"], W=["lamc"])
        P.op("dve", lambda e: e.tensor_scalar(out=lamc[:], in0=lamc[:], scalar1=-8.0, scalar2=None, op0=ALU.mult), R=["lamc"], W=["lamc"])
        if not scan_only:
            d_lng = self.inp("ln1g1", [D]); d_lnb = self.inp("ln1b1", [D])
            kxT, vxaug = self.xattn_setup(1, es)
            gg = S("gg", [RB, 8, G], BF16); hmT = S("hmT", [RB, 8, G], BF16); xqT = S("xqT", [64, 4, G], BF16)
            hat = [S("hat0", [64, 256], BF16), S("hat1", [64, 256], BF16)]; haT = S("haT", [128, 2, G], BF16)
            Esb = S("Esb", [128, 8, 64], BF16); rsb = S("rsb", [64, 4])
            lng = S("lng", [128, D]); lnb = S("lnb", [128, D]); xs = S("xs", [128, D]); st = S("st", [128, 2, 6]); ag = S("ag", [128, 4])
            P.dma("sp", lng[:], d_lng.partition_broadcast(128), W=["lng"])
            P.dma("sp", lnb[:], d_lnb.partition_broadcast(128), W=["lnb"])
        srcs = [(w1b[b], f"w1b{b}") for b in (2, 3)]
        for g in range(ngroups):
            srcs += [(w1b[b], f"w1b{b}") for b in ((2, 3) if scan_only else (2, 3, 0, 1, 4, 5, 6, 7, 8))]
        sched = self.Sched(self, srcs)

        def xr_proj(wi, hb, rhs, n, dst_off, rkey):
            for j in range(4):
                g8 = 4 * hb + j
                bk = 2 + (j % 2)
                for kc in range(8):
                    P.op("pe", lambda e, j=j, kc=kc, bk=bk: e.matmul(
                        ps[0:RB, bk, 0:n], lhsT=self.wblk[wi][:, kc, j * 128:j * 128 + RB], rhs=rhs[:, kc, 0:n],
                        start=(kc == 0), stop=(kc == 7)), R=[rkey, f"wblk{wi}"], W=[f"ps{bk}"])
                P.op("act", lambda e, g8=g8, bk=bk: e.copy(out=xrb[:, g8, dst_off:dst_off + n], in_=ps[0:RB, bk, 0:n]), R=[f"ps{bk}"], W=["xrb"])

        for kc in range(8):
            P.op("pe", lambda e, kc=kc: e.transpose(out=ps[:, 0, kc * 4:kc * 4 + 3], in_=xh[0:3, kc * 128:(kc + 1) * 128], identity=ident[0:3, 0:3]),
                 R=["xh", "ident"], W=["ps0"])
        P.op("act", lambda e: e.copy(out=xTh[:, :, 0:3], in_=ps[:, 0, 0:32].rearrange("p (k c) -> p k c", c=4)[:, :, 0:3]), R=["ps0"], W=["xTh"])
        for hb in range(2):
            wi = sched.get()
            xr_proj(wi, hb, xTh, 3, 0, "xTh")
        mixi = 0
        for g in range(ngroups):
            self.make_xT(xTg, [self.X[:, 2 * g + tt, :] for tt in range(2)], [f"X{2 * g + tt}" for tt in range(2)], "xTg")
            for hb in range(2):
                wi = sched.get()
                xr_proj(wi, hb, xTg, G, 3, "xTg")
            if not scan_only:
                for hb in range(2):
                    wi = sched.get()
                    for j in range(4):
                        g8 = 4 * hb + j
                        bk = 2 + (j % 2)
                        for kc in range(8):
                            P.op("pe", lambda e, j=j, kc=kc, bk=bk: e.matmul(
                                ps[0:RB, bk, 0:G], lhsT=self.wblk[wi][:, kc, j * 128:j * 128 + RB], rhs=xTg[:, kc, :],
                                start=(kc == 0), stop=(kc == 7)), R=["xTg", f"wblk{wi}"], W=[f"ps{bk}"])
                        P.op("act", lambda e, g8=g8, bk=bk: e.activation(out=gg[:, g8, :], in_=ps[0:RB, bk, 0:G], func=AF.Gelu), R=[f"ps{bk}"], W=["gg"])
                wi = sched.get()
                for j in range(4):
                    bk = 2 + (j % 2)
                    for kc in range(8):
                        P.op("pe", lambda e, j=j, kc=kc, bk=bk: e.matmul(
                            ps[0:64, bk, 0:G], lhsT=self.wblk[wi][:, kc, j * 64:(j + 1) * 64], rhs=xTg[:, kc, :],
                            start=(kc == 0), stop=(kc == 7)), R=["xTg", f"wblk{wi}"], W=[f"ps{bk}"])
                    P.op("act", lambda e, j=j, bk=bk: e.copy(out=xqT[:, j, :], in_=ps[0:64, bk, 0:G]), R=[f"ps{bk}"], W=["xqT"])
            B8 = range(8)
            for w in range(4):
                for g8 in B8:
                    if w == 0:
                        P.op("dve", lambda e, g8=g8: e.tensor_scalar(out=xc[:, g8, :], in0=xrb[:, g8, 0:G], scalar1=cw[:, g8, 0:1], scalar2=cb[:, g8:g8 + 1],
                                                                     op0=ALU.mult, op1=ALU.add), R=["xrb", "cw", "cb"], W=[f"xc{g8}"])
                    else:
                        P.op("dve", lambda e, g8=g8, w=w: e.scalar_tensor_tensor(out=xc[:, g8, :], in0=xrb[:, g8, w:w + G], scalar=cw[:, g8, w:w + 1], in1=xc[:, g8, :],
                                                                                 op0=ALU.mult, op1=ALU.add), R=["xrb", "cw", f"xc{g8}"], W=[f"xc{g8}"])
            for g8 in B8:
                P.op("act", lambda e, g8=g8: e.copy(out=xcb[:, g8, :], in_=xc[:, g8, :]), R=[f"xc{g8}"], W=[f"xcb{g8}"])
            P.op("dve", lambda e: e.tensor_copy(out=xrb[:, :, 0:3], in_=xrb[:, :, G:G + 3]), R=["xrb"], W=["xrb"])
            for g8 in B8:
                pa, pb_ = 4 + 2 * (g8 % 2), 5 + 2 * (g8 % 2)
                P.op("pe", lambda e, g8=g8, pa=pa: e.matmul(ps[0:RB, pa, 0:G], lhsT=wab[:, g8, :], rhs=xcb[:, g8, :], start=True, stop=True), R=["wab", f"xcb{g8}"], W=[f"ps{pa}"])
                P.op("pe", lambda e, g8=g8, pb_=pb_: e.matmul(ps[0:RB, pb_, 0:G], lhsT=wxb[:, g8, :], rhs=xcb[:, g8, :], start=True, stop=True), R=["wxb", f"xcb{g8}"], W=[f"ps{pb_}"])
                P.op("act", lambda e, g8=g8, pa=pa: e.activation(out=rt[:, g8, :], in_=ps[0:RB, pa, 0:G], func=AF.Sigmoid, bias=ba[:, g8:g8 + 1]), R=[f"ps{pa}", "ba"], W=[f"rt{g8}"])
                P.op("act", lambda e, g8=g8, pb_=pb_: e.activation(out=it[:, g8, :], in_=ps[0:RB, pb_, 0:G], func=AF.Sigmoid, bias=bx[:, g8:g8 + 1]), R=[f"ps{pb_}", "bx"], W=[f"it{g8}"])
            for g8 in B8:
                P.op("act", lambda e, g8=g8: e.activation(out=at[:, g8, :], in_=rt[:, g8, :], func=AF.Exp, scale=lamc[:, g8:g8 + 1]), R=[f"rt{g8}", "lamc"], W=[f"at{g8}"])
            for g8 in B8:
                P.op("dve", lambda e, g8=g8: e.tensor_tensor(out=tmp[:, g8, :], in0=at[:, g8, :], in1=at[:, g8, :], op=ALU.mult), R=[f"at{g8}"], W=[f"tmp{g8}"])
            for g8 in B8:
                P.op("dve", lambda e, g8=g8: e.tensor_scalar(out=tmp[:, g8, :], in0=tmp[:, g8, :], scalar1=-1.0, scalar2=1.0, op0=ALU.mult, op1=ALU.add), R=[f"tmp{g8}"], W=[f"tmp{g8}"])
            for g8 in B8:
                P.op("dve", lambda e, g8=g8: e.tensor_tensor(out=ut[:, g8, :], in0=it[:, g8, :], in1=xc[:, g8, :], op=ALU.mult), R=[f"it{g8}", f"xc{g8}"], W=[f"ut{g8}"])
            for g8 in B8:
                P.op("act", lambda e, g8=g8: e.activation(out=tmp[:, g8, :], in_=tmp[:, g8, :], func=AF.Sqrt), R=[f"tmp{g8}"], W=[f"tmp{g8}"])
            for g8 in B8:
                P.op("dve", lambda e, g8=g8: e.tensor_tensor(out=ut[:, g8, :], in0=ut[:, g8, :], in1=tmp[:, g8, :], op=ALU.mult), R=[f"ut{g8}", f"tmp{g8}"], W=[f"ut{g8}"])
            for g8 in B8:
                P.op("dve", lambda e, g8=g8: e.tensor_tensor_scan(out=ht[:, g8, :], data0=at[:, g8, :], data1=ut[:, g8, :], initial=hcar[:, g8:g8 + 1], op0=ALU.mult, op1=ALU.add),
                     R=[f"at{g8}", f"ut{g8}", f"hcar{g8}"], W=[f"ht{g8}"])
            for g8 in B8:
                P.op("dve", lambda e, g8=g8: e.tensor_copy(out=hcar[:, g8:g8 + 1], in_=ht[:, g8, G - 1:G]), R=[f"ht{g8}"], W=[f"hcar{g8}"])
            if not scan_only:
                for g8 in B8:
                    P.op("dve", lambda e, g8=g8: e.tensor_tensor(out=hmT[:, g8, :], in0=ht[:, g8, :], in1=gg[:, g8, :], op=ALU.mult), R=[f"ht{g8}", "gg"], W=["hmT"])
            if scan_only:
                continue
            for ck in range(4):
                c0 = ck * 64
                hx = hat[mixi]; hk = f"hat{mixi}"; mixi ^= 1
                self.xattn_chunk(kxT, vxaug, xqT, c0, 64, hx[:, :], hk, Esb, rsb)
                pb = ps[:, 2, :].bitcast(BF16)
                for c2 in range(2):
                    P.op("pe", lambda e, c2=c2: e.transpose(out=pb[:, c2 * 64:(c2 + 1) * 64], in_=hx[:, c2 * 128:(c2 + 1) * 128], identity=self.identb[0:64, 0:64]),
                         R=[hk, "identb"], W=["ps2"])
                P.op("act", lambda e: e.copy(out=haT[:, :, c0:c0 + 64], in_=pb[:, 0:128].rearrange("p (k t) -> p k t", t=64)), R=["ps2"], W=["haT"])
            banks = {0: (0, 1), 1: (5, 6)}
            for cbk in range(4):
                wi = sched.get()
                wv = self.wblk[wi][:].rearrange("p k c -> p (k c)")[:, 0:2560].rearrange("p (ch c) -> p ch c", c=256)
                for tt in range(2):
                    bk = banks[tt][cbk // 2]
                    dst = ps[:, bk, (cbk % 2) * 256:(cbk % 2 + 1) * 256]
                    for ch in range(10):
                        if ch < 8:
                            lhsT = hmT[:, ch, tt * 128:(tt + 1) * 128]; rhs = wv[0:RB, ch, :]; rk = "hmT"
                        else:
                            lhsT = haT[:, ch - 8, tt * 128:(tt + 1) * 128]; rhs = wv[:, ch, :]; rk = "haT"
                        P.op("pe", lambda e, lhsT=lhsT, rhs=rhs, dst=dst, ch=ch: e.matmul(dst, lhsT=lhsT, rhs=rhs, start=(ch == 0), stop=(ch == 9)),
                             R=[rk, f"wblk{wi}"], W=[f"ps{bk}"])
            for tt in range(2):
                b0 = banks[tt][0]
                self.resid_ln(2 * g + tt, ps[:, b0:b0 + 2, :], [f"ps{b0}", f"ps{b0 + 1}"], lng, lnb, xs, st, ag)
        if scan_only:
            if halo_src is None:
                d_hend = self.outp("hend", [RB, 8])
                P.dma("sp", d_hend, hcar[:], R=[f"hcar{k}" for k in range(8)], chan="out")
            else:
                P.dma("sp", self.cc2src, hcar[:], R=[f"hcar{k}" for k in range(8)], W=["cc2src"])
        P.barrier(pool_dma=False)


def _l1_cast(self):
    P = self.P
    d_w1 = self.inp("w1", [NB1, 128, 4096])
    self.w1b = self.scratch("w1b", [NB1, 128, 4096], BF16)
    for b in range(NB1):
        P.dma("pool", self.w1b[b].rearrange("p (a c) -> p a c", c=2048), d_w1[b].rearrange("p (a c) -> p a c", c=2048), W=[f"w1b{b}"])
    P.barrier()


Builder.l1 = _l1_method
Builder.l1_cast = _l1_cast


def _run(stages, inputs, x_cur=None, hinit=None, **kw):
    b, es = build(stages, **kw)
    maps = core_inputs(inputs, stages, x_cur=x_cur, hinit=hinit)
    maps = [{k: v for k, v in m.items() if k in b.din} for m in maps]
    res = run_bass_kernel_spmd(b.nc, maps, core_ids=list(range(8)))
    es.close()
    return res.results


def _gather(results, name="out"):
    x = np.empty((4, 2 * T, D), np.float32)
    for c in range(8):
        x[c // 2, (c % 2) * T:(c % 2 + 1) * T] = results[c][name]
    return x


def kernel(**inputs):
    inputs = {k: np.asarray(v) for k, v in inputs.items()}
    return _gather(_run(["fused"], inputs))
```

```python
import numpy as np
from contextlib import ExitStack
import concourse.bass as bass
import concourse.mybir as mybir
from concourse.bass_utils import run_bass_kernel_spmd

F32 = mybir.dt.float32
BF16 = mybir.dt.bfloat16
U32 = mybir.dt.uint32
I32 = mybir.dt.int32
AF = mybir.ActivationFunctionType
ALU = mybir.AluOpType
AX = mybir.AxisListType


class Prog:
    NDS = 24

    def __init__(self, nc, es):
        self.nc = nc
        self.es = es
        self.eng = {"pe": nc.tensor, "dve": nc.vector, "act": nc.scalar, "pool": nc.gpsimd, "sp": nc.sync}
        self.sem = {k: es.enter_context(nc.semaphore("sem_" + k)) for k in self.eng}
        self.cnt = {k: 0 for k in self.eng}
        self.seen = {k: {} for k in self.eng}
        self.dsem = []
        self.dcnt = []
        self.dq = {}
        for q, n in (("sp", 20), ("pool", 8), ("act", 4)):
            self.dq[q] = [len(self.dsem) + i for i in range(n)]
            self.dsem += [es.enter_context(nc.semaphore(f"dsem_{q}{i}")) for i in range(n)]
            self.dcnt += [0] * n
        self.dnext = {q: 0 for q in self.dq}
        self.last_w = {}
        self.readers = {}
        self.n_ins = 0

    def _semof(self, kind, name):
        return self.sem[name] if kind == "e" else self.dsem[name]

    def _wait_ev(self, eng, kind, name, c):
        if self.seen[eng].get((kind, name), 0) >= c:
            return
        self.seen[eng][(kind, name)] = c
        self.eng[eng].wait_ge(self._semof(kind, name), c)

    def _wait(self, eng, R, W):
        deps = []
        for k in R:
            if k in self.last_w:
                deps.append(self.last_w[k])
        for k in W:
            if k in self.last_w:
                deps.append(self.last_w[k])
            deps.extend(self.readers.get(k, ()))
        best = {}
        for kind, name, c in deps:
            if kind == "e" and name == eng and eng == "pe":
                continue
            key = (kind, name)
            if c > best.get(key, 0):
                best[key] = c
        for (kind, name), c in best.items():
            self._wait_ev(eng, kind, name, c)

    def _commit(self, ev, R, W):
        for k in W:
            self.last_w[k] = ev
            self.readers[k] = []
        for k in R:
            self.readers.setdefault(k, []).append(ev)

    limit = None
    stop_at = None

    def cp(self, name):
        if self.stop_at is not None and name == self.stop_at and self.limit is None:
            self.limit = self.n_ins

    def op(self, eng, fn, R=(), W=()):
        if self.limit is not None and self.n_ins >= self.limit:
            return None
        W = list(W) + [k for k in R if k.startswith("ps") and k not in W]
        self._wait(eng, R, W)
        ins = fn(self.eng[eng])
        self.cnt[eng] += 1
        ins.then_inc(self.sem[eng], 1)
        self._commit(("e", eng, self.cnt[eng]), R, W)
        self.n_ins += 1
        return ins

    def dma(self, q, out, in_, R=(), W=(), chan=None):
        if self.limit is not None and self.n_ins >= self.limit and chan != "out":
            return None
        i = self.dq[q][self.dnext[q]]
        self.dnext[q] = (self.dnext[q] + 1) % len(self.dq[q])
        if self.dcnt[i]:
            self._wait_ev(q, "d", i, self.dcnt[i])
        self._wait(q, R, W)
        ins = self.eng[q].dma_start(out=out, in_=in_)
        self.dcnt[i] += 16
        ins.then_inc(self.dsem[i], 16)
        self._commit(("d", i, self.dcnt[i]), R, W)
        return ins

    def allgather_pairs(self, src, dst, R=(), W=()):
        self._wait("pool", R, W)
        sem = self.es.enter_context(self.nc.semaphore(f"cc_sem{len(self.dsem)}"))
        self.dsem.append(sem)
        self.dcnt.append(0)
        i = len(self.dsem) - 1
        ins = self.nc.gpsimd.collective_compute("AllGather", ALU.bypass, replica_groups=[[0, 1], [2, 3], [4, 5], [6, 7]],
                                               ins=[src.opt()], outs=[dst.opt()])
        ins.then_inc(sem)
        self.dcnt[i] = 1
        self._commit(("d", i, 1), R, W)

    def barrier(self, pool_dma=True):
        for e in self.eng:
            for k, c in self.cnt.items():
                if c:
                    self._wait_ev(e, "e", k, c)
            for i, c in enumerate(self.dcnt):
                if c and (pool_dma or i not in self.dq["pool"]):
                    self._wait_ev(e, "d", i, c)

    def wait_all(self, eng="sp"):
        for k, c in self.cnt.items():
            if c and k != eng:
                self._wait_ev(eng, "e", k, c)
        for i, c in enumerate(self.dcnt):
            if c:
                self._wait_ev(eng, "d", i, c)


D = 1024
T = 2048
NT = T // 128
G = 256
NG = T // G
H = 4
DH = 192
L = 64
ALPHA = float(4 ** 0.25)
EPS = 1e-5
NB0 = 14


def _pad_cols(a, n):
    out = np.zeros((a.shape[0], n), np.float32)
    out[:, : a.shape[1]] = a
    return out


def _blockify(cols):
    return np.ascontiguousarray(cols.reshape(8, 128, 512).transpose(1, 0, 2))


def prep_l0_weights(w_in, w_out):
    q = w_in[:, 0:768]; k = w_in[:, 768:1536]; v = w_in[:, 1536:2304]; o = w_in[:, 2304:3072]
    g = w_in[:, 3072:3080]; xq = w_in[:, 3080:3336]
    blocks = []
    for src in (q, k):
        for hp in range(2):
            cs = []
            for h in (2 * hp, 2 * hp + 1):
                cs.append(src[:, h * 192: h * 192 + 128])
                cs.append(_pad_cols(src[:, h * 192 + 128: (h + 1) * 192], 128))
            blocks.append(np.concatenate(cs, axis=1))
    blocks.append(_pad_cols(xq, 512))
    for src in (k, v, o):
        blocks.append(_pad_cols(src[:, 0:384], 512))
        blocks.append(_pad_cols(src[:, 384:768], 512))
    blocks.append(_pad_cols(g, 512))
    blocks.append(w_out[:, 0:512]); blocks.append(w_out[:, 512:1024])
    return np.stack([_blockify(np.ascontiguousarray(b, dtype=np.float32)) for b in blocks])


class Builder:
    def __init__(self, es, debug=()):
        self.es = es
        self.debug = set(debug)
        self.nc = nc = bass.Bass("TRN2", target_bir_lowering=False)
        self.P = Prog(nc, es)
        self.din = {}
        self.dout = {}
        self.wb_i = 0
        self.peer_casts = {}

    def inp(self, name, shape, dt=F32):
        ap = self.nc.dram_tensor(name, list(shape), dt, kind="ExternalInput").ap()
        self.din[name] = ap
        return ap

    def outp(self, name, shape, dt=F32):
        ap = self.nc.dram_tensor(name, list(shape), dt, kind="ExternalOutput").ap()
        self.dout[name] = ap
        return ap

    def scratch(self, name, shape, dt):
        return self.nc.dram_tensor(name, list(shape), dt, kind="Internal").ap()

    def sb(self, name, shape, dt=F32, es=None):
        return (es or self.es).enter_context(self.nc.sbuf_tensor(name + "_sb", list(shape), dt))

    def wload(self, src):
        i = self.wb_i % len(self.wblk)
        self.wb_i = (i + 1) % len(self.wblk)
        self.P.dma("sp", self.wblk[i][:].rearrange("p k c -> p (k c)"), src[0], R=[src[1]], W=[f"wblk{i}"], chan="w")
        return i

    class Sched:
        def __init__(self, b, srcs):
            self.b = b; self.srcs = srcs; self.n = 0; self.cur = None
            self.single = len(b.wblk) == 1
            self.nxt = b.wload(srcs[0]) if (srcs and not self.single) else None

        def get(self):
            if self.single:
                self.n += 1
                return self.b.wload(self.srcs[self.n - 1])
            cur = self.nxt
            self.n += 1
            self.nxt = self.b.wload(self.srcs[self.n]) if self.n < len(self.srcs) else None
            return cur

    def setup(self):
        nc, P = self.nc, self.P
        sb = self.sb
        self.ps = self.es.enter_context(nc.psum_tensor("ps", [128, 8, 512], F32))
        self.X = sb("X", [128, NT, D])
        self.ident = sb("ident", [128, 128])
        self.identb = sb("identb", [128, 128], BF16)
        self.cmask = sb("cmask", [64, 64], BF16)
        self.flag = sb("flag", [128, 1])
        d_ident = self.inp("ident", [128, 128])
        d_cmask = self.inp("cmask", [64, 64])
        d_flag = self.inp("flag", [128, 1])
        cm32 = sb("cm32", [64, 64])
        P.dma("sp", self.ident[:], d_ident, W=["ident"])
        P.dma("sp", cm32[:], d_cmask, W=["cm32"])
        P.dma("sp", self.flag[:], d_flag, W=["flag"])
        P.op("dve", lambda e: e.tensor_copy(out=self.identb[:], in_=self.ident[:]), R=["ident"], W=["identb"])
        P.op("dve", lambda e: e.tensor_copy(out=self.cmask[:], in_=cm32[:]), R=["cm32"], W=["cmask"])
        d_x = self.inp("x_own", [T, D])
        for q4 in range(4):
            P.dma("sp", self.X[:, 4 * q4:4 * q4 + 4, :],
                  d_x[512 * q4:512 * (q4 + 1), :].rearrange("(t p) d -> p t d", p=128),
                  W=[f"X{t}" for t in range(4 * q4, 4 * q4 + 4)])

    def psb(self, b0, nb=1):
        return [f"ps{b}" for b in range(b0, b0 + nb)]

    def make_xT(self, xTg, src_tiles, src_keys, kout, b0=0):
        P, ps = self.P, self.ps
        for tt in range(2):
            for kc in range(8):
                P.op("pe", lambda e, tt=tt, kc=kc: e.transpose(
                    out=ps[:, b0 + kc // 4, (kc % 4) * 128:(kc % 4 + 1) * 128],
                    in_=src_tiles[tt][:, kc * 128:(kc + 1) * 128], identity=self.ident[:]),
                    R=[src_keys[tt], "ident"], W=[f"ps{b0 + kc // 4}"])
            P.op("act", lambda e, tt=tt: e.copy(
                out=xTg[:, :, tt * 128:(tt + 1) * 128],
                in_=ps[:, b0:b0 + 2, :].rearrange("p b (k c) -> p (b k) c", c=128)),
                R=[f"ps{b0}", f"ps{b0 + 1}"], W=[kout])

    def xattn_setup(self, layer, es):
        P, ps, sb = self.P, self.ps, self.sb
        d_memT = self.din["memT"] if "memT" in self.din else self.inp("memT", [128, 8, 256])
        d_wkv = self.inp(f"wkv{layer}", [128, 8, 512])
        kxT = sb(f"kxT_{layer}", [64, 4, 256], BF16, es=es)
        vxaug = sb(f"vxaug_{layer}", [128, 2, 4, 65], BF16, es=es)
        with ExitStack() as tes:
            memT32 = sb(f"memT32_{layer}", [128, 8, 256], es=tes)
            wkv32 = sb(f"wkv32_{layer}", [128, 8, 512], es=tes)
            memTb = sb(f"memTb_{layer}", [128, 8, 256], BF16, es=tes)
            wkvb = sb(f"wkvb_{layer}", [128, 8, 512], BF16, es=tes)
            P.dma("sp", memT32[:], d_memT, W=["memT32"])
            P.dma("sp", wkv32[:], d_wkv, W=["wkv32"])
            P.op("dve", lambda e: e.tensor_copy(out=memTb[:], in_=memT32[:]), R=["memT32"], W=["memTb"])
            P.op("act", lambda e: e.copy(out=wkvb[:], in_=wkv32[:]), R=["wkv32"], W=["wkvb"])
            for h in range(4):
                for kc in range(8):
                    P.op("pe", lambda e, h=h, kc=kc: e.matmul(
                        ps[0:64, 2, 0:256], lhsT=wkvb[:, kc, h * 64:(h + 1) * 64], rhs=memTb[:, kc, :],
                        start=(kc == 0), stop=(kc == 7)), R=["wkvb", "memTb"], W=["ps2"])
                P.op("act", lambda e, h=h: e.copy(out=kxT[:, h, :], in_=ps[0:64, 2, 0:256]), R=["ps2"], W=["kxT"])
            P.op("dve", lambda e: e.memset(vxaug[:], 1.0), W=["vxaug"])
            for mc in range(2):
                for kc in range(8):
                    P.op("pe", lambda e, mc=mc, kc=kc: e.matmul(
                        ps[:, 3, 0:256], lhsT=memTb[:, kc, mc * 128:(mc + 1) * 128], rhs=wkvb[:, kc, 256:512],
                        start=(kc == 0), stop=(kc == 7)), R=["wkvb", "memTb"], W=["ps3"])
                P.op("act", lambda e, mc=mc: e.copy(
                    out=vxaug[:, mc, :, 0:64], in_=ps[:, 3, 0:256].rearrange("p (h d) -> p h d", h=4)),
                    R=["ps3"], W=["vxaug"])
            P.barrier(pool_dma=False)
        return kxT, vxaug

    def xattn_chunk(self, kxT, vxaug, xqT, c0, ntok, out_ap, out_key, Esb, rsb):
        P, ps = self.P, self.ps
        for h in range(4):
            for mc in range(2):
                P.op("pe", lambda e, h=h, mc=mc: e.matmul(
                    ps[:, 0, (h * 2 + mc) * 64:(h * 2 + mc) * 64 + ntok],
                    lhsT=kxT[:, h, mc * 128:(mc + 1) * 128],
                    rhs=xqT[:, h, c0:c0 + ntok], start=True, stop=True),
                    R=["kxT", "xqT"], W=["ps0"])
        P.cp("xa_st")
        P.op("act", lambda e: e.activation(
            out=Esb[:, :, 0:ntok], in_=ps[:, 0, :].rearrange("p (j t) -> p j t", t=64)[:, :, 0:ntok],
            func=AF.Exp, scale=0.125), R=["ps0"], W=["Esb"])
        P.cp("xa_exp")
        Ov = ps[0:ntok, 1, 0:260].rearrange("p (h d) -> p h d", d=65)
        for h in range(4):
            for mc in range(2):
                P.op("pe", lambda e, h=h, mc=mc: e.matmul(
                    Ov[:, h, :], lhsT=Esb[:, h * 2 + mc, 0:ntok], rhs=vxaug[:, mc, h, :],
                    start=(mc == 0), stop=(mc == 1)), R=["Esb", "vxaug"], W=["ps1"])
        P.cp("xa_ov")
        P.op("dve", lambda e: e.reciprocal(out=rsb[0:ntok, :], in_=Ov[:, :, 64]), R=["ps1"], W=["rsb"])
        P.op("dve", lambda e: e.tensor_tensor(
            out=out_ap.rearrange("p (h d) -> p h d", d=64), in0=Ov[:, :, 0:64],
            in1=rsb[0:ntok, :].unsqueeze(2).broadcast_to([ntok, 4, 64]), op=ALU.mult),
            R=["ps1", "rsb"], W=[out_key])

    def outproj_ln(self, sched, mixT, g, lng, lnb, xs, st, ag):
        P, ps = self.P, self.ps
        banks = {0: (0, 1), 1: (5, 6)}
        for half in range(2):
            wi = sched.get()
            for tt in range(2):
                bk = banks[tt][half]
                for kc in range(8):
                    P.op("pe", lambda e, tt=tt, kc=kc, bk=bk, wi=wi: e.matmul(
                        ps[:, bk, :], lhsT=mixT[:, kc, tt * 128:(tt + 1) * 128], rhs=self.wblk[wi][:, kc, :],
                        start=(kc == 0), stop=(kc == 7)), R=["mixT", f"wblk{wi}"], W=[f"ps{bk}"])
        P.cp("op_mm")
        for tt in range(2):
            t = 2 * g + tt
            b0 = banks[tt][0]
            self.resid_ln(t, self.ps[:, b0:b0 + 2, :], [f"ps{b0}", f"ps{b0 + 1}"], lng, lnb, xs, st, ag)

    def resid_ln(self, t, y_ap, y_keys, lng, lnb, xs, st, ag):
        P = self.P
        Xt = self.X[:, t, :]
        wk = Xt if xs is None else xs[:]
        kk = f"X{t}" if xs is None else "xs"
        P.op("dve", lambda e: e.scalar_tensor_tensor(
            out=wk.rearrange("p (b c) -> p b c", b=2), in0=Xt.rearrange("p (b c) -> p b c", b=2),
            scalar=ALPHA, in1=y_ap, op0=ALU.mult, op1=ALU.add), R=[f"X{t}"] + y_keys, W=[kk])
        P.cp("ln_a")
        for hf in range(2):
            P.op("dve", lambda e, hf=hf: e.bn_stats(out=st[:, hf, :], in_=wk[:, hf * 512:(hf + 1) * 512]),
                 R=[kk], W=[f"st{hf}"])
        P.op("dve", lambda e: e.bn_aggr(out=ag[:, 0:2], in_=st[:].rearrange("p a b -> p (a b)")),
             R=["st0", "st1"], W=["ag"])
        P.cp("ln_b")
        P.op("act", lambda e: e.activation(out=ag[:, 2:3], in_=ag[:, 1:2], func=AF.Sqrt, bias=EPS),
             R=["ag"], W=["ag2"])
        P.op("dve", lambda e: e.reciprocal(out=ag[:, 3:4], in_=ag[:, 2:3]), R=["ag2"], W=["ag3"])
        P.op("dve", lambda e: e.tensor_scalar(out=wk, in0=wk, scalar1=ag[:, 0:1], scalar2=ag[:, 3:4],
                                              op0=ALU.subtract, op1=ALU.mult), R=[kk, "ag", "ag3"], W=[kk])
        P.cp("ln_c")
        P.op("dve", lambda e: e.tensor_tensor(out=wk, in0=wk, in1=lng[:], op=ALU.mult), R=[kk, "lng"], W=[kk])
        P.op("dve", lambda e: e.tensor_tensor(out=Xt, in0=wk, in1=lnb[:], op=ALU.add), R=[kk, "lnb"], W=[f"X{t}"])

    def l0_mixer(self, npre=NG, nmain=NG):
        nc, P, ps, sb = self.nc, self.P, self.ps, self.sb
        ident = self.ident
        with ExitStack() as es:
            d_w0 = self.inp("w0", [NB0, 128, 4096])
            w0b = self.scratch("w0b", [NB0, 128, 4096], BF16)
            for b in range(NB0):
                P.dma("pool", w0b[b].rearrange("p (a c) -> p a c", c=2048), d_w0[b].rearrange("p (a c) -> p a c", c=2048), W=[f"w0b{b}"])
            P.barrier()
            d_xpre = self.inp("x_pre", [T, D])
            d_bgi = self.inp("bgi", [4, 1]); d_bgf = self.inp("bgf", [4, 1])
            d_ng = self.inp("norm_g", [768]); d_lng = self.inp("ln1g0", [D]); d_lnb = self.inp("ln1b0", [D])
            kxT, vxaug = self.xattn_setup(0, es)
            S = lambda n, sh, dt=F32: sb(n, sh, dt, es=es)
            self.wblk = [S("wblk0_m0", [128, 8, 512], BF16), S("wblk1_m0", [128, 8, 512], BF16)]
            xpre = S("xpre", [128, 2, D]); xTg = S("xTg", [128, 8, G], BF16)
            qT = S("qT", [128, 8, G], BF16); kT = S("kT", [128, 8, G], BF16); xqT = S("xqT", [64, 4, G], BF16)
            k_tm = S("k_tm", [64, 4, 768], BF16); vaug = S("vaug", [64, 4, 4, 193], BF16); so = S("so", [64, 4, 768], BF16)
            g_tm = S("g_tm", [64, 4, 8])
            bgi = S("bgi_s", [4, 1]); nbgf = S("nbgf", [4, 1]); zer = S("zer", [4, G])
            ig = S("ig", [4, G]); lsp = S("lsp", [4, G]); Lc = S("Lc", [4, G]); gam = S("gam", [4, G]); Mx = S("Mx", [4, G])
            t1 = S("t1", [4, G]); bet = S("bet", [4, G]); alp = S("alp", [4, G]); flo = S("flo", [4, G])
            Lcar = S("Lcar", [4, 1]); Mcar = S("Mcar", [4, 1]); Mprev = S("Mprev", [4, 4]); dec = S("dec", [4, 4])
            sel = S("sel", [4, 4, 128]); gtm = S("gtm", [64, 4, 3, 4]); decb = S("decb", [128, 4, 4])
            CA = S("CA", [128, 4, 193]); CB = S("CB", [64, 4, 193]); CAb = S("CAb", [128, 4, 193], BF16); CBb = S("CBb", [64, 4, 193], BF16)
            SmT = S("SmT", [64, 4, 64], BF16); num = S("num", [64, 4, 193]); hr = S("hr", [64, 4, 192]); sq = S("sq", [64, 4, 192])
            sm = S("sm", [64, 8, 4]); mix = [S("mix0", [64, D], BF16), S("mix1", [64, D], BF16)]
            mixT = S("mixT", [128, 8, G], BF16); Esb = S("Esb", [128, 8, 64], BF16); rsb = S("rsb", [64, 4])
            ngb = S("ngb", [64, 768]); lng = S("lng", [128, D]); lnb = S("lnb", [128, D])
            xs = S("xs", [128, D]); st = S("st", [128, 2, 6]); ag = S("ag", [128, 4])
            P.dma("sp", bgi[:], d_bgi, W=["bgi"]); P.dma("sp", nbgf[:], d_bgf, W=["nbgf"])
            P.dma("sp", ngb[:], d_ng.partition_broadcast(64), W=["ngb"])
            P.dma("sp", lng[:], d_lng.partition_broadcast(128), W=["lng"])
            P.dma("sp", lnb[:], d_lnb.partition_broadcast(128), W=["lnb"])
            P.op("dve", lambda e: e.tensor_scalar(out=nbgf[:], in0=nbgf[:], scalar1=-1.0, scalar2=None, op0=ALU.mult), R=["nbgf"], W=["nbgf"])
            for t_, k_ in ((zer, "zer"), (Lcar, "Lcar"), (Mcar, "Mcar"), (CA, "CA"), (CB, "CB")):
                P.op("pool", lambda e, t_=t_: e.memset(t_[:], 0.0), W=[k_])
            for h in range(4):
                P.op("dve", lambda e, h=h: e.tensor_copy(out=sel[:, h, :], in_=ident[0:4, h:h + 1].broadcast_to([4, 128])),
                     R=["ident"], W=["sel"])
            if DBG.get("early_cast0", True):
                self.peer_cast(0)
            srcs = []
            for g in range(npre):
                srcs += [(w0b[b], f"w0b{b}") for b in (11, 5, 6, 7, 8)]
            for g in range(nmain):
                srcs += [(w0b[b], f"w0b{b}") for b in (11, 0, 1, 2, 3, 4, 5, 6, 7, 8, 9, 10, 12, 13)]
            sched = self.Sched(self, srcs)
            KS = float(DH ** -0.5)
            mixi = 0

            def tm_proj(wi, ck, ncols):
                bk = 2 + (ck % 2)
                for kc in range(8):
                    P.op("pe", lambda e, kc=kc: e.matmul(
                        ps[0:64, bk, 0:ncols], lhsT=xTg[:, kc, ck * 64:(ck + 1) * 64], rhs=self.wblk[wi][:, kc, 0:ncols],
                        start=(kc == 0), stop=(kc == 7)), R=["xTg", f"wblk{wi}"], W=[f"ps{bk}"])
                return bk

            for phase in (0, 1):
                for g in range(nmain if phase else npre):
                    main = phase == 1
                    if main:
                        srct = [self.X[:, 2 * g + tt, :] for tt in range(2)]; srck = [f"X{2 * g + tt}" for tt in range(2)]
                    else:
                        P.dma("sp", xpre[:], d_xpre[g * G:(g + 1) * G, :].rearrange("(t p) d -> p t d", p=128), W=["xpre0", "xpre1"])
                        srct = [xpre[:, tt, :] for tt in range(2)]; srck = ["xpre0", "xpre1"]
                    self.make_xT(xTg, srct, srck, "xTg")
                    wi = sched.get()
                    for ck in range(4):
                        bk = tm_proj(wi, ck, 8)
                        P.op("act", lambda e, ck=ck, bk=bk: e.copy(out=g_tm[:, ck, :], in_=ps[0:64, bk, 0:8]), R=[f"ps{bk}"], W=["g_tm"])
                    for ck in range(4):
                        for j in range(2):
                            P.op("pe", lambda e, ck=ck, j=j: e.transpose(
                                out=ps[0:4, 4, j * 256 + ck * 64: j * 256 + (ck + 1) * 64],
                                in_=g_tm[:, ck, 4 * j:4 * j + 4], identity=ident[0:64, 0:64]), R=["g_tm", "ident"], W=["ps4"])
                    if main:
                        P.op("dve", lambda e: e.tensor_scalar(out=ig[:], in0=ps[0:4, 4, 0:256], scalar1=bgi[:, 0:1], scalar2=None, op0=ALU.add),
                             R=["ps4", "bgi"], W=["ig"])
                    else:
                        P.op("dve", lambda e: e.tensor_scalar(out=ig[:], in0=ps[0:4, 4, 0:256], scalar1=bgi[:, 0:1], scalar2=self.flag[0:4, 0:1],
                                                              op0=ALU.add, op1=ALU.mult), R=["ps4", "bgi", "flag"], W=["ig"])
                    P.op("act", lambda e: e.activation(out=t1[:], in_=ps[0:4, 4, 256:512], func=AF.Exp, bias=nbgf[:, 0:1], scale=-1.0),
                         R=["ps4", "nbgf"], W=["t1"])
                    P.op("act", lambda e: e.activation(out=lsp[:], in_=t1[:], func=AF.Ln, bias=1.0), R=["t1"], W=["lsp"])
                    if not main:
                        P.op("dve", lambda e: e.tensor_scalar(out=lsp[:], in0=lsp[:], scalar1=self.flag[0:4, 0:1], scalar2=None, op0=ALU.mult),
                             R=["lsp", "flag"], W=["lsp"])
                    P.op("dve", lambda e: e.tensor_tensor_scan(out=Lc[:], data0=lsp[:], data1=zer[:], initial=Lcar[:, 0:1], op0=ALU.add, op1=ALU.add),
                         R=["lsp", "zer", "Lcar"], W=["Lc"])
                    P.op("dve", lambda e: e.tensor_tensor(out=gam[:], in0=ig[:], in1=Lc[:], op=ALU.add), R=["ig", "Lc"], W=["gam"])
                    P.op("dve", lambda e: e.tensor_tensor_scan(out=Mx[:], data0=gam[:], data1=gam[:], initial=Mcar[:, 0:1], op0=ALU.max, op1=ALU.max),
                         R=["gam", "Mcar"], W=["Mx"])
                    Mend = Mx[:].rearrange("p (c l) -> p c l", l=64)[:, :, 63]
                    P.op("dve", lambda e: e.tensor_copy(out=Mprev[:, 0:1], in_=Mcar[:, 0:1]), R=["Mcar"], W=["Mprev"])
                    P.op("dve", lambda e: e.tensor_copy(out=Mprev[:, 1:4], in_=Mend[:, 0:3]), R=["Mx"], W=["Mprev"])
                    P.op("dve", lambda e: e.tensor_copy(out=Mcar[:, 0:1], in_=Mx[:, G - 1:G]), R=["Mx", "Mprev"], W=["Mcar"])
                    P.op("dve", lambda e: e.tensor_copy(out=Lcar[:, 0:1], in_=Lc[:, G - 1:G]), R=["Lc"], W=["Lcar"])
                    P.op("dve", lambda e: e.tensor_tensor(out=dec[:], in0=Mprev[:], in1=Mend, op=ALU.subtract), R=["Mprev", "Mx"], W=["dec"])
                    P.op("act", lambda e: e.activation(out=dec[:], in_=dec[:], func=AF.Exp), R=["dec"], W=["dec"])
                    Mend_bc = Mend.unsqueeze(2).broadcast_to([4, 4, 64])
                    v3 = lambda t_: t_[:].rearrange("p (c l) -> p c l", l=64)
                    P.op("dve", lambda e: e.tensor_tensor(out=v3(bet), in0=v3(gam), in1=Mend_bc, op=ALU.subtract), R=["gam", "Mx"], W=["bet"])
                    P.op("act", lambda e: e.activation(out=bet[:], in_=bet[:], func=AF.Exp), R=["bet"], W=["bet"])
                    if main:
                        P.op("dve", lambda e: e.tensor_tensor(out=v3(alp), in0=Mend_bc, in1=v3(Mx), op=ALU.subtract), R=["Mx"], W=["alp"])
                        P.op("act", lambda e: e.activation(out=alp[:], in_=alp[:], func=AF.Exp), R=["alp"], W=["alp"])
                        P.op("dve", lambda e: e.tensor_tensor(out=flo[:], in0=Lc[:], in1=Mx[:], op=ALU.subtract), R=["Lc", "Mx"], W=["flo"])
                        P.op("act", lambda e: e.activation(out=flo[:], in_=flo[:], func=AF.Exp), R=["flo"], W=["flo"])
                    qs = (bet, alp, flo) if main else (bet,)
                    for ck in range(4):
                        for qi, qt in enumerate(qs):
                            P.op("pe", lambda e, ck=ck, qi=qi, qt=qt: e.transpose(
                                out=ps[0:64, 4, ck * 12 + qi * 4: ck * 12 + qi * 4 + 4], in_=qt[0:4, ck * 64:(ck + 1) * 64],
                                identity=ident[0:4, 0:4]), R=[("bet", "alp", "flo")[qi], "ident"], W=["ps4"])
                    if main:
                        P.op("act", lambda e: e.copy(out=gtm[:].rearrange("p c q h -> p (c q h)"), in_=ps[0:64, 4, 0:48]), R=["ps4"], W=["gtm"])
                    else:
                        P.op("act", lambda e: e.copy(out=gtm[:, :, 0, :], in_=ps[0:64, 4, 0:48].rearrange("p (c q h) -> p c q h", q=3, h=4)[:, :, 0, :]),
                             R=["ps4"], W=["gtm"])
                    for h in range(4):
                        P.op("pe", lambda e, h=h: e.matmul(
                            ps[:, 4, 64:80].rearrange("p (c h) -> p c h", h=4)[:, :, h], lhsT=sel[0:4, h, :], rhs=dec[0:4, 0:4],
                            start=True, stop=True), R=["sel", "dec", "gtm"], W=["ps4"])
                    P.op("act", lambda e: e.copy(out=decb[:].rearrange("p c h -> p (c h)"), in_=ps[:, 4, 64:80]), R=["ps4"], W=["decb"])
                    if main:
                        for blk, dst, scl in ((0, qT, 1.0), (1, qT, 1.0), (2, kT, KS), (3, kT, KS), (4, xqT, 1.0)):
                            wi = sched.get()
                            cw = 128 if blk < 4 else 64
                            for j in range(4):
                                bk = 2 + (j % 2)
                                for kc in range(8):
                                    P.op("pe", lambda e, j=j, kc=kc, bk=bk, wi=wi: e.matmul(
                                        ps[0:cw, bk, 0:G], lhsT=self.wblk[wi][:, kc, j * cw:(j + 1) * cw], rhs=xTg[:, kc, :],
                                        start=(kc == 0), stop=(kc == 7)), R=["xTg", f"wblk{wi}"], W=[f"ps{bk}"])
                                cj = (blk % 2) * 4 + j if blk < 4 else j
                                dk = {0: "qT", 1: "qT", 2: "kT", 3: "kT", 4: "xqT"}[blk]
                                P.op("act", lambda e, dst=dst, cj=cj, bk=bk, scl=scl: e.activation(
                                    out=dst[0:cw, cj, :], in_=ps[0:cw, bk, 0:G], func=AF.Copy, scale=scl), R=[f"ps{bk}"], W=[dk])
                    tms = [(0, "k"), (1, "k"), (0, "v"), (1, "v")] + ([(0, "o"), (1, "o")] if main else [])
                    for hp, kind in tms:
                        wi = sched.get()
                        for ck in range(4):
                            bk = tm_proj(wi, ck, 384)
                            src = ps[0:64, bk, 0:384]
                            if kind == "k":
                                P.op("act", lambda e, ck=ck, hp=hp, src=src: e.activation(
                                    out=k_tm[:, ck, hp * 384:(hp + 1) * 384], in_=src, func=AF.Copy, scale=KS), R=[f"ps{bk}"], W=["k_tm"])
                            elif kind == "o":
                                P.op("act", lambda e, ck=ck, hp=hp, src=src: e.activation(
                                    out=so[:, ck, hp * 384:(hp + 1) * 384], in_=src, func=AF.Sigmoid), R=[f"ps{bk}"], W=["so"])
                            else:
                                P.op("dve", lambda e, ck=ck, hp=hp, src=src: e.tensor_tensor(
                                    out=vaug[:, ck, 2 * hp:2 * hp + 2, 0:192], in0=src.rearrange("p (h d) -> p h d", h=2),
                                    in1=gtm[:, ck, 0, 2 * hp:2 * hp + 2].unsqueeze(2).broadcast_to([64, 2, 192]), op=ALU.mult),
                                    R=[f"ps{bk}", "gtm"], W=["vaug"])
                    for ck in range(4):
                        P.op("act", lambda e, ck=ck: e.copy(out=vaug[:, ck, :, 192], in_=gtm[:, ck, 0, :]), R=["gtm"], W=["vaug"])
                    P.cp(f"proj{phase}")
                    Pv = ps[0:64, 5:7, :].rearrange("p b (h e) -> p (b h) e", h=2)

                    def front(ck):
                        c0 = ck * 64
                        dbc = lambda n: decb[0:n, ck, :].unsqueeze(2).broadcast_to([n, 4, 193])
                        P.op("dve", lambda e: e.tensor_tensor(out=CA[:], in0=CA[:], in1=dbc(128), op=ALU.mult), R=["CA", "decb"], W=["CA"])
                        P.op("dve", lambda e: e.tensor_tensor(out=CB[:], in0=CB[:], in1=dbc(64), op=ALU.mult), R=["CB", "decb"], W=["CB"])
                        if main:
                            P.op("act", lambda e: e.copy(out=CAb[:], in_=CA[:]), R=["CA"], W=["CAb"])
                            P.op("act", lambda e: e.copy(out=CBb[:], in_=CB[:]), R=["CB"], W=["CBb"])
                            for h in range(4):
                                P.op("pe", lambda e, h=h: e.matmul(ps[0:64, 4, 256 + h * 64:256 + (h + 1) * 64], lhsT=kT[:, 2 * h, c0:c0 + 64],
                                                                   rhs=qT[:, 2 * h, c0:c0 + 64], start=True, stop=False), R=["kT", "qT"], W=["ps4"])
                                P.op("pe", lambda e, h=h: e.matmul(ps[0:64, 4, 256 + h * 64:256 + (h + 1) * 64], lhsT=kT[0:64, 2 * h + 1, c0:c0 + 64],
                                                                   rhs=qT[0:64, 2 * h + 1, c0:c0 + 64], start=False, stop=True), R=["kT", "qT"], W=["ps4"])
                            P.op("dve", lambda e: e.tensor_tensor(
                                out=SmT[:], in0=ps[0:64, 4, 256:512].rearrange("p (h l) -> p h l", h=4),
                                in1=self.cmask[:].unsqueeze(1).broadcast_to([64, 4, 64]), op=ALU.mult), R=["ps4", "cmask"], W=["SmT"])
                            for h in range(4):
                                bk = 5 + h // 2
                                P.op("pe", lambda e, h=h: e.matmul(Pv[:, h, 0:193], lhsT=qT[:, 2 * h, c0:c0 + 64], rhs=CAb[:, h, :],
                                                                   start=True, stop=False), R=["qT", "CAb"], W=[f"ps{bk}"])
                                P.op("pe", lambda e, h=h: e.matmul(Pv[:, h, 0:193], lhsT=qT[0:64, 2 * h + 1, c0:c0 + 64], rhs=CBb[:, h, :],
                                                                   start=False, stop=False), R=["qT", "CBb"], W=[f"ps{bk}"])
                                P.op("pe", lambda e, h=h: e.matmul(Pv[:, h, 0:193], lhsT=SmT[:, h, :], rhs=vaug[:, ck, h, :],
                                                                   start=False, stop=True), R=["SmT", "vaug"], W=[f"ps{bk}"])
                        P.cp(f"P{phase}")
                        for hp in range(2):
                            dA = ps[:, 7, :].rearrange("p (h e) -> p h e", h=2)
                            dB = ps[0:64, 3, :].rearrange("p (h e) -> p h e", h=2)
                            for hh in range(2):
                                h = 2 * hp + hh
                                P.op("pe", lambda e, h=h, hh=hh: e.matmul(dA[:, hh, 0:193], lhsT=k_tm[:, ck, h * 192:h * 192 + 128], rhs=vaug[:, ck, h, :],
                                                                          start=True, stop=True), R=["k_tm", "vaug"], W=["ps7"])
                                P.op("pe", lambda e, h=h, hh=hh: e.matmul(dB[:, hh, 0:193], lhsT=k_tm[:, ck, h * 192 + 128:(h + 1) * 192], rhs=vaug[:, ck, h, :],
                                                                          start=True, stop=True), R=["k_tm", "vaug"], W=["ps3"])
                            rA = ["CA"] + (["CAb"] if main else [])
                            P.op("dve", lambda e, hp=hp, dA=dA: e.tensor_tensor(out=CA[:, 2 * hp:2 * hp + 2, :], in0=CA[:, 2 * hp:2 * hp + 2, :], in1=dA[:, :, 0:193], op=ALU.add),
                                 R=["CA", "ps7"], W=["CA"])
                            P.op("dve", lambda e, hp=hp, dB=dB: e.tensor_tensor(out=CB[:, 2 * hp:2 * hp + 2, :], in0=CB[:, 2 * hp:2 * hp + 2, :], in1=dB[:, :, 0:193], op=ALU.add),
                                 R=["CB", "ps3"], W=["CB"])

                    def back_a(ck):
                        abc = gtm[:, ck, 1, :].unsqueeze(2).broadcast_to([64, 4, 193])
                        P.op("dve", lambda e: e.tensor_tensor(out=num[:], in0=Pv[:, :, 0:193], in1=abc, op=ALU.mult), R=["ps5", "ps6", "gtm"], W=["num"])

                    def back_b(ck):
                        nonlocal mixi
                        c0 = ck * 64
                        P.op("act", lambda e: e.activation(out=sm[:, 0, :], in_=num[:, :, 192], func=AF.Abs), R=["num"], W=["sm0"])
                        P.op("dve", lambda e: e.tensor_tensor(out=sm[:, 1, :], in0=sm[:, 0, :], in1=gtm[:, ck, 2, :], op=ALU.max), R=["sm0", "gtm"], W=["sm1"])
                        P.op("dve", lambda e: e.reciprocal(out=sm[:, 2, :], in_=sm[:, 1, :]), R=["sm1"], W=["sm2"])
                        P.op("dve", lambda e: e.tensor_tensor(out=hr[:], in0=num[:, :, 0:192], in1=sm[:, 2, :].unsqueeze(2).broadcast_to([64, 4, 192]), op=ALU.mult),
                             R=["num", "sm2"], W=["hr"])
                        P.cp("hr")
                        P.op("dve", lambda e: e.tensor_reduce(out=sm[:, 3, :], in_=hr[:], axis=AX.X, op=ALU.add), R=["hr"], W=["sm3"])
                        P.op("dve", lambda e: e.tensor_tensor(out=sq[:], in0=hr[:], in1=hr[:], op=ALU.mult), R=["hr"], W=["sq"])
                        P.op("dve", lambda e: e.tensor_reduce(out=sm[:, 4, :], in_=sq[:], axis=AX.X, op=ALU.add), R=["sq"], W=["sm4"])
                        P.op("dve", lambda e: e.tensor_scalar(out=sm[:, 3, :], in0=sm[:, 3, :], scalar1=1.0 / DH, scalar2=None, op0=ALU.mult), R=["sm3"], W=["sm3"])
                        P.op("dve", lambda e: e.tensor_tensor(out=sm[:, 5, :], in0=sm[:, 3, :], in1=sm[:, 3, :], op=ALU.mult), R=["sm3"], W=["sm5"])
                        P.op("dve", lambda e: e.scalar_tensor_tensor(out=sm[:, 6, :], in0=sm[:, 4, :], scalar=1.0 / DH, in1=sm[:, 5, :], op0=ALU.mult, op1=ALU.subtract),
                             R=["sm4", "sm5"], W=["sm6"])
                        P.op("act", lambda e: e.activation(out=sm[:, 6, :], in_=sm[:, 6, :], func=AF.Sqrt, bias=EPS), R=["sm6"], W=["sm6"])
                        P.op("dve", lambda e: e.reciprocal(out=sm[:, 7, :], in_=sm[:, 6, :]), R=["sm6"], W=["sm7"])
                        P.op("dve", lambda e: e.tensor_tensor(out=hr[:], in0=hr[:], in1=sm[:, 3, :].unsqueeze(2).broadcast_to([64, 4, 192]), op=ALU.subtract),
                             R=["hr", "sm3"], W=["hr"])
                        P.op("dve", lambda e: e.tensor_tensor(out=hr[:], in0=hr[:], in1=sm[:, 7, :].unsqueeze(2).broadcast_to([64, 4, 192]), op=ALU.mult),
                             R=["hr", "sm7"], W=["hr"])
                        hf = hr[:].rearrange("p h d -> p (h d)")
                        P.op("dve", lambda e: e.tensor_tensor(out=hf, in0=hf, in1=ngb[:], op=ALU.mult), R=["hr", "ngb"], W=["hr"])
                        mx_ = mix[mixi]; mk = f"mix{mixi}"; mixi ^= 1
                        P.op("dve", lambda e, mx_=mx_: e.tensor_tensor(out=mx_[:, 0:768], in0=hf, in1=so[:, ck, :], op=ALU.mult), R=["hr", "so"], W=[mk])
                        P.cp("hln")
                        self.xattn_chunk(kxT, vxaug, xqT, c0, 64, mx_[:, 768:1024], mk, Esb, rsb)
                        P.cp("xattn")
                        pb = ps[:, 2, :].bitcast(BF16)
                        for kc in range(8):
                            P.op("pe", lambda e, kc=kc, mx_=mx_: e.transpose(out=pb[:, kc * 64:(kc + 1) * 64], in_=mx_[:, kc * 128:(kc + 1) * 128],
                                                                             identity=self.identb[0:64, 0:64]), R=[mk, "identb"], W=["ps2"])
                        P.op("act", lambda e: e.copy(out=mixT[:, :, c0:c0 + 64], in_=pb[:, 0:512].rearrange("p (k t) -> p k t", t=64)), R=["ps2"], W=["mixT"])


                    if not main:
                        for ck in range(4):
                            front(ck)
                    else:
                        front(0)
                        for ck in range(4):
                            back_a(ck)
                            if ck + 1 < 4:
                                front(ck + 1)
                            back_b(ck)
                    P.cp(f"chunks{phase}")
                    if main:
                        self.outproj_ln(sched, mixT, g, lng, lnb, xs, st, ag)
            P.barrier(pool_dma=False)

    def finish(self, out_name="out"):
        P = self.P
        d_out = self.outp(out_name, [T, D])
        for q4 in range(4):
            P.dma("sp", d_out[512 * q4:512 * (q4 + 1), :].rearrange("(t p) d -> p t d", p=128),
                  self.X[:, 4 * q4:4 * q4 + 4, :], R=[f"X{t}" for t in range(4 * q4, 4 * q4 + 4)], chan="out")
        P.wait_all("sp")


def _consts():
    ident = np.eye(128, dtype=np.float32)
    cmask = np.triu(np.ones((64, 64), np.float32))
    return ident, cmask


def build(stages, limit=None, stop_at=None, **kw):
    es = ExitStack()
    b = Builder(es)
    b.P.limit = limit
    b.P.stop_at = stop_at
    b.setup()
    if "fused" in stages:
        P = b.P
        b.l0_mixer()
        b.peer(0)
        cc1src = b.scratch("cc1src", [4, D], F32); cc1dst = b.scratch("cc1dst", [8, D], F32)
        b.cc2src = b.scratch("cc2src", [RB, 8], F32); cc2dst = b.scratch("cc2dst", [2 * RB, 8], F32)
        P.dma("sp", cc1src[0:3], b.X[125:128, NT - 1, :], R=[f"X{NT - 1}"], W=["cc1src"])
        P.dma("sp", cc1src[3:4], b.X[127:128, NT - 1, :], R=[f"X{NT - 1}"], W=["cc1src"])
        P.barrier()
        P.allgather_pairs(cc1src, cc1dst, R=["cc1src"], W=["cc_halo"])
        P.barrier()
        b.l1_cast()
        if DBG.get("early_cast1", True):
            b.peer_cast(1)
        b.l1(scan_only=True, halo_src=cc1dst[0:3])
        P.allgather_pairs(b.cc2src, cc2dst, R=["cc2src"], W=["cc_hend"])
        P.barrier(pool_dma=False)
        b.l1(scan_only=False, halo_src=cc1dst[0:3], hinit_src=cc2dst[0:RB])
        b.peer(1)
        b.finish()
        return b, es
    if "l0mix" in stages:
        b.l0_mixer(**{k: v for k, v in kw.items() if k in ("npre", "nmain")})
    if "peer0" in stages:
        b.peer(0, **{k: v for k, v in kw.items() if k == "ngroups"})
    if "l1scan" in stages:
        b.l1(scan_only=True)
    if "l1mix" in stages:
        b.l1(scan_only=False, **{k: v for k, v in kw.items() if k == "ngroups"})
    if "peer1" in stages:
        b.peer(1, **{k: v for k, v in kw.items() if k == "ngroups"})
    b.finish()
    return b, es


def core_inputs(inputs, stages, x_cur=None, hinit=None, halo=None):
    ident, cmask = _consts()
    x = inputs["x"] if x_cur is None else x_cur
    maps = []
    shared = {"ident": ident, "cmask": cmask}
    if "fused" in stages:
        stages = ["fused", "l0mix", "peer0", "l1scan", "l1mix", "peer1"]
    if "l0mix" in stages:
        shared["w0"] = prep_l0_weights(inputs["mlstm_w_in"][0], inputs["w_out"][0]).reshape(NB0, 128, 4096)
        shared["bgi"] = np.ascontiguousarray(inputs["mlstm_b_gates"][0, 0:4].reshape(4, 1))
        shared["bgf"] = np.ascontiguousarray(inputs["mlstm_b_gates"][0, 4:8].reshape(4, 1))
        shared["norm_g"] = np.ascontiguousarray(inputs["mlstm_norm_g"][0])
        shared["ln1g0"] = np.ascontiguousarray(inputs["ln1_g"][0]); shared["ln1b0"] = np.ascontiguousarray(inputs["ln1_b"][0])
        shared["wkv0"] = _blockify(np.ascontiguousarray(inputs["xattn_w_kv"][0]))
    for l in (0, 1):
        if f"peer{l}" in stages:
            ut, wqb, skT = prep_peer(inputs["peer_u"][l], inputs["peer_w_q"][l], inputs["peer_subkeys"][l])
            shared[f"ut{l}"] = ut; shared[f"wq{l}"] = wqb; shared[f"skT{l}"] = skT
            shared[f"pv{l}"] = np.ascontiguousarray(inputs["peer_v"][l]).reshape(128, 128, 1024)
            shared[f"ln2g{l}"] = np.ascontiguousarray(inputs["ln2_g"][l]); shared[f"ln2b{l}"] = np.ascontiguousarray(inputs["ln2_b"][l])
    if "l1scan" in stages or "l1mix" in stages:
        shared["w1"] = prep_l1_weights(inputs["rglru_w_in"][0], inputs["w_out"][1])
        shared["conv_w"] = np.ascontiguousarray(inputs["rglru_conv_w"][0].reshape(4, 8, 96).transpose(2, 1, 0))
        shared["conv_b"] = _chan(inputs["rglru_conv_b"][0]); shared["rg_ba"] = _chan(inputs["rglru_b_a"][0])
        shared["rg_bx"] = _chan(inputs["rglru_b_x"][0]); shared["rg_lam"] = _chan(inputs["rglru_lam"][0])
        shared["rg_wa"] = np.ascontiguousarray(inputs["rglru_w_a"][0].transpose(1, 0, 2))
        shared["rg_wx"] = np.ascontiguousarray(inputs["rglru_w_x"][0].transpose(1, 0, 2))
        if "l1mix" in stages:
            shared["ln1g1"] = np.ascontiguousarray(inputs["ln1_g"][1]); shared["ln1b1"] = np.ascontiguousarray(inputs["ln1_b"][1])
            shared["wkv1"] = _blockify(np.ascontiguousarray(inputs["xattn_w_kv"][1]))
    for c in range(8):
        bi, half = c // 2, c % 2
        m = dict(shared)
        if "l1scan" in stages or "l1mix" in stages:
            m["xhalo"] = np.ascontiguousarray(x[bi, T - 3:T]) if half else np.zeros((3, D), np.float32)
            m["hinit"] = np.zeros((RB, 8), np.float32) if hinit is None else np.ascontiguousarray(hinit[c])
            m["memT"] = np.ascontiguousarray(inputs["mem"][bi].T.reshape(8, 128, 256).transpose(1, 0, 2))
        m["x_own"] = np.ascontiguousarray(x[bi, half * T:(half + 1) * T])
        m["flag"] = np.full((128, 1), float(half), np.float32)
        if "l0mix" in stages:
            m["x_pre"] = np.ascontiguousarray(x[bi, 0:T]) if half else np.zeros((T, D), np.float32)
            m["memT"] = np.ascontiguousarray(inputs["mem"][bi].T.reshape(8, 128, 256).transpose(1, 0, 2))
        maps.append(m)
    return maps


DBG = {}
def prep_peer(u, wq, sk):
    ut = np.ascontiguousarray(u.reshape(128, 128, 8, 128).transpose(0, 3, 2, 1)).reshape(128, 128, 1024)
    wqb = np.stack([_blockify(np.ascontiguousarray(wq[:, b * 512:(b + 1) * 512])) for b in range(4)]).reshape(4, 128, 4096)
    skT = np.ascontiguousarray(sk.transpose(2, 0, 1))
    return ut, wqb, skT


def _peer_method(self, layer, ngroups=NG):
    nc, P, ps, sb = self.nc, self.P, self.ps, self.sb
    TN = 4
    with ExitStack() as es:
        S = lambda n, sh, dt=F32: sb(f"{n}_p{layer}", sh, dt, es=es)
        self.wblk = [S("wblk0", [128, 8, 512], BF16)]
        d_sk = self.inp(f"skT{layer}", [128, 2, 128])
        d_lng = self.inp(f"ln2g{layer}", [D]); d_lnb = self.inp(f"ln2b{layer}", [D])
        if layer not in self.peer_casts:
            self.peer_cast(layer)
        utb, vb, wqb = self.peer_casts[layer]
        if DBG.get("cast_barrier", True):
            P.barrier()
        GT = S("GT", [128, 128, G], BF16)
        xTg = S("xTg", [128, 8, G], BF16); qT = S("qT", [128, 16, G], BF16)
        ublk = [S(f"ublk{k}", [128, 1024], BF16) for k in range(4)]
        vblk = [S(f"vblk{k}", [128, 1024], BF16) for k in range(4)]
        arA = S("arA", [128, 1024]); arB = S("arB", [128, 1024])
        s_sb = arA[:, 0:512].rearrange("p (j k) -> p j k", k=128); s2 = arA[:, 512:1024].rearrange("p (j k) -> p j k", k=128)
        eq = arA[:, 0:512].rearrange("p (h r a) -> p h r a", r=16, a=16); prod = arA[:, 512:1024].rearrange("p (h r a) -> p h r a", r=16, a=16)
        cand = arB[:, 0:512].rearrange("p (h a b) -> p h a b", a=16, b=16); cand2 = arB[:, 512:1024].rearrange("p (h a b) -> p h a b", a=16, b=16)
        sv = S("sv", [128, 16, 16]); si = S("si", [128, 16, 16], U32); sif = S("sif", [128, 16, 16])
        fv = S("fv", [128, 8, 16]); fp = S("fp", [128, 8, 16], U32); pf = S("pf", [128, 8, 16]); bfl = S("bfl", [128, 8, 16]); af = S("af", [128, 8, 16])
        ex = S("ex", [128, 8, 16]); zs = S("zs", [128, 8]); tri = S("tri", [128, 3, 128])
        hr3 = S("hr3", [128, 3, G], BF16)
        Bt = [S("Bt0", [128, TN, 128], BF16), S("Bt1", [128, TN, 128], BF16)]
        At = [S("At0", [128, TN, 128], BF16), S("At1", [128, TN, 128], BF16)]
        eqt = S("eqt", [128, TN, 128], BF16)
        ge = [S("ge0", [128, G], BF16), S("ge1", [128, G], BF16)]; coef = [S("coef0", [128, G], BF16), S("coef1", [128, G], BF16)]
        skT32 = S("skT32", [128, 2, 128]); skTb = S("skTb", [128, 2, 128], BF16)
        iot32 = S("iot32", [128, 128]); iot = S("iot", [128, 128], BF16); iot16 = S("iot16", [128, 16])
        lng = S("lng", [128, D]); lnb = S("lnb", [128, D]); st = S("st", [128, 2, 6]); ag = S("ag", [128, 4])
        P.dma("sp", skT32[:], d_sk, W=["skT32"])
        P.op("dve", lambda e: e.tensor_copy(out=skTb[:], in_=skT32[:]), R=["skT32"], W=["skTb"])
        P.op("pool", lambda e: e.iota(iot32[:], pattern=[[1, 128]], base=0, channel_multiplier=0, allow_small_or_imprecise_dtypes=True), W=["iot32"])
        P.op("dve", lambda e: e.tensor_copy(out=iot[:], in_=iot32[:]), R=["iot32"], W=["iot"])
        P.op("pool", lambda e: e.iota(iot16[:], pattern=[[1, 16]], base=0, channel_multiplier=0, allow_small_or_imprecise_dtypes=True), W=["iot16"])
        P.dma("sp", lng[:], d_lng.partition_broadcast(128), W=["lng"])
        P.dma("sp", lnb[:], d_lnb.partition_broadcast(128), W=["lnb"])
        def load_tb(i):
            b = i % 4
            P.dma("sp", ublk[b][:], utb[i], R=[f"utb{layer}_{i // 4}"], W=[f"ublk{b}"])
            P.dma("sp", vblk[b][:], vb[i], R=[f"vb{layer}_{i // 4}"], W=[f"vblk{b}"])

        xTgs = [xTg, S("xTgB", [128, 8, G], BF16)]

        def phaseA1(g):
                xTg = xTgs[g % 2]; xk = f"xTg{g % 2}"
                self.make_xT(xTg, [self.X[:, 2 * g + tt, :] for tt in range(2)], [f"X{2 * g + tt}" for tt in range(2)], xk)
                sched = self.Sched(self, [(wqb[b], f"wqb{layer}_{b}") for b in range(4)])
                for blk in range(4):
                    wi = sched.get()
                    for j in range(4):
                        bk = 4 + (j % 2)
                        for kc in range(8):
                            P.op("pe", lambda e, j=j, kc=kc, bk=bk, wi=wi: e.matmul(
                                ps[:, bk, 0:G], lhsT=self.wblk[wi][:, kc, j * 128:(j + 1) * 128], rhs=xTg[:, kc, :],
                                start=(kc == 0), stop=(kc == 7)), R=[xk, f"wblk{wi}"], W=[f"ps{bk}"])
                        P.op("act", lambda e, j=j, bk=bk, blk=blk: e.copy(out=qT[:, blk * 4 + j, :], in_=ps[:, bk, 0:G]), R=[f"ps{bk}"], W=["qT"])

        def phaseA2(g):
            xTg = xTgs[g % 2]
            for tt in range(2):
                tok = slice(tt * 128, (tt + 1) * 128)
                for hh in range(4):
                    bk = 6 + (hh % 2)
                    for j4 in range(4):
                        jj = hh * 4 + j4
                        P.op("pe", lambda e, jj=jj, j4=j4, bk=bk: e.matmul(
                            ps[:, bk, j4 * 128:(j4 + 1) * 128], lhsT=qT[:, jj, tok], rhs=skTb[:, jj % 2, :],
                            start=True, stop=True), R=["qT", "skTb"], W=[f"ps{bk}"])
                    P.op("act", lambda e, bk=bk: e.copy(out=s_sb, in_=ps[:, bk, :].rearrange("p (j k) -> p j k", k=128)),
                         R=[f"ps{bk}"], W=["arA"])
                    yield
                    for j4 in range(4):
                        jj = hh * 4 + j4
                        P.op("dve", lambda e, jj=jj, j4=j4: e.max(out=sv[:, jj, 0:8], in_=s_sb[:, j4, :]), R=["arA"], W=["sv"])
                        P.op("dve", lambda e, jj=jj, j4=j4: e.max_index(out=si[:, jj, 0:8], in_max=sv[:, jj, 0:8], in_values=s_sb[:, j4, :]), R=["arA", "sv"], W=["si"])
                        P.op("dve", lambda e, jj=jj, j4=j4: e.match_replace(out=s2[:, j4, :], in_to_replace=sv[:, jj, 0:8], in_values=s_sb[:, j4, :], imm_value=-1e30),
                             R=["arA", "sv"], W=["arA"])
                        P.op("dve", lambda e, jj=jj, j4=j4: e.max(out=sv[:, jj, 8:16], in_=s2[:, j4, :]), R=["arA"], W=["sv"])
                        P.op("dve", lambda e, jj=jj, j4=j4: e.max_index(out=si[:, jj, 8:16], in_max=sv[:, jj, 8:16], in_values=s2[:, j4, :]), R=["arA", "sv"], W=["si"])
                        yield
                P.op("dve", lambda e: e.tensor_copy(out=sif[:], in_=si[:]), R=["si"], W=["sif"])
                svv = sv[:].rearrange("p (h two) a -> p h two a", two=2)
                sfv = sif[:].rearrange("p (h two) a -> p h two a", two=2)
                for hq in range(4):
                    hs = slice(2 * hq, 2 * hq + 2)
                    P.op("dve", lambda e: e.tensor_tensor(out=cand, in0=svv[:, hs, 0, :].unsqueeze(3).broadcast_to([128, 2, 16, 16]),
                                                          in1=svv[:, hs, 1, :].unsqueeze(2).broadcast_to([128, 2, 16, 16]), op=ALU.add), R=["sv"], W=["arB"])
                    for h4 in range(2):
                        h = 2 * hq + h4
                        c1 = cand[:, h4].rearrange("p a b -> p (a b)"); c2 = cand2[:, h4].rearrange("p a b -> p (a b)")
                        P.op("dve", lambda e: e.max(out=fv[:, h, 0:8], in_=c1), R=["arB"], W=["fv"])
                        P.op("dve", lambda e: e.max_index(out=fp[:, h, 0:8], in_max=fv[:, h, 0:8], in_values=c1), R=["arB", "fv"], W=["fp"])
                        P.op("dve", lambda e: e.match_replace(out=c2, in_to_replace=fv[:, h, 0:8], in_values=c1, imm_value=-1e30), R=["arB", "fv"], W=["arB"])
                        P.op("dve", lambda e: e.max(out=fv[:, h, 8:16], in_=c2), R=["arB"], W=["fv"])
                        P.op("dve", lambda e: e.max_index(out=fp[:, h, 8:16], in_max=fv[:, h, 8:16], in_values=c2), R=["arB", "fv"], W=["fp"])
                        yield
                gwv = tri[:, 2, :].rearrange("p (h r) -> p h r", r=16)
                P.op("dve", lambda e: e.tensor_tensor(out=ex[:], in0=fv[:], in1=fv[:, :, 0:1].broadcast_to([128, 8, 16]), op=ALU.subtract), R=["fv"], W=["ex"])
                P.op("act", lambda e: e.activation(out=ex[:], in_=ex[:], func=AF.Exp), R=["ex"], W=["ex"])
                P.op("dve", lambda e: e.tensor_reduce(out=zs[:], in_=ex[:], axis=AX.X, op=ALU.add), R=["ex"], W=["zs"])
                P.op("dve", lambda e: e.reciprocal(out=zs[:], in_=zs[:]), R=["zs"], W=["zs"])
                P.op("dve", lambda e: e.tensor_tensor(out=gwv, in0=ex[:], in1=zs[:].unsqueeze(2).broadcast_to([128, 8, 16]), op=ALU.mult), R=["ex", "zs"], W=["tri2"])
                yield
                P.op("dve", lambda e: e.tensor_single_scalar(out=pf[:].bitcast(U32), in_=fp[:], scalar=15, op=ALU.bitwise_and), R=["fp"], W=["pf"])
                P.op("dve", lambda e: e.tensor_copy(out=bfl[:], in_=pf[:].bitcast(U32)), R=["pf"], W=["bfl"])
                P.op("dve", lambda e: e.tensor_single_scalar(out=pf[:].bitcast(U32), in_=fp[:], scalar=4, op=ALU.logical_shift_right), R=["fp", "bfl"], W=["pf"])
                P.op("dve", lambda e: e.tensor_copy(out=af[:], in_=pf[:].bitcast(U32)), R=["pf"], W=["af"])
                yield
                i16 = iot16[:].unsqueeze(1).unsqueeze(1).broadcast_to([128, 2, 16, 16])
                for which, (srcidx, two) in enumerate(((af, 0), (bfl, 1))):
                    ov = tri[:, which, :].rearrange("p (h r) -> p h r", r=16)
                    for hq in range(4):
                        hs = slice(2 * hq, 2 * hq + 2)
                        P.op("dve", lambda e: e.tensor_tensor(out=eq, in0=srcidx[:, hs, :].unsqueeze(3).broadcast_to([128, 2, 16, 16]), in1=i16, op=ALU.is_equal),
                             R=["af", "bfl", "iot16"], W=["arA"])
                        P.op("dve", lambda e: e.tensor_tensor(out=prod, in0=eq, in1=sfv[:, hs, two, :].unsqueeze(2).broadcast_to([128, 2, 16, 16]), op=ALU.mult),
                             R=["arA", "sif"], W=["arA"])
                        P.op("dve", lambda e: e.tensor_reduce(out=ov[:, hs, :], in_=prod, axis=AX.X, op=ALU.add), R=["arA"], W=[f"tri{which}"])
                        yield
                for q3 in range(3):
                    P.op("pe", lambda e, q3=q3: e.transpose(out=ps[:, 7, q3 * 128:(q3 + 1) * 128], in_=tri[:, q3, :], identity=self.ident[:]),
                         R=[f"tri{q3}", "ident"], W=["ps7"])
                P.op("act", lambda e: e.copy(out=hr3[:, :, tok], in_=ps[:, 7, 0:384].rearrange("p (q t) -> p q t", q=3)), R=["ps7"], W=["hr3"])

            yield

        phaseA1(0)
        for _ in phaseA2(0):
            pass
        for g in range(ngroups):
            xTg = xTgs[g % 2]; xk = f"xTg{g % 2}"
            P.cp(f"peerA{layer}")
            ib = iot[:].unsqueeze(1).broadcast_to([128, TN, 128])
            for tb in range(G // TN):
                t0 = tb * TN
                bi = tb % 2
                P.op("dve", lambda e: e.tensor_tensor(out=Bt[bi][:], in0=ib, in1=hr3[:, 1, t0:t0 + TN].unsqueeze(2).broadcast_to([128, TN, 128]), op=ALU.is_equal),
                     R=["iot", "hr3"], W=[f"Bt{bi}"])
                P.op("dve", lambda e: e.tensor_tensor(out=eqt[:], in0=ib, in1=hr3[:, 0, t0:t0 + TN].unsqueeze(2).broadcast_to([128, TN, 128]), op=ALU.is_equal),
                     R=["iot", "hr3"], W=["eqt"])
                P.op(DBG.get("at_eng", "dve"), lambda e: e.tensor_tensor(out=At[bi][:], in0=eqt[:], in1=hr3[:, 2, t0:t0 + TN].unsqueeze(2).broadcast_to([128, TN, 128]), op=ALU.mult),
                     R=["eqt", "hr3"], W=[f"At{bi}"])
                for t in range(TN):
                    tg = t0 + t
                    bk = (tg // 4) % 8
                    P.op("pe", lambda e, t=t, tg=tg, bk=bk: e.matmul(ps[:, bk, (tg % 4) * 128:(tg % 4 + 1) * 128], lhsT=Bt[bi][:, t, :], rhs=At[bi][:, t, :],
                                                                     start=True, stop=True), R=[f"Bt{bi}", f"At{bi}"], W=[f"ps{bk}"])
                if (t0 + TN) % 16 == 0 and not DBG.get("noevac"):
                    b0 = ((t0 + TN - 16) // 4) % 8
                    tq = t0 + TN - 16
                    P.op("act", lambda e: e.copy(out=GT[:, :, tq:tq + 16], in_=ps[:, b0:b0 + 4, :].rearrange("p b (t i) -> p i (b t)", t=4)),
                         R=[f"ps{b}" for b in range(b0, b0 + 4)], W=["GT"])
            P.cp(f"peerB{layer}")
            def emit_ht(i):
                hb = 4 + (i % 2)
                for kc in range(8):
                    P.op("pe", lambda e, kc=kc: e.matmul(ps[:, hb, 0:G], lhsT=ublk[i % 4][:, kc * 128:(kc + 1) * 128], rhs=xTg[:, kc, :],
                                                         start=(kc == 0), stop=(kc == 7)), R=[f"ublk{i % 4}", xk], W=[f"ps{hb}"])

            def emit_rest(i):
                hb = 4 + (i % 2)
                gi = i % 2
                P.op("act", lambda e: e.activation(out=ge[gi][:], in_=ps[:, hb, 0:G], func=AF.Gelu), R=[f"ps{hb}"], W=[f"ge{gi}"])
                P.op("dve", lambda e: e.tensor_tensor(out=coef[gi][:], in0=ge[gi][:], in1=GT[:, i, :], op=ALU.mult), R=[f"ge{gi}", "GT"], W=[f"coef{gi}"])
                for tt in range(2):
                    for half in range(2):
                        yb = 2 * tt + half
                        P.op("pe", lambda e, tt=tt, half=half, yb=yb: e.matmul(
                            ps[:, yb, :], lhsT=coef[gi][:, tt * 128:(tt + 1) * 128], rhs=vblk[i % 4][:, half * 512:(half + 1) * 512],
                            start=(i == 0), stop=(i == 127)), R=[f"coef{gi}", f"vblk{i % 4}"], W=[f"ps{yb}"])

            genA = None
            if g + 1 < ngroups:
                phaseA1(g + 1)
                genA = phaseA2(g + 1)
            for k in range(3):
                load_tb(k)
            emit_ht(0)
            for i in range(128):
                if i + 1 < 128:
                    emit_ht(i + 1)
                emit_rest(i)
                if i + 3 < 128:
                    load_tb(i + 3)
                if genA is not None and i >= 4:
                    next(genA, None)
            if genA is not None:
                for _ in genA:
                    pass
            P.cp(f"peerC{layer}")
            for tt in range(2):
                self.resid_ln(2 * g + tt, ps[:, 2 * tt:2 * tt + 2, :], [f"ps{2 * tt}", f"ps{2 * tt + 1}"], lng, lnb, None, st, ag)
        P.barrier()


def _peer_cast(self, layer):
    P = self.P
    d_ut = self.inp(f"ut{layer}", [128, 128, 1024]); d_v = self.inp(f"pv{layer}", [128, 128, 1024])
    d_wq = self.inp(f"wq{layer}", [4, 128, 4096])
    utb = self.scratch(f"utb{layer}", [128, 128, 1024], BF16); vb = self.scratch(f"vb{layer}", [128, 128, 1024], BF16)
    wqb = self.scratch(f"wqb{layer}", [4, 128, 4096], BF16)
    for b in range(4):
        P.dma("pool", wqb[b].rearrange("p (a c) -> p a c", c=2048), d_wq[b].rearrange("p (a c) -> p a c", c=2048), W=[f"wqb{layer}_{b}"])
    for i0 in range(0, 128, 4):
        P.dma("pool", utb[i0:i0 + 4], d_ut[i0:i0 + 4], W=[f"utb{layer}_{i0 // 4}"])
        P.dma("pool", vb[i0:i0 + 4], d_v[i0:i0 + 4], W=[f"vb{layer}_{i0 // 4}"])
    self.peer_casts[layer] = (utb, vb, wqb)


Builder.peer = _peer_method
Builder.peer_cast = _peer_cast


NB1 = 9
RB = 96


def prep_l1_weights(w_in, w_out):
    gate = w_in[:, 0:768]; xr = w_in[:, 768:1536]; xq = w_in[:, 1536:1792]
    blocks = []
    for src in (gate, xr):
        for hb in range(2):
            blocks.append(np.concatenate([_pad_cols(src[:, g * 96:(g + 1) * 96], 128) for g in range(4 * hb, 4 * hb + 4)], axis=1))
    blocks.append(_pad_cols(xq, 512))
    out = [_blockify(np.ascontiguousarray(b, dtype=np.float32)).reshape(128, 4096) for b in blocks]
    for cb in range(4):
        blk = np.zeros((128, 10, 256), np.float32)
        for g in range(8):
            blk[0:96, g, :] = w_out[g * 96:(g + 1) * 96, cb * 256:(cb + 1) * 256]
        for c2 in range(2):
            blk[:, 8 + c2, :] = w_out[768 + c2 * 128:768 + (c2 + 1) * 128, cb * 256:(cb + 1) * 256]
        out.append(_pad_cols(blk.reshape(128, 2560), 4096))
    return np.stack(out)


def _chan(v):
    return np.ascontiguousarray(np.asarray(v, np.float32).reshape(8, 96).T)


def _l1_method(self, scan_only=False, ngroups=NG, halo_src=None, hinit_src=None):
    nc, P, ps, sb = self.nc, self.P, self.ps, self.sb
    ident = self.ident
    tag = "s" if scan_only else "m"
    with ExitStack() as es:
        S = lambda n, sh, dt=F32: sb(f"{n}_l1{tag}", sh, dt, es=es)
        if "w1" not in self.din:
            self.l1_cast()
        w1b = self.w1b
        g_in = lambda n, sh: self.din[n] if n in self.din else self.inp(n, sh)
        d_halo = halo_src if halo_src is not None else g_in("xhalo", [3, D])
        d_hinit = hinit_src if hinit_src is not None else (None if (scan_only and halo_src is not None) else g_in("hinit", [RB, 8]))
        d_cw = g_in("conv_w", [RB, 8, 4]); d_cb = g_in("conv_b", [RB, 8]); d_wa = g_in("rg_wa", [RB, 8, RB]); d_wx = g_in("rg_wx", [RB, 8, RB])
        d_ba = g_in("rg_ba", [RB, 8]); d_bx = g_in("rg_bx", [RB, 8]); d_lam = g_in("rg_lam", [RB, 8])
        self.wblk = [S("wblk0", [128, 8, 512], BF16), S("wblk1", [128, 8, 512], BF16)]
        xTg = S("xTg", [128, 8, G], BF16)
        xrb = S("xrb", [RB, 8, 3 + G]); xc = S("xc", [RB, 8, G]); xcb = S("xcb", [RB, 8, G], BF16)
        cw = S("cw", [RB, 8, 4]); cb = S("cb", [RB, 8]); wa32 = S("wa32", [RB, 8, RB]); wx32 = S("wx32", [RB, 8, RB])
        wab = S("wab", [RB, 8, RB], BF16); wxb = S("wxb", [RB, 8, RB], BF16)
        ba = S("ba", [RB, 8]); bx = S("bx", [RB, 8]); lamc = S("lamc", [RB, 8]); hcar = S("hcar", [RB, 8])
        rt = S("rt", [RB, 8, G]); it = S("it", [RB, 8, G]); at = S("at", [RB, 8, G]); ut = S("ut", [RB, 8, G]); ht = S("ht", [RB, 8, G]); tmp = S("tmp", [RB, 8, G])
        xh = S("xh", [3, D]); xTh = S("xTh", [128, 8, 4], BF16)
        for dst, src, k in ((cw, d_cw, "cw"), (cb, d_cb, "cb"), (wa32, d_wa, "wa32"), (wx32, d_wx, "wx32"), (ba, d_ba, "ba"), (bx, d_bx, "bx"),
                            (lamc, d_lam, "lamc"), (xh, d_halo, "xh")):
            P.dma("sp", dst[:], src, R=(["cc_halo"] if (k == "xh" and halo_src is not None) else []), W=[k])
        if d_hinit is None:
            P.op("dve", lambda e: e.memset(hcar[:], 0.0), W=["hcar"] + [f"hcar{k}" for k in range(8)])
        else:
            P.dma("sp", hcar[:], d_hinit, R=(["cc_hend"] if hinit_src is not None else []), W=["hcar"] + [f"hcar{k}" for k in range(8)])
        if halo_src is not None:
            P.op("dve", lambda e: e.tensor_scalar(out=xh[:], in0=xh[:], scalar1=self.flag[0:3, 0:1], scalar2=None, op0=ALU.mult), R=["xh", "flag"], W=["xh"])
        if hinit_src is not None:
            P.op("dve", lambda e: e.tensor_scalar(out=hcar[:], in0=hcar[:], scalar1=self.flag[0:RB, 0:1], scalar2=None, op0=ALU.mult), R=["hcar", "flag"], W=["hcar"] + [f"hcar{k}" for k in range(8)])
        P.op("dve", lambda e: e.tensor_copy(out=wab[:], in_=wa32[:]), R=["wa32"], W=["wab"])
        P.op("dve", lambda e: e.tensor_copy(out=wxb[:], in_=wx32[:]), R=["wx32"], W=["wxb"])
        P.op("act", lambda e: e.activation(out=lamc[:], in_=lamc[:], func=AF.Exp, scale=-1.0), R=["lamc"], W=["lamc"])
        P.op("act", lambda e: e.activation(out=lamc[:], in_=lamc[:], func=AF.Ln, bias=1.0), R=["lamc"], W=["lamc"])
        P.op("dve", lambda e: e.tensor_scalar(out=lamc[:], in0=lamc[:], scalar1=-8.0, scalar2=None, op0=ALU.mult), R=["lamc"], W=["lamc"])
        if not scan_only:
            d_lng = self.inp("ln1g1", [D]); d_lnb = self.inp("ln1b1", [D])
            kxT, vxaug = self.xattn_setup(1, es)
            gg = S("gg", [RB, 8, G], BF16); hmT = S("hmT", [RB, 8, G], BF16); xqT = S("xqT", [64, 4, G], BF16)
            hat = [S("hat0", [64, 256], BF16), S("hat1", [64, 256], BF16)]; haT = S("haT", [128, 2, G], BF16)
            Esb = S("Esb", [128, 8, 64], BF16); rsb = S("rsb", [64, 4])
            lng = S("lng", [128, D]); lnb = S("lnb", [128, D]); xs = S("xs", [128, D]); st = S("st", [128, 2, 6]); ag = S("ag", [128, 4])
            P.dma("sp", lng[:], d_lng.partition_broadcast(128), W=["lng"])
            P.dma("sp", lnb[:], d_lnb.partition_broadcast(128), W=["lnb"])
        srcs = [(w1b[b], f"w1b{b}") for b in (2, 3)]
        for g in range(ngroups):
            srcs += [(w1b[b], f"w1b{b}") for b in ((2, 3) if scan_only else (2, 3, 0, 1, 4, 5, 6, 7, 8))]
        sched = self.Sched(self, srcs)

        def xr_proj(wi, hb, rhs, n, dst_off, rkey):
            for j in range(4):
                g8 = 4 * hb + j
                bk = 2 + (j % 2)
                for kc in range(8):
                    P.op("pe", lambda e, j=j, kc=kc, bk=bk: e.matmul(
                        ps[0:RB, bk, 0:n], lhsT=self.wblk[wi][:, kc, j * 128:j * 128 + RB], rhs=rhs[:, kc, 0:n],
                        start=(kc == 0), stop=(kc == 7)), R=[rkey, f"wblk{wi}"], W=[f"ps{bk}"])
                P.op("act", lambda e, g8=g8, bk=bk: e.copy(out=xrb[:, g8, dst_off:dst_off + n], in_=ps[0:RB, bk, 0:n]), R=[f"ps{bk}"], W=["xrb"])

        for kc in range(8):
            P.op("pe", lambda e, kc=kc: e.transpose(out=ps[:, 0, kc * 4:kc * 4 + 3], in_=xh[0:3, kc * 128:(kc + 1) * 128], identity=ident[0:3, 0:3]),
                 R=["xh", "ident"], W=["ps0"])
        P.op("act", lambda e: e.copy(out=xTh[:, :, 0:3], in_=ps[:, 0, 0:32].rearrange("p (k c) -> p k c", c=4)[:, :, 0:3]), R=["ps0"], W=["xTh"])
        for hb in range(2):
            wi = sched.get()
            xr_proj(wi, hb, xTh, 3, 0, "xTh")
        mixi = 0
        for g in range(ngroups):
            self.make_xT(xTg, [self.X[:, 2 * g + tt, :] for tt in range(2)], [f"X{2 * g + tt}" for tt in range(2)], "xTg")
            for hb in range(2):
                wi = sched.get()
                xr_proj(wi, hb, xTg, G, 3, "xTg")
            if not scan_only:
                for hb in range(2):
                    wi = sched.get()
                    for j in range(4):
                        g8 = 4 * hb + j
                        bk = 2 + (j % 2)
                        for kc in range(8):
                            P.op("pe", lambda e, j=j, kc=kc, bk=bk: e.matmul(
                                ps[0:RB, bk, 0:G], lhsT=self.wblk[wi][:, kc, j * 128:j * 128 + RB], rhs=xTg[:, kc, :],
                                start=(kc == 0), stop=(kc == 7)), R=["xTg", f"wblk{wi}"], W=[f"ps{bk}"])
                        P.op("act", lambda e, g8=g8, bk=bk: e.activation(out=gg[:, g8, :], in_=ps[0:RB, bk, 0:G], func=AF.Gelu), R=[f"ps{bk}"], W=["gg"])
                wi = sched.get()
                for j in range(4):
                    bk = 2 + (j % 2)
                    for kc in range(8):
                        P.op("pe", lambda e, j=j, kc=kc, bk=bk: e.matmul(
                            ps[0:64, bk, 0:G], lhsT=self.wblk[wi][:, kc, j * 64:(j + 1) * 64], rhs=xTg[:, kc, :],
                            start=(kc == 0), stop=(kc == 7)), R=["xTg", f"wblk{wi}"], W=[f"ps{bk}"])
                    P.op("act", lambda e, j=j, bk=bk: e.copy(out=xqT[:, j, :], in_=ps[0:64, bk, 0:G]), R=[f"ps{bk}"], W=["xqT"])
            B8 = range(8)
            for w in range(4):
                for g8 in B8:
                    if w == 0:
                        P.op("dve", lambda e, g8=g8: e.tensor_scalar(out=xc[:, g8, :], in0=xrb[:, g8, 0:G], scalar1=cw[:, g8, 0:1], scalar2=cb[:, g8:g8 + 1],
                                                                     op0=ALU.mult, op1=ALU.add), R=["xrb", "cw", "cb"], W=[f"xc{g8}"])
                    else:
                        P.op("dve", lambda e, g8=g8, w=w: e.scalar_tensor_tensor(out=xc[:, g8, :], in0=xrb[:, g8, w:w + G], scalar=cw[:, g8, w:w + 1], in1=xc[:, g8, :],
                                                                                 op0=ALU.mult, op1=ALU.add), R=["xrb", "cw", f"xc{g8}"], W=[f"xc{g8}"])
            for g8 in B8:
                P.op("act", lambda e, g8=g8: e.copy(out=xcb[:, g8, :], in_=xc[:, g8, :]), R=[f"xc{g8}"], W=[f"xcb{g8}"])
            P.op("dve", lambda e: e.tensor_copy(out=xrb[:, :, 0:3], in_=xrb[:, :, G:G + 3]), R=["xrb"], W=["xrb"])
            for g8 in B8:
                pa, pb_ = 4 + 2 * (g8 % 2), 5 + 2 * (g8 % 2)
                P.op("pe", lambda e, g8=g8, pa=pa: e.matmul(ps[0:RB, pa, 0:G], lhsT=wab[:, g8, :], rhs=xcb[:, g8, :], start=True, stop=True), R=["wab", f"xcb{g8}"], W=[f"ps{pa}"])
                P.op("pe", lambda e, g8=g8, pb_=pb_: e.matmul(ps[0:RB, pb_, 0:G], lhsT=wxb[:, g8, :], rhs=xcb[:, g8, :], start=True, stop=True), R=["wxb", f"xcb{g8}"], W=[f"ps{pb_}"])
                P.op("act", lambda e, g8=g8, pa=pa: e.activation(out=rt[:, g8, :], in_=ps[0:RB, pa, 0:G], func=AF.Sigmoid, bias=ba[:, g8:g8 + 1]), R=[f"ps{pa}", "ba"], W=[f"rt{g8}"])
                P.op("act", lambda e, g8=g8, pb_=pb_: e.activation(out=it[:, g8, :], in_=ps[0:RB, pb_, 0:G], func=AF.Sigmoid, bias=bx[:, g8:g8 + 1]), R=[f"ps{pb_}", "bx"], W=[f"it{g8}"])
            for g8 in B8:
                P.op("act", lambda e, g8=g8: e.activation(out=at[:, g8, :], in_=rt[:, g8, :], func=AF.Exp, scale=lamc[:, g8:g8 + 1]), R=[f"rt{g8}", "lamc"], W=[f"at{g8}"])
            for g8 in B8:
                P.op("dve", lambda e, g8=g8: e.tensor_tensor(out=tmp[:, g8, :], in0=at[:, g8, :], in1=at[:, g8, :], op=ALU.mult), R=[f"at{g8}"], W=[f"tmp{g8}"])
            for g8 in B8:
                P.op("dve", lambda e, g8=g8: e.tensor_scalar(out=tmp[:, g8, :], in0=tmp[:, g8, :], scalar1=-1.0, scalar2=1.0, op0=ALU.mult, op1=ALU.add), R=[f"tmp{g8}"], W=[f"tmp{g8}"])
            for g8 in B8:
                P.op("dve", lambda e, g8=g8: e.tensor_tensor(out=ut[:, g8, :], in0=it[:, g8, :], in1=xc[:, g8, :], op=ALU.mult), R=[f"it{g8}", f"xc{g8}"], W=[f"ut{g8}"])
            for g8 in B8:
                P.op("act", lambda e, g8=g8: e.activation(out=tmp[:, g8, :], in_=tmp[:, g8, :], func=AF.Sqrt), R=[f"tmp{g8}"], W=[f"tmp{g8}"])
            for g8 in B8:
                P.op("dve", lambda e, g8=g8: e.tensor_tensor(out=ut[:, g8, :], in0=ut[:, g8, :], in1=tmp[:, g8, :], op=ALU.mult), R=[f"ut{g8}", f"tmp{g8}"], W=[f"ut{g8}"])
            for g8 in B8:
                P.op("dve", lambda e, g8=g8: e.tensor_tensor_scan(out=ht[:, g8, :], data0=at[:, g8, :], data1=ut[:, g8, :], initial=hcar[:, g8:g8 + 1], op0=ALU.mult, op1=ALU.add),
                     R=[f"at{g8}", f"ut{g8}", f"hcar{g8}"], W=[f"ht{g8}"])
            for g8 in B8:
                P.op("dve", lambda e, g8=g8: e.tensor_copy(out=hcar[:, g8:g8 + 1], in_=ht[:, g8, G - 1:G]), R=[f"ht{g8}"], W=[f"hcar{g8}"])
            if not scan_only:
                for g8 in B8:
                    P.op("dve", lambda e, g8=g8: e.tensor_tensor(out=hmT[:, g8, :], in0=ht[:, g8, :], in1=gg[:, g8, :], op=ALU.mult), R=[f"ht{g8}", "gg"], W=["hmT"])
            if scan_only:
                continue
            for ck in range(4):
                c0 = ck * 64
                hx = hat[mixi]; hk = f"hat{mixi}"; mixi ^= 1
                self.xattn_chunk(kxT, vxaug, xqT, c0, 64, hx[:, :], hk, Esb, rsb)
                pb = ps[:, 2, :].bitcast(BF16)
                for c2 in range(2):
                    P.op("pe", lambda e, c2=c2: e.transpose(out=pb[:, c2 * 64:(c2 + 1) * 64], in_=hx[:, c2 * 128:(c2 + 1) * 128], identity=self.identb[0:64, 0:64]),
                         R=[hk, "identb"], W=["ps2"])
                P.op("act", lambda e: e.copy(out=haT[:, :, c0:c0 + 64], in_=pb[:, 0:128].rearrange("p (k t) -> p k t", t=64)), R=["ps2"], W=["haT"])
            banks = {0: (0, 1), 1: (5, 6)}
            for cbk in range(4):
                wi = sched.get()
                wv = self.wblk[wi][:].rearrange("p k c -> p (k c)")[:, 0:2560].rearrange("p (ch c) -> p ch c", c=256)
                for tt in range(2):
                    bk = banks[tt][cbk // 2]
                    dst = ps[:, bk, (cbk % 2) * 256:(cbk % 2 + 1) * 256]
                    for ch in range(10):
                        if ch < 8:
                            lhsT = hmT[:, ch, tt * 128:(tt + 1) * 128]; rhs = wv[0:RB, ch, :]; rk = "hmT"
                        else:
                            lhsT = haT[:, ch - 8, tt * 128:(tt + 1) * 128]; rhs = wv[:, ch, :]; rk = "haT"
                        P.op("pe", lambda e, lhsT=lhsT, rhs=rhs, dst=dst, ch=ch: e.matmul(dst, lhsT=lhsT, rhs=rhs, start=(ch == 0), stop=(ch == 9)),
                             R=[rk, f"wblk{wi}"], W=[f"ps{bk}"])
            for tt in range(2):
                b0 = banks[tt][0]
                self.resid_ln(2 * g + tt, ps[:, b0:b0 + 2, :], [f"ps{b0}", f"ps{b0 + 1}"], lng, lnb, xs, st, ag)
        if scan_only:
            if halo_src is None:
                d_hend = self.outp("hend", [RB, 8])
                P.dma("sp", d_hend, hcar[:], R=[f"hcar{k}" for k in range(8)], chan="out")
            else:
                P.dma("sp", self.cc2src, hcar[:], R=[f"hcar{k}" for k in range(8)], W=["cc2src"])
        P.barrier(pool_dma=False)


def _l1_cast(self):
    P = self.P
    d_w1 = self.inp("w1", [NB1, 128, 4096])
    self.w1b = self.scratch("w1b", [NB1, 128, 4096], BF16)
    for b in range(NB1):
        P.dma("pool", self.w1b[b].rearrange("p (a c) -> p a c", c=2048), d_w1[b].rearrange("p (a c) -> p a c", c=2048), W=[f"w1b{b}"])
    P.barrier()


Builder.l1 = _l1_method
Builder.l1_cast = _l1_cast


def _run(stages, inputs, x_cur=None, hinit=None, **kw):
    b, es = build(stages, **kw)
    maps = core_inputs(inputs, stages, x_cur=x_cur, hinit=hinit)
    maps = [{k: v for k, v in m.items() if k in b.din} for m in maps]
    res = run_bass_kernel_spmd(b.nc, maps, core_ids=list(range(8)))
    es.close()
    return res.results


def _gather(results, name="out"):
    x = np.empty((4, 2 * T, D), np.float32)
    for c in range(8):
        x[c // 2, (c % 2) * T:(c % 2 + 1) * T] = results[c][name]
    return x


def kernel(**inputs):
    inputs = {k: np.asarray(v) for k, v in inputs.items()}
    return _gather(_run(["fused"], inputs))
```
